# Optimizing a Trainium2 kernel written in Bass

```python
import math, functools
import jax, jax.numpy as jnp
from jax import lax
import numpy as np

D_MODEL = 1024
BATCH = 8
SEQ = 2048
DEPTH = 2

GRID_W = 64
CTX_LEN = 256
HEAD_DIM = 64
D_MIX = D_MODEL
A_WIDTH = D_MIX // 4
B_WIDTH = 3 * D_MIX // 8
C_WIDTH = D_MIX - A_WIDTH - B_WIDTH
A_BLOCKS = A_WIDTH // HEAD_DIM
B_HEADS = B_WIDTH // HEAD_DIM
C_HEADS = C_WIDTH // HEAD_DIM
CONV_W = 4
CONV_PAD_LEFT = 2
LRU_C = 8.0
GLA_RANK = 16
GLA_TAU = 16.0
CHUNK = 64
D_FF = 4 * D_MODEL
POS_BASE = 10000.0
EPS = 1e-6
SPLIT_SIZES = [A_WIDTH] * 2 + [B_WIDTH] * 4 + [GLA_RANK] + [C_WIDTH] * 4 + [C_HEADS] * 4
D_IN = sum(SPLIT_SIZES)

kernel_name = "hybrid_rglru_gla_mlstm_dit"


def rmsnorm(x, g):
    xf = x.astype(jnp.float32)
    y = xf * lax.rsqrt(jnp.mean(xf * xf, axis=-1, keepdims=True) + EPS)
    return (y * g).astype(x.dtype)


def head_rmsnorm(y, g):
    yh = y.reshape(*y.shape[:-1], -1, HEAD_DIM).astype(jnp.float32)
    yh = yh * lax.rsqrt(jnp.mean(yh * yh, axis=-1, keepdims=True) + EPS)
    return yh.reshape(y.shape) * g


def modulate(h, shift, scale):
    return h * (1.0 + scale) + shift


def grid_pos_embed(rows):
    row = jnp.repeat(jnp.arange(rows, dtype=jnp.float32), GRID_W)
    col = jnp.tile(jnp.arange(GRID_W, dtype=jnp.float32), rows)
    quarter = D_MODEL // 4
    freqs = jnp.exp(-math.log(POS_BASE) * jnp.arange(quarter, dtype=jnp.float32) / quarter)
    ar = row[:, None] * freqs
    ac = col[:, None] * freqs
    return jnp.concatenate([jnp.sin(ar), jnp.cos(ar), jnp.sin(ac), jnp.cos(ac)], axis=-1)


def split_proj(p):
    idx = np.cumsum(SPLIT_SIZES)[:-1].tolist()
    return jnp.split(p, idx, axis=-1)


def dwconv_centred(x, w, b):
    out = lax.conv_general_dilated(
        x, w.astype(x.dtype)[:, None, :], window_strides=(1,),
        padding=[(CONV_PAD_LEFT, CONV_W - 1 - CONV_PAD_LEFT)],
        dimension_numbers=('NWC', 'WIO', 'NWC'), feature_group_count=x.shape[-1])
    return out + b


def _lin_combine(e1, e2):
    a1, b1 = e1
    a2, b2 = e2
    return a1 * a2, a2 * b1 + b2


def run_direction(fn, seqs, state, reverse):
    if reverse:
        out, fin = fn(*(jnp.flip(s, axis=1) for s in seqs), state)
        return jnp.flip(out, axis=1), fin
    return fn(*seqs, state)


def rglru_scan(xc, h0, gate_w, gate_b, lam):
    Bsz, T, _ = xc.shape
    xb = xc.reshape(Bsz, T, A_BLOCKS, HEAD_DIM)
    gates = jnp.einsum('btkd,gkde->gbtke', xb, gate_w).reshape(2, Bsz, T, A_WIDTH) + gate_b[:, None, None, :]
    r = jax.nn.sigmoid(gates[0])
    i = jax.nn.sigmoid(gates[1])
    log_a = -LRU_C * r * jax.nn.softplus(-lam)
    a = jnp.exp(log_a)
    b = jnp.sqrt(-jnp.expm1(2.0 * log_a)) * (i * xc)
    a_cum, h = lax.associative_scan(_lin_combine, (a, b), axis=1)
    h = h + a_cum * h0[:, None, :]
    return h, h[:, -1]


def gla_chunked(q, k, v, log_alpha, s0):
    Bsz, T, H, dk = q.shape
    N = T // CHUNK
    q, k, v, la = (t.reshape(Bsz, N, CHUNK, H, -1) for t in (q, k, v, log_alpha))
    b = jnp.cumsum(la, axis=2)
    b_last = b[:, :, -1]
    q_dec = q * jnp.exp(b)
    k_dec = k * jnp.exp(-b)
    causal = jnp.tril(jnp.ones((CHUNK, CHUNK), dtype=bool))
    scores = jnp.where(causal, jnp.einsum('bnlhd,bnshd->bnhls', q_dec, k_dec), 0.0)
    o_intra = jnp.einsum('bnhls,bnshe->bnlhe', scores, v)
    k_end = k * jnp.exp(b_last[:, :, None] - b)
    u = jnp.einsum('bnlhd,bnlhe->bnhde', k_end, v)
    decay = jnp.exp(b_last)

    def step(s, inp):
        d, uu = inp
        return d[..., None] * s + uu, s

    s_final, s_prev = lax.scan(step, s0, (jnp.moveaxis(decay, 1, 0), jnp.moveaxis(u, 1, 0)))
    s_prev = jnp.moveaxis(s_prev, 0, 1)
    o_inter = jnp.einsum('bnlhd,bnhde->bnlhe', q_dec, s_prev)
    return (o_intra + o_inter).reshape(Bsz, T, H, -1), s_final


def mlstm_chunked(q, k, v, i_pre, log_f, state):
    C0, n0, m0 = state
    Bsz, T, H, d = q.shape
    N = T // CHUNK
    q, k, v = (t.reshape(Bsz, N, CHUNK, H, d) for t in (q, k, v))
    ig = jnp.moveaxis(i_pre.reshape(Bsz, N, CHUNK, H), 3, 2)
    F = jnp.cumsum(jnp.moveaxis(log_f.reshape(Bsz, N, CHUNK, H), 3, 2), axis=-1)
    F_last = F[..., -1]
    causal = jnp.tril(jnp.ones((CHUNK, CHUNK), dtype=bool))
    log_d = jnp.where(causal, F[..., :, None] - F[..., None, :] + ig[..., None, :], -jnp.inf)
    w_end = F_last[..., None] - F + ig
    m_loc = jnp.max(w_end, axis=-1)
    e_end = jnp.exp(w_end - m_loc[..., None])
    U = jnp.einsum('bnhl,bnlhd,bnlhe->bnhde', e_end, k, v)
    nu = jnp.einsum('bnhl,bnlhd->bnhd', e_end, k)

    def step(carry, inp):
        C, n, m = carry
        fl, ml, uu, nn = inp
        m_new = jnp.maximum(fl + m, ml)
        a = jnp.exp(fl + m - m_new)
        b = jnp.exp(ml - m_new)
        C_new = a[..., None, None] * C + b[..., None, None] * uu
        n_new = a[..., None] * n + b[..., None] * nn
        return (C_new, n_new, m_new), (C, n, m)

    xs = tuple(jnp.moveaxis(t, 1, 0) for t in (F_last, m_loc, U, nu))
    final, prev = lax.scan(step, (C0, n0, m0), xs)
    C_prev, n_prev, m_prev = (jnp.moveaxis(t, 0, 1) for t in prev)
    g = F + m_prev[..., None]
    m_row = jnp.maximum(g, jnp.max(log_d, axis=-1))
    w = jnp.einsum('bnlhd,bnshd->bnhls', q, k) * jnp.exp(log_d - m_row[..., None])
    e_in = jnp.exp(g - m_row)
    num = jnp.einsum('bnhls,bnshe->bnlhe', w, v) + jnp.einsum('bnhl,bnlhd,bnhde->bnlhe', e_in, q, C_prev)
    den = jnp.sum(w, axis=-1) + e_in * jnp.einsum('bnlhd,bnhd->bnhl', q, n_prev)
    den = jnp.maximum(jnp.abs(den), jnp.exp(-m_row))
    h = num / jnp.moveaxis(den, 2, 3)[..., None]
    return h.reshape(Bsz, T, H, d), final


def zero_states(Bsz):
    f32 = jnp.float32
    lru = jnp.zeros((Bsz, A_WIDTH), f32)
    gla = jnp.zeros((Bsz, B_HEADS, HEAD_DIM, HEAD_DIM), f32)
    ml = (jnp.zeros((Bsz, C_HEADS, HEAD_DIM, HEAD_DIM), f32), jnp.zeros((Bsz, C_HEADS, HEAD_DIM), f32),
          jnp.zeros((Bsz, C_HEADS), f32))
    return ((lru, lru), (gla, gla), (ml, ml))


def token_mixers(proj, conv_w, conv_b, lru_gate_w, lru_gate_b, lru_lambda, gla_w2, gla_b2,
                 mlstm_gate_b, head_norm_g, states):
    out_dtype = proj.dtype
    proj = proj.astype(jnp.float32)
    (xa, ga, qb, kb, vb, gb, lr, qc, kc, vc, oc, i_fw, i_bw, f_fw, f_bw) = split_proj(proj)
    Bsz, T, _ = proj.shape
    lru_h0, gla_s0, ml_s0 = states

    def heads(t):
        return t.reshape(Bsz, T, -1, HEAD_DIM)

    xconv = dwconv_centred(xa, conv_w, conv_b)
    ya, lru_fin = [], []
    for d in range(2):
        fn = functools.partial(rglru_scan, gate_w=lru_gate_w[d], gate_b=lru_gate_b[d], lam=lru_lambda[d])
        h, fin = run_direction(fn, (xconv,), lru_h0[d], d == 1)
        ya.append(h)
        lru_fin.append(fin)

    qg = heads(qb) * HEAD_DIM ** -0.5
    kg, vg = heads(kb), heads(vb)
    yb, gla_fin = [], []
    for d in range(2):
        la = heads(jax.nn.log_sigmoid(lr @ gla_w2[d] + gla_b2[d]) / GLA_TAU)
        o, fin = run_direction(gla_chunked, (qg, kg, vg, la), gla_s0[d], d == 1)
        yb.append(o.reshape(Bsz, T, B_WIDTH))
        gla_fin.append(fin)

    qm = heads(qc) * HEAD_DIM ** -0.5
    km, vm = heads(kc), heads(vc)
    i_pres = (i_fw, i_bw)
    f_pres = (f_fw, f_bw)
    yc, ml_fin = [], []
    for d in range(2):
        ipre = i_pres[d] + mlstm_gate_b[d, 0]
        logf = jax.nn.log_sigmoid(f_pres[d] + mlstm_gate_b[d, 1])
        o, fin = run_direction(mlstm_chunked, (qm, km, vm, ipre, logf), ml_s0[d], d == 1)
        yc.append(o.reshape(Bsz, T, C_WIDTH))
        ml_fin.append(fin)

    y_raw = jnp.concatenate([ya[0] + ya[1], yb[0] + yb[1], yc[0] + yc[1]], axis=-1)
    gate = jnp.concatenate([jax.nn.gelu(ga), jax.nn.silu(gb), jax.nn.sigmoid(oc)], axis=-1)
    y = head_rmsnorm(y_raw, head_norm_g) * gate
    return y.astype(out_dtype), ((lru_fin[0], lru_fin[1]), (gla_fin[0], gla_fin[1]), (ml_fin[0], ml_fin[1]))


def sq_relu_mlp(h, w1, w2):
    return jnp.square(jax.nn.relu(h @ w1)) @ w2


def setup_inputs(seed: int = 0) -> dict:
    key = jax.random.key(seed)
    ks = jax.random.split(key, 24)

    def nrm(k, shape, s):
        return s * jax.random.normal(k, shape, jnp.float32)

    x = nrm(ks[0], (BATCH, SEQ, D_MODEL), 1.0)
    c = nrm(ks[1], (BATCH, D_MODEL), 1.0)
    ctx = nrm(ks[2], (BATCH, CTX_LEN, D_MODEL), 1.0)
    c_ctx = nrm(ks[3], (D_MODEL,), 1.0)
    ada_w = nrm(ks[4], (DEPTH, D_MODEL, 6 * D_MODEL), D_MODEL ** -0.5)
    ada_b = nrm(ks[5], (DEPTH, 6 * D_MODEL), 0.01)
    norm1_g = 1.0 + nrm(ks[6], (DEPTH, D_MODEL), 0.02)
    norm2_g = 1.0 + nrm(ks[7], (DEPTH, D_MODEL), 0.02)
    w_in = nrm(ks[8], (DEPTH, D_MODEL, D_IN), D_MODEL ** -0.5)
    conv_w = nrm(ks[9], (DEPTH, CONV_W, A_WIDTH), CONV_W ** -0.5)
    conv_b = nrm(ks[10], (DEPTH, A_WIDTH), 0.01)
    lru_gate_w = nrm(ks[11], (DEPTH, 2, 2, A_BLOCKS, HEAD_DIM, HEAD_DIM), HEAD_DIM ** -0.5)
    lru_gate_b = nrm(ks[12], (DEPTH, 2, 2, A_WIDTH), 0.01)
    p = jax.random.uniform(ks[13], (DEPTH, 2, A_WIDTH), jnp.float32, 0.9, 0.999)
    lru_lambda = jnp.log(p) - jnp.log1p(-p)
    gla_w2 = nrm(ks[14], (DEPTH, 2, GLA_RANK, B_WIDTH), GLA_RANK ** -0.5)
    gla_b2 = 2.0 + nrm(ks[15], (DEPTH, 2, B_WIDTH), 0.1)
    i_bias = nrm(ks[16], (DEPTH, 2, C_HEADS), 0.1)
    f_bias = jnp.linspace(3.0, 6.0, C_HEADS, dtype=jnp.float32) + nrm(ks[17], (DEPTH, 2, C_HEADS), 0.1)
    mlstm_gate_b = jnp.stack([i_bias, f_bias], axis=2)
    head_norm_g = 1.0 + nrm(ks[18], (DEPTH, D_MIX), 0.02)
    w_out = nrm(ks[19], (DEPTH, D_MIX, D_MODEL), D_MIX ** -0.5)
    mlp_w1 = nrm(ks[20], (DEPTH, D_MODEL, D_FF), D_MODEL ** -0.5)
    mlp_w2 = nrm(ks[21], (DEPTH, D_FF, D_MODEL), D_FF ** -0.5)
    final_g = 1.0 + nrm(ks[22], (D_MODEL,), 0.02)
    return {"x": x, "c": c, "ctx": ctx, "c_ctx": c_ctx, "ada_w": ada_w, "ada_b": ada_b,
            "norm1_g": norm1_g, "norm2_g": norm2_g, "w_in": w_in, "conv_w": conv_w, "conv_b": conv_b,
            "lru_gate_w": lru_gate_w, "lru_gate_b": lru_gate_b, "lru_lambda": lru_lambda,
            "gla_w2": gla_w2, "gla_b2": gla_b2, "mlstm_gate_b": mlstm_gate_b, "head_norm_g": head_norm_g,
            "w_out": w_out, "mlp_w1": mlp_w1, "mlp_w2": mlp_w2, "final_g": final_g}


def reference(x, c, ctx, c_ctx, ada_w, ada_b, norm1_g, norm2_g, w_in, conv_w, conv_b, lru_gate_w,
              lru_gate_b, lru_lambda, gla_w2, gla_b2, mlstm_gate_b, head_norm_g, w_out, mlp_w1, mlp_w2,
              final_g):
    Bsz, n_lat, _ = x.shape
    ROWS = n_lat // GRID_W
    x = x + grid_pos_embed(ROWS).astype(x.dtype)
    xc = ctx
    c_act = jax.nn.silu(c)
    cctx_act = jax.nn.silu(c_ctx)
    for l in range(DEPTH):
        mod_lat = (c_act @ ada_w[l] + ada_b[l])[:, None, :]
        mod_ctx = (cctx_act @ ada_w[l] + ada_b[l])[None, None, :]
        sh1, sc1, g1, sh2, sc2, g2 = jnp.split(mod_lat, 6, axis=-1)
        csh1, csc1, cg1, csh2, csc2, cg2 = jnp.split(mod_ctx, 6, axis=-1)
        layer_params = (conv_w[l], conv_b[l], lru_gate_w[l], lru_gate_b[l], lru_lambda[l], gla_w2[l],
                        gla_b2[l], mlstm_gate_b[l], head_norm_g[l])
        h_ctx = modulate(rmsnorm(xc, norm1_g[l]), csh1, csc1)
        y_ctx, ctx_states = token_mixers(h_ctx @ w_in[l], *layer_params, zero_states(Bsz))
        h_lat = modulate(rmsnorm(x, norm1_g[l]), sh1, sc1)
        y_lat, _ = token_mixers(h_lat @ w_in[l], *layer_params, ctx_states)
        x = x + g1 * (y_lat @ w_out[l])
        x = x + g2 * sq_relu_mlp(modulate(rmsnorm(x, norm2_g[l]), sh2, sc2), mlp_w1[l], mlp_w2[l])
        if l < DEPTH - 1:
            xc = xc + cg1 * (y_ctx @ w_out[l])
            xc = xc + cg2 * sq_relu_mlp(modulate(rmsnorm(xc, norm2_g[l]), csh2, csc2), mlp_w1[l], mlp_w2[l])
    return rmsnorm(x, final_g)
```

```python
import math
import numpy as np
import ml_dtypes
import concourse.bass as bass
import concourse.mybir as mybir
from concourse.bass_utils import run_bass_kernel_spmd

F32 = mybir.dt.float32
BF16 = mybir.dt.bfloat16
I32 = mybir.dt.int32
U8 = mybir.dt.uint8
AF = mybir.ActivationFunctionType
ALU = mybir.AluOpType

D = 1024
TL = 2048
TC = 256
T = TL + TC
NCH = T // 128
DEPTH = 2
DIN = 3624
DFF = 4096
EPS = 1e-6
ENG = ['tensor', 'vector', 'scalar', 'gpsimd', 'sync']
DSZ = {F32: 4, BF16: 2, I32: 4, U8: 1}


def _dsz(dt):
    for k, v in DSZ.items():
        if k == dt:
            return v
    return 4


def region(ap):
    es = _dsz(ap.dtype)
    aps = ap.ap
    pstep, pcnt = aps[0]
    off = ap.offset
    if pstep == 0:
        pstep = 1 << 40
    p0 = off // pstep
    fo = off % pstep
    lo = fo
    hi = fo
    for st, c in aps[1:]:
        d = st * (c - 1)
        if d < 0:
            lo += d
        else:
            hi += d
    if ap.tensor.name == 'psum':
        return ('psum', 0, 128, (lo * es) // 2048 * 2048, ((hi + 1) * es + 2047) // 2048 * 2048)
    return (ap.tensor.name, p0, p0 + pcnt, lo * es, (hi + 1) * es)


class Sched:
    def __init__(self, nc, n_dma_sems=32):
        self.nc = nc
        self.sem = {}
        for e in ENG:
            self.sem[e] = nc.semaphore('s_' + e).__enter__()
        self.dma_sems = [nc.semaphore('s_dma%d' % i).__enter__() for i in range(n_dma_sems)]
        self.dma_cnt = [0] * n_dma_sems
        self.dma_next = 0
        self.cnt = {e: 0 for e in ENG}
        self.seen = {e: {} for e in ENG}
        self.q = {e: [] for e in ENG}
        self.acc = {}
        self.ninst = 0

    def _deps(self, reads, writes, e=None):
        deps = {}
        for ap in reads:
            name, p0, p1, f0, f1 = region(ap)
            ps_ = (name == 'psum')
            for r in self.acc.get(name, ()):
                if (r[4] or (ps_ and r[5][0] != e)) and r[0] < p1 and p0 < r[1] and r[2] < f1 and f0 < r[3]:
                    k, v = r[5]
                    if deps.get(k, 0) < v:
                        deps[k] = v
        for ap in writes:
            name, p0, p1, f0, f1 = region(ap)
            for r in self.acc.get(name, ()):
                if r[0] < p1 and p0 < r[1] and r[2] < f1 and f0 < r[3]:
                    k, v = r[5]
                    if deps.get(k, 0) < v:
                        deps[k] = v
        return deps

    def _record(self, reads, writes, tok):
        for ap in writes:
            name, p0, p1, f0, f1 = region(ap)
            lst = self.acc.setdefault(name, [])
            lst[:] = [r for r in lst if not (p0 <= r[0] and r[1] <= p1 and f0 <= r[2] and r[3] <= f1)]
            lst.append((p0, p1, f0, f1, True, tok))
        for ap in reads:
            name, p0, p1, f0, f1 = region(ap)
            lst = self.acc.setdefault(name, [])
            lst[:] = [r for r in lst if not ((not r[4]) and r[5][0] == tok[0] and p0 <= r[0] and r[1] <= p1
                                             and f0 <= r[2] and r[3] <= f1)]
            lst.append((p0, p1, f0, f1, False, tok))

    def _waits(self, e, deps):
        out = []
        seen = self.seen[e]
        for k, v in deps.items():
            if seen.get(k, 0) >= v:
                continue
            seen[k] = v
            if k == e and e == 'tensor':
                continue
            out.append((k, v))
        return out

    def _semof(self, k):
        return self.sem[k] if isinstance(k, str) else self.dma_sems[k[1]]

    def op(self, e, fn, reads=(), writes=()):
        reads = [r for r in reads if r is not None and not isinstance(r, (int, float))]
        deps = self._deps(reads, writes, e)
        waits = self._waits(e, deps)
        self.cnt[e] += 1
        tok = (e, self.cnt[e])
        self._record(reads, writes, tok)
        self.q[e].append((waits, fn, self.sem[e], 1))
        self.ninst += 1
        return tok

    def dma(self, qe, out, in_, sb_reads=(), sb_writes=(), **kw):
        deps = self._deps(sb_reads, sb_writes)
        i = self.dma_next
        self.dma_next = (self.dma_next + 1) % len(self.dma_sems)
        if self.dma_cnt[i] > 0:
            k = ('dma', i)
            if deps.get(k, 0) < self.dma_cnt[i] * 16:
                deps[k] = self.dma_cnt[i] * 16
        waits = self._waits(qe, deps)
        self.dma_cnt[i] += 1
        tok = (('dma', i), self.dma_cnt[i] * 16)
        self._record(sb_reads, sb_writes, tok)
        self.q[qe].append((waits, (lambda eng, out=out, in_=in_, kw=kw: eng.dma_start(out=out, in_=in_, **kw)),
                           self.dma_sems[i], 16))
        self.ninst += 1
        return tok

    def wait_tok(self, e, tok):
        waits = self._waits(e, {tok[0]: tok[1]})
        if waits:
            self.q[e].append((waits, None, None, 0))

    def emit(self):
        nc = self.nc
        with nc.Block() as block:
            def mk(e):
                def body(eng):
                    for waits, fn, sem, inc in self.q[e]:
                        for k, v in waits:
                            eng.wait_ge(self._semof(k), v)
                        if fn is not None:
                            fn(eng).then_inc(sem, inc)
                return body
            block.tensor(mk('tensor'))
            block.vector(mk('vector'))
            block.scalar(mk('scalar'))
            block.gpsimd(mk('gpsimd'))
            block.sync(mk('sync'))


NCST = 128 * 7 + 1 + 32 + 64


def make_consts():
    p = np.arange(128)
    c = np.zeros((128, NCST), np.float32)
    o = 0
    c[:, o:o + 128] = np.eye(128); o += 128
    c[:, o:o + 128] = (p[:, None] <= p[None, :]); o += 128
    c[:, o:o + 128] = (p[:, None] >= p[None, :]); o += 128
    c[:, o:o + 128] = 1.0; o += 128
    c[:, o:o + 128] = (p[:, None] // 64 == p[None, :] // 64); o += 128
    c[:, o:o + 128] = (p[None, :] < 64); o += 128
    c[:, o:o + 128] = (p[None, :] >= 64); o += 128
    c[:, o] = p; o += 1
    c[:, o:o + 32] = np.arange(32)[None, :]; o += 32
    c[:, o:o + 64] = np.arange(64)[None, :]; o += 64
    return c


PC = {}
_o = 0
for _n, _w in [('n1g', 8), ('n2g', 8), ('adab', 48), ('hng', 8), ('convw', 8), ('convb', 2), ('lgb', 8),
               ('lam', 4), ('b2', 6), ('mgi', 6), ('mgf', 6)]:
    PC[_n] = (_o, _w)
    _o += _w
NPRM = _o


def pack_params(inp, l):
    P = np.zeros((128, NPRM), np.float32)

    def put(name, arr):
        o, w = PC[name]
        assert arr.shape == (128, w), (name, arr.shape)
        P[:, o:o + w] = arr
    put('n1g', inp['norm1_g'][l].reshape(8, 128).T)
    put('n2g', inp['norm2_g'][l].reshape(8, 128).T)
    put('adab', inp['ada_b'][l].reshape(48, 128).T)
    put('hng', inp['head_norm_g'][l].reshape(8, 128).T)
    cw = inp['conv_w'][l]
    put('convw', np.concatenate([cw[:, a * 128:(a + 1) * 128].T for a in range(2)], axis=1))
    put('convb', inp['conv_b'][l].reshape(2, 128).T)
    gb = inp['lru_gate_b'][l]
    put('lgb', np.stack([gb[d, g, a * 128:(a + 1) * 128] for a in range(2) for d in range(2) for g in range(2)], axis=1))
    lam = inp['lru_lambda'][l]
    put('lam', np.stack([lam[d, a * 128:(a + 1) * 128] for a in range(2) for d in range(2)], axis=1))
    b2 = inp['gla_b2'][l]
    put('b2', np.stack([b2[d, j * 128:(j + 1) * 128] for j in range(3) for d in range(2)], axis=1))
    mg = inp['mlstm_gate_b'][l]
    put('mgi', np.stack([np.repeat(mg[d, 0, 2 * j:2 * j + 2], 64) for j in range(3) for d in range(2)], axis=1))
    put('mgf', np.stack([np.repeat(mg[d, 1, 2 * j:2 * j + 2], 64) for j in range(3) for d in range(2)], axis=1))
    return P


def pack_gatew(inp, l):
    gw = inp['lru_gate_w'][l]
    out = np.zeros((128, 8, 128), np.float32)
    for a in range(2):
        for d in range(2):
            for g in range(2):
                i = a * 4 + d * 2 + g
                for bb in range(2):
                    out[bb * 64:(bb + 1) * 64, i, bb * 64:(bb + 1) * 64] = gw[d, g, a * 2 + bb]
    return out.reshape(128, 8 * 128)


def pack_wrep(inp, l):
    w = inp['w_in'][l]
    blocks = []
    for j in range(3):
        for d in range(2):
            for kind in range(2):
                base = 3600 + kind * 12 + d * 6 + 2 * j
                blocks.append(np.repeat(w[:, base:base + 2], 64, axis=1))
    return np.ascontiguousarray(np.concatenate(blocks, axis=1))


def build(n_layers=DEPTH, dbg=None):
    nc = bass.Bass('TRN2', target_bir_lowering=False)

    def din(name, shape):
        return nc.dram_tensor(name, list(shape), F32, kind='ExternalInput').ap()
    x_d = din('x', [TL, D]); ctx_d = din('ctx', [TC, D]); cc_d = din('cc', [128, 16])
    adaw_d = din('ada_w', [DEPTH, D, 6 * D]); win_d = din('w_in', [DEPTH, D, DIN])
    wrep_d = din('wrep', [DEPTH, D, 12 * 128]); gatew_d = din('gatew', [DEPTH, 128, 8 * 128])
    w2_d = din('gla_w2', [DEPTH, 2, 16, 384]); wout_d = din('w_out', [DEPTH, D, D])
    w1_d = din('mlp_w1', [DEPTH, D, DFF]); wm2_d = din('mlp_w2', [DEPTH, DFF, D])
    prm_d = din('prm', [DEPTH, 128, NPRM]); fg_d = din('fg', [128, D]); cst_d = din('cst', [128, NCST])
    out_d = nc.dram_tensor('out', [TL, D], F32, kind='ExternalOutput').ap()
    dbg_d = None
    if dbg:
        dbg_d = nc.dram_tensor('dbg', [128, dbg[1]], F32, kind='ExternalOutput').ap()

    s = Sched(nc)
    ARENA = 206 * 1024
    arena = nc.alloc_sbuf_tensor('arena', [128, ARENA], U8)
    psum = nc.alloc_psum_tensor('psum', [128, 4096], F32)

    class A:
        top = 0
        peak = 0
        last = 0

    def alloc(shape, dt, np_=128, at=None):
        n = 1
        for v in shape:
            n *= v
        nb = (n * _dsz(dt) + 31) // 32 * 32
        if at is None:
            off = A.top
            A.top += nb
            A.peak = max(A.peak, A.top)
        else:
            off = at
        A.last = off
        assert off + nb <= ARENA, ('arena overflow', off + nb)
        ap = arena[:, off:off + n * _dsz(dt)].bitcast(dt)
        if len(shape) == 2:
            ap = ap.rearrange('p (a b) -> p a b', b=shape[1])
        elif len(shape) == 3:
            ap = ap.rearrange('p (a b c) -> p a b c', b=shape[1], c=shape[2])
        return ap[0:np_] if np_ != 128 else ap

    def ps(off, n, dt=F32, np_=128, p0=0):
        if dt == F32:
            return psum[p0:p0 + np_, off:off + n]
        return psum[p0:p0 + np_, off:off + (n + 1) // 2].bitcast(dt)[:, 0:n]

    def R_(*aps):
        return [a for a in aps if a is not None and not isinstance(a, (int, float))]

    def act(out, in_, func, bias=None, scale=None, e='scalar'):
        kw = {}
        if bias is not None:
            kw['bias'] = bias
        if scale is not None:
            kw['scale'] = scale
        s.op(e, lambda en: en.activation(out=out, in_=in_, func=func, **kw), reads=R_(in_, bias, scale), writes=[out])

    def tt(out, in0, in1, op, e='vector'):
        s.op(e, lambda en: en.tensor_tensor(out=out, in0=in0, in1=in1, op=op), reads=[in0, in1], writes=[out])

    def ts(out, in0, s1, s2, op0, op1=None, e='vector'):
        if op1 is None:
            s.op(e, lambda en: en.tensor_scalar(out=out, in0=in0, scalar1=s1, scalar2=None, op0=op0),
                 reads=R_(in0, s1), writes=[out])
        else:
            s.op(e, lambda en: en.tensor_scalar(out=out, in0=in0, scalar1=s1, scalar2=s2, op0=op0, op1=op1),
                 reads=R_(in0, s1, s2), writes=[out])

    def stt(out, in0, sc, in1, op0, op1):
        s.op('vector', lambda en: en.scalar_tensor_tensor(out=out, in0=in0, scalar=sc, in1=in1, op0=op0, op1=op1),
             reads=R_(in0, sc, in1), writes=[out])

    def cp(out, in_, e='vector'):
        s.op(e, lambda en: en.tensor_copy(out=out, in_=in_), reads=[in_], writes=[out])

    def mset(out, v, e='vector'):
        s.op(e, lambda en: en.memset(out, v), writes=[out])

    def mmg(out, pairs):
        n = len(pairs)

        def fn(en):
            inst = None
            for i, (l, r) in enumerate(pairs):
                inst = en.matmul(out, lhsT=l, rhs=r, start=(i == 0), stop=(i == n - 1))
            return inst
        rd = []
        for l, r in pairs:
            rd += [l, r]
        s.op('tensor', fn, reads=rd, writes=[out])

    def trp(out, in_, ident):
        s.op('tensor', lambda en: en.transpose(out, in_, ident), reads=[in_, ident], writes=[out])

    def scan(out, d0, d1, init):
        s.op('vector', lambda en: en.tensor_tensor_scan(out=out, data0=d0, data1=d1, initial=init, op0=ALU.mult,
                                                         op1=ALU.add), reads=R_(d0, d1, init), writes=[out])

    def ld(out, in_, q='sync'):
        s.dma(q, out, in_, sb_writes=[out])

    X = alloc([8, T], F32)
    CST = alloc([NCST], F32)[:, 0, :] if False else alloc([1, NCST], F32)[:, 0, :]
    ld(CST, cst_d)
    IDf = CST[:, 0:128]
    PIDX = CST[:, 896:897]; RIDX = CST[:, 897:929]; CIDX = CST[:, 929:993]
    CB = alloc([7, 128], BF16)
    cp(CB, CST[:, 0:896].rearrange('p (a b) -> p a b', b=128))
    IDb, MASKF, MASKB, ONESb, BLK64, EAb, EBb = [CB[:, i, :] for i in range(7)]
    PRM = alloc([DEPTH, NPRM], F32)
    for l in range(DEPTH):
        ld(PRM[:, l, :], prm_d[l])
    CCT = alloc([1, 16], F32)[:, 0, :]
    ld(CCT, cc_d)
    CCb = alloc([8, 2], BF16)
    act(CCb, CCT.rearrange('p (k s) -> p k s', s=2), AF.Silu)
    MOD = alloc([DEPTH, 48, 2], F32)
    DRV = alloc([DEPTH, 4, 8, 2], F32) if False else None
    A1 = alloc([8, 2], F32); A2 = alloc([8, 2], F32)
    EPSC = alloc([1, 4], F32)[:, 0, :]
    mset(EPSC[:, 0:1], EPS)
    mset(EPSC[:, 1:2], 1.0)
    mset(EPSC[:, 2:3], 0.0)
    KAP = alloc([DEPTH, 4, 2], F32)
    NB = alloc([DEPTH, 18], F32)
    P_MARK = A.top

    def prm(l, name, i=0, w=1):
        o, _ = PC[name]
        return PRM[:, l, o + i:o + i + w]

    STG = [alloc([1, 1024], F32)[:, 0, :] for _ in range(2)]
    stg_rr = [0]

    def wload(out, src, e='gpsimd'):
        st = STG[stg_rr[0] % len(STG)]
        stg_rr[0] += 1
        shp = list(out.shape)
        np_ = shp[0]
        n = 1
        for v in shp[1:]:
            n *= v
        v_ = st[0:np_, 0:n]
        if len(shp) == 3:
            v_ = v_.rearrange('p (a b) -> p a b', b=shp[2])
        s.dma('sync', v_, src, sb_writes=[v_])
        if e == 'scalar':
            act(out, v_, AF.Copy)
        else:
            cp(out, v_, e=e)

    w1s_d = nc.dram_tensor('w1s', [DEPTH, 32, 128, 1024], BF16, kind='Internal').ap()
    w2s_d = nc.dram_tensor('w2s', [DEPTH, 32, 128, 1024], BF16, kind='Internal').ap()
    stok = {}
    cast_rr = [0]

    def wload_cached(out, src, scr, key):
        if key not in stok:
            e = ('vector', 'scalar')[cast_rr[0] % 2]
            cast_rr[0] += 1
            wload(out, src, e=e)
            dst = scr if len(out.shape) == 2 else scr.rearrange('p (k f) -> p k f', f=out.shape[2])
            stok[key] = s.dma('sync', dst, out, sb_reads=[out])
        else:
            s.wait_tok('sync', stok[key])
            s.dma('sync', out, scr if len(out.shape) == 2 else scr.rearrange('p (k f) -> p k f', f=out.shape[2]),
                  sb_writes=[out])

    def load_x():
        mark = A.top
        POSR = alloc([4, 32], F32); POSC = alloc([4, 64], F32)
        FRQ = alloc([1, 2], F32)[:, 0, :]
        for h in range(2):
            ts(FRQ[:, h:h + 1], PIDX, float(h * 128), None, ALU.add)
        act(FRQ, FRQ, AF.Exp, scale=-math.log(10000.0) / 256.0)
        TWO_PI = 2.0 * math.pi
        tmpi = alloc([1, 64], I32)[:, 0, :]; tmpf = alloc([1, 64], F32)[:, 0, :]; tmpa = alloc([1, 64], F32)[:, 0, :]
        for kk in range(8):
            h = kk % 2
            grp = kk // 2
            n = 32 if grp < 2 else 64
            idx = RIDX if grp < 2 else CIDX
            dst = POSR[:, kk, :] if grp < 2 else POSC[:, kk - 4, :]
            ph = 0.0 if grp % 2 == 0 else math.pi / 2
            ts(tmpa[:, 0:n], idx, FRQ[:, h:h + 1], ph, ALU.mult, ALU.add)
            ts(tmpf[:, 0:n], tmpa[:, 0:n], 1.0 / TWO_PI, None, ALU.mult)
            cp(tmpi[:, 0:n], tmpf[:, 0:n])
            cp(tmpf[:, 0:n], tmpi[:, 0:n])
            stt(tmpa[:, 0:n], tmpf[:, 0:n], -TWO_PI, tmpa[:, 0:n], ALU.mult, ALU.add)
            ts(tmpa[:, 0:n], tmpa[:, 0:n], 3.14159, -3.14159, ALU.min, ALU.max)
            act(dst, tmpa[:, 0:n], AF.Sin)
        ST = [alloc([1, D], F32)[:, 0, :] for _ in range(3)]
        for it in range(NCH):
            st = ST[it % 3]
            if it < 2:
                ld(st, ctx_d[it * 128:(it + 1) * 128, :])
            else:
                ld(st, x_d[(it - 2) * 128:(it - 1) * 128, :])
            for half in range(2):
                pb = ps(((it * 2 + half) % 4) * 512, 512)
                for q in range(4):
                    k = half * 4 + q
                    trp(pb[:, q * 128:(q + 1) * 128], st[:, k * 128:(k + 1) * 128], IDf)
                dstX = X[:, half * 4:half * 4 + 4, it * 128:(it + 1) * 128]
                if it < 2:
                    cp(dstX, pb.rearrange('p (q t) -> p q t', t=128))
                else:
                    r0 = (it - 2) * 2
                    for q in range(4):
                        k = half * 4 + q
                        o3 = X[:, k, it * 128:(it + 1) * 128].rearrange('p (r c) -> p r c', c=64)
                        i3 = pb[:, q * 128:(q + 1) * 128].rearrange('p (r c) -> p r c', c=64)
                        if k < 4:
                            pos = POSR[:, k, r0:r0 + 2].unsqueeze(2).broadcast_to([128, 2, 64])
                        else:
                            pos = POSC[:, k - 4, :].unsqueeze(1).broadcast_to([128, 2, 64])
                        tt(o3, i3, pos, ALU.add)
        A.top = mark

    CHUNKS = [(0, 256, 1)] + [(256 + 512 * i, 512, 0) for i in range(4)]
    pj_rr = [0]

    def pj_bank():
        b = pj_rr[0] % 4
        pj_rr[0] += 1
        return b * 512

    def ada(l):
        mark = A.top
        WB = [alloc([8, 128], BF16) for _ in range(4)]
        for _ in range(3):
            STG.append(alloc([1, 1024], F32)[:, 0, :])
        pm = ps(pj_bank(), 96)
        for ft in range(48):
            wb = WB[ft % 4]
            wload(wb, adaw_d[l][:, ft * 128:(ft + 1) * 128].rearrange('(k p) f -> p k f', p=128),
                  e=('vector', 'scalar')[ft % 2])
            mmg(pm[:, ft * 2:ft * 2 + 2], [(wb[:, k, :], CCb[:, k, :]) for k in range(8)])
        o, _ = PC['adab']
        tt(MOD[:, l], pm.rearrange('p (a b) -> p a b', b=2),
           PRM[:, l, o:o + 48].unsqueeze(2).broadcast_to([128, 48, 2]), ALU.add)
        del STG[2:]
        A.top = mark

    def derive(l):
        for (AA, nm, wh) in ((A1, 'n1g', 1), (A2, 'n2g', 4)):
            o, _ = PC[nm]
            g = PRM[:, l, o:o + 8].unsqueeze(2).broadcast_to([128, 8, 2])
            ts(AA, MOD[:, l, wh * 8:wh * 8 + 8, :], 1.0, None, ALU.add)
            tt(AA, AA, g, ALU.mult)
        o, _ = PC['lam']
        act(KAP[:, l, :, 0], PRM[:, l, o:o + 4], AF.Exp, scale=-1.0)
        act(KAP[:, l, :, 0], KAP[:, l, :, 0], AF.Ln, bias=EPSC[:, 1:2])
        ts(KAP[:, l, :, 1], KAP[:, l, :, 0], -16.0, None, ALU.mult)
        ts(KAP[:, l, :, 0], KAP[:, l, :, 0], -8.0, None, ALU.mult)
        o, _ = PC['b2']
        ts(NB[:, l, 0:6], PRM[:, l, o:o + 6], -1.0, None, ALU.mult)
        o, _ = PC['mgf']
        ts(NB[:, l, 6:12], PRM[:, l, o:o + 6], -1.0, None, ALU.mult)
        o, _ = PC['lgb']
        ts(NB[:, l, 12:18], PRM[:, l, o:o + 6], 1.0, None, ALU.mult)

    NML = int(dbg[0][2:]) if (dbg and dbg[0].startswith('nm')) else 9

    def norm_mod(dst, t0, n, s_, AA, BBl, BBwh, SQ, RS, TMP):
        for k in range(8):
            act(SQ[:, k, 0:n], X[:, k, t0:t0 + n], AF.Square)
        if NML < 2:
            return
        pb = ps(pj_bank(), n)
        mmg(pb, [(ONESb, SQ[:, k, 0:n]) for k in range(8)])
        if NML < 3:
            return
        act(RS[:, 0:n], pb, AF.Ln, bias=EPSC[:, 0:1], scale=1.0 / D)
        act(RS[:, 0:n], RS[:, 0:n], AF.Exp, scale=-0.5)
        if NML < 4:
            return
        for k in range(8):
            stt(TMP[:, 0:n], X[:, k, t0:t0 + n], AA[:, k, s_:s_ + 1], RS[:, 0:n], ALU.mult, ALU.mult)
            if NML < 5:
                continue
            act(dst[:, k, 0:n], TMP[:, 0:n], AF.Identity, bias=MOD[:, BBl, BBwh * 8 + k, s_:s_ + 1])

    def proj(HT, W, M, evac, chunks=CHUNKS):
        for (t0, n, s_) in chunks:
            pb = ps(pj_bank(), n, np_=M)
            mmg(pb, [(W[:, k, 0:M], HT[:, k, t0:t0 + n]) for k in range(8)])
            evac(pb, t0, n, s_)

    def finish_group(l, g8, OUT, G, Wo, last, SQ, Y, RS):
        mark = A.top
        act(SQ, OUT, AF.Square)
        for (t0, n, s_) in CHUNKS:
            pb = ps(pj_bank(), n)
            mmg(pb, [(BLK64, SQ[:, t0:t0 + n])])
            act(RS[:, 0:n], pb, AF.Ln, bias=EPSC[:, 0:1], scale=1.0 / 64)
            act(RS[:, 0:n], RS[:, 0:n], AF.Exp, scale=-0.5)
            tt(RS[:, 0:n], RS[:, 0:n], OUT[:, t0:t0 + n], ALU.mult)
            stt(Y[:, t0:t0 + n], RS[:, 0:n], prm(l, 'hng', g8), G[:, t0:t0 + n], ALU.mult, ALU.mult)
        for (t0, n, s_) in CHUNKS:
            if last and s_ == 1:
                continue
            for k in range(8):
                pb = ps(pj_bank(), n)
                mmg(pb, [(Wo[:, k * 128:(k + 1) * 128], Y[:, t0:t0 + n])])
                stt(X[:, k, t0:t0 + n], pb, MOD[:, l, 2 * 8 + k, s_:s_ + 1], X[:, k, t0:t0 + n], ALU.mult, ALU.add)
        A.top = mark

    def group_a(l, a, HT, last):
        mark = A.top
        Wx = alloc([8, 128], BF16); Wg = alloc([8, 128], BF16); Wo = alloc([1, D], BF16)[:, 0, :]
        GW = alloc([4, 128], BF16)
        wload(Wx, win_d[l][:, a * 128:(a + 1) * 128].rearrange('(k p) f -> p k f', p=128))
        wload(Wg, win_d[l][:, 256 + a * 128:256 + (a + 1) * 128].rearrange('(k p) f -> p k f', p=128))
        wload(GW, gatew_d[l][:, a * 512:(a + 1) * 512].rearrange('p (i c) -> p i c', c=128))
        wload(Wo, wout_d[l][a * 128:(a + 1) * 128, :])
        XA = alloc([1, T], F32)[:, 0, :]
        H2 = XA
        XCv = alloc([1, T], F32)[:, 0, :]
        XCb = alloc([1, T], BF16)[:, 0, :]
        G = alloc([1, T], BF16)[:, 0, :]
        AB = alloc([1, T], F32)[:, 0, :]; ab_off = A.last
        BBf = alloc([1, T], F32)[:, 0, :]; bb_off = A.last
        H = alloc([1, T], F32)[:, 0, :]
        T1 = alloc([1, 512], F32)[:, 0, :]; T2 = alloc([1, 512], F32)[:, 0, :]
        proj(HT, Wx, 128, lambda pb, t0, n, s_: cp(XA[:, t0:t0 + n], pb))

        def gelu_ev(pb, t0, n, s_):
            act(T1[:, 0:n], pb, AF.Square)
            ts(T1[:, 0:n], T1[:, 0:n], 0.044715, 1.0, ALU.mult, ALU.add)
            tt(T1[:, 0:n], T1[:, 0:n], pb, ALU.mult)
            act(T1[:, 0:n], T1[:, 0:n], AF.Sigmoid, scale=2.0 * math.sqrt(2.0 / math.pi))
            tt(G[:, t0:t0 + n], T1[:, 0:n], pb, ALU.mult)
        proj(HT, Wg, 128, gelu_ev)
        o, _ = PC['convw']
        cw = lambda j: PRM[:, l, o + a * 4 + j:o + a * 4 + j + 1]
        for (s0, n) in ((0, TC), (TC, TL)):
            ts(XCv[:, s0:s0 + n], XA[:, s0:s0 + n], cw(2), prm(l, 'convb', a), ALU.mult, ALU.add)
            for j, sh in ((0, -2), (1, -1), (3, 1)):
                lo = max(0, -sh); hi = n - max(0, sh)
                stt(XCv[:, s0 + lo:s0 + hi], XA[:, s0 + lo + sh:s0 + hi + sh], cw(j), XCv[:, s0 + lo:s0 + hi],
                    ALU.mult, ALU.add)
        cp(XCb, XCv)
        for d in range(2):
            kap = KAP[:, l, a * 2 + d, 0:1]; kap2 = KAP[:, l, a * 2 + d, 1:2]
            o, _ = PC['lgb']
            br = PRM[:, l, o + a * 4 + d * 2:o + a * 4 + d * 2 + 1]
            bi = PRM[:, l, o + a * 4 + d * 2 + 1:o + a * 4 + d * 2 + 2]
            for (t0, n, s_) in CHUNKS:
                pr = ps(pj_bank(), n)
                mmg(pr, [(GW[:, d * 2 + 0, :], XCb[:, t0:t0 + n])])
                pi = ps(pj_bank(), n)
                mmg(pi, [(GW[:, d * 2 + 1, :], XCb[:, t0:t0 + n])])
                act(T1[:, 0:n], pr, AF.Sigmoid, bias=br)
                act(AB[:, t0:t0 + n], T1[:, 0:n], AF.Exp, scale=kap)
                act(T1[:, 0:n], T1[:, 0:n], AF.Exp, scale=kap2)
                ts(T1[:, 0:n], T1[:, 0:n], -1.0, 1.0, ALU.mult, ALU.add)
                act(T1[:, 0:n], T1[:, 0:n], AF.Sqrt)
                act(T2[:, 0:n], pi, AF.Sigmoid, bias=bi)
                tt(T1[:, 0:n], T1[:, 0:n], T2[:, 0:n], ALU.mult)
                tt(BBf[:, t0:t0 + n], T1[:, 0:n], XCv[:, t0:t0 + n], ALU.mult)
            if d == 0:
                scan(H, AB, BBf, 0.0)
            else:
                scan(H2[:, 0:TC][:, ::-1], AB[:, 0:TC][:, ::-1], BBf[:, 0:TC][:, ::-1], 0.0)
                scan(H2[:, TC:T][:, ::-1], AB[:, TC:T][:, ::-1], BBf[:, TC:T][:, ::-1], H2[:, 0:1])
        tt(H, H, H2, ALU.add)
        finish_group(l, a, H, G, Wo, last, alloc([1, T], BF16, at=ab_off)[:, 0, :],
                     alloc([1, T], BF16, at=bb_off)[:, 0, :], T1)
        A.top = mark

    class Stop(Exception):
        pass
    PLIM = int(dbg[0][1:]) if (dbg and dbg[0][0] == 'p') else 99

    def cut(n):
        if PLIM == n:
            raise Stop()

    def group_pair(l, kind, j, HT, LRT, last):
        mark = A.top
        g8 = 2 + kind * 3 + j
        cq = (512 if kind == 0 else 2064) + j * 128
        ck = (896 if kind == 0 else 2448) + j * 128
        cv = (1280 if kind == 0 else 2832) + j * 128
        cg = (1664 if kind == 0 else 3216) + j * 128
        WB2 = [alloc([8, 128], BF16) for _ in range(2)]
        wb_rr = [0]

        def wcol(c0, src=None):
            Wb = WB2[wb_rr[0] % 2]
            wb_rr[0] += 1
            wload(Wb, (win_d[l] if src is None else src)[:, c0:c0 + 128].rearrange('(k p) f -> p k f', p=128))
            return Wb
        Wo = alloc([1, D], BF16)[:, 0, :]
        wload(Wo, wout_d[l][g8 * 128:(g8 + 1) * 128, :])
        if kind == 0:
            W2 = alloc([2, 128], BF16, np_=16)
            for d in range(2):
                wload(W2[:, d, :], w2_d[l][d][:, j * 128:(j + 1) * 128])
        Qb = alloc([1, T], BF16)[:, 0, :]; qb_off = A.last
        Kb = alloc([1, T], BF16)[:, 0, :]; kb_off = A.last
        V1 = alloc([NCH, 130], BF16); VA = alloc([NCH, 128], BF16); VB = alloc([NCH, 128], BF16)
        OUT = alloc([1, T], F32)[:, 0, :]
        PL = alloc([1, T], F32)[:, 0, :]
        QDs = [alloc([1, T], BF16)[:, 0, :] for _ in range(2)]
        KDs = [alloc([1, T], BF16)[:, 0, :] for _ in range(2)]
        DECs = [alloc([1, NCH], F32)[:, 0, :] for _ in range(2)]
        Rsts = [alloc([1, 130], F32)[:, 0, :] for _ in range(2)]
        Sbs = [alloc([2, 128], BF16) for _ in range(2)]
        nSbs = [alloc([2, 128], BF16) for _ in range(2)]
        T1s = [alloc([1, 512], F32)[:, 0, :] for _ in range(2)]
        T2s = [alloc([1, 128], F32)[:, 0, :] for _ in range(2)]
        SCbs = [alloc([2, 256], BF16) for _ in range(2)]
        t1_rr = [0]

        def T1n():
            t1_rr[0] += 1
            return T1s[t1_rr[0] % 2]
        proj(HT, wcol(cq), 128, lambda pb, t0, n, s_: act(Qb[:, t0:t0 + n], pb, AF.Copy, scale=0.125))
        proj(HT, wcol(ck), 128, lambda pb, t0, n, s_: cp(Kb[:, t0:t0 + n], pb))
        cut(1)
        mset(VA, 0.0); mset(VB, 0.0); mset(V1, 1.0)
        Wv = wcol(cv)
        for c in range(NCH):
            pb = ps(pj_bank(), 128)
            mmg(pb, [(HT[:, k, c * 128:(c + 1) * 128], Wv[:, k, :]) for k in range(8)])
            cp(V1[:, c, 0:128], pb)
            cp(VA[:, c, 0:64], V1[:, c, 0:64], e='gpsimd')
            cp(VB[:, c, 64:128], V1[:, c, 64:128], e='gpsimd')
        cut(2)
        sc_ = (1.0 / 16.0) if kind == 0 else 1.0
        for d in range(2):
            QD = QDs[d]; KD = KDs[d]; DEC = DECs[d]
            nb = NB[:, l, (0 if kind == 0 else 6) + j * 2 + d:(0 if kind == 0 else 6) + j * 2 + d + 1]

            def softplus_ev(pb, t0, n, s_):
                t1 = T1n()
                act(t1[:, 0:n], pb, AF.Exp, bias=nb, scale=-1.0)
                act(PL[:, t0:t0 + n], t1[:, 0:n], AF.Ln, bias=EPSC[:, 1:2])
            if kind == 0:
                for (t0, n, s_) in CHUNKS:
                    pb = ps(pj_bank(), n)
                    mmg(pb, [(W2[:, d, :], LRT[:, t0:t0 + n])])
                    softplus_ev(pb, t0, n, s_)
            else:
                proj(HT, wcol(((j * 2 + d) * 2 + 1) * 128, wrep_d[l]), 128, softplus_ev)
            for c in range(NCH):
                plc = PL[:, c * 128:(c + 1) * 128]
                if d == 0:
                    scan(plc, ONESb, plc, 0.0)
                else:
                    scan(plc[:, ::-1], ONESb, plc[:, ::-1], 0.0)
            Pv = PL.rearrange('p (c t) -> p c t', t=128)
            pend = Pv[:, :, 127] if d == 0 else Pv[:, :, 0]
            act(DEC, pend, AF.Exp, scale=-sc_)
            if kind == 0:
                for (t0, n, s_) in CHUNKS:
                    t1 = T1n()
                    act(t1[:, 0:n], PL[:, t0:t0 + n], AF.Exp, scale=-sc_)
                    tt(QD[:, t0:t0 + n], t1[:, 0:n], Qb[:, t0:t0 + n], ALU.mult)
                    t1 = T1n()
                    act(t1[:, 0:n], PL[:, t0:t0 + n], AF.Exp, scale=sc_)
                    tt(KD[:, t0:t0 + n], t1[:, 0:n], Kb[:, t0:t0 + n], ALU.mult)
            else:
                for (t0, n, s_) in CHUNKS:
                    t1 = T1n()
                    act(t1[:, 0:n], PL[:, t0:t0 + n], AF.Exp, scale=-1.0)
                    tt(QD[:, t0:t0 + n], t1[:, 0:n], Qb[:, t0:t0 + n], ALU.mult)
                bi_ = prm(l, 'mgi', j * 2 + d)

                def ig_ev(pb, t0, n, s_, KD=KD):
                    t1 = T1n()
                    tt(t1[:, 0:n], pb, PL[:, t0:t0 + n], ALU.add)
                    act(t1[:, 0:n], t1[:, 0:n], AF.Exp, bias=bi_)
                    tt(KD[:, t0:t0 + n], t1[:, 0:n], Kb[:, t0:t0 + n], ALU.mult)
                proj(HT, wcol(((j * 2 + d) * 2 + 0) * 128, wrep_d[l]), 128, ig_ev)
        KTs = [alloc([NCH, 128], BF16, at=qb_off), alloc([NCH, 128], BF16, at=kb_off)]
        KE = [alloc([1, 128], BF16)[:, 0, :] for _ in range(4)]
        ke_rr = [0]
        for c in range(NCH):
            for d in range(2):
                ke = KE[ke_rr[0] % 4]
                ke_rr[0] += 1
                ts(ke, KDs[d][:, c * 128:(c + 1) * 128], DECs[d][:, c:c + 1], None, ALU.mult, e='gpsimd')
                pt = ps(pj_bank(), 128, dt=BF16)
                trp(pt, ke, IDb)
                if d == 0:
                    act(KTs[d][:, c, :], pt, AF.Copy)
                else:
                    cp(KTs[d][:, c, :], pt)
        orders = [list(range(NCH)), [1, 0] + list(range(NCH - 1, 1, -1))]
        MASKS = [MASKF, MASKB]
        for d in range(2):
            mset(Sbs[d], 0.0)
            if kind == 1:
                mset(nSbs[d], 0.0)
        pj2 = [0]

        def pjb2():
            pj2[0] += 1
            return (pj2[0] % 2) * 512
        pus = {}

        def stageA(d, i):
            c = orders[d][i]
            tsl = slice(c * 128, (c + 1) * 128)
            par = i % 2
            QD = QDs[d]; KD = KDs[d]
            base = 2048 + d * 1024
            pscA = ps(base, 128)
            pscB = ps(base + 512, 128)
            mmg(pscA, [(KD[0:64, tsl], QD[0:64, tsl])])
            mmg(pscB, [(KD[64:128, tsl], QD[64:128, tsl])])
            psc2 = psum[:, base:base + 1024].rearrange('p (h t) -> p h t', t=512)[:, :, 0:128]
            tt(SCbs[d][:, par, :].rearrange('p (h t) -> p h t', t=128), psc2,
               MASKS[d].unsqueeze(1).broadcast_to([128, 2, 128]), ALU.mult)
            pu = ps(1024 + d * 512 + par * 256, 130)
            mmg(pu, [(KTs[d][:, c, :], V1[:, c, :])])
            pus[(d, i)] = pu

        def stageB(d, i):
            c = orders[d][i]
            tsl = slice(c * 128, (c + 1) * 128)
            par = i % 2
            QD = QDs[d]; DEC = DECs[d]; Rst = Rsts[d]; Sb = Sbs[d]; nSb = nSbs[d]; T2 = T2s[d]; SCb = SCbs[d]
            pob = pjb2()
            po = ps(pob, 128)
            mmg(po, [(Sb[:, i % 2, :], QD[:, tsl]), (VA[:, c, :], SCb[:, par, 0:128]),
                     (VB[:, c, :], SCb[:, par, 128:256])])
            if kind == 1:
                pd = ps(pob + 128, 128)
                mmg(pd, [(nSb[:, i % 2, :], QD[:, tsl]), (EAb, SCb[:, par, 0:128]), (EBb, SCb[:, par, 128:256])])
            pu = pus.pop((d, i))
            if i == 0:
                cp(Rst, pu)
            else:
                stt(Rst, Rst, DEC[:, c:c + 1], pu, ALU.mult, ALU.add)
            if i + 1 < NCH:
                nx = (i + 1) % 2
                tt(Sb[:, nx, :], Rst[:, 0:128], BLK64, ALU.mult, e='gpsimd')
                if kind == 1:
                    act(nSb[:, nx, :], BLK64, AF.Copy, scale=Rst[:, 128:129])
            if kind == 0:
                tt(OUT[:, tsl], po, OUT[:, tsl], ALU.add)
            else:
                act(T2, pd, AF.Abs)
                s.op('vector', lambda en, o_=T2: en.reciprocal(out=o_, in_=o_), reads=[T2], writes=[T2])
                stt(T2, T2, 1.0, po, ALU.min, ALU.mult)
                tt(OUT[:, tsl], OUT[:, tsl], T2, ALU.add, e='gpsimd')
        mset(OUT, 0.0, e='gpsimd')
        stageA(0, 0); stageA(1, 0)
        for i in range(NCH):
            if i + 1 < NCH:
                stageA(0, i + 1); stageA(1, i + 1)
            stageB(0, i); stageB(1, i)
        G = QDs[0]
        Wg = wcol(cg)
        if kind == 0:
            proj(HT, Wg, 128, lambda pb, t0, n, s_: act(G[:, t0:t0 + n], pb, AF.Silu))
        else:
            proj(HT, Wg, 128, lambda pb, t0, n, s_: act(G[:, t0:t0 + n], pb, AF.Sigmoid))
        finish_group(l, g8, OUT, G, Wo, last, KDs[0], alloc([1, T], BF16, at=qb_off)[:, 0, :], T1s[0])
        A.top = mark

    def mlp(l, last):
        mark = A.top
        W1 = [alloc([8, 128], BF16) for _ in range(4)]
        W2b = [alloc([1, D], BF16)[:, 0, :] for _ in range(4)]
        H2 = alloc([8, 1024], BF16)
        A1T = alloc([32, 1024], BF16); a1t_off = A.last
        SQ = alloc([8, 512], BF16, at=a1t_off); RS = alloc([1, 512], F32)[:, 0, :]; TMP = alloc([1, 512], F32)[:, 0, :]
        TM3 = [alloc([1, 512], F32)[:, 0, :] for _ in range(3)]
        for _ in range(3):
            STG.append(alloc([1, 1024], F32)[:, 0, :])
        tm_rr = [0]
        passes = [(256, 1024, 0), (1280, 1024, 0)]
        if not last:
            passes = [(0, 256, 1)] + passes
        for (p0, pn, s_) in passes:
            subs = [(p0 + i, min(512, pn - i)) for i in range(0, pn, 512)]
            for (t0, n) in subs:
                norm_mod(H2[:, :, t0 - p0:t0 - p0 + n], t0, n, s_, A2, l, 3, SQ, RS, TMP)
            for f in range(32):
                w1 = W1[f % 4]
                wload_cached(w1, w1_d[l][:, f * 128:(f + 1) * 128].rearrange('(k p) f -> p k f', p=128),
                             w1s_d[l, f], ('w1', l, f))
                for (t0, n) in subs:
                    pb = ps(pj_bank(), n)
                    mmg(pb, [(w1[:, k, :], H2[:, k, t0 - p0:t0 - p0 + n]) for k in range(8)])
                    tm_rr[0] += 1
                    tm = TM3[tm_rr[0] % 3]
                    act(tm[:, 0:n], pb, AF.Relu)
                    tt(A1T[:, f, t0 - p0:t0 - p0 + n], tm[:, 0:n], tm[:, 0:n], ALU.mult)
            for (t0, n) in subs:
                for f in range(32):
                    w2 = W2b[f % 4]
                    wload_cached(w2, wm2_d[l][f * 128:(f + 1) * 128, :], w2s_d[l, f], ('w2', l, f))
                    for k in range(8):
                        pb = ps(k * 512, n)
                        s.op('tensor', (lambda en, pb=pb, w=w2[:, k * 128:(k + 1) * 128],
                                        r=A1T[:, f, t0 - p0:t0 - p0 + n], st_=(f == 0), sp_=(f == 31):
                                        en.matmul(pb, lhsT=w, rhs=r, start=st_, stop=sp_)),
                             reads=[w2[:, k * 128:(k + 1) * 128], A1T[:, f, t0 - p0:t0 - p0 + n]], writes=[pb])
                for k in range(8):
                    pb = ps(k * 512, n)
                    stt(X[:, k, t0:t0 + n], pb, MOD[:, l, 5 * 8 + k, s_:s_ + 1], X[:, k, t0:t0 + n], ALU.mult, ALU.add)
        del STG[2:]
        A.top = mark

    def final():
        mark = A.top
        FG = alloc([1, D], F32)[:, 0, :]
        ld(FG, fg_d)
        OT = [alloc([1, D], F32)[:, 0, :] for _ in range(2)]
        SS = alloc([1, 4], F32)[:, 0, :]
        JK = alloc([1, D], F32)[:, 0, :]
        toks = []
        for it in range(TL // 128):
            ot = OT[it % 2]
            t0 = TC + it * 128
            for half in range(2):
                pb = ps(((it * 2 + half) % 4) * 512, 512)
                for q in range(4):
                    k = half * 4 + q
                    trp(pb[:, q * 128:(q + 1) * 128], X[:, k, t0:t0 + 128], IDf)
                cp(ot[:, half * 512:(half + 1) * 512], pb)
            c0 = it % 2
            s.op('scalar', lambda en, ot=ot, c0=c0: en.activation(out=JK, in_=ot, func=AF.Square, accum_out=SS[:, c0:c0 + 1]),
                 reads=[ot], writes=[JK, SS[:, c0:c0 + 1]])
            act(SS[:, 2 + c0:3 + c0], SS[:, c0:c0 + 1], AF.Ln, bias=EPSC[:, 0:1], scale=1.0 / D)
            act(SS[:, 2 + c0:3 + c0], SS[:, 2 + c0:3 + c0], AF.Exp, scale=-0.5)
            stt(ot, ot, SS[:, 2 + c0:3 + c0], FG, ALU.mult, ALU.mult)
            toks.append(s.dma('sync', out_d[it * 128:(it + 1) * 128, :], ot, sb_reads=[ot]))
        for tk in toks[-4:]:
            s.wait_tok('sync', tk)
        A.top = mark

    def dump(ap, ncol):
        mark = A.top
        DB = alloc([1, dbg[1]], F32)[:, 0, :]
        mset(DB, 0.0)
        cp(DB[:, 0:ncol], ap)
        tk = s.dma('sync', dbg_d, DB, sb_reads=[DB])
        s.wait_tok('sync', tk)
        A.top = mark

    load_x()
    stage = dbg[0] if dbg else None
    if stage == 'x':
        dump(X[:, int(dbg[2]), :], T)
        n_layers = 0
    for l in range(n_layers):
        last = (l == DEPTH - 1)
        ada(l)
        if stage == 'ada':
            dump(MOD[:, l].rearrange('p a b -> p (a b)'), 96)
            break
        derive(l)
        if stage == 'drv':
            dump(A1.rearrange('p a b -> p (a b)'), 16)
            break
        mark_l = A.top
        HT = alloc([8, T], BF16)
        SQ = alloc([8, 512], BF16); RS = alloc([1, 512], F32)[:, 0, :]; TMP = alloc([1, 512], F32)[:, 0, :]
        for (t0, n, s_) in CHUNKS:
            norm_mod(HT[:, :, t0:t0 + n], t0, n, s_, A1, l, 0, SQ, RS, TMP)
        A.top = mark_l + 8 * T * 2
        if stage and (stage == 'h%d' % l or stage.startswith('nm')):
            mark = A.top
            HF = alloc([1, T], F32)[:, 0, :]
            cp(HF, HT[:, 0, :])
            dump(HF, T)
            A.top = mark
            break
        LRT = alloc([1, T], BF16, np_=16)[:, 0, :]
        Wl = alloc([8, 16], BF16)
        wload(Wl, win_d[l][:, 2048:2064].rearrange('(k p) f -> p k f', p=128))
        proj(HT, Wl, 16, lambda pb, t0, n, s_: cp(LRT[:, t0:t0 + n], pb))
        groups = [('a', 0), ('a', 1)] + [('p', 0, j) for j in range(3)] + [('p', 1, j) for j in range(3)]
        if stage and stage.startswith('g'):
            groups = groups[:int(stage[1:]) + 1]
        if stage and stage[0] == 'p':
            groups = [('p', int(dbg[2]), 0)]
        try:
            for g in groups:
                if g[0] == 'a':
                    group_a(l, g[1], HT, last)
                else:
                    group_pair(l, g[1], g[2], HT, LRT, last)
        except Stop:
            pass
        A.top = mark_l
        if stage and (stage.startswith('g') or stage[0] == 'p'):
            dump(X[:, int(dbg[2]), :], T)
            break
        mlp(l, last)
        if stage == 'l%d' % l:
            dump(X[:, int(dbg[2]), :], T)
            break
    if not stage:
        final()
    s.emit()
    print('ninst', s.ninst, 'arena peak', A.peak, 'persistent', P_MARK)
    return nc


_NC = {}


def prep_inputs(inp):
    inp = {k: np.asarray(v) for k, v in inp.items()}
    f = lambda a: np.ascontiguousarray(a, dtype=np.float32)
    shared = {
        'ada_w': f(inp['ada_w']), 'w_in': f(inp['w_in']),
        'wrep': f(np.stack([pack_wrep(inp, l) for l in range(DEPTH)])),
        'gatew': f(np.stack([pack_gatew(inp, l) for l in range(DEPTH)])),
        'gla_w2': f(inp['gla_w2']), 'w_out': f(inp['w_out']), 'mlp_w1': f(inp['mlp_w1']), 'mlp_w2': f(inp['mlp_w2']),
        'prm': f(np.stack([pack_params(inp, l) for l in range(DEPTH)])),
        'fg': f(np.broadcast_to(inp['final_g'][None, :], (128, D))), 'cst': make_consts(),
    }
    maps = []
    for b in range(8):
        cc = np.zeros((128, 16), np.float32)
        cc[:, 0::2] = inp['c'][b].reshape(8, 128).T
        cc[:, 1::2] = inp['c_ctx'].reshape(8, 128).T
        m = dict(shared)
        m['x'] = f(inp['x'][b]); m['ctx'] = f(inp['ctx'][b]); m['cc'] = cc
        maps.append(m)
    return maps


def kernel(**inputs):
    if 'nc' not in _NC:
        _NC['nc'] = build()
    maps = prep_inputs(inputs)
    res = run_bass_kernel_spmd(_NC['nc'], maps, core_ids=list(range(8)))
    return np.stack([np.asarray(res.results[b]['out'], dtype=np.float32) for b in range(8)], axis=0)
```

```python
import math
import numpy as np
import ml_dtypes
import concourse.bass as bass
import concourse.mybir as mybir
from concourse.bass_utils import run_bass_kernel_spmd

F32 = mybir.dt.float32
BF16 = mybir.dt.bfloat16
I32 = mybir.dt.int32
U8 = mybir.dt.uint8
AF = mybir.ActivationFunctionType
ALU = mybir.AluOpType

D = 1024
TL = 2048
TC = 256
T = TL + TC
NCH = T // 128
DEPTH = 2
DIN = 3624
DFF = 4096
EPS = 1e-6
ENG = ['tensor', 'vector', 'scalar', 'gpsimd', 'sync']
DSZ = {F32: 4, BF16: 2, I32: 4, U8: 1}


def _dsz(dt):
    for k, v in DSZ.items():
        if k == dt:
            return v
    return 4


def region(ap):
    es = _dsz(ap.dtype)
    aps = ap.ap
    pstep, pcnt = aps[0]
    off = ap.offset
    if pstep == 0:
        pstep = 1 << 40
    p0 = off // pstep
    fo = off % pstep
    lo = fo
    hi = fo
    for st, c in aps[1:]:
        d = st * (c - 1)
        if d < 0:
            lo += d
        else:
            hi += d
    if ap.tensor.name == 'psum':
        return ('psum', 0, 128, (lo * es) // 2048 * 2048, ((hi + 1) * es + 2047) // 2048 * 2048)
    return (ap.tensor.name, p0, p0 + pcnt, lo * es, (hi + 1) * es)


class Sched:
    def __init__(self, nc, n_dma_sems=32):
        self.nc = nc
        self.sem = {}
        for e in ENG:
            self.sem[e] = nc.semaphore('s_' + e).__enter__()
        self.dma_sems = [nc.semaphore('s_dma%d' % i).__enter__() for i in range(n_dma_sems)]
        self.dma_cnt = [0] * n_dma_sems
        self.dma_next = 0
        self.cnt = {e: 0 for e in ENG}
        self.seen = {e: {} for e in ENG}
        self.q = {e: [] for e in ENG}
        self.acc = {}
        self.ninst = 0

    def _deps(self, reads, writes, e=None):
        deps = {}
        for ap in reads:
            name, p0, p1, f0, f1 = region(ap)
            ps_ = (name == 'psum')
            for r in self.acc.get(name, ()):
                if (r[4] or (ps_ and r[5][0] != e)) and r[0] < p1 and p0 < r[1] and r[2] < f1 and f0 < r[3]:
                    k, v = r[5]
                    if deps.get(k, 0) < v:
                        deps[k] = v
        for ap in writes:
            name, p0, p1, f0, f1 = region(ap)
            for r in self.acc.get(name, ()):
                if r[0] < p1 and p0 < r[1] and r[2] < f1 and f0 < r[3]:
                    k, v = r[5]
                    if deps.get(k, 0) < v:
                        deps[k] = v
        return deps

    def _record(self, reads, writes, tok):
        for ap in writes:
            name, p0, p1, f0, f1 = region(ap)
            lst = self.acc.setdefault(name, [])
            lst[:] = [r for r in lst if not (p0 <= r[0] and r[1] <= p1 and f0 <= r[2] and r[3] <= f1)]
            lst.append((p0, p1, f0, f1, True, tok))
        for ap in reads:
            name, p0, p1, f0, f1 = region(ap)
            lst = self.acc.setdefault(name, [])
            lst[:] = [r for r in lst if not ((not r[4]) and r[5][0] == tok[0] and p0 <= r[0] and r[1] <= p1
                                             and f0 <= r[2] and r[3] <= f1)]
            lst.append((p0, p1, f0, f1, False, tok))

    def _waits(self, e, deps):
        out = []
        seen = self.seen[e]
        for k, v in deps.items():
            if seen.get(k, 0) >= v:
                continue
            seen[k] = v
            if k == e and e == 'tensor':
                continue
            out.append((k, v))
        return out

    def _semof(self, k):
        return self.sem[k] if isinstance(k, str) else self.dma_sems[k[1]]

    def op(self, e, fn, reads=(), writes=()):
        reads = [r for r in reads if r is not None and not isinstance(r, (int, float))]
        deps = self._deps(reads, writes, e)
        waits = self._waits(e, deps)
        self.cnt[e] += 1
        tok = (e, self.cnt[e])
        self._record(reads, writes, tok)
        self.q[e].append((waits, fn, self.sem[e], 1))
        self.ninst += 1
        return tok

    def dma(self, qe, out, in_, sb_reads=(), sb_writes=(), **kw):
        deps = self._deps(sb_reads, sb_writes)
        i = self.dma_next
        self.dma_next = (self.dma_next + 1) % len(self.dma_sems)
        if self.dma_cnt[i] > 0:
            k = ('dma', i)
            if deps.get(k, 0) < self.dma_cnt[i] * 16:
                deps[k] = self.dma_cnt[i] * 16
        waits = self._waits(qe, deps)
        self.dma_cnt[i] += 1
        tok = (('dma', i), self.dma_cnt[i] * 16)
        self._record(sb_reads, sb_writes, tok)
        self.q[qe].append((waits, (lambda eng, out=out, in_=in_, kw=kw: eng.dma_start(out=out, in_=in_, **kw)),
                           self.dma_sems[i], 16))
        self.ninst += 1
        return tok

    def wait_tok(self, e, tok):
        waits = self._waits(e, {tok[0]: tok[1]})
        if waits:
            self.q[e].append((waits, None, None, 0))

    def emit(self):
        nc = self.nc
        with nc.Block() as block:
            def mk(e):
                def body(eng):
                    for waits, fn, sem, inc in self.q[e]:
                        for k, v in waits:
                            eng.wait_ge(self._semof(k), v)
                        if fn is not None:
                            fn(eng).then_inc(sem, inc)
                return body
            block.tensor(mk('tensor'))
            block.vector(mk('vector'))
            block.scalar(mk('scalar'))
            block.gpsimd(mk('gpsimd'))
            block.sync(mk('sync'))


NCST = 128 * 7 + 1 + 32 + 64


def make_consts():
    p = np.arange(128)
    c = np.zeros((128, NCST), np.float32)
    o = 0
    c[:, o:o + 128] = np.eye(128); o += 128
    c[:, o:o + 128] = (p[:, None] <= p[None, :]); o += 128
    c[:, o:o + 128] = (p[:, None] >= p[None, :]); o += 128
    c[:, o:o + 128] = 1.0; o += 128
    c[:, o:o + 128] = (p[:, None] // 64 == p[None, :] // 64); o += 128
    c[:, o:o + 128] = (p[None, :] < 64); o += 128
    c[:, o:o + 128] = (p[None, :] >= 64); o += 128
    c[:, o] = p; o += 1
    c[:, o:o + 32] = np.arange(32)[None, :]; o += 32
    c[:, o:o + 64] = np.arange(64)[None, :]; o += 64
    return c


PC = {}
_o = 0
for _n, _w in [('n1g', 8), ('n2g', 8), ('adab', 48), ('hng', 8), ('convw', 8), ('convb', 2), ('lgb', 8),
               ('lam', 4), ('b2', 6), ('mgi', 6), ('mgf', 6)]:
    PC[_n] = (_o, _w)
    _o += _w
NPRM = _o


def pack_params(inp, l):
    P = np.zeros((128, NPRM), np.float32)

    def put(name, arr):
        o, w = PC[name]
        assert arr.shape == (128, w), (name, arr.shape)
        P[:, o:o + w] = arr
    put('n1g', inp['norm1_g'][l].reshape(8, 128).T)
    put('n2g', inp['norm2_g'][l].reshape(8, 128).T)
    put('adab', inp['ada_b'][l].reshape(48, 128).T)
    put('hng', inp['head_norm_g'][l].reshape(8, 128).T)
    cw = inp['conv_w'][l]
    put('convw', np.concatenate([cw[:, a * 128:(a + 1) * 128].T for a in range(2)], axis=1))
    put('convb', inp['conv_b'][l].reshape(2, 128).T)
    gb = inp['lru_gate_b'][l]
    put('lgb', np.stack([gb[d, g, a * 128:(a + 1) * 128] for a in range(2) for d in range(2) for g in range(2)], axis=1))
    lam = inp['lru_lambda'][l]
    put('lam', np.stack([lam[d, a * 128:(a + 1) * 128] for a in range(2) for d in range(2)], axis=1))
    b2 = inp['gla_b2'][l]
    put('b2', np.stack([b2[d, j * 128:(j + 1) * 128] for j in range(3) for d in range(2)], axis=1))
    mg = inp['mlstm_gate_b'][l]
    put('mgi', np.stack([np.repeat(mg[d, 0, 2 * j:2 * j + 2], 64) for j in range(3) for d in range(2)], axis=1))
    put('mgf', np.stack([np.repeat(mg[d, 1, 2 * j:2 * j + 2], 64) for j in range(3) for d in range(2)], axis=1))
    return P


def pack_gatew(inp, l):
    gw = inp['lru_gate_w'][l]
    out = np.zeros((128, 8, 128), np.float32)
    for a in range(2):
        for d in range(2):
            for g in range(2):
                i = a * 4 + d * 2 + g
                for bb in range(2):
                    out[bb * 64:(bb + 1) * 64, i, bb * 64:(bb + 1) * 64] = gw[d, g, a * 2 + bb]
    return out.reshape(128, 8 * 128)


def pack_wrep(inp, l):
    w = inp['w_in'][l]
    blocks = []
    for j in range(3):
        for d in range(2):
            for kind in range(2):
                base = 3600 + kind * 12 + d * 6 + 2 * j
                blocks.append(np.repeat(w[:, base:base + 2], 64, axis=1))
    return np.ascontiguousarray(np.concatenate(blocks, axis=1))


def build(n_layers=DEPTH, dbg=None):
    nc = bass.Bass('TRN2', target_bir_lowering=False)

    def din(name, shape):
        return nc.dram_tensor(name, list(shape), F32, kind='ExternalInput').ap()
    x_d = din('x', [TL, D]); ctx_d = din('ctx', [TC, D]); cc_d = din('cc', [128, 16])
    adaw_d = din('ada_w', [DEPTH, D, 6 * D]); win_d = din('w_in', [DEPTH, D, DIN])
    wrep_d = din('wrep', [DEPTH, D, 12 * 128]); gatew_d = din('gatew', [DEPTH, 128, 8 * 128])
    w2_d = din('gla_w2', [DEPTH, 2, 16, 384]); wout_d = din('w_out', [DEPTH, D, D])
    w1_d = din('mlp_w1', [DEPTH, D, DFF]); wm2_d = din('mlp_w2', [DEPTH, DFF, D])
    prm_d = din('prm', [DEPTH, 128, NPRM]); fg_d = din('fg', [128, D]); cst_d = din('cst', [128, NCST])
    out_d = nc.dram_tensor('out', [TL, D], F32, kind='ExternalOutput').ap()
    dbg_d = None
    if dbg:
        dbg_d = nc.dram_tensor('dbg', [128, dbg[1]], F32, kind='ExternalOutput').ap()

    s = Sched(nc)
    ARENA = 206 * 1024
    arena = nc.alloc_sbuf_tensor('arena', [128, ARENA], U8)
    psum = nc.alloc_psum_tensor('psum', [128, 4096], F32)

    class A:
        top = 0
        peak = 0
        last = 0

    def alloc(shape, dt, np_=128, at=None):
        n = 1
        for v in shape:
            n *= v
        nb = (n * _dsz(dt) + 31) // 32 * 32
        if at is None:
            off = A.top
            A.top += nb
            A.peak = max(A.peak, A.top)
        else:
            off = at
        A.last = off
        assert off + nb <= ARENA, ('arena overflow', off + nb)
        ap = arena[:, off:off + n * _dsz(dt)].bitcast(dt)
        if len(shape) == 2:
            ap = ap.rearrange('p (a b) -> p a b', b=shape[1])
        elif len(shape) == 3:
            ap = ap.rearrange('p (a b c) -> p a b c', b=shape[1], c=shape[2])
        return ap[0:np_] if np_ != 128 else ap

    def ps(off, n, dt=F32, np_=128, p0=0):
        if dt == F32:
            return psum[p0:p0 + np_, off:off + n]
        return psum[p0:p0 + np_, off:off + (n + 1) // 2].bitcast(dt)[:, 0:n]

    def R_(*aps):
        return [a for a in aps if a is not None and not isinstance(a, (int, float))]

    def act(out, in_, func, bias=None, scale=None, e='scalar'):
        kw = {}
        if bias is not None:
            kw['bias'] = bias
        if scale is not None:
            kw['scale'] = scale
        s.op(e, lambda en: en.activation(out=out, in_=in_, func=func, **kw), reads=R_(in_, bias, scale), writes=[out])

    def tt(out, in0, in1, op, e='vector'):
        s.op(e, lambda en: en.tensor_tensor(out=out, in0=in0, in1=in1, op=op), reads=[in0, in1], writes=[out])

    def ts(out, in0, s1, s2, op0, op1=None, e='vector'):
        if op1 is None:
            s.op(e, lambda en: en.tensor_scalar(out=out, in0=in0, scalar1=s1, scalar2=None, op0=op0),
                 reads=R_(in0, s1), writes=[out])
        else:
            s.op(e, lambda en: en.tensor_scalar(out=out, in0=in0, scalar1=s1, scalar2=s2, op0=op0, op1=op1),
                 reads=R_(in0, s1, s2), writes=[out])

    def stt(out, in0, sc, in1, op0, op1):
        s.op('vector', lambda en: en.scalar_tensor_tensor(out=out, in0=in0, scalar=sc, in1=in1, op0=op0, op1=op1),
             reads=R_(in0, sc, in1), writes=[out])

    def cp(out, in_, e='vector'):
        s.op(e, lambda en: en.tensor_copy(out=out, in_=in_), reads=[in_], writes=[out])

    def mset(out, v, e='vector'):
        s.op(e, lambda en: en.memset(out, v), writes=[out])

    def mmg(out, pairs):
        n = len(pairs)

        def fn(en):
            inst = None
            for i, (l, r) in enumerate(pairs):
                inst = en.matmul(out, lhsT=l, rhs=r, start=(i == 0), stop=(i == n - 1))
            return inst
        rd = []
        for l, r in pairs:
            rd += [l, r]
        s.op('tensor', fn, reads=rd, writes=[out])

    def trp(out, in_, ident):
        s.op('tensor', lambda en: en.transpose(out, in_, ident), reads=[in_, ident], writes=[out])

    def scan(out, d0, d1, init):
        s.op('vector', lambda en: en.tensor_tensor_scan(out=out, data0=d0, data1=d1, initial=init, op0=ALU.mult,
                                                         op1=ALU.add), reads=R_(d0, d1, init), writes=[out])

    def ld(out, in_, q='sync'):
        s.dma(q, out, in_, sb_writes=[out])

    X = alloc([8, T], F32)
    CST = alloc([NCST], F32)[:, 0, :] if False else alloc([1, NCST], F32)[:, 0, :]
    ld(CST, cst_d)
    IDf = CST[:, 0:128]
    PIDX = CST[:, 896:897]; RIDX = CST[:, 897:929]; CIDX = CST[:, 929:993]
    CB = alloc([7, 128], BF16)
    cp(CB, CST[:, 0:896].rearrange('p (a b) -> p a b', b=128))
    IDb, MASKF, MASKB, ONESb, BLK64, EAb, EBb = [CB[:, i, :] for i in range(7)]
    PRM = alloc([DEPTH, NPRM], F32)
    for l in range(DEPTH):
        ld(PRM[:, l, :], prm_d[l])
    CCT = alloc([1, 16], F32)[:, 0, :]
    ld(CCT, cc_d)
    CCb = alloc([8, 2], BF16)
    act(CCb, CCT.rearrange('p (k s) -> p k s', s=2), AF.Silu)
    MOD = alloc([DEPTH, 48, 2], F32)
    DRV = alloc([DEPTH, 4, 8, 2], F32) if False else None
    A1 = alloc([8, 2], F32); A2 = alloc([8, 2], F32)
    EPSC = alloc([1, 4], F32)[:, 0, :]
    mset(EPSC[:, 0:1], EPS)
    mset(EPSC[:, 1:2], 1.0)
    mset(EPSC[:, 2:3], 0.0)
    KAP = alloc([DEPTH, 4, 2], F32)
    NB = alloc([DEPTH, 18], F32)
    P_MARK = A.top

    def prm(l, name, i=0, w=1):
        o, _ = PC[name]
        return PRM[:, l, o + i:o + i + w]

    STG = [alloc([1, 1024], F32)[:, 0, :] for _ in range(2)]
    stg_rr = [0]

    def wload(out, src, e='gpsimd'):
        st = STG[stg_rr[0] % len(STG)]
        stg_rr[0] += 1
        shp = list(out.shape)
        np_ = shp[0]
        n = 1
        for v in shp[1:]:
            n *= v
        v_ = st[0:np_, 0:n]
        if len(shp) == 3:
            v_ = v_.rearrange('p (a b) -> p a b', b=shp[2])
        s.dma('sync', v_, src, sb_writes=[v_])
        if e == 'scalar':
            act(out, v_, AF.Copy)
        else:
            cp(out, v_, e=e)

    w1s_d = nc.dram_tensor('w1s', [DEPTH, 32, 128, 1024], BF16, kind='Internal').ap()
    w2s_d = nc.dram_tensor('w2s', [DEPTH, 32, 128, 1024], BF16, kind='Internal').ap()
    stok = {}
    cast_rr = [0]

    def wload_cached(out, src, scr, key):
        if key not in stok:
            e = ('vector', 'scalar')[cast_rr[0] % 2]
            cast_rr[0] += 1
            wload(out, src, e=e)
            dst = scr if len(out.shape) == 2 else scr.rearrange('p (k f) -> p k f', f=out.shape[2])
            stok[key] = s.dma('sync', dst, out, sb_reads=[out])
        else:
            s.wait_tok('sync', stok[key])
            s.dma('sync', out, scr if len(out.shape) == 2 else scr.rearrange('p (k f) -> p k f', f=out.shape[2]),
                  sb_writes=[out])

    bg_jobs = []
    bg_bufs = {}
    bg_rr = [0]

    def bg_setup(l):
        del bg_jobs[:]
        for f in range(32):
            bg_jobs.append((w1_d[l][:, f * 128:(f + 1) * 128].rearrange('(k p) f -> p k f', p=128), w1s_d[l, f], ('w1', l, f), 3))
        for f in range(32):
            bg_jobs.append((wm2_d[l][f * 128:(f + 1) * 128, :], w2s_d[l, f], ('w2', l, f), 2))

    def bg_alloc():
        bg_bufs['st'] = [alloc([1, 1024], F32)[:, 0, :] for _ in range(2)]
        bg_bufs['bf'] = [alloc([1, 1024], BF16)[:, 0, :] for _ in range(1)]

    def bg(n):
        for _ in range(n):
            if not bg_jobs or 'st' not in bg_bufs:
                return
            src, scr, key, nd = bg_jobs.pop(0)
            i = bg_rr[0] % 2
            bg_rr[0] += 1
            st = bg_bufs['st'][i]; bf = bg_bufs['bf'][0]
            if nd == 3:
                st = st.rearrange('p (k f) -> p k f', f=128); bf = bf.rearrange('p (k f) -> p k f', f=128)
                scr = scr.rearrange('p (k f) -> p k f', f=128)
            s.dma('sync', st, src, sb_writes=[st])
            cp(bf, st, e='gpsimd')
            stok[key] = s.dma('sync', scr, bf, sb_reads=[bf])

    def load_x():
        mark = A.top
        POSR = alloc([4, 32], F32); POSC = alloc([4, 64], F32)
        FRQ = alloc([1, 2], F32)[:, 0, :]
        for h in range(2):
            ts(FRQ[:, h:h + 1], PIDX, float(h * 128), None, ALU.add)
        act(FRQ, FRQ, AF.Exp, scale=-math.log(10000.0) / 256.0)
        TWO_PI = 2.0 * math.pi
        tmpi = alloc([1, 64], I32)[:, 0, :]; tmpf = alloc([1, 64], F32)[:, 0, :]; tmpa = alloc([1, 64], F32)[:, 0, :]
        for kk in range(8):
            h = kk % 2
            grp = kk // 2
            n = 32 if grp < 2 else 64
            idx = RIDX if grp < 2 else CIDX
            dst = POSR[:, kk, :] if grp < 2 else POSC[:, kk - 4, :]
            ph = 0.0 if grp % 2 == 0 else math.pi / 2
            ts(tmpa[:, 0:n], idx, FRQ[:, h:h + 1], ph, ALU.mult, ALU.add)
            ts(tmpf[:, 0:n], tmpa[:, 0:n], 1.0 / TWO_PI, None, ALU.mult)
            cp(tmpi[:, 0:n], tmpf[:, 0:n])
            cp(tmpf[:, 0:n], tmpi[:, 0:n])
            stt(tmpa[:, 0:n], tmpf[:, 0:n], -TWO_PI, tmpa[:, 0:n], ALU.mult, ALU.add)
            ts(tmpa[:, 0:n], tmpa[:, 0:n], 3.14159, -3.14159, ALU.min, ALU.max)
            act(dst, tmpa[:, 0:n], AF.Sin)
        ST = [alloc([1, D], F32)[:, 0, :] for _ in range(3)]
        for it in range(NCH):
            st = ST[it % 3]
            if it < 2:
                ld(st, ctx_d[it * 128:(it + 1) * 128, :])
            else:
                ld(st, x_d[(it - 2) * 128:(it - 1) * 128, :])
            for half in range(2):
                pb = ps(((it * 2 + half) % 4) * 512, 512)
                for q in range(4):
                    k = half * 4 + q
                    trp(pb[:, q * 128:(q + 1) * 128], st[:, k * 128:(k + 1) * 128], IDf)
                dstX = X[:, half * 4:half * 4 + 4, it * 128:(it + 1) * 128]
                if it < 2:
                    cp(dstX, pb.rearrange('p (q t) -> p q t', t=128))
                else:
                    r0 = (it - 2) * 2
                    for q in range(4):
                        k = half * 4 + q
                        o3 = X[:, k, it * 128:(it + 1) * 128].rearrange('p (r c) -> p r c', c=64)
                        i3 = pb[:, q * 128:(q + 1) * 128].rearrange('p (r c) -> p r c', c=64)
                        if k < 4:
                            pos = POSR[:, k, r0:r0 + 2].unsqueeze(2).broadcast_to([128, 2, 64])
                        else:
                            pos = POSC[:, k - 4, :].unsqueeze(1).broadcast_to([128, 2, 64])
                        tt(o3, i3, pos, ALU.add)
        A.top = mark

    CHUNKS = [(0, 256, 1)] + [(256 + 512 * i, 512, 0) for i in range(4)]
    pj_rr = [0]

    def pj_bank():
        b = pj_rr[0] % 4
        pj_rr[0] += 1
        return b * 512

    def ada(l):
        mark = A.top
        WB = [alloc([8, 128], BF16) for _ in range(4)]
        for _ in range(3):
            STG.append(alloc([1, 1024], F32)[:, 0, :])
        pm = ps(pj_bank(), 96)
        for ft in range(48):
            wb = WB[ft % 4]
            wload(wb, adaw_d[l][:, ft * 128:(ft + 1) * 128].rearrange('(k p) f -> p k f', p=128),
                  e=('vector', 'scalar')[ft % 2])
            mmg(pm[:, ft * 2:ft * 2 + 2], [(wb[:, k, :], CCb[:, k, :]) for k in range(8)])
        o, _ = PC['adab']
        tt(MOD[:, l], pm.rearrange('p (a b) -> p a b', b=2),
           PRM[:, l, o:o + 48].unsqueeze(2).broadcast_to([128, 48, 2]), ALU.add)
        del STG[2:]
        A.top = mark

    def derive(l):
        for (AA, nm, wh) in ((A1, 'n1g', 1), (A2, 'n2g', 4)):
            o, _ = PC[nm]
            g = PRM[:, l, o:o + 8].unsqueeze(2).broadcast_to([128, 8, 2])
            ts(AA, MOD[:, l, wh * 8:wh * 8 + 8, :], 1.0, None, ALU.add)
            tt(AA, AA, g, ALU.mult)
        o, _ = PC['lam']
        act(KAP[:, l, :, 0], PRM[:, l, o:o + 4], AF.Exp, scale=-1.0)
        act(KAP[:, l, :, 0], KAP[:, l, :, 0], AF.Ln, bias=EPSC[:, 1:2])
        ts(KAP[:, l, :, 1], KAP[:, l, :, 0], -16.0, None, ALU.mult)
        ts(KAP[:, l, :, 0], KAP[:, l, :, 0], -8.0, None, ALU.mult)
        o, _ = PC['b2']
        ts(NB[:, l, 0:6], PRM[:, l, o:o + 6], -1.0, None, ALU.mult)
        o, _ = PC['mgf']
        ts(NB[:, l, 6:12], PRM[:, l, o:o + 6], -1.0, None, ALU.mult)
        o, _ = PC['lgb']
        ts(NB[:, l, 12:18], PRM[:, l, o:o + 6], 1.0, None, ALU.mult)

    NML = int(dbg[0][2:]) if (dbg and dbg[0].startswith('nm')) else 9

    def norm_mod(dst, t0, n, s_, AA, BBl, BBwh, SQ, RS, TMP):
        for k in range(8):
            act(SQ[:, k, 0:n], X[:, k, t0:t0 + n], AF.Square)
        if NML < 2:
            return
        pb = ps(pj_bank(), n)
        mmg(pb, [(ONESb, SQ[:, k, 0:n]) for k in range(8)])
        if NML < 3:
            return
        act(RS[:, 0:n], pb, AF.Ln, bias=EPSC[:, 0:1], scale=1.0 / D)
        act(RS[:, 0:n], RS[:, 0:n], AF.Exp, scale=-0.5)
        if NML < 4:
            return
        for k in range(8):
            stt(TMP[:, 0:n], X[:, k, t0:t0 + n], AA[:, k, s_:s_ + 1], RS[:, 0:n], ALU.mult, ALU.mult)
            if NML < 5:
                continue
            act(dst[:, k, 0:n], TMP[:, 0:n], AF.Identity, bias=MOD[:, BBl, BBwh * 8 + k, s_:s_ + 1])

    def proj(HT, W, M, evac, chunks=CHUNKS):
        for (t0, n, s_) in chunks:
            pb = ps(pj_bank(), n, np_=M)
            mmg(pb, [(W[:, k, 0:M], HT[:, k, t0:t0 + n]) for k in range(8)])
            evac(pb, t0, n, s_)

    def finish_group(l, g8, OUT, G, Wo, last, SQ, Y, RS):
        mark = A.top
        act(SQ, OUT, AF.Square)
        for (t0, n, s_) in CHUNKS:
            pb = ps(pj_bank(), n)
            mmg(pb, [(BLK64, SQ[:, t0:t0 + n])])
            act(RS[:, 0:n], pb, AF.Ln, bias=EPSC[:, 0:1], scale=1.0 / 64)
            act(RS[:, 0:n], RS[:, 0:n], AF.Exp, scale=-0.5)
            tt(RS[:, 0:n], RS[:, 0:n], OUT[:, t0:t0 + n], ALU.mult)
            stt(Y[:, t0:t0 + n], RS[:, 0:n], prm(l, 'hng', g8), G[:, t0:t0 + n], ALU.mult, ALU.mult)
        for (t0, n, s_) in CHUNKS:
            if last and s_ == 1:
                continue
            for k in range(8):
                pb = ps(pj_bank(), n)
                mmg(pb, [(Wo[:, k * 128:(k + 1) * 128], Y[:, t0:t0 + n])])
                stt(X[:, k, t0:t0 + n], pb, MOD[:, l, 2 * 8 + k, s_:s_ + 1], X[:, k, t0:t0 + n], ALU.mult, ALU.add)
        A.top = mark

    def group_a(l, a, HT, last):
        mark = A.top
        Wx = alloc([8, 128], BF16); Wg = alloc([8, 128], BF16); Wo = alloc([1, D], BF16)[:, 0, :]
        GW = alloc([4, 128], BF16)
        wload(Wx, win_d[l][:, a * 128:(a + 1) * 128].rearrange('(k p) f -> p k f', p=128))
        wload(Wg, win_d[l][:, 256 + a * 128:256 + (a + 1) * 128].rearrange('(k p) f -> p k f', p=128))
        wload(GW, gatew_d[l][:, a * 512:(a + 1) * 512].rearrange('p (i c) -> p i c', c=128))
        wload(Wo, wout_d[l][a * 128:(a + 1) * 128, :])
        XA = alloc([1, T], F32)[:, 0, :]
        H2 = XA
        XCv = alloc([1, T], F32)[:, 0, :]
        XCb = alloc([1, T], BF16)[:, 0, :]
        G = alloc([1, T], BF16)[:, 0, :]
        AB = alloc([1, T], F32)[:, 0, :]; ab_off = A.last
        BBf = alloc([1, T], F32)[:, 0, :]; bb_off = A.last
        H = alloc([1, T], F32)[:, 0, :]
        T1 = alloc([1, 512], F32)[:, 0, :]; T2 = alloc([1, 512], F32)[:, 0, :]
        T1b = alloc([1, 512], F32)[:, 0, :]
        bg_alloc()
        bg(2)
        proj(HT, Wx, 128, lambda pb, t0, n, s_: (cp(XA[:, t0:t0 + n], pb), bg(1)))
        ga_rr = [0]

        def gelu_ev(pb, t0, n, s_):
            ga_rr[0] += 1
            T1 = (T1b, T2)[ga_rr[0] % 2]
            bg(1)
            act(T1[:, 0:n], pb, AF.Square)
            ts(T1[:, 0:n], T1[:, 0:n], 0.044715, 1.0, ALU.mult, ALU.add)
            tt(T1[:, 0:n], T1[:, 0:n], pb, ALU.mult)
            act(T1[:, 0:n], T1[:, 0:n], AF.Sigmoid, scale=2.0 * math.sqrt(2.0 / math.pi))
            tt(G[:, t0:t0 + n], T1[:, 0:n], pb, ALU.mult)
        proj(HT, Wg, 128, gelu_ev)
        o, _ = PC['convw']
        cw = lambda j: PRM[:, l, o + a * 4 + j:o + a * 4 + j + 1]
        for (s0, n) in ((0, TC), (TC, TL)):
            ts(XCv[:, s0:s0 + n], XA[:, s0:s0 + n], cw(2), prm(l, 'convb', a), ALU.mult, ALU.add)
            for j, sh in ((0, -2), (1, -1), (3, 1)):
                lo = max(0, -sh); hi = n - max(0, sh)
                stt(XCv[:, s0 + lo:s0 + hi], XA[:, s0 + lo + sh:s0 + hi + sh], cw(j), XCv[:, s0 + lo:s0 + hi],
                    ALU.mult, ALU.add)
        cp(XCb, XCv)
        T1A, T2A = T1, T2
        bg(2)
        for d in range(2):
            kap = KAP[:, l, a * 2 + d, 0:1]; kap2 = KAP[:, l, a * 2 + d, 1:2]
            o, _ = PC['lgb']
            br = PRM[:, l, o + a * 4 + d * 2:o + a * 4 + d * 2 + 1]
            bi = PRM[:, l, o + a * 4 + d * 2 + 1:o + a * 4 + d * 2 + 2]
            for ic, (t0, n, s_) in enumerate(CHUNKS):
                T1 = T1A if ic % 2 == 0 else T1b
                bg(2)
                pr = ps(pj_bank(), n)
                mmg(pr, [(GW[:, d * 2 + 0, :], XCb[:, t0:t0 + n])])
                pi = ps(pj_bank(), n)
                mmg(pi, [(GW[:, d * 2 + 1, :], XCb[:, t0:t0 + n])])
                act(T1[:, 0:n], pr, AF.Sigmoid, bias=br)
                act(AB[:, t0:t0 + n], T1[:, 0:n], AF.Exp, scale=kap)
                act(T1[:, 0:n], T1[:, 0:n], AF.Exp, scale=kap2)
                act(T1[:, 0:n], T1[:, 0:n], AF.Sqrt, bias=EPSC[:, 1:2], scale=-1.0)
                act(T2[:, 0:n], pi, AF.Sigmoid, bias=bi)
                tt(T1[:, 0:n], T1[:, 0:n], T2[:, 0:n], ALU.mult)
                tt(BBf[:, t0:t0 + n], T1[:, 0:n], XCv[:, t0:t0 + n], ALU.mult)
            if d == 0:
                scan(H, AB, BBf, 0.0)
            else:
                scan(H2[:, 0:TC][:, ::-1], AB[:, 0:TC][:, ::-1], BBf[:, 0:TC][:, ::-1], 0.0)
                scan(H2[:, TC:T][:, ::-1], AB[:, TC:T][:, ::-1], BBf[:, TC:T][:, ::-1], H2[:, 0:1])
        tt(H, H, H2, ALU.add)
        bg(6)
        finish_group(l, a, H, G, Wo, last, alloc([1, T], BF16, at=ab_off)[:, 0, :],
                     alloc([1, T], BF16, at=bb_off)[:, 0, :], T1A)
        if a == 1:
            bg(64)
        bg_bufs.clear()
        A.top = mark

    class Stop(Exception):
        pass
    PLIM = int(dbg[0][1:]) if (dbg and dbg[0][0] == 'p') else 99

    def cut(n):
        if PLIM == n:
            raise Stop()

    def group_pair(l, kind, j, HT, LRT, last):
        mark = A.top
        g8 = 2 + kind * 3 + j
        cq = (512 if kind == 0 else 2064) + j * 128
        ck = (896 if kind == 0 else 2448) + j * 128
        cv = (1280 if kind == 0 else 2832) + j * 128
        cg = (1664 if kind == 0 else 3216) + j * 128
        WB2 = [alloc([8, 128], BF16) for _ in range(2)]
        wb_rr = [0]

        def wcol(c0, src=None):
            Wb = WB2[wb_rr[0] % 2]
            wb_rr[0] += 1
            wload(Wb, (win_d[l] if src is None else src)[:, c0:c0 + 128].rearrange('(k p) f -> p k f', p=128))
            return Wb
        Wo = alloc([1, D], BF16)[:, 0, :]
        wload(Wo, wout_d[l][g8 * 128:(g8 + 1) * 128, :])
        if kind == 0:
            W2 = alloc([2, 128], BF16, np_=16)
            for d in range(2):
                wload(W2[:, d, :], w2_d[l][d][:, j * 128:(j + 1) * 128])
        Qb = alloc([1, T], BF16)[:, 0, :]; qb_off = A.last
        Kb = alloc([1, T], BF16)[:, 0, :]; kb_off = A.last
        V1 = alloc([NCH, 130], BF16); VA = alloc([NCH, 128], BF16); VB = alloc([NCH, 128], BF16)
        OUT = alloc([1, T], F32)[:, 0, :]
        PL = alloc([1, T], F32)[:, 0, :]
        QDs = [alloc([1, T], BF16)[:, 0, :] for _ in range(2)]
        KDs = [alloc([1, T], BF16)[:, 0, :] for _ in range(2)]
        DECs = [alloc([1, NCH], F32)[:, 0, :] for _ in range(2)]
        Rsts = [alloc([1, 130], F32)[:, 0, :] for _ in range(2)]
        Sbs = [alloc([2, 128], BF16) for _ in range(2)]
        nSbs = [alloc([2, 128], BF16) for _ in range(2)]
        T1s = [alloc([1, 512], F32)[:, 0, :] for _ in range(2)]
        T2s = [alloc([1, 128], F32)[:, 0, :] for _ in range(2)]
        SCbs = [alloc([2, 256], BF16) for _ in range(2)]
        t1_rr = [0]

        def T1n():
            t1_rr[0] += 1
            return T1s[t1_rr[0] % 2]
        proj(HT, wcol(cq), 128, lambda pb, t0, n, s_: act(Qb[:, t0:t0 + n], pb, AF.Copy, scale=0.125))
        proj(HT, wcol(ck), 128, lambda pb, t0, n, s_: cp(Kb[:, t0:t0 + n], pb))
        cut(1)
        mset(VA, 0.0); mset(VB, 0.0); mset(V1, 1.0)
        Wv = wcol(cv)
        for c in range(NCH):
            pb = ps(pj_bank(), 128)
            mmg(pb, [(HT[:, k, c * 128:(c + 1) * 128], Wv[:, k, :]) for k in range(8)])
            cp(V1[:, c, 0:128], pb)
            cp(VA[:, c, 0:64], V1[:, c, 0:64], e='gpsimd')
            cp(VB[:, c, 64:128], V1[:, c, 64:128], e='gpsimd')
        cut(2)
        sc_ = (1.0 / 16.0) if kind == 0 else 1.0
        for d in range(2):
            QD = QDs[d]; KD = KDs[d]; DEC = DECs[d]
            nb = NB[:, l, (0 if kind == 0 else 6) + j * 2 + d:(0 if kind == 0 else 6) + j * 2 + d + 1]

            def softplus_ev(pb, t0, n, s_):
                t1 = T1n()
                act(t1[:, 0:n], pb, AF.Exp, bias=nb, scale=-1.0)
                act(PL[:, t0:t0 + n], t1[:, 0:n], AF.Ln, bias=EPSC[:, 1:2])
            if kind == 0:
                for (t0, n, s_) in CHUNKS:
                    pb = ps(pj_bank(), n)
                    mmg(pb, [(W2[:, d, :], LRT[:, t0:t0 + n])])
                    softplus_ev(pb, t0, n, s_)
            else:
                proj(HT, wcol(((j * 2 + d) * 2 + 1) * 128, wrep_d[l]), 128, softplus_ev)
            for c in range(NCH):
                plc = PL[:, c * 128:(c + 1) * 128]
                if d == 0:
                    scan(plc, ONESb, plc, 0.0)
                else:
                    scan(plc[:, ::-1], ONESb, plc[:, ::-1], 0.0)
            Pv = PL.rearrange('p (c t) -> p c t', t=128)
            pend = Pv[:, :, 127] if d == 0 else Pv[:, :, 0]
            act(DEC, pend, AF.Exp, scale=-sc_)
            if kind == 0:
                for (t0, n, s_) in CHUNKS:
                    t1 = T1n()
                    act(t1[:, 0:n], PL[:, t0:t0 + n], AF.Exp, scale=-sc_)
                    tt(QD[:, t0:t0 + n], t1[:, 0:n], Qb[:, t0:t0 + n], ALU.mult)
                    t1 = T1n()
                    act(t1[:, 0:n], PL[:, t0:t0 + n], AF.Exp, scale=sc_)
                    tt(KD[:, t0:t0 + n], t1[:, 0:n], Kb[:, t0:t0 + n], ALU.mult)
            else:
                for (t0, n, s_) in CHUNKS:
                    t1 = T1n()
                    act(t1[:, 0:n], PL[:, t0:t0 + n], AF.Exp, scale=-1.0)
                    tt(QD[:, t0:t0 + n], t1[:, 0:n], Qb[:, t0:t0 + n], ALU.mult)
                bi_ = prm(l, 'mgi', j * 2 + d)

                def ig_ev(pb, t0, n, s_, KD=KD):
                    t1 = T1n()
                    tt(t1[:, 0:n], pb, PL[:, t0:t0 + n], ALU.add)
                    act(t1[:, 0:n], t1[:, 0:n], AF.Exp, bias=bi_)
                    tt(KD[:, t0:t0 + n], t1[:, 0:n], Kb[:, t0:t0 + n], ALU.mult)
                proj(HT, wcol(((j * 2 + d) * 2 + 0) * 128, wrep_d[l]), 128, ig_ev)
        cut(3)
        KTs = [alloc([NCH, 128], BF16, at=qb_off), alloc([NCH, 128], BF16, at=kb_off)]
        KE = [alloc([1, 128], BF16)[:, 0, :] for _ in range(4)]
        ke_rr = [0]
        for c in range(NCH):
            for d in range(2):
                ke = KE[ke_rr[0] % 4]
                ke_rr[0] += 1
                ts(ke, KDs[d][:, c * 128:(c + 1) * 128], DECs[d][:, c:c + 1], None, ALU.mult, e='gpsimd')
                pt = ps(pj_bank(), 128, dt=BF16)
                trp(pt, ke, IDb)
                if d == 0:
                    act(KTs[d][:, c, :], pt, AF.Copy)
                else:
                    cp(KTs[d][:, c, :], pt)
        cut(4)
        orders = [list(range(NCH)), [1, 0] + list(range(NCH - 1, 1, -1))]
        MASKS = [MASKF, MASKB]
        for d in range(2):
            mset(Sbs[d], 0.0)
            if kind == 1:
                mset(nSbs[d], 0.0)
        pj2 = [0]

        def pjb2():
            pj2[0] += 1
            return (pj2[0] % 2) * 512
        pus = {}

        def stageA(d, i):
            c = orders[d][i]
            tsl = slice(c * 128, (c + 1) * 128)
            par = i % 2
            QD = QDs[d]; KD = KDs[d]
            base = 2048 + d * 1024
            pscA = ps(base, 128)
            pscB = ps(base + 512, 128)
            mmg(pscA, [(KD[0:64, tsl], QD[0:64, tsl])])
            mmg(pscB, [(KD[64:128, tsl], QD[64:128, tsl])])
            psc2 = psum[:, base:base + 1024].rearrange('p (h t) -> p h t', t=512)[:, :, 0:128]
            tt(SCbs[d][:, par, :].rearrange('p (h t) -> p h t', t=128), psc2,
               MASKS[d].unsqueeze(1).broadcast_to([128, 2, 128]), ALU.mult)
            pu = ps(1024 + d * 512 + par * 256, 130)
            mmg(pu, [(KTs[d][:, c, :], V1[:, c, :])])
            pus[(d, i)] = pu

        def stageB(d, i):
            c = orders[d][i]
            tsl = slice(c * 128, (c + 1) * 128)
            par = i % 2
            QD = QDs[d]; DEC = DECs[d]; Rst = Rsts[d]; Sb = Sbs[d]; nSb = nSbs[d]; T2 = T2s[d]; SCb = SCbs[d]
            pob = pjb2()
            po = ps(pob, 128)
            mmg(po, [(Sb[:, i % 2, :], QD[:, tsl]), (VA[:, c, :], SCb[:, par, 0:128]),
                     (VB[:, c, :], SCb[:, par, 128:256])])
            if kind == 1:
                pd = ps(pob + 128, 128)
                mmg(pd, [(nSb[:, i % 2, :], QD[:, tsl]), (EAb, SCb[:, par, 0:128]), (EBb, SCb[:, par, 128:256])])
            pu = pus.pop((d, i))
            if i == 0:
                cp(Rst, pu)
            else:
                stt(Rst, Rst, DEC[:, c:c + 1], pu, ALU.mult, ALU.add)
            if i + 1 < NCH:
                nx = (i + 1) % 2
                tt(Sb[:, nx, :], Rst[:, 0:128], BLK64, ALU.mult, e='gpsimd')
                if kind == 1:
                    act(nSb[:, nx, :], BLK64, AF.Copy, scale=Rst[:, 128:129])
            if kind == 0:
                tt(OUT[:, tsl], po, OUT[:, tsl], ALU.add)
            else:
                act(T2, pd, AF.Abs)
                s.op('vector', lambda en, o_=T2: en.reciprocal(out=o_, in_=o_), reads=[T2], writes=[T2])
                stt(T2, T2, 1.0, po, ALU.min, ALU.mult)
                tt(OUT[:, tsl], OUT[:, tsl], T2, ALU.add, e='gpsimd')
        mset(OUT, 0.0, e='gpsimd')
        stageA(0, 0); stageA(1, 0)
        for i in range(NCH):
            if i + 1 < NCH:
                stageA(0, i + 1); stageA(1, i + 1)
            stageB(0, i); stageB(1, i)
        cut(5)
        G = QDs[0]
        Wg = wcol(cg)
        if kind == 0:
            proj(HT, Wg, 128, lambda pb, t0, n, s_: act(G[:, t0:t0 + n], pb, AF.Silu))
        else:
            proj(HT, Wg, 128, lambda pb, t0, n, s_: act(G[:, t0:t0 + n], pb, AF.Sigmoid))
        cut(6)
        finish_group(l, g8, OUT, G, Wo, last, KDs[0], alloc([1, T], BF16, at=qb_off)[:, 0, :], T1s[0])
        A.top = mark

    def mlp(l, last):
        mark = A.top
        NW = 6
        W1 = [alloc([8, 128], BF16) for _ in range(NW)]
        W2b = [alloc([1, D], BF16)[:, 0, :] for _ in range(NW)]
        H2 = alloc([8, 1024], BF16)
        A1T = alloc([32, 1024], BF16); a1t_off = A.last
        SQ = alloc([8, 512], BF16, at=a1t_off); RS = alloc([1, 512], F32)[:, 0, :]; TMP = alloc([1, 512], F32)[:, 0, :]
        TM3 = [alloc([1, 512], F32)[:, 0, :] for _ in range(3)]
        tm_rr = [0]
        passes = [(256, 1024, 0), (1280, 1024, 0)]
        if not last:
            passes = [(0, 256, 1)] + passes
        for (p0, pn, s_) in passes:
            subs = [(p0 + i, min(512, pn - i)) for i in range(0, pn, 512)]
            for (t0, n) in subs:
                norm_mod(H2[:, :, t0 - p0:t0 - p0 + n], t0, n, s_, A2, l, 3, SQ, RS, TMP)
            for f in range(32):
                w1 = W1[f % NW]
                wload_cached(w1, w1_d[l][:, f * 128:(f + 1) * 128].rearrange('(k p) f -> p k f', p=128),
                             w1s_d[l, f], ('w1', l, f))
                for (t0, n) in subs:
                    pb = ps(pj_bank(), n)
                    mmg(pb, [(w1[:, k, :], H2[:, k, t0 - p0:t0 - p0 + n]) for k in range(8)])
                    tm_rr[0] += 1
                    tm = TM3[tm_rr[0] % 3]
                    act(tm[:, 0:n], pb, AF.Relu)
                    tt(A1T[:, f, t0 - p0:t0 - p0 + n], tm[:, 0:n], tm[:, 0:n], ALU.mult)
            for (t0, n) in subs:
                for f in range(32):
                    w2 = W2b[f % NW]
                    wload_cached(w2, wm2_d[l][f * 128:(f + 1) * 128, :], w2s_d[l, f], ('w2', l, f))
                    for k in range(8):
                        pb = ps(k * 512, n)
                        s.op('tensor', (lambda en, pb=pb, w=w2[:, k * 128:(k + 1) * 128],
                                        r=A1T[:, f, t0 - p0:t0 - p0 + n], st_=(f == 0), sp_=(f == 31):
                                        en.matmul(pb, lhsT=w, rhs=r, start=st_, stop=sp_)),
                             reads=[w2[:, k * 128:(k + 1) * 128], A1T[:, f, t0 - p0:t0 - p0 + n]], writes=[pb])
                for k in range(8):
                    pb = ps(k * 512, n)
                    stt(X[:, k, t0:t0 + n], pb, MOD[:, l, 5 * 8 + k, s_:s_ + 1], X[:, k, t0:t0 + n], ALU.mult, ALU.add)
        del STG[2:]
        A.top = mark

    def final():
        mark = A.top
        FG = alloc([1, D], F32)[:, 0, :]
        ld(FG, fg_d)
        OT = [alloc([1, D], F32)[:, 0, :] for _ in range(2)]
        SS = alloc([1, 4], F32)[:, 0, :]
        JK = alloc([1, D], F32)[:, 0, :]
        toks = []
        for it in range(TL // 128):
            ot = OT[it % 2]
            t0 = TC + it * 128
            for half in range(2):
                pb = ps(((it * 2 + half) % 4) * 512, 512)
                for q in range(4):
                    k = half * 4 + q
                    trp(pb[:, q * 128:(q + 1) * 128], X[:, k, t0:t0 + 128], IDf)
                cp(ot[:, half * 512:(half + 1) * 512], pb)
            c0 = it % 2
            s.op('scalar', lambda en, ot=ot, c0=c0: en.activation(out=JK, in_=ot, func=AF.Square, accum_out=SS[:, c0:c0 + 1]),
                 reads=[ot], writes=[JK, SS[:, c0:c0 + 1]])
            act(SS[:, 2 + c0:3 + c0], SS[:, c0:c0 + 1], AF.Ln, bias=EPSC[:, 0:1], scale=1.0 / D)
            act(SS[:, 2 + c0:3 + c0], SS[:, 2 + c0:3 + c0], AF.Exp, scale=-0.5)
            stt(ot, ot, SS[:, 2 + c0:3 + c0], FG, ALU.mult, ALU.mult)
            toks.append(s.dma('sync', out_d[it * 128:(it + 1) * 128, :], ot, sb_reads=[ot]))
        for tk in toks[-4:]:
            s.wait_tok('sync', tk)
        A.top = mark

    def dump(ap, ncol):
        mark = A.top
        DB = alloc([1, dbg[1]], F32)[:, 0, :]
        mset(DB, 0.0)
        cp(DB[:, 0:ncol], ap)
        tk = s.dma('sync', dbg_d, DB, sb_reads=[DB])
        s.wait_tok('sync', tk)
        A.top = mark

    load_x()
    stage = dbg[0] if dbg else None
    if stage == 'x':
        dump(X[:, int(dbg[2]), :], T)
        n_layers = 0
    for l in range(n_layers):
        last = (l == DEPTH - 1)
        ada(l)
        if stage == 'ada':
            dump(MOD[:, l].rearrange('p a b -> p (a b)'), 96)
            break
        derive(l)
        if stage == 'drv':
            dump(A1.rearrange('p a b -> p (a b)'), 16)
            break
        mark_l = A.top
        HT = alloc([8, T], BF16)
        SQ = alloc([8, 512], BF16); RS = alloc([1, 512], F32)[:, 0, :]; TMP = alloc([1, 512], F32)[:, 0, :]
        for (t0, n, s_) in CHUNKS:
            norm_mod(HT[:, :, t0:t0 + n], t0, n, s_, A1, l, 0, SQ, RS, TMP)
        A.top = mark_l + 8 * T * 2
        if stage and (stage == 'h%d' % l or stage.startswith('nm')):
            mark = A.top
            HF = alloc([1, T], F32)[:, 0, :]
            cp(HF, HT[:, 0, :])
            dump(HF, T)
            A.top = mark
            break
        LRT = alloc([1, T], BF16, np_=16)[:, 0, :]
        Wl = alloc([8, 16], BF16)
        wload(Wl, win_d[l][:, 2048:2064].rearrange('(k p) f -> p k f', p=128))
        proj(HT, Wl, 16, lambda pb, t0, n, s_: cp(LRT[:, t0:t0 + n], pb))
        bg_setup(l)
        groups = [('a', 0), ('a', 1)] + [('p', 0, j) for j in range(3)] + [('p', 1, j) for j in range(3)]
        if stage and stage.startswith('g'):
            groups = groups[:int(stage[1:]) + 1]
        if stage and stage[0] == 'p':
            groups = [('p', int(dbg[2]), 0)]
        try:
            for g in groups:
                if g[0] == 'a':
                    group_a(l, g[1], HT, last)
                else:
                    group_pair(l, g[1], g[2], HT, LRT, last)
        except Stop:
            pass
        A.top = mark_l
        if stage and (stage.startswith('g') or stage[0] == 'p'):
            dump(X[:, int(dbg[2]), :], T)
            break
        mlp(l, last)
        if stage == 'l%d' % l:
            dump(X[:, int(dbg[2]), :], T)
            break
    if not stage:
        final()
    s.emit()
    print('ninst', s.ninst, 'arena peak', A.peak, 'persistent', P_MARK)
    return nc


_NC = {}


def prep_inputs(inp):
    inp = {k: np.asarray(v) for k, v in inp.items()}
    f = lambda a: np.ascontiguousarray(a, dtype=np.float32)
    shared = {
        'ada_w': f(inp['ada_w']), 'w_in': f(inp['w_in']),
        'wrep': f(np.stack([pack_wrep(inp, l) for l in range(DEPTH)])),
        'gatew': f(np.stack([pack_gatew(inp, l) for l in range(DEPTH)])),
        'gla_w2': f(inp['gla_w2']), 'w_out': f(inp['w_out']), 'mlp_w1': f(inp['mlp_w1']), 'mlp_w2': f(inp['mlp_w2']),
        'prm': f(np.stack([pack_params(inp, l) for l in range(DEPTH)])),
        'fg': f(np.broadcast_to(inp['final_g'][None, :], (128, D))), 'cst': make_consts(),
    }
    maps = []
    for b in range(8):
        cc = np.zeros((128, 16), np.float32)
        cc[:, 0::2] = inp['c'][b].reshape(8, 128).T
        cc[:, 1::2] = inp['c_ctx'].reshape(8, 128).T
        m = dict(shared)
        m['x'] = f(inp['x'][b]); m['ctx'] = f(inp['ctx'][b]); m['cc'] = cc
        maps.append(m)
    return maps


def kernel(**inputs):
    if 'nc' not in _NC:
        _NC['nc'] = build()
    maps = prep_inputs(inputs)
    res = run_bass_kernel_spmd(_NC['nc'], maps, core_ids=list(range(8)))
    return np.stack([np.asarray(res.results[b]['out'], dtype=np.float32) for b in range(8)], axis=0)
```

```python
import math
import numpy as np
import ml_dtypes
import concourse.bass as bass
import concourse.mybir as mybir
from concourse.bass_utils import run_bass_kernel_spmd

F32 = mybir.dt.float32
BF16 = mybir.dt.bfloat16
I32 = mybir.dt.int32
U8 = mybir.dt.uint8
AF = mybir.ActivationFunctionType
ALU = mybir.AluOpType

D = 1024
TL = 2048
TC = 256
T = TL + TC
NCH = T // 128
DEPTH = 2
DIN = 3624
DFF = 4096
EPS = 1e-6
ENG = ['tensor', 'vector', 'scalar', 'gpsimd', 'sync']
DSZ = {F32: 4, BF16: 2, I32: 4, U8: 1}


def _dsz(dt):
    for k, v in DSZ.items():
        if k == dt:
            return v
    return 4


def region(ap):
    es = _dsz(ap.dtype)
    aps = ap.ap
    pstep, pcnt = aps[0]
    off = ap.offset
    if pstep == 0:
        pstep = 1 << 40
    p0 = off // pstep
    fo = off % pstep
    lo = fo
    hi = fo
    for st, c in aps[1:]:
        d = st * (c - 1)
        if d < 0:
            lo += d
        else:
            hi += d
    if ap.tensor.name == 'psum':
        return ('psum', 0, 128, (lo * es) // 2048 * 2048, ((hi + 1) * es + 2047) // 2048 * 2048)
    return (ap.tensor.name, p0, p0 + pcnt, lo * es, (hi + 1) * es)


class Sched:
    def __init__(self, nc, n_dma_sems=32):
        self.nc = nc
        self.sem = {}
        for e in ENG:
            self.sem[e] = nc.semaphore('s_' + e).__enter__()
        self.dma_sems = [nc.semaphore('s_dma%d' % i).__enter__() for i in range(n_dma_sems)]
        self.dma_cnt = [0] * n_dma_sems
        self.dma_next = 0
        self.cnt = {e: 0 for e in ENG}
        self.seen = {e: {} for e in ENG}
        self.q = {e: [] for e in ENG}
        self.acc = {}
        self.ninst = 0

    def _deps(self, reads, writes, e=None):
        deps = {}
        for ap in reads:
            name, p0, p1, f0, f1 = region(ap)
            ps_ = (name == 'psum')
            for r in self.acc.get(name, ()):
                if (r[4] or (ps_ and r[5][0] != e)) and r[0] < p1 and p0 < r[1] and r[2] < f1 and f0 < r[3]:
                    k, v = r[5]
                    if deps.get(k, 0) < v:
                        deps[k] = v
        for ap in writes:
            name, p0, p1, f0, f1 = region(ap)
            for r in self.acc.get(name, ()):
                if r[0] < p1 and p0 < r[1] and r[2] < f1 and f0 < r[3]:
                    k, v = r[5]
                    if deps.get(k, 0) < v:
                        deps[k] = v
        return deps

    def _record(self, reads, writes, tok):
        for ap in writes:
            name, p0, p1, f0, f1 = region(ap)
            lst = self.acc.setdefault(name, [])
            lst[:] = [r for r in lst if not (p0 <= r[0] and r[1] <= p1 and f0 <= r[2] and r[3] <= f1)]
            lst.append((p0, p1, f0, f1, True, tok))
        for ap in reads:
            name, p0, p1, f0, f1 = region(ap)
            lst = self.acc.setdefault(name, [])
            lst[:] = [r for r in lst if not ((not r[4]) and r[5][0] == tok[0] and p0 <= r[0] and r[1] <= p1
                                             and f0 <= r[2] and r[3] <= f1)]
            lst.append((p0, p1, f0, f1, False, tok))

    def _waits(self, e, deps):
        out = []
        seen = self.seen[e]
        for k, v in deps.items():
            if seen.get(k, 0) >= v:
                continue
            seen[k] = v
            if k == e and e == 'tensor':
                continue
            out.append((k, v))
        return out

    def _semof(self, k):
        return self.sem[k] if isinstance(k, str) else self.dma_sems[k[1]]

    def op(self, e, fn, reads=(), writes=()):
        reads = [r for r in reads if r is not None and not isinstance(r, (int, float))]
        deps = self._deps(reads, writes, e)
        waits = self._waits(e, deps)
        self.cnt[e] += 1
        tok = (e, self.cnt[e])
        self._record(reads, writes, tok)
        self.q[e].append((waits, fn, self.sem[e], 1))
        self.ninst += 1
        return tok

    def dma(self, qe, out, in_, sb_reads=(), sb_writes=(), **kw):
        deps = self._deps(sb_reads, sb_writes)
        i = self.dma_next
        self.dma_next = (self.dma_next + 1) % len(self.dma_sems)
        if self.dma_cnt[i] > 0:
            k = ('dma', i)
            if deps.get(k, 0) < self.dma_cnt[i] * 16:
                deps[k] = self.dma_cnt[i] * 16
        waits = self._waits(qe, deps)
        self.dma_cnt[i] += 1
        tok = (('dma', i), self.dma_cnt[i] * 16)
        self._record(sb_reads, sb_writes, tok)
        self.q[qe].append((waits, (lambda eng, out=out, in_=in_, kw=kw: eng.dma_start(out=out, in_=in_, **kw)),
                           self.dma_sems[i], 16))
        self.ninst += 1
        return tok

    def wait_tok(self, e, tok):
        waits = self._waits(e, {tok[0]: tok[1]})
        if waits:
            self.q[e].append((waits, None, None, 0))

    def emit(self):
        nc = self.nc
        with nc.Block() as block:
            def mk(e):
                def body(eng):
                    for waits, fn, sem, inc in self.q[e]:
                        for k, v in waits:
                            eng.wait_ge(self._semof(k), v)
                        if fn is not None:
                            fn(eng).then_inc(sem, inc)
                return body
            block.tensor(mk('tensor'))
            block.vector(mk('vector'))
            block.scalar(mk('scalar'))
            block.gpsimd(mk('gpsimd'))
            block.sync(mk('sync'))


NCST = 128 * 7 + 1 + 32 + 64


def make_consts():
    p = np.arange(128)
    c = np.zeros((128, NCST), np.float32)
    o = 0
    c[:, o:o + 128] = np.eye(128); o += 128
    c[:, o:o + 128] = (p[:, None] <= p[None, :]); o += 128
    c[:, o:o + 128] = (p[:, None] >= p[None, :]); o += 128
    c[:, o:o + 128] = 1.0; o += 128
    c[:, o:o + 128] = (p[:, None] // 64 == p[None, :] // 64); o += 128
    c[:, o:o + 128] = (p[None, :] < 64); o += 128
    c[:, o:o + 128] = (p[None, :] >= 64); o += 128
    c[:, o] = p; o += 1
    c[:, o:o + 32] = np.arange(32)[None, :]; o += 32
    c[:, o:o + 64] = np.arange(64)[None, :]; o += 64
    return c


PC = {}
_o = 0
for _n, _w in [('n1g', 8), ('n2g', 8), ('adab', 48), ('hng', 8), ('convw', 8), ('convb', 2), ('lgb', 8),
               ('lam', 4), ('b2', 6), ('mgi', 6), ('mgf', 6)]:
    PC[_n] = (_o, _w)
    _o += _w
NPRM = _o


def pack_params(inp, l):
    P = np.zeros((128, NPRM), np.float32)

    def put(name, arr):
        o, w = PC[name]
        assert arr.shape == (128, w), (name, arr.shape)
        P[:, o:o + w] = arr
    put('n1g', inp['norm1_g'][l].reshape(8, 128).T)
    put('n2g', inp['norm2_g'][l].reshape(8, 128).T)
    put('adab', inp['ada_b'][l].reshape(48, 128).T)
    put('hng', inp['head_norm_g'][l].reshape(8, 128).T)
    cw = inp['conv_w'][l]
    put('convw', np.concatenate([cw[:, a * 128:(a + 1) * 128].T for a in range(2)], axis=1))
    put('convb', inp['conv_b'][l].reshape(2, 128).T)
    gb = inp['lru_gate_b'][l]
    put('lgb', np.stack([gb[d, g, a * 128:(a + 1) * 128] for a in range(2) for d in range(2) for g in range(2)], axis=1))
    lam = inp['lru_lambda'][l]
    put('lam', np.stack([lam[d, a * 128:(a + 1) * 128] for a in range(2) for d in range(2)], axis=1))
    b2 = inp['gla_b2'][l]
    put('b2', np.stack([b2[d, j * 128:(j + 1) * 128] for j in range(3) for d in range(2)], axis=1))
    mg = inp['mlstm_gate_b'][l]
    put('mgi', np.stack([np.repeat(mg[d, 0, 2 * j:2 * j + 2], 64) for j in range(3) for d in range(2)], axis=1))
    put('mgf', np.stack([np.repeat(mg[d, 1, 2 * j:2 * j + 2], 64) for j in range(3) for d in range(2)], axis=1))
    return P


def pack_gatew(inp, l):
    gw = inp['lru_gate_w'][l]
    out = np.zeros((128, 8, 128), np.float32)
    for a in range(2):
        for d in range(2):
            for g in range(2):
                i = a * 4 + d * 2 + g
                for bb in range(2):
                    out[bb * 64:(bb + 1) * 64, i, bb * 64:(bb + 1) * 64] = gw[d, g, a * 2 + bb]
    return out.reshape(128, 8 * 128)


def pack_wrep(inp, l):
    w = inp['w_in'][l]
    blocks = []
    for j in range(3):
        for d in range(2):
            for kind in range(2):
                base = 3600 + kind * 12 + d * 6 + 2 * j
                blocks.append(np.repeat(w[:, base:base + 2], 64, axis=1))
    return np.ascontiguousarray(np.concatenate(blocks, axis=1))


def build(n_layers=DEPTH, dbg=None):
    nc = bass.Bass('TRN2', target_bir_lowering=False)

    def din(name, shape):
        return nc.dram_tensor(name, list(shape), F32, kind='ExternalInput').ap()
    x_d = din('x', [TL, D]); ctx_d = din('ctx', [TC, D]); cc_d = din('cc', [128, 16])
    adaw_d = din('ada_w', [DEPTH, D, 6 * D]); win_d = din('w_in', [DEPTH, D, DIN])
    wrep_d = din('wrep', [DEPTH, D, 12 * 128]); gatew_d = din('gatew', [DEPTH, 128, 8 * 128])
    w2_d = din('gla_w2', [DEPTH, 2, 16, 384]); wout_d = din('w_out', [DEPTH, D, D])
    w1_d = din('mlp_w1', [DEPTH, D, DFF]); wm2_d = din('mlp_w2', [DEPTH, DFF, D])
    prm_d = din('prm', [DEPTH, 128, NPRM]); fg_d = din('fg', [128, D]); cst_d = din('cst', [128, NCST])
    out_d = nc.dram_tensor('out', [TL, D], F32, kind='ExternalOutput').ap()
    dbg_d = None
    if dbg:
        dbg_d = nc.dram_tensor('dbg', [128, dbg[1]], F32, kind='ExternalOutput').ap()

    s = Sched(nc)
    ARENA = 206 * 1024
    arena = nc.alloc_sbuf_tensor('arena', [128, ARENA], U8)
    psum = nc.alloc_psum_tensor('psum', [128, 4096], F32)

    class A:
        top = 0
        peak = 0
        last = 0

    def alloc(shape, dt, np_=128, at=None):
        n = 1
        for v in shape:
            n *= v
        nb = (n * _dsz(dt) + 31) // 32 * 32
        if at is None:
            off = A.top
            A.top += nb
            A.peak = max(A.peak, A.top)
        else:
            off = at
        A.last = off
        assert off + nb <= ARENA, ('arena overflow', off + nb)
        ap = arena[:, off:off + n * _dsz(dt)].bitcast(dt)
        if len(shape) == 2:
            ap = ap.rearrange('p (a b) -> p a b', b=shape[1])
        elif len(shape) == 3:
            ap = ap.rearrange('p (a b c) -> p a b c', b=shape[1], c=shape[2])
        return ap[0:np_] if np_ != 128 else ap

    def ps(off, n, dt=F32, np_=128, p0=0):
        if dt == F32:
            return psum[p0:p0 + np_, off:off + n]
        return psum[p0:p0 + np_, off:off + (n + 1) // 2].bitcast(dt)[:, 0:n]

    def R_(*aps):
        return [a for a in aps if a is not None and not isinstance(a, (int, float))]

    def act(out, in_, func, bias=None, scale=None, e='scalar'):
        kw = {}
        if bias is not None:
            kw['bias'] = bias
        if scale is not None:
            kw['scale'] = scale
        s.op(e, lambda en: en.activation(out=out, in_=in_, func=func, **kw), reads=R_(in_, bias, scale), writes=[out])

    def tt(out, in0, in1, op, e='vector'):
        s.op(e, lambda en: en.tensor_tensor(out=out, in0=in0, in1=in1, op=op), reads=[in0, in1], writes=[out])

    def ts(out, in0, s1, s2, op0, op1=None, e='vector'):
        if op1 is None:
            s.op(e, lambda en: en.tensor_scalar(out=out, in0=in0, scalar1=s1, scalar2=None, op0=op0),
                 reads=R_(in0, s1), writes=[out])
        else:
            s.op(e, lambda en: en.tensor_scalar(out=out, in0=in0, scalar1=s1, scalar2=s2, op0=op0, op1=op1),
                 reads=R_(in0, s1, s2), writes=[out])

    def stt(out, in0, sc, in1, op0, op1):
        s.op('vector', lambda en: en.scalar_tensor_tensor(out=out, in0=in0, scalar=sc, in1=in1, op0=op0, op1=op1),
             reads=R_(in0, sc, in1), writes=[out])

    def cp(out, in_, e='vector'):
        s.op(e, lambda en: en.tensor_copy(out=out, in_=in_), reads=[in_], writes=[out])

    def mset(out, v, e='vector'):
        s.op(e, lambda en: en.memset(out, v), writes=[out])

    def mmg(out, pairs):
        n = len(pairs)

        def fn(en):
            inst = None
            for i, (l, r) in enumerate(pairs):
                inst = en.matmul(out, lhsT=l, rhs=r, start=(i == 0), stop=(i == n - 1))
            return inst
        rd = []
        for l, r in pairs:
            rd += [l, r]
        s.op('tensor', fn, reads=rd, writes=[out])

    def trp(out, in_, ident):
        s.op('tensor', lambda en: en.transpose(out, in_, ident), reads=[in_, ident], writes=[out])

    def scan(out, d0, d1, init):
        s.op('vector', lambda en: en.tensor_tensor_scan(out=out, data0=d0, data1=d1, initial=init, op0=ALU.mult,
                                                         op1=ALU.add), reads=R_(d0, d1, init), writes=[out])

    def ld(out, in_, q='sync'):
        s.dma(q, out, in_, sb_writes=[out])

    X = alloc([8, T], F32)
    CST = alloc([NCST], F32)[:, 0, :] if False else alloc([1, NCST], F32)[:, 0, :]
    ld(CST, cst_d)
    IDf = CST[:, 0:128]
    PIDX = CST[:, 896:897]; RIDX = CST[:, 897:929]; CIDX = CST[:, 929:993]
    CB = alloc([7, 128], BF16)
    cp(CB, CST[:, 0:896].rearrange('p (a b) -> p a b', b=128))
    IDb, MASKF, MASKB, ONESb, BLK64, EAb, EBb = [CB[:, i, :] for i in range(7)]
    PRM = alloc([DEPTH, NPRM], F32)
    for l in range(DEPTH):
        ld(PRM[:, l, :], prm_d[l])
    CCT = alloc([1, 16], F32)[:, 0, :]
    ld(CCT, cc_d)
    CCb = alloc([8, 2], BF16)
    act(CCb, CCT.rearrange('p (k s) -> p k s', s=2), AF.Silu)
    MOD = alloc([DEPTH, 48, 2], F32)
    DRV = alloc([DEPTH, 4, 8, 2], F32) if False else None
    A1 = alloc([8, 2], F32); A2 = alloc([8, 2], F32)
    EPSC = alloc([1, 4], F32)[:, 0, :]
    mset(EPSC[:, 0:1], EPS)
    mset(EPSC[:, 1:2], 1.0)
    mset(EPSC[:, 2:3], 0.0)
    KAP = alloc([DEPTH, 4, 2], F32)
    NB = alloc([DEPTH, 18], F32)
    P_MARK = A.top

    def prm(l, name, i=0, w=1):
        o, _ = PC[name]
        return PRM[:, l, o + i:o + i + w]

    STG = [alloc([1, 1024], F32)[:, 0, :] for _ in range(2)]
    stg_rr = [0]

    def wload(out, src, e='gpsimd'):
        st = STG[stg_rr[0] % len(STG)]
        stg_rr[0] += 1
        shp = list(out.shape)
        np_ = shp[0]
        n = 1
        for v in shp[1:]:
            n *= v
        v_ = st[0:np_, 0:n]
        if len(shp) == 3:
            v_ = v_.rearrange('p (a b) -> p a b', b=shp[2])
        s.dma('sync', v_, src, sb_writes=[v_])
        if e == 'scalar':
            act(out, v_, AF.Copy)
        else:
            cp(out, v_, e=e)

    w1s_d = nc.dram_tensor('w1s', [DEPTH, 32, 128, 1024], BF16, kind='Internal').ap()
    w2s_d = nc.dram_tensor('w2s', [DEPTH, 32, 128, 1024], BF16, kind='Internal').ap()
    stok = {}
    cast_rr = [0]

    def wload_cached(out, src, scr, key):
        if key not in stok:
            e = ('vector', 'scalar')[cast_rr[0] % 2]
            cast_rr[0] += 1
            wload(out, src, e=e)
            dst = scr if len(out.shape) == 2 else scr.rearrange('p (k f) -> p k f', f=out.shape[2])
            stok[key] = s.dma('sync', dst, out, sb_reads=[out])
        else:
            s.wait_tok('sync', stok[key])
            s.dma('sync', out, scr if len(out.shape) == 2 else scr.rearrange('p (k f) -> p k f', f=out.shape[2]),
                  sb_writes=[out])

    bg_jobs = []
    bg_bufs = {}
    bg_rr = [0]

    def bg_setup(l):
        del bg_jobs[:]
        for f in range(32):
            bg_jobs.append((w1_d[l][:, f * 128:(f + 1) * 128].rearrange('(k p) f -> p k f', p=128), w1s_d[l, f], ('w1', l, f), 3))
        for f in range(32):
            bg_jobs.append((wm2_d[l][f * 128:(f + 1) * 128, :], w2s_d[l, f], ('w2', l, f), 2))

    bg_pend = []

    def bg_flush():
        while bg_pend:
            scr, bf, key = bg_pend.pop(0)
            stok[key] = s.dma('sync', scr, bf, sb_reads=[bf])

    def bg_alloc():
        bg_bufs['st'] = [alloc([1, 1024], F32)[:, 0, :] for _ in range(2)]
        bg_bufs['bf'] = [alloc([1, 1024], BF16)[:, 0, :] for _ in range(2)]

    def bg(n):
        for _ in range(n):
            if not bg_jobs or 'st' not in bg_bufs:
                return
            src, scr, key, nd = bg_jobs.pop(0)
            i = bg_rr[0] % 2
            bg_rr[0] += 1
            st = bg_bufs['st'][i]; bf = bg_bufs['bf'][i]
            if nd == 3:
                st = st.rearrange('p (k f) -> p k f', f=128); bf = bf.rearrange('p (k f) -> p k f', f=128)
                scr = scr.rearrange('p (k f) -> p k f', f=128)
            s.dma('sync', st, src, sb_writes=[st])
            cp(bf, st, e='gpsimd')
            bg_flush()
            bg_pend.append((scr, bf, key))

    def load_x():
        mark = A.top
        POSR = alloc([4, 32], F32); POSC = alloc([4, 64], F32)
        FRQ = alloc([1, 2], F32)[:, 0, :]
        for h in range(2):
            ts(FRQ[:, h:h + 1], PIDX, float(h * 128), None, ALU.add)
        act(FRQ, FRQ, AF.Exp, scale=-math.log(10000.0) / 256.0)
        TWO_PI = 2.0 * math.pi
        tmpi = alloc([1, 64], I32)[:, 0, :]; tmpf = alloc([1, 64], F32)[:, 0, :]; tmpa = alloc([1, 64], F32)[:, 0, :]
        for kk in range(8):
            h = kk % 2
            grp = kk // 2
            n = 32 if grp < 2 else 64
            idx = RIDX if grp < 2 else CIDX
            dst = POSR[:, kk, :] if grp < 2 else POSC[:, kk - 4, :]
            ph = 0.0 if grp % 2 == 0 else math.pi / 2
            ts(tmpa[:, 0:n], idx, FRQ[:, h:h + 1], ph, ALU.mult, ALU.add)
            ts(tmpf[:, 0:n], tmpa[:, 0:n], 1.0 / TWO_PI, None, ALU.mult)
            cp(tmpi[:, 0:n], tmpf[:, 0:n])
            cp(tmpf[:, 0:n], tmpi[:, 0:n])
            stt(tmpa[:, 0:n], tmpf[:, 0:n], -TWO_PI, tmpa[:, 0:n], ALU.mult, ALU.add)
            ts(tmpa[:, 0:n], tmpa[:, 0:n], 3.14159, -3.14159, ALU.min, ALU.max)
            act(dst, tmpa[:, 0:n], AF.Sin)
        ST = [alloc([1, D], F32)[:, 0, :] for _ in range(3)]
        for it in range(NCH):
            st = ST[it % 3]
            if it < 2:
                ld(st, ctx_d[it * 128:(it + 1) * 128, :])
            else:
                ld(st, x_d[(it - 2) * 128:(it - 1) * 128, :])
            for half in range(2):
                pb = ps(((it * 2 + half) % 4) * 512, 512)
                for q in range(4):
                    k = half * 4 + q
                    trp(pb[:, q * 128:(q + 1) * 128], st[:, k * 128:(k + 1) * 128], IDf)
                dstX = X[:, half * 4:half * 4 + 4, it * 128:(it + 1) * 128]
                if it < 2:
                    cp(dstX, pb.rearrange('p (q t) -> p q t', t=128))
                else:
                    r0 = (it - 2) * 2
                    for q in range(4):
                        k = half * 4 + q
                        o3 = X[:, k, it * 128:(it + 1) * 128].rearrange('p (r c) -> p r c', c=64)
                        i3 = pb[:, q * 128:(q + 1) * 128].rearrange('p (r c) -> p r c', c=64)
                        if k < 4:
                            pos = POSR[:, k, r0:r0 + 2].unsqueeze(2).broadcast_to([128, 2, 64])
                        else:
                            pos = POSC[:, k - 4, :].unsqueeze(1).broadcast_to([128, 2, 64])
                        tt(o3, i3, pos, ALU.add)
        A.top = mark

    CHUNKS = [(0, 256, 1)] + [(256 + 512 * i, 512, 0) for i in range(4)]
    pj_rr = [0]

    def pj_bank():
        b = pj_rr[0] % 4
        pj_rr[0] += 1
        return b * 512

    def ada(l):
        mark = A.top
        WB = [alloc([8, 128], BF16) for _ in range(4)]
        for _ in range(3):
            STG.append(alloc([1, 1024], F32)[:, 0, :])
        pm = ps(pj_bank(), 96)
        for ft in range(48):
            wb = WB[ft % 4]
            wload(wb, adaw_d[l][:, ft * 128:(ft + 1) * 128].rearrange('(k p) f -> p k f', p=128),
                  e=('vector', 'scalar')[ft % 2])
            mmg(pm[:, ft * 2:ft * 2 + 2], [(wb[:, k, :], CCb[:, k, :]) for k in range(8)])
        o, _ = PC['adab']
        tt(MOD[:, l], pm.rearrange('p (a b) -> p a b', b=2),
           PRM[:, l, o:o + 48].unsqueeze(2).broadcast_to([128, 48, 2]), ALU.add)
        del STG[2:]
        A.top = mark

    def derive(l):
        for (AA, nm, wh) in ((A1, 'n1g', 1), (A2, 'n2g', 4)):
            o, _ = PC[nm]
            g = PRM[:, l, o:o + 8].unsqueeze(2).broadcast_to([128, 8, 2])
            ts(AA, MOD[:, l, wh * 8:wh * 8 + 8, :], 1.0, None, ALU.add)
            tt(AA, AA, g, ALU.mult)
        o, _ = PC['lam']
        act(KAP[:, l, :, 0], PRM[:, l, o:o + 4], AF.Exp, scale=-1.0)
        act(KAP[:, l, :, 0], KAP[:, l, :, 0], AF.Ln, bias=EPSC[:, 1:2])
        ts(KAP[:, l, :, 1], KAP[:, l, :, 0], -16.0, None, ALU.mult)
        ts(KAP[:, l, :, 0], KAP[:, l, :, 0], -8.0, None, ALU.mult)
        o, _ = PC['b2']
        ts(NB[:, l, 0:6], PRM[:, l, o:o + 6], -1.0, None, ALU.mult)
        o, _ = PC['mgf']
        ts(NB[:, l, 6:12], PRM[:, l, o:o + 6], -1.0, None, ALU.mult)
        o, _ = PC['lgb']
        ts(NB[:, l, 12:18], PRM[:, l, o:o + 6], 1.0, None, ALU.mult)

    NML = int(dbg[0][2:]) if (dbg and dbg[0].startswith('nm')) else 9

    def norm_mod(dst, t0, n, s_, AA, BBl, BBwh, SQ, RS, TMP):
        for k in range(8):
            act(SQ[:, k, 0:n], X[:, k, t0:t0 + n], AF.Square)
        if NML < 2:
            return
        pb = ps(pj_bank(), n)
        mmg(pb, [(ONESb, SQ[:, k, 0:n]) for k in range(8)])
        if NML < 3:
            return
        act(RS[:, 0:n], pb, AF.Ln, bias=EPSC[:, 0:1], scale=1.0 / D)
        act(RS[:, 0:n], RS[:, 0:n], AF.Exp, scale=-0.5)
        if NML < 4:
            return
        for k in range(8):
            stt(TMP[:, 0:n], X[:, k, t0:t0 + n], AA[:, k, s_:s_ + 1], RS[:, 0:n], ALU.mult, ALU.mult)
            if NML < 5:
                continue
            act(dst[:, k, 0:n], TMP[:, 0:n], AF.Identity, bias=MOD[:, BBl, BBwh * 8 + k, s_:s_ + 1])

    def proj(HT, W, M, evac, chunks=CHUNKS):
        for (t0, n, s_) in chunks:
            pb = ps(pj_bank(), n, np_=M)
            mmg(pb, [(W[:, k, 0:M], HT[:, k, t0:t0 + n]) for k in range(8)])
            evac(pb, t0, n, s_)

    def finish_group(l, g8, OUT, G, Wo, last, SQ, Y, RS):
        mark = A.top
        act(SQ, OUT, AF.Square)
        for (t0, n, s_) in CHUNKS:
            pb = ps(pj_bank(), n)
            mmg(pb, [(BLK64, SQ[:, t0:t0 + n])])
            act(RS[:, 0:n], pb, AF.Ln, bias=EPSC[:, 0:1], scale=1.0 / 64)
            act(RS[:, 0:n], RS[:, 0:n], AF.Exp, scale=-0.5)
            tt(RS[:, 0:n], RS[:, 0:n], OUT[:, t0:t0 + n], ALU.mult)
            stt(Y[:, t0:t0 + n], RS[:, 0:n], prm(l, 'hng', g8), G[:, t0:t0 + n], ALU.mult, ALU.mult)
        for (t0, n, s_) in CHUNKS:
            if last and s_ == 1:
                continue
            for k in range(8):
                pb = ps(pj_bank(), n)
                mmg(pb, [(Wo[:, k * 128:(k + 1) * 128], Y[:, t0:t0 + n])])
                stt(X[:, k, t0:t0 + n], pb, MOD[:, l, 2 * 8 + k, s_:s_ + 1], X[:, k, t0:t0 + n], ALU.mult, ALU.add)
        A.top = mark

    def group_a(l, a, HT, last):
        mark = A.top
        Wx = alloc([8, 128], BF16); wx_off = A.last
        Wg = alloc([8, 128], BF16)
        Wo = alloc([1, D], BF16, at=wx_off)[:, 0, :]
        GW = alloc([4, 128], BF16)
        wload(Wx, win_d[l][:, a * 128:(a + 1) * 128].rearrange('(k p) f -> p k f', p=128))
        wload(Wg, win_d[l][:, 256 + a * 128:256 + (a + 1) * 128].rearrange('(k p) f -> p k f', p=128))
        wload(GW, gatew_d[l][:, a * 512:(a + 1) * 512].rearrange('p (i c) -> p i c', c=128))
        XA = alloc([1, T], F32)[:, 0, :]
        H2 = XA
        XCv = alloc([1, T], F32)[:, 0, :]
        XCb = alloc([1, T], BF16)[:, 0, :]
        G = alloc([1, T], BF16)[:, 0, :]
        AB = alloc([1, T], F32)[:, 0, :]; ab_off = A.last
        BBf = alloc([1, T], F32)[:, 0, :]; bb_off = A.last
        H = alloc([1, T], F32)[:, 0, :]
        T1 = alloc([1, 512], F32)[:, 0, :]; T2 = alloc([1, 512], F32)[:, 0, :]
        T1b = alloc([1, 512], F32)[:, 0, :]
        bg_alloc()
        bg(2)
        proj(HT, Wx, 128, lambda pb, t0, n, s_: (cp(XA[:, t0:t0 + n], pb), bg(1)))
        wload(Wo, wout_d[l][a * 128:(a + 1) * 128, :])
        ga_rr = [0]

        def gelu_ev(pb, t0, n, s_):
            ga_rr[0] += 1
            T1 = (T1b, T2)[ga_rr[0] % 2]
            bg(1)
            act(T1[:, 0:n], pb, AF.Square)
            ts(T1[:, 0:n], T1[:, 0:n], 0.044715, 1.0, ALU.mult, ALU.add)
            tt(T1[:, 0:n], T1[:, 0:n], pb, ALU.mult)
            act(T1[:, 0:n], T1[:, 0:n], AF.Sigmoid, scale=2.0 * math.sqrt(2.0 / math.pi))
            tt(G[:, t0:t0 + n], T1[:, 0:n], pb, ALU.mult)
        proj(HT, Wg, 128, gelu_ev)
        o, _ = PC['convw']
        cw = lambda j: PRM[:, l, o + a * 4 + j:o + a * 4 + j + 1]
        for (s0, n) in ((0, TC), (TC, TL)):
            ts(XCv[:, s0:s0 + n], XA[:, s0:s0 + n], cw(2), prm(l, 'convb', a), ALU.mult, ALU.add)
            for j, sh in ((0, -2), (1, -1), (3, 1)):
                lo = max(0, -sh); hi = n - max(0, sh)
                stt(XCv[:, s0 + lo:s0 + hi], XA[:, s0 + lo + sh:s0 + hi + sh], cw(j), XCv[:, s0 + lo:s0 + hi],
                    ALU.mult, ALU.add)
        cp(XCb, XCv)
        T1A, T2A = T1, T2
        bg(2)
        for d in range(2):
            kap = KAP[:, l, a * 2 + d, 0:1]; kap2 = KAP[:, l, a * 2 + d, 1:2]
            o, _ = PC['lgb']
            br = PRM[:, l, o + a * 4 + d * 2:o + a * 4 + d * 2 + 1]
            bi = PRM[:, l, o + a * 4 + d * 2 + 1:o + a * 4 + d * 2 + 2]
            for ic, (t0, n, s_) in enumerate(CHUNKS):
                T1 = T1A if ic % 2 == 0 else T1b
                bg(2)
                pr = ps(pj_bank(), n)
                mmg(pr, [(GW[:, d * 2 + 0, :], XCb[:, t0:t0 + n])])
                pi = ps(pj_bank(), n)
                mmg(pi, [(GW[:, d * 2 + 1, :], XCb[:, t0:t0 + n])])
                act(T1[:, 0:n], pr, AF.Sigmoid, bias=br)
                act(AB[:, t0:t0 + n], T1[:, 0:n], AF.Exp, scale=kap)
                act(T1[:, 0:n], T1[:, 0:n], AF.Exp, scale=kap2)
                act(T1[:, 0:n], T1[:, 0:n], AF.Sqrt, bias=EPSC[:, 1:2], scale=-1.0)
                act(T2[:, 0:n], pi, AF.Sigmoid, bias=bi)
                tt(T1[:, 0:n], T1[:, 0:n], T2[:, 0:n], ALU.mult)
                tt(BBf[:, t0:t0 + n], T1[:, 0:n], XCv[:, t0:t0 + n], ALU.mult)
            if d == 0:
                scan(H, AB, BBf, 0.0)
            else:
                scan(H2[:, 0:TC][:, ::-1], AB[:, 0:TC][:, ::-1], BBf[:, 0:TC][:, ::-1], 0.0)
                scan(H2[:, TC:T][:, ::-1], AB[:, TC:T][:, ::-1], BBf[:, TC:T][:, ::-1], H2[:, 0:1])
        tt(H, H, H2, ALU.add)
        bg(6)
        finish_group(l, a, H, G, Wo, last, alloc([1, T], BF16, at=ab_off)[:, 0, :],
                     alloc([1, T], BF16, at=bb_off)[:, 0, :], T1A)
        if a == 1:
            bg(64)
        bg_flush()
        bg_bufs.clear()
        A.top = mark

    class Stop(Exception):
        pass
    PLIM = int(dbg[0][1:]) if (dbg and dbg[0][0] == 'p') else 99

    def cut(n):
        if PLIM == n:
            raise Stop()

    def group_pair(l, kind, j, HT, LRT, last):
        mark = A.top
        g8 = 2 + kind * 3 + j
        cq = (512 if kind == 0 else 2064) + j * 128
        ck = (896 if kind == 0 else 2448) + j * 128
        cv = (1280 if kind == 0 else 2832) + j * 128
        cg = (1664 if kind == 0 else 3216) + j * 128
        WB2 = [alloc([8, 128], BF16) for _ in range(2)]
        wb_rr = [0]

        def wcol(c0, src=None):
            Wb = WB2[wb_rr[0] % 2]
            wb_rr[0] += 1
            wload(Wb, (win_d[l] if src is None else src)[:, c0:c0 + 128].rearrange('(k p) f -> p k f', p=128))
            return Wb
        Wo = alloc([1, D], BF16)[:, 0, :]
        wload(Wo, wout_d[l][g8 * 128:(g8 + 1) * 128, :])
        if kind == 0:
            W2 = alloc([2, 128], BF16, np_=16)
            for d in range(2):
                wload(W2[:, d, :], w2_d[l][d][:, j * 128:(j + 1) * 128])
        Qb = alloc([1, T], BF16)[:, 0, :]; qb_off = A.last
        Kb = alloc([1, T], BF16)[:, 0, :]; kb_off = A.last
        V1 = alloc([NCH, 130], BF16); VA = alloc([NCH, 128], BF16); VB = alloc([NCH, 128], BF16)
        OUT = alloc([1, T], F32)[:, 0, :]
        PL = alloc([1, T], F32)[:, 0, :]
        QDs = [alloc([1, T], BF16)[:, 0, :] for _ in range(2)]
        KDs = [alloc([1, T], BF16)[:, 0, :] for _ in range(2)]
        DECs = [alloc([1, NCH], F32)[:, 0, :] for _ in range(2)]
        Rsts = [alloc([1, 130], F32)[:, 0, :] for _ in range(2)]
        Sbs = [alloc([2, 128], BF16) for _ in range(2)]
        nSbs = [alloc([2, 128], BF16) for _ in range(2)]
        T1s = [alloc([1, 512], F32)[:, 0, :] for _ in range(2)]
        T2s = [alloc([1, 128], F32)[:, 0, :] for _ in range(2)]
        SCbs = [alloc([2, 256], BF16) for _ in range(2)]
        t1_rr = [0]

        def T1n():
            t1_rr[0] += 1
            return T1s[t1_rr[0] % 2]
        proj(HT, wcol(cq), 128, lambda pb, t0, n, s_: act(Qb[:, t0:t0 + n], pb, AF.Copy, scale=0.125))
        proj(HT, wcol(ck), 128, lambda pb, t0, n, s_: cp(Kb[:, t0:t0 + n], pb))
        cut(1)
        mset(VA, 0.0); mset(VB, 0.0); mset(V1, 1.0)
        Wv = wcol(cv)
        for c in range(NCH):
            pb = ps(pj_bank(), 128)
            mmg(pb, [(HT[:, k, c * 128:(c + 1) * 128], Wv[:, k, :]) for k in range(8)])
            cp(V1[:, c, 0:128], pb)
            cp(VA[:, c, 0:64], V1[:, c, 0:64], e='gpsimd')
            cp(VB[:, c, 64:128], V1[:, c, 64:128], e='gpsimd')
        cut(2)
        sc_ = (1.0 / 16.0) if kind == 0 else 1.0
        for d in range(2):
            QD = QDs[d]; KD = KDs[d]; DEC = DECs[d]
            nb = NB[:, l, (0 if kind == 0 else 6) + j * 2 + d:(0 if kind == 0 else 6) + j * 2 + d + 1]

            def softplus_ev(pb, t0, n, s_):
                t1 = T1n()
                act(t1[:, 0:n], pb, AF.Exp, bias=nb, scale=-1.0)
                act(PL[:, t0:t0 + n], t1[:, 0:n], AF.Ln, bias=EPSC[:, 1:2])
            if kind == 0:
                for (t0, n, s_) in CHUNKS:
                    pb = ps(pj_bank(), n)
                    mmg(pb, [(W2[:, d, :], LRT[:, t0:t0 + n])])
                    softplus_ev(pb, t0, n, s_)
            else:
                proj(HT, wcol(((j * 2 + d) * 2 + 1) * 128, wrep_d[l]), 128, softplus_ev)
            for c in range(NCH):
                plc = PL[:, c * 128:(c + 1) * 128]
                if d == 0:
                    scan(plc, ONESb, plc, 0.0)
                else:
                    scan(plc[:, ::-1], ONESb, plc[:, ::-1], 0.0)
            Pv = PL.rearrange('p (c t) -> p c t', t=128)
            pend = Pv[:, :, 127] if d == 0 else Pv[:, :, 0]
            act(DEC, pend, AF.Exp, scale=-sc_)
            if kind == 0:
                for (t0, n, s_) in CHUNKS:
                    t1 = T1n()
                    act(t1[:, 0:n], PL[:, t0:t0 + n], AF.Exp, scale=-sc_)
                    tt(QD[:, t0:t0 + n], t1[:, 0:n], Qb[:, t0:t0 + n], ALU.mult)
                    t1 = T1n()
                    act(t1[:, 0:n], PL[:, t0:t0 + n], AF.Exp, scale=sc_)
                    tt(KD[:, t0:t0 + n], t1[:, 0:n], Kb[:, t0:t0 + n], ALU.mult)
            else:
                for (t0, n, s_) in CHUNKS:
                    t1 = T1n()
                    act(t1[:, 0:n], PL[:, t0:t0 + n], AF.Exp, scale=-1.0)
                    tt(QD[:, t0:t0 + n], t1[:, 0:n], Qb[:, t0:t0 + n], ALU.mult)
                bi_ = prm(l, 'mgi', j * 2 + d)

                def ig_ev(pb, t0, n, s_, KD=KD):
                    t1 = T1n()
                    tt(t1[:, 0:n], pb, PL[:, t0:t0 + n], ALU.add)
                    act(t1[:, 0:n], t1[:, 0:n], AF.Exp, bias=bi_)
                    tt(KD[:, t0:t0 + n], t1[:, 0:n], Kb[:, t0:t0 + n], ALU.mult)
                proj(HT, wcol(((j * 2 + d) * 2 + 0) * 128, wrep_d[l]), 128, ig_ev)
        cut(3)
        KTs = [alloc([NCH, 128], BF16, at=qb_off), alloc([NCH, 128], BF16, at=kb_off)]
        KE = [alloc([1, 128], BF16)[:, 0, :] for _ in range(4)]
        ke_rr = [0]
        for c in range(NCH):
            for d in range(2):
                ke = KE[ke_rr[0] % 4]
                ke_rr[0] += 1
                ts(ke, KDs[d][:, c * 128:(c + 1) * 128], DECs[d][:, c:c + 1], None, ALU.mult, e='gpsimd')
                pt = ps(pj_bank(), 128, dt=BF16)
                trp(pt, ke, IDb)
                if d == 0:
                    act(KTs[d][:, c, :], pt, AF.Copy)
                else:
                    cp(KTs[d][:, c, :], pt)
        cut(4)
        orders = [list(range(NCH)), [1, 0] + list(range(NCH - 1, 1, -1))]
        MASKS = [MASKF, MASKB]
        for d in range(2):
            mset(Sbs[d], 0.0)
            if kind == 1:
                mset(nSbs[d], 0.0)
        pj2 = [0]

        def pjb2():
            pj2[0] += 1
            return (pj2[0] % 2) * 512
        pus = {}

        def stageA(d, i):
            c = orders[d][i]
            tsl = slice(c * 128, (c + 1) * 128)
            par = i % 2
            QD = QDs[d]; KD = KDs[d]
            base = 2048 + d * 1024
            pscA = ps(base, 128)
            pscB = ps(base + 512, 128)
            mmg(pscA, [(KD[0:64, tsl], QD[0:64, tsl])])
            mmg(pscB, [(KD[64:128, tsl], QD[64:128, tsl])])
            psc2 = psum[:, base:base + 1024].rearrange('p (h t) -> p h t', t=512)[:, :, 0:128]
            tt(SCbs[d][:, par, :].rearrange('p (h t) -> p h t', t=128), psc2,
               MASKS[d].unsqueeze(1).broadcast_to([128, 2, 128]), ALU.mult)
            pu = ps(1024 + d * 512 + par * 256, 130)
            mmg(pu, [(KTs[d][:, c, :], V1[:, c, :])])
            pus[(d, i)] = pu

        def stageB(d, i):
            c = orders[d][i]
            tsl = slice(c * 128, (c + 1) * 128)
            par = i % 2
            QD = QDs[d]; DEC = DECs[d]; Rst = Rsts[d]; Sb = Sbs[d]; nSb = nSbs[d]; T2 = T2s[d]; SCb = SCbs[d]
            pob = pjb2()
            po = ps(pob, 128)
            mmg(po, [(Sb[:, i % 2, :], QD[:, tsl]), (VA[:, c, :], SCb[:, par, 0:128]),
                     (VB[:, c, :], SCb[:, par, 128:256])])
            if kind == 1:
                pd = ps(pob + 128, 128)
                mmg(pd, [(nSb[:, i % 2, :], QD[:, tsl]), (EAb, SCb[:, par, 0:128]), (EBb, SCb[:, par, 128:256])])
            pu = pus.pop((d, i))
            if i == 0:
                cp(Rst, pu)
            else:
                stt(Rst, Rst, DEC[:, c:c + 1], pu, ALU.mult, ALU.add)
            if i + 1 < NCH:
                nx = (i + 1) % 2
                tt(Sb[:, nx, :], Rst[:, 0:128], BLK64, ALU.mult, e='gpsimd')
                if kind == 1:
                    act(nSb[:, nx, :], BLK64, AF.Copy, scale=Rst[:, 128:129])
            if kind == 0:
                tt(OUT[:, tsl], po, OUT[:, tsl], ALU.add)
            else:
                act(T2, pd, AF.Abs)
                s.op('vector', lambda en, o_=T2: en.reciprocal(out=o_, in_=o_), reads=[T2], writes=[T2])
                stt(T2, T2, 1.0, po, ALU.min, ALU.mult)
                tt(OUT[:, tsl], OUT[:, tsl], T2, ALU.add, e='gpsimd')
        mset(OUT, 0.0, e='gpsimd')
        stageA(0, 0); stageA(1, 0)
        for i in range(NCH):
            if i + 1 < NCH:
                stageA(0, i + 1); stageA(1, i + 1)
            stageB(0, i); stageB(1, i)
        cut(5)
        G = QDs[0]
        Wg = wcol(cg)
        if kind == 0:
            proj(HT, Wg, 128, lambda pb, t0, n, s_: act(G[:, t0:t0 + n], pb, AF.Silu))
        else:
            proj(HT, Wg, 128, lambda pb, t0, n, s_: act(G[:, t0:t0 + n], pb, AF.Sigmoid))
        cut(6)
        finish_group(l, g8, OUT, G, Wo, last, KDs[0], alloc([1, T], BF16, at=qb_off)[:, 0, :], T1s[0])
        A.top = mark

    def mlp(l, last):
        mark = A.top
        NW = 6
        W1 = [alloc([8, 128], BF16) for _ in range(NW)]
        W2b = [alloc([1, D], BF16)[:, 0, :] for _ in range(NW)]
        H2 = alloc([8, 1024], BF16)
        A1T = alloc([32, 1024], BF16); a1t_off = A.last
        SQ = alloc([8, 512], BF16, at=a1t_off); RS = alloc([1, 512], F32)[:, 0, :]; TMP = alloc([1, 512], F32)[:, 0, :]
        TM3 = [alloc([1, 512], F32)[:, 0, :] for _ in range(3)]
        tm_rr = [0]
        passes = [(256, 1024, 0), (1280, 1024, 0)]
        if not last:
            passes = [(0, 256, 1)] + passes
        for (p0, pn, s_) in passes:
            subs = [(p0 + i, min(512, pn - i)) for i in range(0, pn, 512)]
            for (t0, n) in subs:
                norm_mod(H2[:, :, t0 - p0:t0 - p0 + n], t0, n, s_, A2, l, 3, SQ, RS, TMP)
            for f in range(32):
                w1 = W1[f % NW]
                wload_cached(w1, w1_d[l][:, f * 128:(f + 1) * 128].rearrange('(k p) f -> p k f', p=128),
                             w1s_d[l, f], ('w1', l, f))
                for (t0, n) in subs:
                    pb = ps(pj_bank(), n)
                    mmg(pb, [(w1[:, k, :], H2[:, k, t0 - p0:t0 - p0 + n]) for k in range(8)])
                    tm_rr[0] += 1
                    tm = TM3[tm_rr[0] % 3]
                    act(tm[:, 0:n], pb, AF.Relu)
                    tt(A1T[:, f, t0 - p0:t0 - p0 + n], tm[:, 0:n], tm[:, 0:n], ALU.mult)
            for (t0, n) in subs:
                for f in range(32):
                    w2 = W2b[f % NW]
                    wload_cached(w2, wm2_d[l][f * 128:(f + 1) * 128, :], w2s_d[l, f], ('w2', l, f))
                    for k in range(8):
                        pb = ps(k * 512, n)
                        s.op('tensor', (lambda en, pb=pb, w=w2[:, k * 128:(k + 1) * 128],
                                        r=A1T[:, f, t0 - p0:t0 - p0 + n], st_=(f == 0), sp_=(f == 31):
                                        en.matmul(pb, lhsT=w, rhs=r, start=st_, stop=sp_)),
                             reads=[w2[:, k * 128:(k + 1) * 128], A1T[:, f, t0 - p0:t0 - p0 + n]], writes=[pb])
                for k in range(8):
                    pb = ps(k * 512, n)
                    stt(X[:, k, t0:t0 + n], pb, MOD[:, l, 5 * 8 + k, s_:s_ + 1], X[:, k, t0:t0 + n], ALU.mult, ALU.add)
        del STG[2:]
        A.top = mark

    def final():
        mark = A.top
        FG = alloc([1, D], F32)[:, 0, :]
        ld(FG, fg_d)
        OT = [alloc([1, D], F32)[:, 0, :] for _ in range(2)]
        SS = alloc([1, 4], F32)[:, 0, :]
        JK = alloc([1, D], F32)[:, 0, :]
        toks = []
        for it in range(TL // 128):
            ot = OT[it % 2]
            t0 = TC + it * 128
            for half in range(2):
                pb = ps(((it * 2 + half) % 4) * 512, 512)
                for q in range(4):
                    k = half * 4 + q
                    trp(pb[:, q * 128:(q + 1) * 128], X[:, k, t0:t0 + 128], IDf)
                cp(ot[:, half * 512:(half + 1) * 512], pb)
            c0 = it % 2
            s.op('scalar', lambda en, ot=ot, c0=c0: en.activation(out=JK, in_=ot, func=AF.Square, accum_out=SS[:, c0:c0 + 1]),
                 reads=[ot], writes=[JK, SS[:, c0:c0 + 1]])
            act(SS[:, 2 + c0:3 + c0], SS[:, c0:c0 + 1], AF.Ln, bias=EPSC[:, 0:1], scale=1.0 / D)
            act(SS[:, 2 + c0:3 + c0], SS[:, 2 + c0:3 + c0], AF.Exp, scale=-0.5)
            stt(ot, ot, SS[:, 2 + c0:3 + c0], FG, ALU.mult, ALU.mult)
            toks.append(s.dma('sync', out_d[it * 128:(it + 1) * 128, :], ot, sb_reads=[ot]))
        for tk in toks[-4:]:
            s.wait_tok('sync', tk)
        A.top = mark

    def dump(ap, ncol):
        mark = A.top
        DB = alloc([1, dbg[1]], F32)[:, 0, :]
        mset(DB, 0.0)
        cp(DB[:, 0:ncol], ap)
        tk = s.dma('sync', dbg_d, DB, sb_reads=[DB])
        s.wait_tok('sync', tk)
        A.top = mark

    load_x()
    stage = dbg[0] if dbg else None
    if stage == 'x':
        dump(X[:, int(dbg[2]), :], T)
        n_layers = 0
    for l in range(n_layers):
        last = (l == DEPTH - 1)
        ada(l)
        if stage == 'ada':
            dump(MOD[:, l].rearrange('p a b -> p (a b)'), 96)
            break
        derive(l)
        if stage == 'drv':
            dump(A1.rearrange('p a b -> p (a b)'), 16)
            break
        mark_l = A.top
        HT = alloc([8, T], BF16)
        SQ = alloc([8, 512], BF16); RS = alloc([1, 512], F32)[:, 0, :]; TMP = alloc([1, 512], F32)[:, 0, :]
        for (t0, n, s_) in CHUNKS:
            norm_mod(HT[:, :, t0:t0 + n], t0, n, s_, A1, l, 0, SQ, RS, TMP)
        A.top = mark_l + 8 * T * 2
        if stage and (stage == 'h%d' % l or stage.startswith('nm')):
            mark = A.top
            HF = alloc([1, T], F32)[:, 0, :]
            cp(HF, HT[:, 0, :])
            dump(HF, T)
            A.top = mark
            break
        LRT = alloc([1, T], BF16, np_=16)[:, 0, :]
        Wl = alloc([8, 16], BF16)
        wload(Wl, win_d[l][:, 2048:2064].rearrange('(k p) f -> p k f', p=128))
        proj(HT, Wl, 16, lambda pb, t0, n, s_: cp(LRT[:, t0:t0 + n], pb))
        bg_setup(l)
        groups = [('a', 0), ('a', 1)] + [('p', 0, j) for j in range(3)] + [('p', 1, j) for j in range(3)]
        if stage and stage.startswith('g'):
            groups = groups[:int(stage[1:]) + 1]
        if stage and stage[0] == 'p':
            groups = [('p', int(dbg[2]), 0)]
        try:
            for g in groups:
                if g[0] == 'a':
                    group_a(l, g[1], HT, last)
                else:
                    group_pair(l, g[1], g[2], HT, LRT, last)
        except Stop:
            pass
        A.top = mark_l
        if stage and (stage.startswith('g') or stage[0] == 'p'):
            dump(X[:, int(dbg[2]), :], T)
            break
        mlp(l, last)
        if stage == 'l%d' % l:
            dump(X[:, int(dbg[2]), :], T)
            break
    if not stage:
        final()
    s.emit()
    print('ninst', s.ninst, 'arena peak', A.peak, 'persistent', P_MARK)
    return nc


_NC = {}


def prep_inputs(inp):
    inp = {k: np.asarray(v) for k, v in inp.items()}
    f = lambda a: np.ascontiguousarray(a, dtype=np.float32)
    shared = {
        'ada_w': f(inp['ada_w']), 'w_in': f(inp['w_in']),
        'wrep': f(np.stack([pack_wrep(inp, l) for l in range(DEPTH)])),
        'gatew': f(np.stack([pack_gatew(inp, l) for l in range(DEPTH)])),
        'gla_w2': f(inp['gla_w2']), 'w_out': f(inp['w_out']), 'mlp_w1': f(inp['mlp_w1']), 'mlp_w2': f(inp['mlp_w2']),
        'prm': f(np.stack([pack_params(inp, l) for l in range(DEPTH)])),
        'fg': f(np.broadcast_to(inp['final_g'][None, :], (128, D))), 'cst': make_consts(),
    }
    maps = []
    for b in range(8):
        cc = np.zeros((128, 16), np.float32)
        cc[:, 0::2] = inp['c'][b].reshape(8, 128).T
        cc[:, 1::2] = inp['c_ctx'].reshape(8, 128).T
        m = dict(shared)
        m['x'] = f(inp['x'][b]); m['ctx'] = f(inp['ctx'][b]); m['cc'] = cc
        maps.append(m)
    return maps


def kernel(**inputs):
    if 'nc' not in _NC:
        _NC['nc'] = build()
    maps = prep_inputs(inputs)
    res = run_bass_kernel_spmd(_NC['nc'], maps, core_ids=list(range(8)))
    return np.stack([np.asarray(res.results[b]['out'], dtype=np.float32) for b in range(8)], axis=0)
```

```python
import math
import numpy as np
import ml_dtypes
import concourse.bass as bass
import concourse.mybir as mybir
from concourse.bass_utils import run_bass_kernel_spmd

F32 = mybir.dt.float32
BF16 = mybir.dt.bfloat16
I32 = mybir.dt.int32
U8 = mybir.dt.uint8
AF = mybir.ActivationFunctionType
ALU = mybir.AluOpType

D = 1024
TL = 2048
TC = 256
T = TL + TC
NCH = T // 128
DEPTH = 2
DIN = 3624
DFF = 4096
EPS = 1e-6
ENG = ['tensor', 'vector', 'scalar', 'gpsimd', 'sync']
DSZ = {F32: 4, BF16: 2, I32: 4, U8: 1}


def _dsz(dt):
    for k, v in DSZ.items():
        if k == dt:
            return v
    return 4


def region(ap):
    es = _dsz(ap.dtype)
    aps = ap.ap
    pstep, pcnt = aps[0]
    off = ap.offset
    if pstep == 0:
        pstep = 1 << 40
    p0 = off // pstep
    fo = off % pstep
    lo = fo
    hi = fo
    for st, c in aps[1:]:
        d = st * (c - 1)
        if d < 0:
            lo += d
        else:
            hi += d
    if ap.tensor.name == 'psum':
        return ('psum', 0, 128, (lo * es) // 2048 * 2048, ((hi + 1) * es + 2047) // 2048 * 2048)
    return (ap.tensor.name, p0, p0 + pcnt, lo * es, (hi + 1) * es)


class Sched:
    def __init__(self, nc, n_dma_sems=32):
        self.nc = nc
        self.sem = {}
        for e in ENG:
            self.sem[e] = nc.semaphore('s_' + e).__enter__()
        self.dma_sems = [nc.semaphore('s_dma%d' % i).__enter__() for i in range(n_dma_sems)]
        self.dma_cnt = [0] * n_dma_sems
        self.dma_next = 0
        self.cnt = {e: 0 for e in ENG}
        self.seen = {e: {} for e in ENG}
        self.q = {e: [] for e in ENG}
        self.acc = {}
        self.ninst = 0

    def _deps(self, reads, writes, e=None):
        deps = {}
        for ap in reads:
            name, p0, p1, f0, f1 = region(ap)
            ps_ = (name == 'psum')
            for r in self.acc.get(name, ()):
                if (r[4] or (ps_ and r[5][0] != e)) and r[0] < p1 and p0 < r[1] and r[2] < f1 and f0 < r[3]:
                    k, v = r[5]
                    if deps.get(k, 0) < v:
                        deps[k] = v
        for ap in writes:
            name, p0, p1, f0, f1 = region(ap)
            for r in self.acc.get(name, ()):
                if r[0] < p1 and p0 < r[1] and r[2] < f1 and f0 < r[3]:
                    k, v = r[5]
                    if deps.get(k, 0) < v:
                        deps[k] = v
        return deps

    def _record(self, reads, writes, tok):
        for ap in writes:
            name, p0, p1, f0, f1 = region(ap)
            lst = self.acc.setdefault(name, [])
            lst[:] = [r for r in lst if not (p0 <= r[0] and r[1] <= p1 and f0 <= r[2] and r[3] <= f1)]
            lst.append((p0, p1, f0, f1, True, tok))
        for ap in reads:
            name, p0, p1, f0, f1 = region(ap)
            lst = self.acc.setdefault(name, [])
            lst[:] = [r for r in lst if not ((not r[4]) and r[5][0] == tok[0] and p0 <= r[0] and r[1] <= p1
                                             and f0 <= r[2] and r[3] <= f1)]
            lst.append((p0, p1, f0, f1, False, tok))

    def _waits(self, e, deps):
        out = []
        seen = self.seen[e]
        for k, v in deps.items():
            if seen.get(k, 0) >= v:
                continue
            seen[k] = v
            if k == e and e == 'tensor':
                continue
            out.append((k, v))
        return out

    def _semof(self, k):
        return self.sem[k] if isinstance(k, str) else self.dma_sems[k[1]]

    def op(self, e, fn, reads=(), writes=()):
        reads = [r for r in reads if r is not None and not isinstance(r, (int, float))]
        deps = self._deps(reads, writes, e)
        waits = self._waits(e, deps)
        self.cnt[e] += 1
        tok = (e, self.cnt[e])
        self._record(reads, writes, tok)
        self.q[e].append((waits, fn, self.sem[e], 1))
        self.ninst += 1
        return tok

    def dma(self, qe, out, in_, sb_reads=(), sb_writes=(), **kw):
        deps = self._deps(sb_reads, sb_writes)
        i = self.dma_next
        self.dma_next = (self.dma_next + 1) % len(self.dma_sems)
        if self.dma_cnt[i] > 0:
            k = ('dma', i)
            if deps.get(k, 0) < self.dma_cnt[i] * 16:
                deps[k] = self.dma_cnt[i] * 16
        waits = self._waits(qe, deps)
        self.dma_cnt[i] += 1
        tok = (('dma', i), self.dma_cnt[i] * 16)
        self._record(sb_reads, sb_writes, tok)
        self.q[qe].append((waits, (lambda eng, out=out, in_=in_, kw=kw: eng.dma_start(out=out, in_=in_, **kw)),
                           self.dma_sems[i], 16))
        self.ninst += 1
        return tok

    def wait_tok(self, e, tok):
        waits = self._waits(e, {tok[0]: tok[1]})
        if waits:
            self.q[e].append((waits, None, None, 0))

    def emit(self):
        nc = self.nc
        with nc.Block() as block:
            def mk(e):
                def body(eng):
                    for waits, fn, sem, inc in self.q[e]:
                        for k, v in waits:
                            eng.wait_ge(self._semof(k), v)
                        if fn is not None:
                            fn(eng).then_inc(sem, inc)
                return body
            block.tensor(mk('tensor'))
            block.vector(mk('vector'))
            block.scalar(mk('scalar'))
            block.gpsimd(mk('gpsimd'))
            block.sync(mk('sync'))


NCST = 128 * 7 + 1 + 32 + 64


def make_consts():
    p = np.arange(128)
    c = np.zeros((128, NCST), np.float32)
    o = 0
    c[:, o:o + 128] = np.eye(128); o += 128
    c[:, o:o + 128] = (p[:, None] <= p[None, :]); o += 128
    c[:, o:o + 128] = (p[:, None] >= p[None, :]); o += 128
    c[:, o:o + 128] = 1.0; o += 128
    c[:, o:o + 128] = (p[:, None] // 64 == p[None, :] // 64); o += 128
    c[:, o:o + 128] = (p[None, :] < 64); o += 128
    c[:, o:o + 128] = (p[None, :] >= 64); o += 128
    c[:, o] = p; o += 1
    c[:, o:o + 32] = np.arange(32)[None, :]; o += 32
    c[:, o:o + 64] = np.arange(64)[None, :]; o += 64
    return c


PC = {}
_o = 0
for _n, _w in [('n1g', 8), ('n2g', 8), ('adab', 48), ('hng', 8), ('convw', 8), ('convb', 2), ('lgb', 8),
               ('lam', 4), ('b2', 6), ('mgi', 6), ('mgf', 6)]:
    PC[_n] = (_o, _w)
    _o += _w
NPRM = _o


def pack_params(inp, l):
    P = np.zeros((128, NPRM), np.float32)

    def put(name, arr):
        o, w = PC[name]
        assert arr.shape == (128, w), (name, arr.shape)
        P[:, o:o + w] = arr
    put('n1g', inp['norm1_g'][l].reshape(8, 128).T)
    put('n2g', inp['norm2_g'][l].reshape(8, 128).T)
    put('adab', inp['ada_b'][l].reshape(48, 128).T)
    put('hng', inp['head_norm_g'][l].reshape(8, 128).T)
    cw = inp['conv_w'][l]
    put('convw', np.concatenate([cw[:, a * 128:(a + 1) * 128].T for a in range(2)], axis=1))
    put('convb', inp['conv_b'][l].reshape(2, 128).T)
    gb = inp['lru_gate_b'][l]
    put('lgb', np.stack([gb[d, g, a * 128:(a + 1) * 128] for a in range(2) for d in range(2) for g in range(2)], axis=1))
    lam = inp['lru_lambda'][l]
    put('lam', np.stack([lam[d, a * 128:(a + 1) * 128] for a in range(2) for d in range(2)], axis=1))
    b2 = inp['gla_b2'][l]
    put('b2', np.stack([b2[d, j * 128:(j + 1) * 128] for j in range(3) for d in range(2)], axis=1))
    mg = inp['mlstm_gate_b'][l]
    put('mgi', np.stack([np.repeat(mg[d, 0, 2 * j:2 * j + 2], 64) for j in range(3) for d in range(2)], axis=1))
    put('mgf', np.stack([np.repeat(mg[d, 1, 2 * j:2 * j + 2], 64) for j in range(3) for d in range(2)], axis=1))
    return P


def pack_gatew(inp, l):
    gw = inp['lru_gate_w'][l]
    out = np.zeros((128, 8, 128), np.float32)
    for a in range(2):
        for d in range(2):
            for g in range(2):
                i = a * 4 + d * 2 + g
                for bb in range(2):
                    out[bb * 64:(bb + 1) * 64, i, bb * 64:(bb + 1) * 64] = gw[d, g, a * 2 + bb]
    return out.reshape(128, 8 * 128)


def pack_wrep(inp, l):
    w = inp['w_in'][l]
    blocks = []
    for j in range(3):
        for d in range(2):
            for kind in range(2):
                base = 3600 + kind * 12 + d * 6 + 2 * j
                blocks.append(np.repeat(w[:, base:base + 2], 64, axis=1))
    return np.ascontiguousarray(np.concatenate(blocks, axis=1))


def build(n_layers=DEPTH, dbg=None):
    nc = bass.Bass('TRN2', target_bir_lowering=False)

    def din(name, shape):
        return nc.dram_tensor(name, list(shape), F32, kind='ExternalInput').ap()
    x_d = din('x', [TL, D]); ctx_d = din('ctx', [TC, D]); cc_d = din('cc', [128, 16])
    adaw_d = din('ada_w', [DEPTH, D, 6 * D]); win_d = din('w_in', [DEPTH, D, DIN])
    wrep_d = din('wrep', [DEPTH, D, 12 * 128]); gatew_d = din('gatew', [DEPTH, 128, 8 * 128])
    w2_d = din('gla_w2', [DEPTH, 2, 16, 384]); wout_d = din('w_out', [DEPTH, D, D])
    w1_d = din('mlp_w1', [DEPTH, D, DFF]); wm2_d = din('mlp_w2', [DEPTH, DFF, D])
    prm_d = din('prm', [DEPTH, 128, NPRM]); fg_d = din('fg', [128, D]); cst_d = din('cst', [128, NCST])
    out_d = nc.dram_tensor('out', [TL, D], F32, kind='ExternalOutput').ap()
    dbg_d = None
    if dbg:
        dbg_d = nc.dram_tensor('dbg', [128, dbg[1]], F32, kind='ExternalOutput').ap()

    s = Sched(nc)
    ARENA = 206 * 1024
    arena = nc.alloc_sbuf_tensor('arena', [128, ARENA], U8)
    psum = nc.alloc_psum_tensor('psum', [128, 4096], F32)

    class A:
        top = 0
        peak = 0
        last = 0

    def alloc(shape, dt, np_=128, at=None):
        n = 1
        for v in shape:
            n *= v
        nb = (n * _dsz(dt) + 31) // 32 * 32
        if at is None:
            off = A.top
            A.top += nb
            A.peak = max(A.peak, A.top)
        else:
            off = at
        A.last = off
        assert off + nb <= ARENA, ('arena overflow', off + nb)
        ap = arena[:, off:off + n * _dsz(dt)].bitcast(dt)
        if len(shape) == 2:
            ap = ap.rearrange('p (a b) -> p a b', b=shape[1])
        elif len(shape) == 3:
            ap = ap.rearrange('p (a b c) -> p a b c', b=shape[1], c=shape[2])
        return ap[0:np_] if np_ != 128 else ap

    def ps(off, n, dt=F32, np_=128, p0=0):
        if dt == F32:
            return psum[p0:p0 + np_, off:off + n]
        return psum[p0:p0 + np_, off:off + (n + 1) // 2].bitcast(dt)[:, 0:n]

    def R_(*aps):
        return [a for a in aps if a is not None and not isinstance(a, (int, float))]

    def act(out, in_, func, bias=None, scale=None, e='scalar'):
        kw = {}
        if bias is not None:
            kw['bias'] = bias
        if scale is not None:
            kw['scale'] = scale
        s.op(e, lambda en: en.activation(out=out, in_=in_, func=func, **kw), reads=R_(in_, bias, scale), writes=[out])

    def tt(out, in0, in1, op, e='vector'):
        s.op(e, lambda en: en.tensor_tensor(out=out, in0=in0, in1=in1, op=op), reads=[in0, in1], writes=[out])

    def ts(out, in0, s1, s2, op0, op1=None, e='vector'):
        if op1 is None:
            s.op(e, lambda en: en.tensor_scalar(out=out, in0=in0, scalar1=s1, scalar2=None, op0=op0),
                 reads=R_(in0, s1), writes=[out])
        else:
            s.op(e, lambda en: en.tensor_scalar(out=out, in0=in0, scalar1=s1, scalar2=s2, op0=op0, op1=op1),
                 reads=R_(in0, s1, s2), writes=[out])

    def stt(out, in0, sc, in1, op0, op1):
        s.op('vector', lambda en: en.scalar_tensor_tensor(out=out, in0=in0, scalar=sc, in1=in1, op0=op0, op1=op1),
             reads=R_(in0, sc, in1), writes=[out])

    def cp(out, in_, e='vector'):
        s.op(e, lambda en: en.tensor_copy(out=out, in_=in_), reads=[in_], writes=[out])

    def mset(out, v, e='vector'):
        s.op(e, lambda en: en.memset(out, v), writes=[out])

    def mmg(out, pairs):
        n = len(pairs)

        def fn(en):
            inst = None
            for i, (l, r) in enumerate(pairs):
                inst = en.matmul(out, lhsT=l, rhs=r, start=(i == 0), stop=(i == n - 1))
            return inst
        rd = []
        for l, r in pairs:
            rd += [l, r]
        s.op('tensor', fn, reads=rd, writes=[out])

    def trp(out, in_, ident):
        s.op('tensor', lambda en: en.transpose(out, in_, ident), reads=[in_, ident], writes=[out])

    def scan(out, d0, d1, init):
        s.op('vector', lambda en: en.tensor_tensor_scan(out=out, data0=d0, data1=d1, initial=init, op0=ALU.mult,
                                                         op1=ALU.add), reads=R_(d0, d1, init), writes=[out])

    def ld(out, in_, q='sync'):
        s.dma(q, out, in_, sb_writes=[out])

    X = alloc([8, T], F32)
    CST = alloc([NCST], F32)[:, 0, :] if False else alloc([1, NCST], F32)[:, 0, :]
    ld(CST, cst_d)
    IDf = CST[:, 0:128]
    PIDX = CST[:, 896:897]; RIDX = CST[:, 897:929]; CIDX = CST[:, 929:993]
    CB = alloc([7, 128], BF16)
    cp(CB, CST[:, 0:896].rearrange('p (a b) -> p a b', b=128))
    IDb, MASKF, MASKB, ONESb, BLK64, EAb, EBb = [CB[:, i, :] for i in range(7)]
    PRM = alloc([DEPTH, NPRM], F32)
    for l in range(DEPTH):
        ld(PRM[:, l, :], prm_d[l])
    CCT = alloc([1, 16], F32)[:, 0, :]
    ld(CCT, cc_d)
    CCb = alloc([8, 2], BF16)
    act(CCb, CCT.rearrange('p (k s) -> p k s', s=2), AF.Silu)
    MOD = alloc([DEPTH, 48, 2], F32)
    DRV = alloc([DEPTH, 4, 8, 2], F32) if False else None
    A1 = alloc([8, 2], F32); A2 = alloc([8, 2], F32)
    EPSC = alloc([1, 4], F32)[:, 0, :]
    mset(EPSC[:, 0:1], EPS)
    mset(EPSC[:, 1:2], 1.0)
    mset(EPSC[:, 2:3], 0.0)
    KAP = alloc([DEPTH, 4, 2], F32)
    NB = alloc([DEPTH, 18], F32)
    P_MARK = A.top

    def prm(l, name, i=0, w=1):
        o, _ = PC[name]
        return PRM[:, l, o + i:o + i + w]

    STG = [alloc([1, 1024], F32)[:, 0, :] for _ in range(2)]
    stg_rr = [0]

    def wload(out, src, e='gpsimd'):
        st = STG[stg_rr[0] % len(STG)]
        stg_rr[0] += 1
        shp = list(out.shape)
        np_ = shp[0]
        n = 1
        for v in shp[1:]:
            n *= v
        v_ = st[0:np_, 0:n]
        if len(shp) == 3:
            v_ = v_.rearrange('p (a b) -> p a b', b=shp[2])
        s.dma('sync', v_, src, sb_writes=[v_])
        if e == 'scalar':
            act(out, v_, AF.Copy)
        else:
            cp(out, v_, e=e)

    w1s_d = nc.dram_tensor('w1s', [DEPTH, 32, 128, 1024], BF16, kind='Internal').ap()
    w2s_d = nc.dram_tensor('w2s', [DEPTH, 32, 128, 1024], BF16, kind='Internal').ap()
    stok = {}
    cast_rr = [0]

    def wload_cached(out, src, scr, key):
        if key not in stok:
            e = ('vector', 'scalar')[cast_rr[0] % 2]
            cast_rr[0] += 1
            wload(out, src, e=e)
            dst = scr if len(out.shape) == 2 else scr.rearrange('p (k f) -> p k f', f=out.shape[2])
            stok[key] = s.dma('sync', dst, out, sb_reads=[out])
        else:
            s.wait_tok('sync', stok[key])
            s.dma('sync', out, scr if len(out.shape) == 2 else scr.rearrange('p (k f) -> p k f', f=out.shape[2]),
                  sb_writes=[out])

    bg_jobs = []
    bg_bufs = {}
    bg_rr = [0]

    def bg_setup(l):
        del bg_jobs[:]
        for f in range(32):
            bg_jobs.append((w1_d[l][:, f * 128:(f + 1) * 128].rearrange('(k p) f -> p k f', p=128), w1s_d[l, f], ('w1', l, f), 3))
        for f in range(32):
            bg_jobs.append((wm2_d[l][f * 128:(f + 1) * 128, :], w2s_d[l, f], ('w2', l, f), 2))

    bg_pend = []

    def bg_flush():
        while bg_pend:
            scr, bf, key = bg_pend.pop(0)
            stok[key] = s.dma('sync', scr, bf, sb_reads=[bf])

    def bg_alloc():
        bg_bufs['st'] = [alloc([1, 1024], F32)[:, 0, :] for _ in range(2)]
        bg_bufs['bf'] = [alloc([1, 1024], BF16)[:, 0, :] for _ in range(2)]

    def bg(n):
        for _ in range(n):
            if not bg_jobs or 'st' not in bg_bufs:
                return
            src, scr, key, nd = bg_jobs.pop(0)
            i = bg_rr[0] % 2
            bg_rr[0] += 1
            st = bg_bufs['st'][i]; bf = bg_bufs['bf'][i]
            if nd == 3:
                st = st.rearrange('p (k f) -> p k f', f=128); bf = bf.rearrange('p (k f) -> p k f', f=128)
                scr = scr.rearrange('p (k f) -> p k f', f=128)
            s.dma('sync', st, src, sb_writes=[st])
            cp(bf, st, e='gpsimd')
            bg_flush()
            bg_pend.append((scr, bf, key))

    def load_x():
        mark = A.top
        POSR = alloc([4, 32], F32); POSC = alloc([4, 64], F32)
        FRQ = alloc([1, 2], F32)[:, 0, :]
        for h in range(2):
            ts(FRQ[:, h:h + 1], PIDX, float(h * 128), None, ALU.add)
        act(FRQ, FRQ, AF.Exp, scale=-math.log(10000.0) / 256.0)
        TWO_PI = 2.0 * math.pi
        tmpi = alloc([1, 64], I32)[:, 0, :]; tmpf = alloc([1, 64], F32)[:, 0, :]; tmpa = alloc([1, 64], F32)[:, 0, :]
        for kk in range(8):
            h = kk % 2
            grp = kk // 2
            n = 32 if grp < 2 else 64
            idx = RIDX if grp < 2 else CIDX
            dst = POSR[:, kk, :] if grp < 2 else POSC[:, kk - 4, :]
            ph = 0.0 if grp % 2 == 0 else math.pi / 2
            ts(tmpa[:, 0:n], idx, FRQ[:, h:h + 1], ph, ALU.mult, ALU.add)
            ts(tmpf[:, 0:n], tmpa[:, 0:n], 1.0 / TWO_PI, None, ALU.mult)
            cp(tmpi[:, 0:n], tmpf[:, 0:n])
            cp(tmpf[:, 0:n], tmpi[:, 0:n])
            stt(tmpa[:, 0:n], tmpf[:, 0:n], -TWO_PI, tmpa[:, 0:n], ALU.mult, ALU.add)
            ts(tmpa[:, 0:n], tmpa[:, 0:n], 3.14159, -3.14159, ALU.min, ALU.max)
            act(dst, tmpa[:, 0:n], AF.Sin)
        ST = [alloc([1, D], F32)[:, 0, :] for _ in range(3)]
        for it in range(NCH):
            st = ST[it % 3]
            if it < 2:
                ld(st, ctx_d[it * 128:(it + 1) * 128, :])
            else:
                ld(st, x_d[(it - 2) * 128:(it - 1) * 128, :])
            for half in range(2):
                pb = ps(((it * 2 + half) % 4) * 512, 512)
                for q in range(4):
                    k = half * 4 + q
                    trp(pb[:, q * 128:(q + 1) * 128], st[:, k * 128:(k + 1) * 128], IDf)
                dstX = X[:, half * 4:half * 4 + 4, it * 128:(it + 1) * 128]
                if it < 2:
                    cp(dstX, pb.rearrange('p (q t) -> p q t', t=128))
                else:
                    r0 = (it - 2) * 2
                    for q in range(4):
                        k = half * 4 + q
                        o3 = X[:, k, it * 128:(it + 1) * 128].rearrange('p (r c) -> p r c', c=64)
                        i3 = pb[:, q * 128:(q + 1) * 128].rearrange('p (r c) -> p r c', c=64)
                        if k < 4:
                            pos = POSR[:, k, r0:r0 + 2].unsqueeze(2).broadcast_to([128, 2, 64])
                        else:
                            pos = POSC[:, k - 4, :].unsqueeze(1).broadcast_to([128, 2, 64])
                        tt(o3, i3, pos, ALU.add)
        A.top = mark

    CHUNKS = [(0, 256, 1)] + [(256 + 512 * i, 512, 0) for i in range(4)]
    pj_rr = [0]

    def pj_bank():
        b = pj_rr[0] % 4
        pj_rr[0] += 1
        return b * 512

    def ada(l):
        mark = A.top
        WB = [alloc([8, 128], BF16) for _ in range(4)]
        for _ in range(3):
            STG.append(alloc([1, 1024], F32)[:, 0, :])
        pm = ps(pj_bank(), 96)
        for ft in range(48):
            wb = WB[ft % 4]
            wload(wb, adaw_d[l][:, ft * 128:(ft + 1) * 128].rearrange('(k p) f -> p k f', p=128),
                  e=('vector', 'scalar')[ft % 2])
            mmg(pm[:, ft * 2:ft * 2 + 2], [(wb[:, k, :], CCb[:, k, :]) for k in range(8)])
        o, _ = PC['adab']
        tt(MOD[:, l], pm.rearrange('p (a b) -> p a b', b=2),
           PRM[:, l, o:o + 48].unsqueeze(2).broadcast_to([128, 48, 2]), ALU.add)
        del STG[2:]
        A.top = mark

    def derive(l):
        for (AA, nm, wh) in ((A1, 'n1g', 1), (A2, 'n2g', 4)):
            o, _ = PC[nm]
            g = PRM[:, l, o:o + 8].unsqueeze(2).broadcast_to([128, 8, 2])
            ts(AA, MOD[:, l, wh * 8:wh * 8 + 8, :], 1.0, None, ALU.add)
            tt(AA, AA, g, ALU.mult)
        o, _ = PC['lam']
        act(KAP[:, l, :, 0], PRM[:, l, o:o + 4], AF.Exp, scale=-1.0)
        act(KAP[:, l, :, 0], KAP[:, l, :, 0], AF.Ln, bias=EPSC[:, 1:2])
        ts(KAP[:, l, :, 1], KAP[:, l, :, 0], -16.0, None, ALU.mult)
        ts(KAP[:, l, :, 0], KAP[:, l, :, 0], -8.0, None, ALU.mult)
        o, _ = PC['b2']
        ts(NB[:, l, 0:6], PRM[:, l, o:o + 6], -1.0, None, ALU.mult)
        o, _ = PC['mgf']
        ts(NB[:, l, 6:12], PRM[:, l, o:o + 6], -1.0, None, ALU.mult)
        o, _ = PC['lgb']
        ts(NB[:, l, 12:18], PRM[:, l, o:o + 6], 1.0, None, ALU.mult)

    NML = int(dbg[0][2:]) if (dbg and dbg[0].startswith('nm')) else 9

    def norm_mod(dst, t0, n, s_, AA, BBl, BBwh, SQ, RS, TMP):
        for k in range(8):
            act(SQ[:, k, 0:n], X[:, k, t0:t0 + n], AF.Square)
        if NML < 2:
            return
        pb = ps(pj_bank(), n)
        mmg(pb, [(ONESb, SQ[:, k, 0:n]) for k in range(8)])
        if NML < 3:
            return
        act(RS[:, 0:n], pb, AF.Ln, bias=EPSC[:, 0:1], scale=1.0 / D)
        act(RS[:, 0:n], RS[:, 0:n], AF.Exp, scale=-0.5)
        if NML < 4:
            return
        for k in range(8):
            stt(TMP[:, 0:n], X[:, k, t0:t0 + n], AA[:, k, s_:s_ + 1], RS[:, 0:n], ALU.mult, ALU.mult)
            if NML < 5:
                continue
            act(dst[:, k, 0:n], TMP[:, 0:n], AF.Identity, bias=MOD[:, BBl, BBwh * 8 + k, s_:s_ + 1])

    def proj(HT, W, M, evac, chunks=CHUNKS):
        for (t0, n, s_) in chunks:
            pb = ps(pj_bank(), n, np_=M)
            mmg(pb, [(W[:, k, 0:M], HT[:, k, t0:t0 + n]) for k in range(8)])
            evac(pb, t0, n, s_)

    def finish_group(l, g8, OUT, G, Wo, last, SQ, Y, RS):
        mark = A.top
        act(SQ, OUT, AF.Square)
        for (t0, n, s_) in CHUNKS:
            pb = ps(pj_bank(), n)
            mmg(pb, [(BLK64, SQ[:, t0:t0 + n])])
            act(RS[:, 0:n], pb, AF.Ln, bias=EPSC[:, 0:1], scale=1.0 / 64)
            act(RS[:, 0:n], RS[:, 0:n], AF.Exp, scale=-0.5)
            tt(RS[:, 0:n], RS[:, 0:n], OUT[:, t0:t0 + n], ALU.mult)
            stt(Y[:, t0:t0 + n], RS[:, 0:n], prm(l, 'hng', g8), G[:, t0:t0 + n], ALU.mult, ALU.mult)
        for (t0, n, s_) in CHUNKS:
            if last and s_ == 1:
                continue
            for k in range(8):
                pb = ps(pj_bank(), n)
                mmg(pb, [(Wo[:, k * 128:(k + 1) * 128], Y[:, t0:t0 + n])])
                stt(X[:, k, t0:t0 + n], pb, MOD[:, l, 2 * 8 + k, s_:s_ + 1], X[:, k, t0:t0 + n], ALU.mult, ALU.add)
        A.top = mark

    def group_a(l, a, HT, last):
        mark = A.top
        Wx = alloc([8, 128], BF16); wx_off = A.last
        Wg = alloc([8, 128], BF16)
        Wo = alloc([1, D], BF16, at=wx_off)[:, 0, :]
        GW = alloc([4, 128], BF16)
        wload(Wx, win_d[l][:, a * 128:(a + 1) * 128].rearrange('(k p) f -> p k f', p=128))
        wload(Wg, win_d[l][:, 256 + a * 128:256 + (a + 1) * 128].rearrange('(k p) f -> p k f', p=128))
        wload(GW, gatew_d[l][:, a * 512:(a + 1) * 512].rearrange('p (i c) -> p i c', c=128))
        XA = alloc([1, T], F32)[:, 0, :]
        H2 = XA
        XCv = alloc([1, T], F32)[:, 0, :]
        XCb = alloc([1, T], BF16)[:, 0, :]
        G = alloc([1, T], BF16)[:, 0, :]
        AB = alloc([1, T], F32)[:, 0, :]; ab_off = A.last
        BBf = alloc([1, T], F32)[:, 0, :]; bb_off = A.last
        H = alloc([1, T], F32)[:, 0, :]
        T1 = alloc([1, 512], F32)[:, 0, :]; T2 = alloc([1, 512], F32)[:, 0, :]
        T1b = alloc([1, 512], F32)[:, 0, :]
        bg_alloc()
        bg(2)
        proj(HT, Wx, 128, lambda pb, t0, n, s_: (cp(XA[:, t0:t0 + n], pb), bg(1)))
        wload(Wo, wout_d[l][a * 128:(a + 1) * 128, :])
        ga_rr = [0]

        def gelu_ev(pb, t0, n, s_):
            ga_rr[0] += 1
            T1 = (T1b, T2)[ga_rr[0] % 2]
            bg(1)
            act(T1[:, 0:n], pb, AF.Square)
            ts(T1[:, 0:n], T1[:, 0:n], 0.044715, 1.0, ALU.mult, ALU.add)
            tt(T1[:, 0:n], T1[:, 0:n], pb, ALU.mult)
            act(T1[:, 0:n], T1[:, 0:n], AF.Sigmoid, scale=2.0 * math.sqrt(2.0 / math.pi))
            tt(G[:, t0:t0 + n], T1[:, 0:n], pb, ALU.mult)
        proj(HT, Wg, 128, gelu_ev)
        o, _ = PC['convw']
        cw = lambda j: PRM[:, l, o + a * 4 + j:o + a * 4 + j + 1]
        for (s0, n) in ((0, TC), (TC, TL)):
            ts(XCv[:, s0:s0 + n], XA[:, s0:s0 + n], cw(2), prm(l, 'convb', a), ALU.mult, ALU.add)
            for j, sh in ((0, -2), (1, -1), (3, 1)):
                lo = max(0, -sh); hi = n - max(0, sh)
                stt(XCv[:, s0 + lo:s0 + hi], XA[:, s0 + lo + sh:s0 + hi + sh], cw(j), XCv[:, s0 + lo:s0 + hi],
                    ALU.mult, ALU.add)
        cp(XCb, XCv)
        T1A, T2A = T1, T2
        bg(2)
        for d in range(2):
            kap = KAP[:, l, a * 2 + d, 0:1]; kap2 = KAP[:, l, a * 2 + d, 1:2]
            o, _ = PC['lgb']
            br = PRM[:, l, o + a * 4 + d * 2:o + a * 4 + d * 2 + 1]
            bi = PRM[:, l, o + a * 4 + d * 2 + 1:o + a * 4 + d * 2 + 2]
            for ic, (t0, n, s_) in enumerate(CHUNKS):
                T1 = T1A if ic % 2 == 0 else T1b
                bg(2)
                pr = ps(pj_bank(), n)
                mmg(pr, [(GW[:, d * 2 + 0, :], XCb[:, t0:t0 + n])])
                pi = ps(pj_bank(), n)
                mmg(pi, [(GW[:, d * 2 + 1, :], XCb[:, t0:t0 + n])])
                act(T1[:, 0:n], pr, AF.Sigmoid, bias=br)
                act(AB[:, t0:t0 + n], T1[:, 0:n], AF.Exp, scale=kap)
                act(T1[:, 0:n], T1[:, 0:n], AF.Exp, scale=kap2)
                act(T1[:, 0:n], T1[:, 0:n], AF.Sqrt, bias=EPSC[:, 1:2], scale=-1.0)
                act(T2[:, 0:n], pi, AF.Sigmoid, bias=bi)
                tt(T1[:, 0:n], T1[:, 0:n], T2[:, 0:n], ALU.mult)
                tt(BBf[:, t0:t0 + n], T1[:, 0:n], XCv[:, t0:t0 + n], ALU.mult)
            if d == 0:
                scan(H, AB, BBf, 0.0)
            else:
                scan(H2[:, 0:TC][:, ::-1], AB[:, 0:TC][:, ::-1], BBf[:, 0:TC][:, ::-1], 0.0)
                scan(H2[:, TC:T][:, ::-1], AB[:, TC:T][:, ::-1], BBf[:, TC:T][:, ::-1], H2[:, 0:1])
        tt(H, H, H2, ALU.add)
        bg(6)
        finish_group(l, a, H, G, Wo, last, alloc([1, T], BF16, at=ab_off)[:, 0, :],
                     alloc([1, T], BF16, at=bb_off)[:, 0, :], T1A)
        if a == 1:
            bg(64)
        bg_flush()
        bg_bufs.clear()
        A.top = mark

    class Stop(Exception):
        pass
    PLIM = int(dbg[0][1:]) if (dbg and dbg[0][0] == 'p') else 99

    def cut(n):
        if PLIM == n:
            raise Stop()

    def group_pair(l, kind, j, HT, LRT, last):
        mark = A.top
        g8 = 2 + kind * 3 + j
        cq = (512 if kind == 0 else 2064) + j * 128
        ck = (896 if kind == 0 else 2448) + j * 128
        cv = (1280 if kind == 0 else 2832) + j * 128
        cg = (1664 if kind == 0 else 3216) + j * 128
        WB2 = [alloc([8, 128], BF16) for _ in range(2)]
        wb_rr = [0]

        def wcol(c0, src=None):
            Wb = WB2[wb_rr[0] % 2]
            wb_rr[0] += 1
            wload(Wb, (win_d[l] if src is None else src)[:, c0:c0 + 128].rearrange('(k p) f -> p k f', p=128), e='scalar')
            return Wb
        Wo = alloc([1, D], BF16)[:, 0, :]
        wload(Wo, wout_d[l][g8 * 128:(g8 + 1) * 128, :])
        if kind == 0:
            W2 = alloc([2, 128], BF16, np_=16)
            for d in range(2):
                wload(W2[:, d, :], w2_d[l][d][:, j * 128:(j + 1) * 128])
        Qb = alloc([1, T], BF16)[:, 0, :]; qb_off = A.last
        Kb = alloc([1, T], BF16)[:, 0, :]; kb_off = A.last
        V1 = alloc([NCH, 130], BF16); VA = alloc([NCH, 128], BF16); VB = alloc([NCH, 128], BF16)
        OUT = alloc([1, T], F32)[:, 0, :]
        PL = alloc([1, T], F32)[:, 0, :]; pl_off = A.last
        QDs = [alloc([1, T], BF16)[:, 0, :] for _ in range(2)]
        KDs = [alloc([1, T], BF16)[:, 0, :] for _ in range(2)]
        DECs = [alloc([1, NCH], F32)[:, 0, :] for _ in range(2)]
        Rsts = [alloc([1, 130], F32)[:, 0, :] for _ in range(2)]
        Sbs = [alloc([2, 128], BF16) for _ in range(2)]
        nSbs = [alloc([2, 128], BF16) for _ in range(2)] if kind == 1 else [None, None]
        T1s = []
        T1bs = []
        for _ in range(2):
            T1s.append(alloc([1, 512], F32)[:, 0, :])
            T1bs.append(alloc([1, 512], BF16, at=A.last)[:, 0, :])
        SbFs = [alloc([2, 130], BF16) for _ in range(2)]
        T2s = [alloc([2, 128], F32) for _ in range(2)] if kind == 1 else [None, None]
        SCbs = [alloc([2, 256], BF16) for _ in range(2)]
        t1_rr = [0]

        def T1n():
            t1_rr[0] += 1
            return T1s[t1_rr[0] % 2]

        def T1bn():
            t1_rr[0] += 1
            return T1bs[t1_rr[0] % 2]
        proj(HT, wcol(cq), 128, lambda pb, t0, n, s_: act(Qb[:, t0:t0 + n], pb, AF.Copy, scale=0.125))
        proj(HT, wcol(ck), 128, lambda pb, t0, n, s_: cp(Kb[:, t0:t0 + n], pb))
        cut(1)
        mset(VA[:, :, 64:128], 0.0); mset(VB[:, :, 0:64], 0.0); mset(V1[:, :, 128:130], 1.0)
        Wv = wcol(cv)
        for cb in range(0, NCH, 4):
            nb4 = min(4, NCH - cb)
            pb = ps(pj_bank(), 512)
            for q in range(nb4):
                c = cb + q
                mmg(pb[:, q * 128:(q + 1) * 128], [(HT[:, k, c * 128:(c + 1) * 128], Wv[:, k, :]) for k in range(8)])
            cp(V1[:, cb:cb + nb4, 0:128], pb[:, 0:nb4 * 128].rearrange('p (q t) -> p q t', t=128))
        cp(VA[:, :, 0:64], V1[:, :, 0:64], e='gpsimd')
        cp(VB[:, :, 64:128], V1[:, :, 64:128], e='gpsimd')
        cut(2)
        sc_ = (1.0 / 16.0) if kind == 0 else 1.0
        for d in range(2):
            QD = QDs[d]; KD = KDs[d]; DEC = DECs[d]
            nb = NB[:, l, (0 if kind == 0 else 6) + j * 2 + d:(0 if kind == 0 else 6) + j * 2 + d + 1]

            def softplus_ev(pb, t0, n, s_):
                t1 = T1n()
                act(t1[:, 0:n], pb, AF.Exp, bias=nb, scale=-1.0)
                act(PL[:, t0:t0 + n], t1[:, 0:n], AF.Ln, bias=EPSC[:, 1:2])
            if kind == 0:
                for (t0, n, s_) in CHUNKS:
                    pb = ps(pj_bank(), n)
                    mmg(pb, [(W2[:, d, :], LRT[:, t0:t0 + n])])
                    softplus_ev(pb, t0, n, s_)
            else:
                proj(HT, wcol(((j * 2 + d) * 2 + 1) * 128, wrep_d[l]), 128, softplus_ev)
            for c in range(NCH):
                plc = PL[:, c * 128:(c + 1) * 128]
                if d == 0:
                    scan(plc, ONESb, plc, 0.0)
                else:
                    scan(plc[:, ::-1], ONESb, plc[:, ::-1], 0.0)
            Pv = PL.rearrange('p (c t) -> p c t', t=128)
            pend = Pv[:, :, 127] if d == 0 else Pv[:, :, 0]
            act(DEC, pend, AF.Exp, scale=-sc_)
            if kind == 0:
                for (t0, n, s_) in CHUNKS:
                    t1 = T1bn()
                    act(t1[:, 0:n], PL[:, t0:t0 + n], AF.Exp, scale=-sc_)
                    tt(QD[:, t0:t0 + n], t1[:, 0:n], Qb[:, t0:t0 + n], ALU.mult)
                    t1 = T1bn()
                    act(t1[:, 0:n], PL[:, t0:t0 + n], AF.Exp, scale=sc_)
                    tt(KD[:, t0:t0 + n], t1[:, 0:n], Kb[:, t0:t0 + n], ALU.mult)
            else:
                for (t0, n, s_) in CHUNKS:
                    t1 = T1bn()
                    act(t1[:, 0:n], PL[:, t0:t0 + n], AF.Exp, scale=-1.0)
                    tt(QD[:, t0:t0 + n], t1[:, 0:n], Qb[:, t0:t0 + n], ALU.mult)
                bi_ = prm(l, 'mgi', j * 2 + d)

                def ig_ev(pb, t0, n, s_, KD=KD):
                    t1 = T1n()
                    tt(t1[:, 0:n], pb, PL[:, t0:t0 + n], ALU.add)
                    act(t1[:, 0:n], t1[:, 0:n], AF.Exp, bias=bi_)
                    tt(KD[:, t0:t0 + n], t1[:, 0:n], Kb[:, t0:t0 + n], ALU.mult)
                proj(HT, wcol(((j * 2 + d) * 2 + 0) * 128, wrep_d[l]), 128, ig_ev)
        cut(3)
        G = alloc([1, T], BF16, at=pl_off)[:, 0, :]
        Wg = wcol(cg)
        if kind == 0:
            proj(HT, Wg, 128, lambda pb, t0, n, s_: act(G[:, t0:t0 + n], pb, AF.Silu))
        else:
            proj(HT, Wg, 128, lambda pb, t0, n, s_: act(G[:, t0:t0 + n], pb, AF.Sigmoid))
        KTs = [alloc([NCH, 128], BF16, at=qb_off), alloc([NCH, 128], BF16, at=kb_off)]
        for d in range(2):
            for cb in range(0, NCH, 4):
                nb4 = min(4, NCH - cb)
                pt = ps(pj_bank(), 512, dt=BF16)
                for q in range(nb4):
                    c = cb + q
                    trp(pt[:, q * 128:(q + 1) * 128], KDs[d][:, c * 128:(c + 1) * 128], IDb)
                src = pt[:, 0:nb4 * 128].rearrange('p (q t) -> p q t', t=128)
                if d == 0:
                    act(KTs[d][:, cb:cb + nb4, :], src, AF.Copy)
                else:
                    cp(KTs[d][:, cb:cb + nb4, :], src)
        cut(4)
        orders = [list(range(NCH)), [1, 0] + list(range(NCH - 1, 1, -1))]
        MASKS = [MASKF, MASKB]
        for d in range(2):
            mset(Sbs[d], 0.0)
            if kind == 1:
                mset(nSbs[d], 0.0)
        pj2 = [0]

        def pjb2():
            pj2[0] += 1
            return (pj2[0] % 2) * 512
        pus = {}

        def stageA(d, i):
            c = orders[d][i]
            tsl = slice(c * 128, (c + 1) * 128)
            par = i % 2
            QD = QDs[d]; KD = KDs[d]
            pscA = ps(2048, 128)
            pscB = ps(2560, 128)
            mmg(pscA, [(KD[0:64, tsl], QD[0:64, tsl])])
            mmg(pscB, [(KD[64:128, tsl], QD[64:128, tsl])])
            psc2 = psum[:, 2048:3072].rearrange('p (h t) -> p h t', t=512)[:, :, 0:128]
            tt(SCbs[d][:, par, :].rearrange('p (h t) -> p h t', t=128), psc2,
               MASKS[d].unsqueeze(1).broadcast_to([128, 2, 128]), ALU.mult)
            pu = ps((6 + d) * 512 + par * 256, 130)
            mmg(pu, [(KTs[d][:, c, :], V1[:, c, :])])
            pus[(d, i)] = pu

        def stageB1(d, i):
            c = orders[d][i]
            tsl = slice(c * 128, (c + 1) * 128)
            par = i % 2
            QD = QDs[d]; DEC = DECs[d]; Rst = Rsts[d]; Sb = Sbs[d]; nSb = nSbs[d]; SCb = SCbs[d]
            pu = pus.pop((d, i))
            if i == 0:
                cp(Rst, pu)
            else:
                cprev = orders[d][i - 1]
                stt(Rst, Rst, DEC[:, cprev:cprev + 1], pu, ALU.mult, ALU.add)
            if i + 1 < NCH:
                nx = (i + 1) % 2
                sbf = SbFs[d][:, nx, :]
                act(sbf, Rst, AF.Copy, scale=DEC[:, c:c + 1])
                tt(Sb[:, nx, :], sbf[:, 0:128], BLK64, ALU.mult, e='gpsimd')
                if kind == 1:
                    ts(nSb[:, nx, :], BLK64, sbf[:, 128:129], 1.0, ALU.mult, ALU.mult, e='gpsimd')
            pob = (d * 2 + par) * 512
            po = ps(pob, 128)
            mmg(po, [(Sb[:, i % 2, :], QD[:, tsl]), (VA[:, c, :], SCb[:, par, 0:128]),
                     (VB[:, c, :], SCb[:, par, 128:256])])
            if kind == 1:
                pd = ps(pob + 128, 128)
                mmg(pd, [(nSb[:, i % 2, :], QD[:, tsl]), (EAb, SCb[:, par, 0:128]), (EBb, SCb[:, par, 128:256])])

        def stageB2a(d, i):
            if kind == 0:
                return
            par = i % 2
            pd = ps((d * 2 + par) * 512 + 128, 128)
            T2 = T2s[d][:, par, :]
            act(T2, pd, AF.Abs)
            act(T2, T2, AF.Ln, bias=EPSC[:, 0:1])
            act(T2, T2, AF.Exp, scale=-1.0)

        def stageB2b(d, i):
            c = orders[d][i]
            tsl = slice(c * 128, (c + 1) * 128)
            par = i % 2
            po = ps((d * 2 + par) * 512, 128)
            if kind == 0:
                tt(OUT[:, tsl], po, OUT[:, tsl], ALU.add)
            else:
                T2 = T2s[d][:, par, :]
                stt(T2, T2, 1.0, po, ALU.min, ALU.mult)
                tt(OUT[:, tsl], OUT[:, tsl], T2, ALU.add, e='gpsimd')
        mset(OUT, 0.0, e='gpsimd')
        stageA(0, 0); stageA(1, 0)
        for i in range(NCH):
            if i + 1 < NCH:
                stageA(0, i + 1); stageA(1, i + 1)
            stageB1(0, i); stageB1(1, i)
            stageB2a(0, i); stageB2a(1, i)
            if i >= 1:
                stageB2b(0, i - 1); stageB2b(1, i - 1)
        stageB2b(0, NCH - 1); stageB2b(1, NCH - 1)
        cut(5)
        cut(6)
        finish_group(l, g8, OUT, G, Wo, last, KDs[0], alloc([1, T], BF16, at=qb_off)[:, 0, :], T1s[0])
        A.top = mark

    def mlp(l, last):
        mark = A.top
        NW = 6
        W1 = [alloc([8, 128], BF16) for _ in range(NW)]
        W2b = [alloc([1, D], BF16)[:, 0, :] for _ in range(NW)]
        H2 = alloc([8, 1024], BF16)
        A1T = alloc([32, 1024], BF16); a1t_off = A.last
        SQ = alloc([8, 512], BF16, at=a1t_off); RS = alloc([1, 512], F32)[:, 0, :]; TMP = alloc([1, 512], F32)[:, 0, :]
        TM3 = [alloc([1, 512], F32)[:, 0, :] for _ in range(3)]
        tm_rr = [0]
        passes = [(256, 1024, 0), (1280, 1024, 0)]
        if not last:
            passes = [(0, 256, 1)] + passes
        for (p0, pn, s_) in passes:
            subs = [(p0 + i, min(512, pn - i)) for i in range(0, pn, 512)]
            for (t0, n) in subs:
                norm_mod(H2[:, :, t0 - p0:t0 - p0 + n], t0, n, s_, A2, l, 3, SQ, RS, TMP)
            for f in range(32):
                w1 = W1[f % NW]
                wload_cached(w1, w1_d[l][:, f * 128:(f + 1) * 128].rearrange('(k p) f -> p k f', p=128),
                             w1s_d[l, f], ('w1', l, f))
                for (t0, n) in subs:
                    pb = ps(pj_bank(), n)
                    mmg(pb, [(w1[:, k, :], H2[:, k, t0 - p0:t0 - p0 + n]) for k in range(8)])
                    tm_rr[0] += 1
                    tm = TM3[tm_rr[0] % 3]
                    act(tm[:, 0:n], pb, AF.Relu)
                    tt(A1T[:, f, t0 - p0:t0 - p0 + n], tm[:, 0:n], tm[:, 0:n], ALU.mult)
            for (t0, n) in subs:
                for f in range(32):
                    w2 = W2b[f % NW]
                    wload_cached(w2, wm2_d[l][f * 128:(f + 1) * 128, :], w2s_d[l, f], ('w2', l, f))
                    for k in range(8):
                        pb = ps(k * 512, n)
                        s.op('tensor', (lambda en, pb=pb, w=w2[:, k * 128:(k + 1) * 128],
                                        r=A1T[:, f, t0 - p0:t0 - p0 + n], st_=(f == 0), sp_=(f == 31):
                                        en.matmul(pb, lhsT=w, rhs=r, start=st_, stop=sp_)),
                             reads=[w2[:, k * 128:(k + 1) * 128], A1T[:, f, t0 - p0:t0 - p0 + n]], writes=[pb])
                for k in range(8):
                    pb = ps(k * 512, n)
                    stt(X[:, k, t0:t0 + n], pb, MOD[:, l, 5 * 8 + k, s_:s_ + 1], X[:, k, t0:t0 + n], ALU.mult, ALU.add)
        del STG[2:]
        A.top = mark

    def final():
        mark = A.top
        FG = alloc([1, D], F32)[:, 0, :]
        ld(FG, fg_d)
        OT = [alloc([1, D], F32)[:, 0, :] for _ in range(2)]
        SS = alloc([1, 4], F32)[:, 0, :]
        JK = alloc([1, D], F32)[:, 0, :]
        toks = []
        for it in range(TL // 128):
            ot = OT[it % 2]
            t0 = TC + it * 128
            for half in range(2):
                pb = ps(((it * 2 + half) % 4) * 512, 512)
                for q in range(4):
                    k = half * 4 + q
                    trp(pb[:, q * 128:(q + 1) * 128], X[:, k, t0:t0 + 128], IDf)
                cp(ot[:, half * 512:(half + 1) * 512], pb)
            c0 = it % 2
            s.op('scalar', lambda en, ot=ot, c0=c0: en.activation(out=JK, in_=ot, func=AF.Square, accum_out=SS[:, c0:c0 + 1]),
                 reads=[ot], writes=[JK, SS[:, c0:c0 + 1]])
            act(SS[:, 2 + c0:3 + c0], SS[:, c0:c0 + 1], AF.Ln, bias=EPSC[:, 0:1], scale=1.0 / D)
            act(SS[:, 2 + c0:3 + c0], SS[:, 2 + c0:3 + c0], AF.Exp, scale=-0.5)
            stt(ot, ot, SS[:, 2 + c0:3 + c0], FG, ALU.mult, ALU.mult)
            toks.append(s.dma('sync', out_d[it * 128:(it + 1) * 128, :], ot, sb_reads=[ot]))
        for tk in toks[-4:]:
            s.wait_tok('sync', tk)
        A.top = mark

    def dump(ap, ncol):
        mark = A.top
        DB = alloc([1, dbg[1]], F32)[:, 0, :]
        mset(DB, 0.0)
        cp(DB[:, 0:ncol], ap)
        tk = s.dma('sync', dbg_d, DB, sb_reads=[DB])
        s.wait_tok('sync', tk)
        A.top = mark

    load_x()
    stage = dbg[0] if dbg else None
    if stage == 'x':
        dump(X[:, int(dbg[2]), :], T)
        n_layers = 0
    for l in range(n_layers):
        last = (l == DEPTH - 1)
        ada(l)
        if stage == 'ada':
            dump(MOD[:, l].rearrange('p a b -> p (a b)'), 96)
            break
        derive(l)
        if stage == 'drv':
            dump(A1.rearrange('p a b -> p (a b)'), 16)
            break
        mark_l = A.top
        HT = alloc([8, T], BF16)
        SQ = alloc([8, 512], BF16); RS = alloc([1, 512], F32)[:, 0, :]; TMP = alloc([1, 512], F32)[:, 0, :]
        for (t0, n, s_) in CHUNKS:
            norm_mod(HT[:, :, t0:t0 + n], t0, n, s_, A1, l, 0, SQ, RS, TMP)
        A.top = mark_l + 8 * T * 2
        if stage and (stage == 'h%d' % l or stage.startswith('nm')):
            mark = A.top
            HF = alloc([1, T], F32)[:, 0, :]
            cp(HF, HT[:, 0, :])
            dump(HF, T)
            A.top = mark
            break
        LRT = alloc([1, T], BF16, np_=16)[:, 0, :]
        Wl = alloc([8, 16], BF16)
        wload(Wl, win_d[l][:, 2048:2064].rearrange('(k p) f -> p k f', p=128))
        proj(HT, Wl, 16, lambda pb, t0, n, s_: cp(LRT[:, t0:t0 + n], pb))
        bg_setup(l)
        groups = [('a', 0), ('a', 1)] + [('p', 0, j) for j in range(3)] + [('p', 1, j) for j in range(3)]
        if stage and stage.startswith('g'):
            groups = groups[:int(stage[1:]) + 1]
        if stage and stage[0] == 'p':
            groups = [('p', int(dbg[2]), 0)]
        try:
            for g in groups:
                if g[0] == 'a':
                    group_a(l, g[1], HT, last)
                else:
                    group_pair(l, g[1], g[2], HT, LRT, last)
        except Stop:
            pass
        A.top = mark_l
        if stage and (stage.startswith('g') or stage[0] == 'p'):
            dump(X[:, int(dbg[2]), :], T)
            break
        mlp(l, last)
        if stage == 'l%d' % l:
            dump(X[:, int(dbg[2]), :], T)
            break
    if not stage:
        final()
    s.emit()
    print('ninst', s.ninst, 'arena peak', A.peak, 'persistent', P_MARK)
    return nc


_NC = {}


def prep_inputs(inp):
    inp = {k: np.asarray(v) for k, v in inp.items()}
    f = lambda a: np.ascontiguousarray(a, dtype=np.float32)
    shared = {
        'ada_w': f(inp['ada_w']), 'w_in': f(inp['w_in']),
        'wrep': f(np.stack([pack_wrep(inp, l) for l in range(DEPTH)])),
        'gatew': f(np.stack([pack_gatew(inp, l) for l in range(DEPTH)])),
        'gla_w2': f(inp['gla_w2']), 'w_out': f(inp['w_out']), 'mlp_w1': f(inp['mlp_w1']), 'mlp_w2': f(inp['mlp_w2']),
        'prm': f(np.stack([pack_params(inp, l) for l in range(DEPTH)])),
        'fg': f(np.broadcast_to(inp['final_g'][None, :], (128, D))), 'cst': make_consts(),
    }
    maps = []
    for b in range(8):
        cc = np.zeros((128, 16), np.float32)
        cc[:, 0::2] = inp['c'][b].reshape(8, 128).T
        cc[:, 1::2] = inp['c_ctx'].reshape(8, 128).T
        m = dict(shared)
        m['x'] = f(inp['x'][b]); m['ctx'] = f(inp['ctx'][b]); m['cc'] = cc
        maps.append(m)
    return maps


def kernel(**inputs):
    if 'nc' not in _NC:
        _NC['nc'] = build()
    maps = prep_inputs(inputs)
    res = run_bass_kernel_spmd(_NC['nc'], maps, core_ids=list(range(8)))
    return np.stack([np.asarray(res.results[b]['out'], dtype=np.float32) for b in range(8)], axis=0)
```

```python
import math
import numpy as np
import ml_dtypes
import concourse.bass as bass
import concourse.mybir as mybir
from concourse.bass_utils import run_bass_kernel_spmd

F32 = mybir.dt.float32
BF16 = mybir.dt.bfloat16
I32 = mybir.dt.int32
U8 = mybir.dt.uint8
AF = mybir.ActivationFunctionType
ALU = mybir.AluOpType

D = 1024
TL = 2048
TC = 256
T = TL + TC
NCH = T // 128
DEPTH = 2
DIN = 3624
DFF = 4096
EPS = 1e-6
ENG = ['tensor', 'vector', 'scalar', 'gpsimd', 'sync']
DSZ = {F32: 4, BF16: 2, I32: 4, U8: 1}


def _dsz(dt):
    for k, v in DSZ.items():
        if k == dt:
            return v
    return 4


def region(ap):
    es = _dsz(ap.dtype)
    aps = ap.ap
    pstep, pcnt = aps[0]
    off = ap.offset
    if pstep == 0:
        pstep = 1 << 40
    p0 = off // pstep
    fo = off % pstep
    lo = fo
    hi = fo
    for st, c in aps[1:]:
        d = st * (c - 1)
        if d < 0:
            lo += d
        else:
            hi += d
    if ap.tensor.name == 'psum':
        return ('psum', 0, 128, (lo * es) // 2048 * 2048, ((hi + 1) * es + 2047) // 2048 * 2048)
    return (ap.tensor.name, p0, p0 + pcnt, lo * es, (hi + 1) * es)


class Sched:
    def __init__(self, nc, n_dma_sems=32):
        self.nc = nc
        self.sem = {}
        for e in ENG:
            self.sem[e] = nc.semaphore('s_' + e).__enter__()
        self.dma_sems = [nc.semaphore('s_dma%d' % i).__enter__() for i in range(n_dma_sems)]
        self.dma_cnt = [0] * n_dma_sems
        self.dma_next = 0
        self.cnt = {e: 0 for e in ENG}
        self.seen = {e: {} for e in ENG}
        self.q = {e: [] for e in ENG}
        self.acc = {}
        self.ninst = 0

    def _deps(self, reads, writes, e=None):
        deps = {}
        for ap in reads:
            name, p0, p1, f0, f1 = region(ap)
            ps_ = (name == 'psum')
            for r in self.acc.get(name, ()):
                if (r[4] or (ps_ and r[5][0] != e)) and r[0] < p1 and p0 < r[1] and r[2] < f1 and f0 < r[3]:
                    k, v = r[5]
                    if deps.get(k, 0) < v:
                        deps[k] = v
        for ap in writes:
            name, p0, p1, f0, f1 = region(ap)
            for r in self.acc.get(name, ()):
                if r[0] < p1 and p0 < r[1] and r[2] < f1 and f0 < r[3]:
                    k, v = r[5]
                    if deps.get(k, 0) < v:
                        deps[k] = v
        return deps

    def _record(self, reads, writes, tok):
        for ap in writes:
            name, p0, p1, f0, f1 = region(ap)
            lst = self.acc.setdefault(name, [])
            lst[:] = [r for r in lst if not (p0 <= r[0] and r[1] <= p1 and f0 <= r[2] and r[3] <= f1)]
            lst.append((p0, p1, f0, f1, True, tok))
        for ap in reads:
            name, p0, p1, f0, f1 = region(ap)
            lst = self.acc.setdefault(name, [])
            lst[:] = [r for r in lst if not ((not r[4]) and r[5][0] == tok[0] and p0 <= r[0] and r[1] <= p1
                                             and f0 <= r[2] and r[3] <= f1)]
            lst.append((p0, p1, f0, f1, False, tok))

    def _waits(self, e, deps):
        out = []
        seen = self.seen[e]
        for k, v in deps.items():
            if seen.get(k, 0) >= v:
                continue
            seen[k] = v
            if k == e and e == 'tensor':
                continue
            out.append((k, v))
        return out

    def _semof(self, k):
        return self.sem[k] if isinstance(k, str) else self.dma_sems[k[1]]

    def op(self, e, fn, reads=(), writes=()):
        reads = [r for r in reads if r is not None and not isinstance(r, (int, float))]
        deps = self._deps(reads, writes, e)
        waits = self._waits(e, deps)
        self.cnt[e] += 1
        tok = (e, self.cnt[e])
        self._record(reads, writes, tok)
        self.q[e].append((waits, fn, self.sem[e], 1))
        self.ninst += 1
        return tok

    def dma(self, qe, out, in_, sb_reads=(), sb_writes=(), **kw):
        deps = self._deps(sb_reads, sb_writes)
        i = self.dma_next
        self.dma_next = (self.dma_next + 1) % len(self.dma_sems)
        if self.dma_cnt[i] > 0:
            k = ('dma', i)
            if deps.get(k, 0) < self.dma_cnt[i] * 16:
                deps[k] = self.dma_cnt[i] * 16
        waits = self._waits(qe, deps)
        self.dma_cnt[i] += 1
        tok = (('dma', i), self.dma_cnt[i] * 16)
        self._record(sb_reads, sb_writes, tok)
        self.q[qe].append((waits, (lambda eng, out=out, in_=in_, kw=kw: eng.dma_start(out=out, in_=in_, **kw)),
                           self.dma_sems[i], 16))
        self.ninst += 1
        return tok

    def wait_tok(self, e, tok):
        waits = self._waits(e, {tok[0]: tok[1]})
        if waits:
            self.q[e].append((waits, None, None, 0))

    def emit(self):
        nc = self.nc
        with nc.Block() as block:
            def mk(e):
                def body(eng):
                    for waits, fn, sem, inc in self.q[e]:
                        for k, v in waits:
                            eng.wait_ge(self._semof(k), v)
                        if fn is not None:
                            fn(eng).then_inc(sem, inc)
                return body
            block.tensor(mk('tensor'))
            block.vector(mk('vector'))
            block.scalar(mk('scalar'))
            block.gpsimd(mk('gpsimd'))
            block.sync(mk('sync'))


NCST = 128 * 7 + 1 + 32 + 64


def make_consts():
    p = np.arange(128)
    c = np.zeros((128, NCST), np.float32)
    o = 0
    c[:, o:o + 128] = np.eye(128); o += 128
    c[:, o:o + 128] = (p[:, None] <= p[None, :]); o += 128
    c[:, o:o + 128] = (p[:, None] >= p[None, :]); o += 128
    c[:, o:o + 128] = 1.0; o += 128
    c[:, o:o + 128] = (p[:, None] // 64 == p[None, :] // 64); o += 128
    c[:, o:o + 128] = (p[None, :] < 64); o += 128
    c[:, o:o + 128] = (p[None, :] >= 64); o += 128
    c[:, o] = p; o += 1
    c[:, o:o + 32] = np.arange(32)[None, :]; o += 32
    c[:, o:o + 64] = np.arange(64)[None, :]; o += 64
    return c


PC = {}
_o = 0
for _n, _w in [('n1g', 8), ('n2g', 8), ('adab', 48), ('hng', 8), ('convw', 8), ('convb', 2), ('lgb', 8),
               ('lam', 4), ('b2', 6), ('mgi', 6), ('mgf', 6)]:
    PC[_n] = (_o, _w)
    _o += _w
NPRM = _o


def pack_params(inp, l):
    P = np.zeros((128, NPRM), np.float32)

    def put(name, arr):
        o, w = PC[name]
        assert arr.shape == (128, w), (name, arr.shape)
        P[:, o:o + w] = arr
    put('n1g', inp['norm1_g'][l].reshape(8, 128).T)
    put('n2g', inp['norm2_g'][l].reshape(8, 128).T)
    put('adab', inp['ada_b'][l].reshape(48, 128).T)
    put('hng', inp['head_norm_g'][l].reshape(8, 128).T)
    cw = inp['conv_w'][l]
    put('convw', np.concatenate([cw[:, a * 128:(a + 1) * 128].T for a in range(2)], axis=1))
    put('convb', inp['conv_b'][l].reshape(2, 128).T)
    gb = inp['lru_gate_b'][l]
    put('lgb', np.stack([gb[d, g, a * 128:(a + 1) * 128] for a in range(2) for d in range(2) for g in range(2)], axis=1))
    lam = inp['lru_lambda'][l]
    put('lam', np.stack([lam[d, a * 128:(a + 1) * 128] for a in range(2) for d in range(2)], axis=1))
    b2 = inp['gla_b2'][l]
    put('b2', np.stack([b2[d, j * 128:(j + 1) * 128] for j in range(3) for d in range(2)], axis=1))
    mg = inp['mlstm_gate_b'][l]
    put('mgi', np.stack([np.repeat(mg[d, 0, 2 * j:2 * j + 2], 64) for j in range(3) for d in range(2)], axis=1))
    put('mgf', np.stack([np.repeat(mg[d, 1, 2 * j:2 * j + 2], 64) for j in range(3) for d in range(2)], axis=1))
    return P


def pack_gatew(inp, l):
    gw = inp['lru_gate_w'][l]
    out = np.zeros((128, 8, 128), np.float32)
    for a in range(2):
        for d in range(2):
            for g in range(2):
                i = a * 4 + d * 2 + g
                for bb in range(2):
                    out[bb * 64:(bb + 1) * 64, i, bb * 64:(bb + 1) * 64] = gw[d, g, a * 2 + bb]
    return out.reshape(128, 8 * 128)


def pack_wrep(inp, l):
    w = inp['w_in'][l]
    blocks = []
    for j in range(3):
        for d in range(2):
            for kind in range(2):
                base = 3600 + kind * 12 + d * 6 + 2 * j
                blocks.append(np.repeat(w[:, base:base + 2], 64, axis=1))
    return np.ascontiguousarray(np.concatenate(blocks, axis=1))


def build(n_layers=DEPTH, dbg=None):
    nc = bass.Bass('TRN2', target_bir_lowering=False)

    def din(name, shape):
        return nc.dram_tensor(name, list(shape), F32, kind='ExternalInput').ap()
    x_d = din('x', [TL, D]); ctx_d = din('ctx', [TC, D]); cc_d = din('cc', [128, 16])
    adaw_d = din('ada_w', [DEPTH, D, 6 * D]); win_d = din('w_in', [DEPTH, D, DIN])
    wrep_d = din('wrep', [DEPTH, D, 12 * 128]); gatew_d = din('gatew', [DEPTH, 128, 8 * 128])
    w2_d = din('gla_w2', [DEPTH, 2, 16, 384]); wout_d = din('w_out', [DEPTH, D, D])
    w1_d = din('mlp_w1', [DEPTH, D, DFF]); wm2_d = din('mlp_w2', [DEPTH, DFF, D])
    prm_d = din('prm', [DEPTH, 128, NPRM]); fg_d = din('fg', [128, D]); cst_d = din('cst', [128, NCST])
    out_d = nc.dram_tensor('out', [TL, D], F32, kind='ExternalOutput').ap()
    dbg_d = None
    if dbg:
        dbg_d = nc.dram_tensor('dbg', [128, dbg[1]], F32, kind='ExternalOutput').ap()

    s = Sched(nc)
    ARENA = 206 * 1024
    arena = nc.alloc_sbuf_tensor('arena', [128, ARENA], U8)
    psum = nc.alloc_psum_tensor('psum', [128, 4096], F32)

    class A:
        top = 0
        peak = 0
        last = 0

    def alloc(shape, dt, np_=128, at=None):
        n = 1
        for v in shape:
            n *= v
        nb = (n * _dsz(dt) + 31) // 32 * 32
        if at is None:
            off = A.top
            A.top += nb
            A.peak = max(A.peak, A.top)
        else:
            off = at
        A.last = off
        assert off + nb <= ARENA, ('arena overflow', off + nb)
        ap = arena[:, off:off + n * _dsz(dt)].bitcast(dt)
        if len(shape) == 2:
            ap = ap.rearrange('p (a b) -> p a b', b=shape[1])
        elif len(shape) == 3:
            ap = ap.rearrange('p (a b c) -> p a b c', b=shape[1], c=shape[2])
        return ap[0:np_] if np_ != 128 else ap

    def ps(off, n, dt=F32, np_=128, p0=0):
        if dt == F32:
            return psum[p0:p0 + np_, off:off + n]
        return psum[p0:p0 + np_, off:off + (n + 1) // 2].bitcast(dt)[:, 0:n]

    def R_(*aps):
        return [a for a in aps if a is not None and not isinstance(a, (int, float))]

    def act(out, in_, func, bias=None, scale=None, e='scalar'):
        kw = {}
        if bias is not None:
            kw['bias'] = bias
        if scale is not None:
            kw['scale'] = scale
        s.op(e, lambda en: en.activation(out=out, in_=in_, func=func, **kw), reads=R_(in_, bias, scale), writes=[out])

    def tt(out, in0, in1, op, e='vector'):
        s.op(e, lambda en: en.tensor_tensor(out=out, in0=in0, in1=in1, op=op), reads=[in0, in1], writes=[out])

    def ts(out, in0, s1, s2, op0, op1=None, e='vector'):
        if op1 is None:
            s.op(e, lambda en: en.tensor_scalar(out=out, in0=in0, scalar1=s1, scalar2=None, op0=op0),
                 reads=R_(in0, s1), writes=[out])
        else:
            s.op(e, lambda en: en.tensor_scalar(out=out, in0=in0, scalar1=s1, scalar2=s2, op0=op0, op1=op1),
                 reads=R_(in0, s1, s2), writes=[out])

    def stt(out, in0, sc, in1, op0, op1):
        s.op('vector', lambda en: en.scalar_tensor_tensor(out=out, in0=in0, scalar=sc, in1=in1, op0=op0, op1=op1),
             reads=R_(in0, sc, in1), writes=[out])

    def cp(out, in_, e='vector'):
        s.op(e, lambda en: en.tensor_copy(out=out, in_=in_), reads=[in_], writes=[out])

    def mset(out, v, e='vector'):
        s.op(e, lambda en: en.memset(out, v), writes=[out])

    def mmg(out, pairs):
        n = len(pairs)

        def fn(en):
            inst = None
            for i, (l, r) in enumerate(pairs):
                inst = en.matmul(out, lhsT=l, rhs=r, start=(i == 0), stop=(i == n - 1))
            return inst
        rd = []
        for l, r in pairs:
            rd += [l, r]
        s.op('tensor', fn, reads=rd, writes=[out])

    def trp(out, in_, ident):
        s.op('tensor', lambda en: en.transpose(out, in_, ident), reads=[in_, ident], writes=[out])

    def scan(out, d0, d1, init):
        s.op('vector', lambda en: en.tensor_tensor_scan(out=out, data0=d0, data1=d1, initial=init, op0=ALU.mult,
                                                         op1=ALU.add), reads=R_(d0, d1, init), writes=[out])

    def ld(out, in_, q='sync'):
        s.dma(q, out, in_, sb_writes=[out])

    X = alloc([8, T], F32)
    CST = alloc([NCST], F32)[:, 0, :] if False else alloc([1, NCST], F32)[:, 0, :]
    ld(CST, cst_d)
    IDf = CST[:, 0:128]
    PIDX = CST[:, 896:897]; RIDX = CST[:, 897:929]; CIDX = CST[:, 929:993]
    CB = alloc([7, 128], BF16)
    cp(CB, CST[:, 0:896].rearrange('p (a b) -> p a b', b=128))
    IDb, MASKF, MASKB, ONESb, BLK64, EAb, EBb = [CB[:, i, :] for i in range(7)]
    PRM = alloc([DEPTH, NPRM], F32)
    for l in range(DEPTH):
        ld(PRM[:, l, :], prm_d[l])
    CCT = alloc([1, 16], F32)[:, 0, :]
    ld(CCT, cc_d)
    CCb = alloc([8, 2], BF16)
    act(CCb, CCT.rearrange('p (k s) -> p k s', s=2), AF.Silu)
    MOD = alloc([DEPTH, 48, 2], F32)
    DRV = alloc([DEPTH, 4, 8, 2], F32) if False else None
    A1 = alloc([8, 2], F32); A2 = alloc([8, 2], F32)
    EPSC = alloc([1, 4], F32)[:, 0, :]
    mset(EPSC[:, 0:1], EPS)
    mset(EPSC[:, 1:2], 1.0)
    mset(EPSC[:, 2:3], 0.0)
    KAP = alloc([DEPTH, 4, 2], F32)
    NB = alloc([DEPTH, 18], F32)
    P_MARK = A.top

    def prm(l, name, i=0, w=1):
        o, _ = PC[name]
        return PRM[:, l, o + i:o + i + w]

    STG = [alloc([1, 1024], F32)[:, 0, :] for _ in range(2)]
    stg_rr = [0]

    def wload(out, src, e='gpsimd'):
        st = STG[stg_rr[0] % len(STG)]
        stg_rr[0] += 1
        shp = list(out.shape)
        np_ = shp[0]
        n = 1
        for v in shp[1:]:
            n *= v
        v_ = st[0:np_, 0:n]
        if len(shp) == 3:
            v_ = v_.rearrange('p (a b) -> p a b', b=shp[2])
        s.dma('sync', v_, src, sb_writes=[v_])
        if e == 'scalar':
            act(out, v_, AF.Copy)
        else:
            cp(out, v_, e=e)

    w1s_d = nc.dram_tensor('w1s', [DEPTH, 32, 128, 1024], BF16, kind='Internal').ap()
    w2s_d = nc.dram_tensor('w2s', [DEPTH, 32, 128, 1024], BF16, kind='Internal').ap()
    stok = {}
    cast_rr = [0]

    def wload_cached(out, src, scr, key):
        if key not in stok:
            e = ('vector', 'scalar')[cast_rr[0] % 2]
            cast_rr[0] += 1
            wload(out, src, e=e)
            dst = scr if len(out.shape) == 2 else scr.rearrange('p (k f) -> p k f', f=out.shape[2])
            stok[key] = s.dma('sync', dst, out, sb_reads=[out])
        else:
            s.wait_tok('sync', stok[key])
            s.dma('sync', out, scr if len(out.shape) == 2 else scr.rearrange('p (k f) -> p k f', f=out.shape[2]),
                  sb_writes=[out])

    bg_jobs = []
    bg_bufs = {}
    bg_rr = [0]

    def bg_setup(l):
        del bg_jobs[:]
        for f in range(32):
            bg_jobs.append((w1_d[l][:, f * 128:(f + 1) * 128].rearrange('(k p) f -> p k f', p=128), w1s_d[l, f], ('w1', l, f), 3))
        for f in range(32):
            bg_jobs.append((wm2_d[l][f * 128:(f + 1) * 128, :], w2s_d[l, f], ('w2', l, f), 2))

    bg_pend = []

    def bg_flush():
        while bg_pend:
            scr, bf, key = bg_pend.pop(0)
            stok[key] = s.dma('sync', scr, bf, sb_reads=[bf])

    def bg_alloc():
        bg_bufs['st'] = [alloc([1, 1024], F32)[:, 0, :] for _ in range(2)]
        bg_bufs['bf'] = [alloc([1, 1024], BF16)[:, 0, :] for _ in range(2)]

    def bg(n):
        for _ in range(n):
            if not bg_jobs or 'st' not in bg_bufs:
                return
            src, scr, key, nd = bg_jobs.pop(0)
            i = bg_rr[0] % 2
            bg_rr[0] += 1
            st = bg_bufs['st'][i]; bf = bg_bufs['bf'][i]
            if nd == 3:
                st = st.rearrange('p (k f) -> p k f', f=128); bf = bf.rearrange('p (k f) -> p k f', f=128)
                scr = scr.rearrange('p (k f) -> p k f', f=128)
            s.dma('sync', st, src, sb_writes=[st])
            cp(bf, st, e='gpsimd')
            bg_flush()
            bg_pend.append((scr, bf, key))

    def load_x():
        mark = A.top
        POSR = alloc([4, 32], F32); POSC = alloc([4, 64], F32)
        FRQ = alloc([1, 2], F32)[:, 0, :]
        for h in range(2):
            ts(FRQ[:, h:h + 1], PIDX, float(h * 128), None, ALU.add)
        act(FRQ, FRQ, AF.Exp, scale=-math.log(10000.0) / 256.0)
        TWO_PI = 2.0 * math.pi
        tmpi = alloc([1, 64], I32)[:, 0, :]; tmpf = alloc([1, 64], F32)[:, 0, :]; tmpa = alloc([1, 64], F32)[:, 0, :]
        for kk in range(8):
            h = kk % 2
            grp = kk // 2
            n = 32 if grp < 2 else 64
            idx = RIDX if grp < 2 else CIDX
            dst = POSR[:, kk, :] if grp < 2 else POSC[:, kk - 4, :]
            ph = 0.0 if grp % 2 == 0 else math.pi / 2
            ts(tmpa[:, 0:n], idx, FRQ[:, h:h + 1], ph, ALU.mult, ALU.add)
            ts(tmpf[:, 0:n], tmpa[:, 0:n], 1.0 / TWO_PI, None, ALU.mult)
            cp(tmpi[:, 0:n], tmpf[:, 0:n])
            cp(tmpf[:, 0:n], tmpi[:, 0:n])
            stt(tmpa[:, 0:n], tmpf[:, 0:n], -TWO_PI, tmpa[:, 0:n], ALU.mult, ALU.add)
            ts(tmpa[:, 0:n], tmpa[:, 0:n], 3.14159, -3.14159, ALU.min, ALU.max)
            act(dst, tmpa[:, 0:n], AF.Sin)
        ST = [alloc([1, D], F32)[:, 0, :] for _ in range(3)]
        for it in range(NCH):
            st = ST[it % 3]
            if it < 2:
                ld(st, ctx_d[it * 128:(it + 1) * 128, :])
            else:
                ld(st, x_d[(it - 2) * 128:(it - 1) * 128, :])
            for half in range(2):
                pb = ps(((it * 2 + half) % 4) * 512, 512)
                for q in range(4):
                    k = half * 4 + q
                    trp(pb[:, q * 128:(q + 1) * 128], st[:, k * 128:(k + 1) * 128], IDf)
                dstX = X[:, half * 4:half * 4 + 4, it * 128:(it + 1) * 128]
                if it < 2:
                    cp(dstX, pb.rearrange('p (q t) -> p q t', t=128))
                else:
                    r0 = (it - 2) * 2
                    for q in range(4):
                        k = half * 4 + q
                        o3 = X[:, k, it * 128:(it + 1) * 128].rearrange('p (r c) -> p r c', c=64)
                        i3 = pb[:, q * 128:(q + 1) * 128].rearrange('p (r c) -> p r c', c=64)
                        if k < 4:
                            pos = POSR[:, k, r0:r0 + 2].unsqueeze(2).broadcast_to([128, 2, 64])
                        else:
                            pos = POSC[:, k - 4, :].unsqueeze(1).broadcast_to([128, 2, 64])
                        tt(o3, i3, pos, ALU.add)
        A.top = mark

    CHUNKS = [(0, 256, 1)] + [(256 + 512 * i, 512, 0) for i in range(4)]
    pj_rr = [0]

    def pj_bank():
        b = pj_rr[0] % 4
        pj_rr[0] += 1
        return b * 512

    def ada(l):
        mark = A.top
        WB = [alloc([8, 128], BF16) for _ in range(4)]
        for _ in range(3):
            STG.append(alloc([1, 1024], F32)[:, 0, :])
        pm = ps(pj_bank(), 96)
        for ft in range(48):
            wb = WB[ft % 4]
            wload(wb, adaw_d[l][:, ft * 128:(ft + 1) * 128].rearrange('(k p) f -> p k f', p=128),
                  e=('vector', 'scalar')[ft % 2])
            mmg(pm[:, ft * 2:ft * 2 + 2], [(wb[:, k, :], CCb[:, k, :]) for k in range(8)])
        o, _ = PC['adab']
        tt(MOD[:, l], pm.rearrange('p (a b) -> p a b', b=2),
           PRM[:, l, o:o + 48].unsqueeze(2).broadcast_to([128, 48, 2]), ALU.add)
        del STG[2:]
        A.top = mark

    def derive(l):
        for (AA, nm, wh) in ((A1, 'n1g', 1), (A2, 'n2g', 4)):
            o, _ = PC[nm]
            g = PRM[:, l, o:o + 8].unsqueeze(2).broadcast_to([128, 8, 2])
            ts(AA, MOD[:, l, wh * 8:wh * 8 + 8, :], 1.0, None, ALU.add)
            tt(AA, AA, g, ALU.mult)
        o, _ = PC['lam']
        act(KAP[:, l, :, 0], PRM[:, l, o:o + 4], AF.Exp, scale=-1.0)
        act(KAP[:, l, :, 0], KAP[:, l, :, 0], AF.Ln, bias=EPSC[:, 1:2])
        ts(KAP[:, l, :, 1], KAP[:, l, :, 0], -16.0, None, ALU.mult)
        ts(KAP[:, l, :, 0], KAP[:, l, :, 0], -8.0, None, ALU.mult)
        o, _ = PC['b2']
        ts(NB[:, l, 0:6], PRM[:, l, o:o + 6], -1.0, None, ALU.mult)
        o, _ = PC['mgf']
        ts(NB[:, l, 6:12], PRM[:, l, o:o + 6], -1.0, None, ALU.mult)
        o, _ = PC['lgb']
        ts(NB[:, l, 12:18], PRM[:, l, o:o + 6], 1.0, None, ALU.mult)

    NML = int(dbg[0][2:]) if (dbg and dbg[0].startswith('nm')) else 9

    def norm_mod(dst, t0, n, s_, AA, BBl, BBwh, SQ, RS, TMP):
        for k in range(8):
            act(SQ[:, k, 0:n], X[:, k, t0:t0 + n], AF.Square)
        if NML < 2:
            return
        pb = ps(pj_bank(), n)
        mmg(pb, [(ONESb, SQ[:, k, 0:n]) for k in range(8)])
        if NML < 3:
            return
        act(RS[:, 0:n], pb, AF.Ln, bias=EPSC[:, 0:1], scale=1.0 / D)
        act(RS[:, 0:n], RS[:, 0:n], AF.Exp, scale=-0.5)
        if NML < 4:
            return
        TMl = TMP if isinstance(TMP, list) else [TMP]
        for k in range(8):
            TMP = TMl[k % len(TMl)]
            stt(TMP[:, 0:n], X[:, k, t0:t0 + n], AA[:, k, s_:s_ + 1], RS[:, 0:n], ALU.mult, ALU.mult)
            if NML < 5:
                continue
            act(dst[:, k, 0:n], TMP[:, 0:n], AF.Identity, bias=MOD[:, BBl, BBwh * 8 + k, s_:s_ + 1])

    def proj(HT, W, M, evac, chunks=CHUNKS):
        for (t0, n, s_) in chunks:
            pb = ps(pj_bank(), n, np_=M)
            mmg(pb, [(W[:, k, 0:M], HT[:, k, t0:t0 + n]) for k in range(8)])
            evac(pb, t0, n, s_)

    def finish_group(l, g8, OUT, G, Wo, last, SQ, Y, RS):
        mark = A.top
        act(SQ, OUT, AF.Square)
        RSl = RS if isinstance(RS, list) else [RS]
        for ic, (t0, n, s_) in enumerate(CHUNKS):
            RS = RSl[ic % len(RSl)]
            pb = ps(pj_bank(), n)
            mmg(pb, [(BLK64, SQ[:, t0:t0 + n])])
            act(RS[:, 0:n], pb, AF.Ln, bias=EPSC[:, 0:1], scale=1.0 / 64)
            act(RS[:, 0:n], RS[:, 0:n], AF.Exp, scale=-0.5)
            tt(RS[:, 0:n], RS[:, 0:n], OUT[:, t0:t0 + n], ALU.mult)
            stt(Y[:, t0:t0 + n], RS[:, 0:n], prm(l, 'hng', g8), G[:, t0:t0 + n], ALU.mult, ALU.mult)
        for (t0, n, s_) in CHUNKS:
            if last and s_ == 1:
                continue
            for k in range(8):
                pb = ps(pj_bank(), n)
                mmg(pb, [(Wo[:, k * 128:(k + 1) * 128], Y[:, t0:t0 + n])])
                stt(X[:, k, t0:t0 + n], pb, MOD[:, l, 2 * 8 + k, s_:s_ + 1], X[:, k, t0:t0 + n], ALU.mult, ALU.add)
        A.top = mark

    def group_a(l, a, HT, last):
        mark = A.top
        Wx = alloc([8, 128], BF16); wx_off = A.last
        Wg = alloc([8, 128], BF16)
        Wo = alloc([1, D], BF16, at=wx_off)[:, 0, :]
        GW = alloc([4, 128], BF16)
        wload(Wx, win_d[l][:, a * 128:(a + 1) * 128].rearrange('(k p) f -> p k f', p=128))
        wload(Wg, win_d[l][:, 256 + a * 128:256 + (a + 1) * 128].rearrange('(k p) f -> p k f', p=128))
        wload(GW, gatew_d[l][:, a * 512:(a + 1) * 512].rearrange('p (i c) -> p i c', c=128))
        XA = alloc([1, T], F32)[:, 0, :]
        H2 = XA
        XCv = alloc([1, T], F32)[:, 0, :]
        XCb = alloc([1, T], BF16)[:, 0, :]
        G = alloc([1, T], BF16)[:, 0, :]
        AB = alloc([1, T], F32)[:, 0, :]; ab_off = A.last
        BBf = alloc([1, T], F32)[:, 0, :]; bb_off = A.last
        H = alloc([1, T], F32)[:, 0, :]
        T1 = alloc([1, 512], F32)[:, 0, :]; T2 = alloc([1, 512], F32)[:, 0, :]
        T1b = alloc([1, 512], F32)[:, 0, :]
        bg_alloc()
        bg(2)
        proj(HT, Wx, 128, lambda pb, t0, n, s_: (cp(XA[:, t0:t0 + n], pb), bg(1)))
        wload(Wo, wout_d[l][a * 128:(a + 1) * 128, :])
        ga_rr = [0]

        def gelu_ev(pb, t0, n, s_):
            ga_rr[0] += 1
            T1 = (T1b, T2)[ga_rr[0] % 2]
            bg(1)
            act(T1[:, 0:n], pb, AF.Square)
            ts(T1[:, 0:n], T1[:, 0:n], 0.044715, 1.0, ALU.mult, ALU.add)
            tt(T1[:, 0:n], T1[:, 0:n], pb, ALU.mult)
            act(T1[:, 0:n], T1[:, 0:n], AF.Sigmoid, scale=2.0 * math.sqrt(2.0 / math.pi))
            tt(G[:, t0:t0 + n], T1[:, 0:n], pb, ALU.mult)
        proj(HT, Wg, 128, gelu_ev)
        o, _ = PC['convw']
        cw = lambda j: PRM[:, l, o + a * 4 + j:o + a * 4 + j + 1]
        for (s0, n) in ((0, TC), (TC, TL)):
            ts(XCv[:, s0:s0 + n], XA[:, s0:s0 + n], cw(2), prm(l, 'convb', a), ALU.mult, ALU.add)
            for j, sh in ((0, -2), (1, -1), (3, 1)):
                lo = max(0, -sh); hi = n - max(0, sh)
                stt(XCv[:, s0 + lo:s0 + hi], XA[:, s0 + lo + sh:s0 + hi + sh], cw(j), XCv[:, s0 + lo:s0 + hi],
                    ALU.mult, ALU.add)
        cp(XCb, XCv)
        T1A, T2A = T1, T2
        bg(2)
        for d in range(2):
            kap = KAP[:, l, a * 2 + d, 0:1]; kap2 = KAP[:, l, a * 2 + d, 1:2]
            o, _ = PC['lgb']
            br = PRM[:, l, o + a * 4 + d * 2:o + a * 4 + d * 2 + 1]
            bi = PRM[:, l, o + a * 4 + d * 2 + 1:o + a * 4 + d * 2 + 2]
            for ic, (t0, n, s_) in enumerate(CHUNKS):
                T1 = T1A if ic % 2 == 0 else T1b
                bg(2)
                pr = ps(pj_bank(), n)
                mmg(pr, [(GW[:, d * 2 + 0, :], XCb[:, t0:t0 + n])])
                pi = ps(pj_bank(), n)
                mmg(pi, [(GW[:, d * 2 + 1, :], XCb[:, t0:t0 + n])])
                act(T1[:, 0:n], pr, AF.Sigmoid, bias=br)
                act(AB[:, t0:t0 + n], T1[:, 0:n], AF.Exp, scale=kap)
                act(T1[:, 0:n], T1[:, 0:n], AF.Exp, scale=kap2)
                act(T1[:, 0:n], T1[:, 0:n], AF.Sqrt, bias=EPSC[:, 1:2], scale=-1.0)
                act(T2[:, 0:n], pi, AF.Sigmoid, bias=bi)
                tt(T1[:, 0:n], T1[:, 0:n], T2[:, 0:n], ALU.mult)
                tt(BBf[:, t0:t0 + n], T1[:, 0:n], XCv[:, t0:t0 + n], ALU.mult)
            if d == 0:
                scan(H, AB, BBf, 0.0)
            else:
                scan(H2[:, 0:TC][:, ::-1], AB[:, 0:TC][:, ::-1], BBf[:, 0:TC][:, ::-1], 0.0)
                scan(H2[:, TC:T][:, ::-1], AB[:, TC:T][:, ::-1], BBf[:, TC:T][:, ::-1], H2[:, 0:1])
        tt(H, H, H2, ALU.add)
        bg(6)
        finish_group(l, a, H, G, Wo, last, alloc([1, T], BF16, at=ab_off)[:, 0, :],
                     alloc([1, T], BF16, at=bb_off)[:, 0, :], [T1A, T1b])
        if a == 1:
            bg(64)
        bg_flush()
        bg_bufs.clear()
        A.top = mark

    class Stop(Exception):
        pass
    PLIM = int(dbg[0][1:]) if (dbg and dbg[0][0] == 'p') else 99

    def cut(n):
        if PLIM == n:
            raise Stop()

    def group_pair(l, kind, j, HT, LRT, last):
        mark = A.top
        g8 = 2 + kind * 3 + j
        cq = (512 if kind == 0 else 2064) + j * 128
        ck = (896 if kind == 0 else 2448) + j * 128
        cv = (1280 if kind == 0 else 2832) + j * 128
        cg = (1664 if kind == 0 else 3216) + j * 128
        WB2 = [alloc([8, 128], BF16) for _ in range(2)]
        wb_rr = [0]

        def wcol(c0, src=None):
            Wb = WB2[wb_rr[0] % 2]
            wb_rr[0] += 1
            wload(Wb, (win_d[l] if src is None else src)[:, c0:c0 + 128].rearrange('(k p) f -> p k f', p=128), e='scalar')
            return Wb
        Wo = alloc([1, D], BF16)[:, 0, :]
        wload(Wo, wout_d[l][g8 * 128:(g8 + 1) * 128, :])
        if kind == 0:
            W2 = alloc([2, 128], BF16, np_=16)
            for d in range(2):
                wload(W2[:, d, :], w2_d[l][d][:, j * 128:(j + 1) * 128])
        Qb = alloc([1, T], BF16)[:, 0, :]; qb_off = A.last
        Kb = alloc([1, T], BF16)[:, 0, :]; kb_off = A.last
        V1 = alloc([NCH, 130], BF16); VA = alloc([NCH, 128], BF16); VB = alloc([NCH, 128], BF16)
        OUT = alloc([1, T], F32)[:, 0, :]
        PL = alloc([1, T], F32)[:, 0, :]; pl_off = A.last
        QDs = [alloc([1, T], BF16)[:, 0, :] for _ in range(2)]
        KDs = [alloc([1, T], BF16)[:, 0, :] for _ in range(2)]
        DECs = [alloc([1, NCH], F32)[:, 0, :] for _ in range(2)]
        Rsts = [alloc([1, 130], F32)[:, 0, :] for _ in range(2)]
        Sbs = [alloc([2, 128], BF16) for _ in range(2)]
        nSbs = [alloc([2, 128], BF16) for _ in range(2)] if kind == 1 else [None, None]
        T1s = []
        T1bs = []
        for _ in range(2):
            T1s.append(alloc([1, 512], F32)[:, 0, :])
            T1bs.append(alloc([1, 512], BF16, at=A.last)[:, 0, :])
        SbFs = [alloc([2, 130], BF16) for _ in range(2)]
        T2s = [alloc([2, 128], F32) for _ in range(2)] if kind == 1 else [None, None]
        SCbs = [alloc([2, 256], BF16) for _ in range(2)]
        t1_rr = [0]

        def T1n():
            t1_rr[0] += 1
            return T1s[t1_rr[0] % 2]

        def T1bn():
            t1_rr[0] += 1
            return T1bs[t1_rr[0] % 2]
        proj(HT, wcol(cq), 128, lambda pb, t0, n, s_: act(Qb[:, t0:t0 + n], pb, AF.Copy, scale=0.125))
        proj(HT, wcol(ck), 128, lambda pb, t0, n, s_: cp(Kb[:, t0:t0 + n], pb))
        cut(1)
        mset(VA[:, :, 64:128], 0.0); mset(VB[:, :, 0:64], 0.0); mset(V1[:, :, 128:130], 1.0)
        Wv = wcol(cv)
        for cb in range(0, NCH, 4):
            nb4 = min(4, NCH - cb)
            pb = ps(pj_bank(), 512)
            for q in range(nb4):
                c = cb + q
                mmg(pb[:, q * 128:(q + 1) * 128], [(HT[:, k, c * 128:(c + 1) * 128], Wv[:, k, :]) for k in range(8)])
            cp(V1[:, cb:cb + nb4, 0:128], pb[:, 0:nb4 * 128].rearrange('p (q t) -> p q t', t=128))
        cp(VA[:, :, 0:64], V1[:, :, 0:64], e='gpsimd')
        cp(VB[:, :, 64:128], V1[:, :, 64:128], e='gpsimd')
        cut(2)
        sc_ = (1.0 / 16.0) if kind == 0 else 1.0
        for d in range(2):
            QD = QDs[d]; KD = KDs[d]; DEC = DECs[d]
            nb = NB[:, l, (0 if kind == 0 else 6) + j * 2 + d:(0 if kind == 0 else 6) + j * 2 + d + 1]

            def softplus_ev(pb, t0, n, s_):
                t1 = T1n()
                act(t1[:, 0:n], pb, AF.Exp, bias=nb, scale=-1.0)
                act(PL[:, t0:t0 + n], t1[:, 0:n], AF.Ln, bias=EPSC[:, 1:2])
            if kind == 0:
                for (t0, n, s_) in CHUNKS:
                    pb = ps(pj_bank(), n)
                    mmg(pb, [(W2[:, d, :], LRT[:, t0:t0 + n])])
                    softplus_ev(pb, t0, n, s_)
            else:
                proj(HT, wcol(((j * 2 + d) * 2 + 1) * 128, wrep_d[l]), 128, softplus_ev)
            for c in range(NCH):
                plc = PL[:, c * 128:(c + 1) * 128]
                if d == 0:
                    scan(plc, ONESb, plc, 0.0)
                else:
                    scan(plc[:, ::-1], ONESb, plc[:, ::-1], 0.0)
            Pv = PL.rearrange('p (c t) -> p c t', t=128)
            pend = Pv[:, :, 127] if d == 0 else Pv[:, :, 0]
            act(DEC, pend, AF.Exp, scale=-sc_)
            if kind == 0:
                for (t0, n, s_) in CHUNKS:
                    t1 = T1bn()
                    act(t1[:, 0:n], PL[:, t0:t0 + n], AF.Exp, scale=-sc_)
                    tt(QD[:, t0:t0 + n], t1[:, 0:n], Qb[:, t0:t0 + n], ALU.mult)
                    t1 = T1bn()
                    act(t1[:, 0:n], PL[:, t0:t0 + n], AF.Exp, scale=sc_)
                    tt(KD[:, t0:t0 + n], t1[:, 0:n], Kb[:, t0:t0 + n], ALU.mult)
            else:
                for (t0, n, s_) in CHUNKS:
                    t1 = T1bn()
                    act(t1[:, 0:n], PL[:, t0:t0 + n], AF.Exp, scale=-1.0)
                    tt(QD[:, t0:t0 + n], t1[:, 0:n], Qb[:, t0:t0 + n], ALU.mult)
                bi_ = prm(l, 'mgi', j * 2 + d)

                def ig_ev(pb, t0, n, s_, KD=KD):
                    t1 = T1n()
                    tt(t1[:, 0:n], pb, PL[:, t0:t0 + n], ALU.add)
                    act(t1[:, 0:n], t1[:, 0:n], AF.Exp, bias=bi_)
                    tt(KD[:, t0:t0 + n], t1[:, 0:n], Kb[:, t0:t0 + n], ALU.mult)
                proj(HT, wcol(((j * 2 + d) * 2 + 0) * 128, wrep_d[l]), 128, ig_ev)
        cut(3)
        G = alloc([1, T], BF16, at=pl_off)[:, 0, :]
        Wg = wcol(cg)
        if kind == 0:
            proj(HT, Wg, 128, lambda pb, t0, n, s_: act(G[:, t0:t0 + n], pb, AF.Silu))
        else:
            proj(HT, Wg, 128, lambda pb, t0, n, s_: act(G[:, t0:t0 + n], pb, AF.Sigmoid))
        KTs = [alloc([NCH, 128], BF16, at=qb_off), alloc([NCH, 128], BF16, at=kb_off)]
        for d in range(2):
            for cb in range(0, NCH, 4):
                nb4 = min(4, NCH - cb)
                pt = ps(pj_bank(), 512, dt=BF16)
                for q in range(nb4):
                    c = cb + q
                    trp(pt[:, q * 128:(q + 1) * 128], KDs[d][:, c * 128:(c + 1) * 128], IDb)
                src = pt[:, 0:nb4 * 128].rearrange('p (q t) -> p q t', t=128)
                if d == 0:
                    act(KTs[d][:, cb:cb + nb4, :], src, AF.Copy)
                else:
                    cp(KTs[d][:, cb:cb + nb4, :], src)
        cut(4)
        orders = [list(range(NCH)), [1, 0] + list(range(NCH - 1, 1, -1))]
        MASKS = [MASKF, MASKB]
        for d in range(2):
            mset(Sbs[d], 0.0)
            if kind == 1:
                mset(nSbs[d], 0.0)
        pj2 = [0]

        def pjb2():
            pj2[0] += 1
            return (pj2[0] % 2) * 512
        pus = {}

        def stageA(d, i):
            c = orders[d][i]
            tsl = slice(c * 128, (c + 1) * 128)
            par = i % 2
            QD = QDs[d]; KD = KDs[d]
            pscA = ps(2048, 128)
            pscB = ps(2560, 128)
            mmg(pscA, [(KD[0:64, tsl], QD[0:64, tsl])])
            mmg(pscB, [(KD[64:128, tsl], QD[64:128, tsl])])
            psc2 = psum[:, 2048:3072].rearrange('p (h t) -> p h t', t=512)[:, :, 0:128]
            tt(SCbs[d][:, par, :].rearrange('p (h t) -> p h t', t=128), psc2,
               MASKS[d].unsqueeze(1).broadcast_to([128, 2, 128]), ALU.mult)
            pu = ps((6 + d) * 512 + par * 256, 130)
            mmg(pu, [(KTs[d][:, c, :], V1[:, c, :])])
            pus[(d, i)] = pu

        def stageB1(d, i):
            c = orders[d][i]
            tsl = slice(c * 128, (c + 1) * 128)
            par = i % 2
            QD = QDs[d]; DEC = DECs[d]; Rst = Rsts[d]; Sb = Sbs[d]; nSb = nSbs[d]; SCb = SCbs[d]
            pu = pus.pop((d, i))
            if i == 0:
                cp(Rst, pu)
            else:
                cprev = orders[d][i - 1]
                stt(Rst, Rst, DEC[:, cprev:cprev + 1], pu, ALU.mult, ALU.add)
            if i + 1 < NCH:
                nx = (i + 1) % 2
                sbf = SbFs[d][:, nx, :]
                act(sbf, Rst, AF.Copy, scale=DEC[:, c:c + 1])
                tt(Sb[:, nx, :], sbf[:, 0:128], BLK64, ALU.mult, e='gpsimd')
                if kind == 1:
                    ts(nSb[:, nx, :], BLK64, sbf[:, 128:129], 1.0, ALU.mult, ALU.mult, e='gpsimd')
            pob = (d * 2 + par) * 512
            po = ps(pob, 128)
            mmg(po, [(Sb[:, i % 2, :], QD[:, tsl]), (VA[:, c, :], SCb[:, par, 0:128]),
                     (VB[:, c, :], SCb[:, par, 128:256])])
            if kind == 1:
                pd = ps(pob + 128, 128)
                mmg(pd, [(nSb[:, i % 2, :], QD[:, tsl]), (EAb, SCb[:, par, 0:128]), (EBb, SCb[:, par, 128:256])])

        def stageB2a(d, i):
            if kind == 0:
                return
            par = i % 2
            pd = ps((d * 2 + par) * 512 + 128, 128)
            T2 = T2s[d][:, par, :]
            act(T2, pd, AF.Abs)
            act(T2, T2, AF.Ln, bias=EPSC[:, 0:1])
            act(T2, T2, AF.Exp, scale=-1.0)

        def stageB2b(d, i):
            c = orders[d][i]
            tsl = slice(c * 128, (c + 1) * 128)
            par = i % 2
            po = ps((d * 2 + par) * 512, 128)
            if kind == 0:
                tt(OUT[:, tsl], po, OUT[:, tsl], ALU.add)
            else:
                T2 = T2s[d][:, par, :]
                stt(T2, T2, 1.0, po, ALU.min, ALU.mult)
                tt(OUT[:, tsl], OUT[:, tsl], T2, ALU.add, e='gpsimd')
        mset(OUT, 0.0, e='gpsimd')
        stageA(0, 0); stageA(1, 0)
        for i in range(NCH):
            if i + 1 < NCH:
                stageA(0, i + 1); stageA(1, i + 1)
            stageB1(0, i); stageB1(1, i)
            stageB2a(0, i); stageB2a(1, i)
            if i >= 1:
                stageB2b(0, i - 1); stageB2b(1, i - 1)
        stageB2b(0, NCH - 1); stageB2b(1, NCH - 1)
        cut(5)
        cut(6)
        finish_group(l, g8, OUT, G, Wo, last, KDs[0], alloc([1, T], BF16, at=qb_off)[:, 0, :], [T1s[0], T1s[1]])
        A.top = mark

    def mlp(l, last):
        mark = A.top
        NW = 6
        W1 = [alloc([8, 128], BF16) for _ in range(NW)]
        W2b = [alloc([1, D], BF16)[:, 0, :] for _ in range(NW)]
        H2 = alloc([8, 1024], BF16)
        A1T = alloc([32, 1024], BF16); a1t_off = A.last
        SQ = alloc([8, 512], BF16, at=a1t_off); RS = alloc([1, 512], F32)[:, 0, :]; TMP = alloc([1, 512], F32)[:, 0, :]
        TM3 = [alloc([1, 512], F32)[:, 0, :] for _ in range(3)]
        tm_rr = [0]
        passes = [(256, 1024, 0), (1280, 1024, 0)]
        if not last:
            passes = [(0, 256, 1)] + passes
        for (p0, pn, s_) in passes:
            subs = [(p0 + i, min(512, pn - i)) for i in range(0, pn, 512)]
            for (t0, n) in subs:
                norm_mod(H2[:, :, t0 - p0:t0 - p0 + n], t0, n, s_, A2, l, 3, SQ, RS, TM3)
            for f in range(32):
                w1 = W1[f % NW]
                wload_cached(w1, w1_d[l][:, f * 128:(f + 1) * 128].rearrange('(k p) f -> p k f', p=128),
                             w1s_d[l, f], ('w1', l, f))
                for (t0, n) in subs:
                    pb = ps(pj_bank(), n)
                    mmg(pb, [(w1[:, k, :], H2[:, k, t0 - p0:t0 - p0 + n]) for k in range(8)])
                    tm_rr[0] += 1
                    tm = TM3[tm_rr[0] % 3]
                    act(tm[:, 0:n], pb, AF.Relu)
                    tt(A1T[:, f, t0 - p0:t0 - p0 + n], tm[:, 0:n], tm[:, 0:n], ALU.mult)
            for (t0, n) in subs:
                for f in range(32):
                    w2 = W2b[f % NW]
                    wload_cached(w2, wm2_d[l][f * 128:(f + 1) * 128, :], w2s_d[l, f], ('w2', l, f))
                    for k in range(8):
                        pb = ps(k * 512, n)
                        s.op('tensor', (lambda en, pb=pb, w=w2[:, k * 128:(k + 1) * 128],
                                        r=A1T[:, f, t0 - p0:t0 - p0 + n], st_=(f == 0), sp_=(f == 31):
                                        en.matmul(pb, lhsT=w, rhs=r, start=st_, stop=sp_)),
                             reads=[w2[:, k * 128:(k + 1) * 128], A1T[:, f, t0 - p0:t0 - p0 + n]], writes=[pb])
                for k in range(8):
                    pb = ps(k * 512, n)
                    stt(X[:, k, t0:t0 + n], pb, MOD[:, l, 5 * 8 + k, s_:s_ + 1], X[:, k, t0:t0 + n], ALU.mult, ALU.add)
        del STG[2:]
        A.top = mark

    def final():
        mark = A.top
        FG = alloc([1, D], F32)[:, 0, :]
        ld(FG, fg_d)
        OT = [alloc([1, D], F32)[:, 0, :] for _ in range(2)]
        SS = alloc([1, 4], F32)[:, 0, :]
        JK = alloc([1, D], F32)[:, 0, :]
        toks = []
        for it in range(TL // 128):
            ot = OT[it % 2]
            t0 = TC + it * 128
            for half in range(2):
                pb = ps(((it * 2 + half) % 4) * 512, 512)
                for q in range(4):
                    k = half * 4 + q
                    trp(pb[:, q * 128:(q + 1) * 128], X[:, k, t0:t0 + 128], IDf)
                cp(ot[:, half * 512:(half + 1) * 512], pb)
            c0 = it % 2
            s.op('scalar', lambda en, ot=ot, c0=c0: en.activation(out=JK, in_=ot, func=AF.Square, accum_out=SS[:, c0:c0 + 1]),
                 reads=[ot], writes=[JK, SS[:, c0:c0 + 1]])
            act(SS[:, 2 + c0:3 + c0], SS[:, c0:c0 + 1], AF.Ln, bias=EPSC[:, 0:1], scale=1.0 / D)
            act(SS[:, 2 + c0:3 + c0], SS[:, 2 + c0:3 + c0], AF.Exp, scale=-0.5)
            stt(ot, ot, SS[:, 2 + c0:3 + c0], FG, ALU.mult, ALU.mult)
            toks.append(s.dma('sync', out_d[it * 128:(it + 1) * 128, :], ot, sb_reads=[ot]))
        for tk in toks[-4:]:
            s.wait_tok('sync', tk)
        A.top = mark

    def dump(ap, ncol):
        mark = A.top
        DB = alloc([1, dbg[1]], F32)[:, 0, :]
        mset(DB, 0.0)
        cp(DB[:, 0:ncol], ap)
        tk = s.dma('sync', dbg_d, DB, sb_reads=[DB])
        s.wait_tok('sync', tk)
        A.top = mark

    load_x()
    stage = dbg[0] if dbg else None
    if stage == 'x':
        dump(X[:, int(dbg[2]), :], T)
        n_layers = 0
    for l in range(n_layers):
        last = (l == DEPTH - 1)
        ada(l)
        if stage == 'ada':
            dump(MOD[:, l].rearrange('p a b -> p (a b)'), 96)
            break
        derive(l)
        if stage == 'drv':
            dump(A1.rearrange('p a b -> p (a b)'), 16)
            break
        mark_l = A.top
        HT = alloc([8, T], BF16)
        SQs = [alloc([8, 512], BF16) for _ in range(2)]
        RSs = [alloc([1, 512], F32)[:, 0, :] for _ in range(2)]
        TMPs = [alloc([1, 512], F32)[:, 0, :] for _ in range(3)]
        TMP = TMPs[0]
        for ic, (t0, n, s_) in enumerate(CHUNKS):
            norm_mod(HT[:, :, t0:t0 + n], t0, n, s_, A1, l, 0, SQs[ic % 2], RSs[ic % 2], TMPs)
        A.top = mark_l + 8 * T * 2
        if stage and (stage == 'h%d' % l or stage.startswith('nm')):
            mark = A.top
            HF = alloc([1, T], F32)[:, 0, :]
            cp(HF, HT[:, 0, :])
            dump(HF, T)
            A.top = mark
            break
        LRT = alloc([1, T], BF16, np_=16)[:, 0, :]
        Wl = alloc([8, 16], BF16)
        wload(Wl, win_d[l][:, 2048:2064].rearrange('(k p) f -> p k f', p=128))
        proj(HT, Wl, 16, lambda pb, t0, n, s_: cp(LRT[:, t0:t0 + n], pb))
        bg_setup(l)
        groups = [('a', 0), ('a', 1)] + [('p', 0, j) for j in range(3)] + [('p', 1, j) for j in range(3)]
        if stage and stage.startswith('g'):
            groups = groups[:int(stage[1:]) + 1]
        if stage and stage[0] == 'p':
            groups = [('p', int(dbg[2]), 0)]
        try:
            for g in groups:
                if g[0] == 'a':
                    group_a(l, g[1], HT, last)
                else:
                    group_pair(l, g[1], g[2], HT, LRT, last)
        except Stop:
            pass
        A.top = mark_l
        if stage and (stage.startswith('g') or stage[0] == 'p'):
            dump(X[:, int(dbg[2]), :], T)
            break
        mlp(l, last)
        if stage == 'l%d' % l:
            dump(X[:, int(dbg[2]), :], T)
            break
    if not stage:
        final()
    s.emit()
    print('ninst', s.ninst, 'arena peak', A.peak, 'persistent', P_MARK)
    return nc


_NC = {}


def prep_inputs(inp):
    inp = {k: np.asarray(v) for k, v in inp.items()}
    f = lambda a: np.ascontiguousarray(a, dtype=np.float32)
    shared = {
        'ada_w': f(inp['ada_w']), 'w_in': f(inp['w_in']),
        'wrep': f(np.stack([pack_wrep(inp, l) for l in range(DEPTH)])),
        'gatew': f(np.stack([pack_gatew(inp, l) for l in range(DEPTH)])),
        'gla_w2': f(inp['gla_w2']), 'w_out': f(inp['w_out']), 'mlp_w1': f(inp['mlp_w1']), 'mlp_w2': f(inp['mlp_w2']),
        'prm': f(np.stack([pack_params(inp, l) for l in range(DEPTH)])),
        'fg': f(np.broadcast_to(inp['final_g'][None, :], (128, D))), 'cst': make_consts(),
    }
    maps = []
    for b in range(8):
        cc = np.zeros((128, 16), np.float32)
        cc[:, 0::2] = inp['c'][b].reshape(8, 128).T
        cc[:, 1::2] = inp['c_ctx'].reshape(8, 128).T
        m = dict(shared)
        m['x'] = f(inp['x'][b]); m['ctx'] = f(inp['ctx'][b]); m['cc'] = cc
        maps.append(m)
    return maps


def kernel(**inputs):
    if 'nc' not in _NC:
        _NC['nc'] = build()
    maps = prep_inputs(inputs)
    res = run_bass_kernel_spmd(_NC['nc'], maps, core_ids=list(range(8)))
    return np.stack([np.asarray(res.results[b]['out'], dtype=np.float32) for b in range(8)], axis=0)
```

```python
import math
import numpy as np
import ml_dtypes
import concourse.bass as bass
import concourse.mybir as mybir
from concourse.bass_utils import run_bass_kernel_spmd

F32 = mybir.dt.float32
BF16 = mybir.dt.bfloat16
I32 = mybir.dt.int32
U8 = mybir.dt.uint8
AF = mybir.ActivationFunctionType
ALU = mybir.AluOpType

D = 1024
TL = 2048
TC = 256
T = TL + TC
NCH = T // 128
DEPTH = 2
DIN = 3624
DFF = 4096
EPS = 1e-6
ENG = ['tensor', 'vector', 'scalar', 'gpsimd', 'sync']
DSZ = {F32: 4, BF16: 2, I32: 4, U8: 1}


def _dsz(dt):
    for k, v in DSZ.items():
        if k == dt:
            return v
    return 4


def region(ap):
    es = _dsz(ap.dtype)
    aps = ap.ap
    pstep, pcnt = aps[0]
    off = ap.offset
    if pstep == 0:
        pstep = 1 << 40
    p0 = off // pstep
    fo = off % pstep
    lo = fo
    hi = fo
    for st, c in aps[1:]:
        d = st * (c - 1)
        if d < 0:
            lo += d
        else:
            hi += d
    if ap.tensor.name == 'psum':
        return ('psum', 0, 128, (lo * es) // 2048 * 2048, ((hi + 1) * es + 2047) // 2048 * 2048)
    return (ap.tensor.name, p0, p0 + pcnt, lo * es, (hi + 1) * es)


class Sched:
    def __init__(self, nc, n_dma_sems=32):
        self.nc = nc
        self.sem = {}
        for e in ENG:
            self.sem[e] = nc.semaphore('s_' + e).__enter__()
        self.dma_sems = [nc.semaphore('s_dma%d' % i).__enter__() for i in range(n_dma_sems)]
        self.dma_cnt = [0] * n_dma_sems
        self.dma_next = 0
        self.cnt = {e: 0 for e in ENG}
        self.seen = {e: {} for e in ENG}
        self.q = {e: [] for e in ENG}
        self.acc = {}
        self.ninst = 0

    def _deps(self, reads, writes, e=None):
        deps = {}
        for ap in reads:
            name, p0, p1, f0, f1 = region(ap)
            ps_ = (name == 'psum')
            for r in self.acc.get(name, ()):
                if (r[4] or (ps_ and r[5][0] != e)) and r[0] < p1 and p0 < r[1] and r[2] < f1 and f0 < r[3]:
                    k, v = r[5]
                    if deps.get(k, 0) < v:
                        deps[k] = v
        for ap in writes:
            name, p0, p1, f0, f1 = region(ap)
            for r in self.acc.get(name, ()):
                if r[0] < p1 and p0 < r[1] and r[2] < f1 and f0 < r[3]:
                    k, v = r[5]
                    if deps.get(k, 0) < v:
                        deps[k] = v
        return deps

    def _record(self, reads, writes, tok):
        for ap in writes:
            name, p0, p1, f0, f1 = region(ap)
            lst = self.acc.setdefault(name, [])
            lst[:] = [r for r in lst if not (p0 <= r[0] and r[1] <= p1 and f0 <= r[2] and r[3] <= f1)]
            lst.append((p0, p1, f0, f1, True, tok))
        for ap in reads:
            name, p0, p1, f0, f1 = region(ap)
            lst = self.acc.setdefault(name, [])
            lst[:] = [r for r in lst if not ((not r[4]) and r[5][0] == tok[0] and p0 <= r[0] and r[1] <= p1
                                             and f0 <= r[2] and r[3] <= f1)]
            lst.append((p0, p1, f0, f1, False, tok))

    def _waits(self, e, deps):
        out = []
        seen = self.seen[e]
        for k, v in deps.items():
            if seen.get(k, 0) >= v:
                continue
            seen[k] = v
            if k == e and e == 'tensor':
                continue
            out.append((k, v))
        return out

    def _semof(self, k):
        return self.sem[k] if isinstance(k, str) else self.dma_sems[k[1]]

    def op(self, e, fn, reads=(), writes=()):
        reads = [r for r in reads if r is not None and not isinstance(r, (int, float))]
        deps = self._deps(reads, writes, e)
        waits = self._waits(e, deps)
        self.cnt[e] += 1
        tok = (e, self.cnt[e])
        self._record(reads, writes, tok)
        self.q[e].append((waits, fn, self.sem[e], 1))
        self.ninst += 1
        return tok

    def dma(self, qe, out, in_, sb_reads=(), sb_writes=(), **kw):
        deps = self._deps(sb_reads, sb_writes)
        i = self.dma_next
        self.dma_next = (self.dma_next + 1) % len(self.dma_sems)
        if self.dma_cnt[i] > 0:
            k = ('dma', i)
            if deps.get(k, 0) < self.dma_cnt[i] * 16:
                deps[k] = self.dma_cnt[i] * 16
        waits = self._waits(qe, deps)
        self.dma_cnt[i] += 1
        tok = (('dma', i), self.dma_cnt[i] * 16)
        self._record(sb_reads, sb_writes, tok)
        self.q[qe].append((waits, (lambda eng, out=out, in_=in_, kw=kw: eng.dma_start(out=out, in_=in_, **kw)),
                           self.dma_sems[i], 16))
        self.ninst += 1
        return tok

    def wait_tok(self, e, tok):
        waits = self._waits(e, {tok[0]: tok[1]})
        if waits:
            self.q[e].append((waits, None, None, 0))

    def emit(self):
        nc = self.nc
        with nc.Block() as block:
            def mk(e):
                def body(eng):
                    for waits, fn, sem, inc in self.q[e]:
                        for k, v in waits:
                            eng.wait_ge(self._semof(k), v)
                        if fn is not None:
                            fn(eng).then_inc(sem, inc)
                return body
            block.tensor(mk('tensor'))
            block.vector(mk('vector'))
            block.scalar(mk('scalar'))
            block.gpsimd(mk('gpsimd'))
            block.sync(mk('sync'))


NCST = 128 * 7 + 1 + 32 + 64


def make_consts():
    p = np.arange(128)
    c = np.zeros((128, NCST), np.float32)
    o = 0
    c[:, o:o + 128] = np.eye(128); o += 128
    c[:, o:o + 128] = (p[:, None] <= p[None, :]); o += 128
    c[:, o:o + 128] = (p[:, None] >= p[None, :]); o += 128
    c[:, o:o + 128] = 1.0; o += 128
    c[:, o:o + 128] = (p[:, None] // 64 == p[None, :] // 64); o += 128
    c[:, o:o + 128] = (p[None, :] < 64); o += 128
    c[:, o:o + 128] = (p[None, :] >= 64); o += 128
    c[:, o] = p; o += 1
    c[:, o:o + 32] = np.arange(32)[None, :]; o += 32
    c[:, o:o + 64] = np.arange(64)[None, :]; o += 64
    return c


PC = {}
_o = 0
for _n, _w in [('n1g', 8), ('n2g', 8), ('adab', 48), ('hng', 8), ('convw', 8), ('convb', 2), ('lgb', 8),
               ('lam', 4), ('b2', 6), ('mgi', 6), ('mgf', 6)]:
    PC[_n] = (_o, _w)
    _o += _w
NPRM = _o


def pack_params(inp, l):
    P = np.zeros((128, NPRM), np.float32)

    def put(name, arr):
        o, w = PC[name]
        assert arr.shape == (128, w), (name, arr.shape)
        P[:, o:o + w] = arr
    put('n1g', inp['norm1_g'][l].reshape(8, 128).T)
    put('n2g', inp['norm2_g'][l].reshape(8, 128).T)
    put('adab', inp['ada_b'][l].reshape(48, 128).T)
    put('hng', inp['head_norm_g'][l].reshape(8, 128).T)
    cw = inp['conv_w'][l]
    put('convw', np.concatenate([cw[:, a * 128:(a + 1) * 128].T for a in range(2)], axis=1))
    put('convb', inp['conv_b'][l].reshape(2, 128).T)
    gb = inp['lru_gate_b'][l]
    put('lgb', np.stack([gb[d, g, a * 128:(a + 1) * 128] for a in range(2) for d in range(2) for g in range(2)], axis=1))
    lam = inp['lru_lambda'][l]
    put('lam', np.stack([lam[d, a * 128:(a + 1) * 128] for a in range(2) for d in range(2)], axis=1))
    b2 = inp['gla_b2'][l]
    put('b2', np.stack([b2[d, j * 128:(j + 1) * 128] for j in range(3) for d in range(2)], axis=1))
    mg = inp['mlstm_gate_b'][l]
    put('mgi', np.stack([np.repeat(mg[d, 0, 2 * j:2 * j + 2], 64) for j in range(3) for d in range(2)], axis=1))
    put('mgf', np.stack([np.repeat(mg[d, 1, 2 * j:2 * j + 2], 64) for j in range(3) for d in range(2)], axis=1))
    return P


def pack_gatew(inp, l):
    gw = inp['lru_gate_w'][l]
    out = np.zeros((128, 8, 128), np.float32)
    for a in range(2):
        for d in range(2):
            for g in range(2):
                i = a * 4 + d * 2 + g
                for bb in range(2):
                    out[bb * 64:(bb + 1) * 64, i, bb * 64:(bb + 1) * 64] = gw[d, g, a * 2 + bb]
    return out.reshape(128, 8 * 128)


def pack_wrep(inp, l):
    w = inp['w_in'][l]
    blocks = []
    for j in range(3):
        for d in range(2):
            for kind in range(2):
                base = 3600 + kind * 12 + d * 6 + 2 * j
                blocks.append(np.repeat(w[:, base:base + 2], 64, axis=1))
    return np.ascontiguousarray(np.concatenate(blocks, axis=1))


def build(n_layers=DEPTH, dbg=None):
    nc = bass.Bass('TRN2', target_bir_lowering=False)

    def din(name, shape):
        return nc.dram_tensor(name, list(shape), F32, kind='ExternalInput').ap()
    x_d = din('x', [TL, D]); ctx_d = din('ctx', [TC, D]); cc_d = din('cc', [128, 16])
    adaw_d = din('ada_w', [DEPTH, D, 6 * D]); win_d = din('w_in', [DEPTH, D, DIN])
    wrep_d = din('wrep', [DEPTH, D, 12 * 128]); gatew_d = din('gatew', [DEPTH, 128, 8 * 128])
    w2_d = din('gla_w2', [DEPTH, 2, 16, 384]); wout_d = din('w_out', [DEPTH, D, D])
    w1_d = din('mlp_w1', [DEPTH, D, DFF]); wm2_d = din('mlp_w2', [DEPTH, DFF, D])
    prm_d = din('prm', [DEPTH, 128, NPRM]); fg_d = din('fg', [128, D]); cst_d = din('cst', [128, NCST])
    out_d = nc.dram_tensor('out', [TL, D], F32, kind='ExternalOutput').ap()
    dbg_d = None
    if dbg:
        dbg_d = nc.dram_tensor('dbg', [128, dbg[1]], F32, kind='ExternalOutput').ap()

    s = Sched(nc)
    ARENA = 206 * 1024
    arena = nc.alloc_sbuf_tensor('arena', [128, ARENA], U8)
    psum = nc.alloc_psum_tensor('psum', [128, 4096], F32)

    class A:
        top = 0
        peak = 0
        last = 0

    def alloc(shape, dt, np_=128, at=None):
        n = 1
        for v in shape:
            n *= v
        nb = (n * _dsz(dt) + 31) // 32 * 32
        if at is None:
            off = A.top
            A.top += nb
            A.peak = max(A.peak, A.top)
        else:
            off = at
        A.last = off
        assert off + nb <= ARENA, ('arena overflow', off + nb)
        ap = arena[:, off:off + n * _dsz(dt)].bitcast(dt)
        if len(shape) == 2:
            ap = ap.rearrange('p (a b) -> p a b', b=shape[1])
        elif len(shape) == 3:
            ap = ap.rearrange('p (a b c) -> p a b c', b=shape[1], c=shape[2])
        return ap[0:np_] if np_ != 128 else ap

    def ps(off, n, dt=F32, np_=128, p0=0):
        if dt == F32:
            return psum[p0:p0 + np_, off:off + n]
        return psum[p0:p0 + np_, off:off + (n + 1) // 2].bitcast(dt)[:, 0:n]

    def R_(*aps):
        return [a for a in aps if a is not None and not isinstance(a, (int, float))]

    def act(out, in_, func, bias=None, scale=None, e='scalar'):
        kw = {}
        if bias is not None:
            kw['bias'] = bias
        if scale is not None:
            kw['scale'] = scale
        s.op(e, lambda en: en.activation(out=out, in_=in_, func=func, **kw), reads=R_(in_, bias, scale), writes=[out])

    def tt(out, in0, in1, op, e='vector'):
        s.op(e, lambda en: en.tensor_tensor(out=out, in0=in0, in1=in1, op=op), reads=[in0, in1], writes=[out])

    def ts(out, in0, s1, s2, op0, op1=None, e='vector'):
        if op1 is None:
            s.op(e, lambda en: en.tensor_scalar(out=out, in0=in0, scalar1=s1, scalar2=None, op0=op0),
                 reads=R_(in0, s1), writes=[out])
        else:
            s.op(e, lambda en: en.tensor_scalar(out=out, in0=in0, scalar1=s1, scalar2=s2, op0=op0, op1=op1),
                 reads=R_(in0, s1, s2), writes=[out])

    def stt(out, in0, sc, in1, op0, op1):
        s.op('vector', lambda en: en.scalar_tensor_tensor(out=out, in0=in0, scalar=sc, in1=in1, op0=op0, op1=op1),
             reads=R_(in0, sc, in1), writes=[out])

    def cp(out, in_, e='vector'):
        s.op(e, lambda en: en.tensor_copy(out=out, in_=in_), reads=[in_], writes=[out])

    def mset(out, v, e='vector'):
        s.op(e, lambda en: en.memset(out, v), writes=[out])

    def mmg(out, pairs):
        n = len(pairs)

        def fn(en):
            inst = None
            for i, (l, r) in enumerate(pairs):
                inst = en.matmul(out, lhsT=l, rhs=r, start=(i == 0), stop=(i == n - 1))
            return inst
        rd = []
        for l, r in pairs:
            rd += [l, r]
        s.op('tensor', fn, reads=rd, writes=[out])

    def trp(out, in_, ident):
        s.op('tensor', lambda en: en.transpose(out, in_, ident), reads=[in_, ident], writes=[out])

    def scan(out, d0, d1, init):
        s.op('vector', lambda en: en.tensor_tensor_scan(out=out, data0=d0, data1=d1, initial=init, op0=ALU.mult,
                                                         op1=ALU.add), reads=R_(d0, d1, init), writes=[out])

    def ld(out, in_, q='sync'):
        s.dma(q, out, in_, sb_writes=[out])

    X = alloc([8, T], F32)
    CST = alloc([NCST], F32)[:, 0, :] if False else alloc([1, NCST], F32)[:, 0, :]
    ld(CST, cst_d)
    IDf = CST[:, 0:128]
    PIDX = CST[:, 896:897]; RIDX = CST[:, 897:929]; CIDX = CST[:, 929:993]
    CB = alloc([7, 128], BF16)
    cp(CB, CST[:, 0:896].rearrange('p (a b) -> p a b', b=128))
    IDb, MASKF, MASKB, ONESb, BLK64, EAb, EBb = [CB[:, i, :] for i in range(7)]
    PRM = alloc([DEPTH, NPRM], F32)
    for l in range(DEPTH):
        ld(PRM[:, l, :], prm_d[l])
    CCT = alloc([1, 16], F32)[:, 0, :]
    ld(CCT, cc_d)
    CCb = alloc([8, 2], BF16)
    act(CCb, CCT.rearrange('p (k s) -> p k s', s=2), AF.Silu)
    MOD = alloc([DEPTH, 48, 2], F32)
    DRV = alloc([DEPTH, 4, 8, 2], F32) if False else None
    A1 = alloc([8, 2], F32); A2 = alloc([8, 2], F32)
    EPSC = alloc([1, 4], F32)[:, 0, :]
    mset(EPSC[:, 0:1], EPS)
    mset(EPSC[:, 1:2], 1.0)
    mset(EPSC[:, 2:3], 0.0)
    KAP = alloc([DEPTH, 4, 2], F32)
    NB = alloc([DEPTH, 18], F32)
    P_MARK = A.top

    def prm(l, name, i=0, w=1):
        o, _ = PC[name]
        return PRM[:, l, o + i:o + i + w]

    STG = [alloc([1, 1024], F32)[:, 0, :] for _ in range(2)]
    stg_rr = [0]

    def wload(out, src, e='gpsimd'):
        st = STG[stg_rr[0] % len(STG)]
        stg_rr[0] += 1
        shp = list(out.shape)
        np_ = shp[0]
        n = 1
        for v in shp[1:]:
            n *= v
        v_ = st[0:np_, 0:n]
        if len(shp) == 3:
            v_ = v_.rearrange('p (a b) -> p a b', b=shp[2])
        s.dma('sync', v_, src, sb_writes=[v_])
        if e == 'scalar':
            act(out, v_, AF.Copy)
        else:
            cp(out, v_, e=e)

    w1s_d = nc.dram_tensor('w1s', [DEPTH, 32, 128, 1024], BF16, kind='Internal').ap()
    w2s_d = nc.dram_tensor('w2s', [DEPTH, 32, 128, 1024], BF16, kind='Internal').ap()
    stok = {}
    cast_rr = [0]

    def wload_cached(out, src, scr, key):
        if key not in stok:
            e = ('vector', 'scalar')[cast_rr[0] % 2]
            cast_rr[0] += 1
            wload(out, src, e=e)
            dst = scr if len(out.shape) == 2 else scr.rearrange('p (k f) -> p k f', f=out.shape[2])
            stok[key] = s.dma('sync', dst, out, sb_reads=[out])
        else:
            s.wait_tok('sync', stok[key])
            s.dma('sync', out, scr if len(out.shape) == 2 else scr.rearrange('p (k f) -> p k f', f=out.shape[2]),
                  sb_writes=[out])

    bg_jobs = []
    bg_bufs = {}
    bg_rr = [0]

    def bg_setup(l):
        del bg_jobs[:]
        for f in range(32):
            bg_jobs.append((w1_d[l][:, f * 128:(f + 1) * 128].rearrange('(k p) f -> p k f', p=128), w1s_d[l, f], ('w1', l, f), 3))
        for f in range(32):
            bg_jobs.append((wm2_d[l][f * 128:(f + 1) * 128, :], w2s_d[l, f], ('w2', l, f), 2))

    bg_pend = []

    def bg_flush():
        while bg_pend:
            scr, bf, key = bg_pend.pop(0)
            stok[key] = s.dma('sync', scr, bf, sb_reads=[bf])

    def bg_alloc():
        bg_bufs['st'] = [alloc([1, 1024], F32)[:, 0, :] for _ in range(2)]
        bg_bufs['bf'] = [alloc([1, 1024], BF16)[:, 0, :] for _ in range(2)]

    def bg(n):
        for _ in range(n):
            if not bg_jobs or 'st' not in bg_bufs:
                return
            src, scr, key, nd = bg_jobs.pop(0)
            i = bg_rr[0] % 2
            bg_rr[0] += 1
            st = bg_bufs['st'][i]; bf = bg_bufs['bf'][i]
            if nd == 3:
                st = st.rearrange('p (k f) -> p k f', f=128); bf = bf.rearrange('p (k f) -> p k f', f=128)
                scr = scr.rearrange('p (k f) -> p k f', f=128)
            s.dma('sync', st, src, sb_writes=[st])
            cp(bf, st, e='gpsimd')
            bg_flush()
            bg_pend.append((scr, bf, key))

    def load_x():
        mark = A.top
        POSR = alloc([4, 32], F32); POSC = alloc([4, 64], F32)
        FRQ = alloc([1, 2], F32)[:, 0, :]
        for h in range(2):
            ts(FRQ[:, h:h + 1], PIDX, float(h * 128), None, ALU.add)
        act(FRQ, FRQ, AF.Exp, scale=-math.log(10000.0) / 256.0)
        TWO_PI = 2.0 * math.pi
        tmpi = alloc([1, 64], I32)[:, 0, :]; tmpf = alloc([1, 64], F32)[:, 0, :]; tmpa = alloc([1, 64], F32)[:, 0, :]
        for kk in range(8):
            h = kk % 2
            grp = kk // 2
            n = 32 if grp < 2 else 64
            idx = RIDX if grp < 2 else CIDX
            dst = POSR[:, kk, :] if grp < 2 else POSC[:, kk - 4, :]
            ph = 0.0 if grp % 2 == 0 else math.pi / 2
            ts(tmpa[:, 0:n], idx, FRQ[:, h:h + 1], ph, ALU.mult, ALU.add)
            ts(tmpf[:, 0:n], tmpa[:, 0:n], 1.0 / TWO_PI, None, ALU.mult)
            cp(tmpi[:, 0:n], tmpf[:, 0:n])
            cp(tmpf[:, 0:n], tmpi[:, 0:n])
            stt(tmpa[:, 0:n], tmpf[:, 0:n], -TWO_PI, tmpa[:, 0:n], ALU.mult, ALU.add)
            ts(tmpa[:, 0:n], tmpa[:, 0:n], 3.14159, -3.14159, ALU.min, ALU.max)
            act(dst, tmpa[:, 0:n], AF.Sin)
        ST = [alloc([1, D], F32)[:, 0, :] for _ in range(3)]
        for it in range(NCH):
            st = ST[it % 3]
            if it < 2:
                ld(st, ctx_d[it * 128:(it + 1) * 128, :])
            else:
                ld(st, x_d[(it - 2) * 128:(it - 1) * 128, :])
            for half in range(2):
                pb = ps(((it * 2 + half) % 4) * 512, 512)
                for q in range(4):
                    k = half * 4 + q
                    trp(pb[:, q * 128:(q + 1) * 128], st[:, k * 128:(k + 1) * 128], IDf)
                dstX = X[:, half * 4:half * 4 + 4, it * 128:(it + 1) * 128]
                if it < 2:
                    cp(dstX, pb.rearrange('p (q t) -> p q t', t=128))
                else:
                    r0 = (it - 2) * 2
                    for q in range(4):
                        k = half * 4 + q
                        o3 = X[:, k, it * 128:(it + 1) * 128].rearrange('p (r c) -> p r c', c=64)
                        i3 = pb[:, q * 128:(q + 1) * 128].rearrange('p (r c) -> p r c', c=64)
                        if k < 4:
                            pos = POSR[:, k, r0:r0 + 2].unsqueeze(2).broadcast_to([128, 2, 64])
                        else:
                            pos = POSC[:, k - 4, :].unsqueeze(1).broadcast_to([128, 2, 64])
                        tt(o3, i3, pos, ALU.add)
        A.top = mark

    CHUNKS = [(0, 256, 1)] + [(256 + 512 * i, 512, 0) for i in range(4)]
    pj_rr = [0]

    def pj_bank():
        b = pj_rr[0] % 4
        pj_rr[0] += 1
        return b * 512

    def ada(l):
        mark = A.top
        WB = [alloc([8, 128], BF16) for _ in range(4)]
        for _ in range(3):
            STG.append(alloc([1, 1024], F32)[:, 0, :])
        pm = ps(pj_bank(), 96)
        for ft in range(48):
            wb = WB[ft % 4]
            wload(wb, adaw_d[l][:, ft * 128:(ft + 1) * 128].rearrange('(k p) f -> p k f', p=128),
                  e=('vector', 'scalar')[ft % 2])
            mmg(pm[:, ft * 2:ft * 2 + 2], [(wb[:, k, :], CCb[:, k, :]) for k in range(8)])
        o, _ = PC['adab']
        tt(MOD[:, l], pm.rearrange('p (a b) -> p a b', b=2),
           PRM[:, l, o:o + 48].unsqueeze(2).broadcast_to([128, 48, 2]), ALU.add)
        del STG[2:]
        A.top = mark

    def derive(l):
        for (AA, nm, wh) in ((A1, 'n1g', 1), (A2, 'n2g', 4)):
            o, _ = PC[nm]
            g = PRM[:, l, o:o + 8].unsqueeze(2).broadcast_to([128, 8, 2])
            ts(AA, MOD[:, l, wh * 8:wh * 8 + 8, :], 1.0, None, ALU.add)
            tt(AA, AA, g, ALU.mult)
        o, _ = PC['lam']
        act(KAP[:, l, :, 0], PRM[:, l, o:o + 4], AF.Exp, scale=-1.0)
        act(KAP[:, l, :, 0], KAP[:, l, :, 0], AF.Ln, bias=EPSC[:, 1:2])
        ts(KAP[:, l, :, 1], KAP[:, l, :, 0], -16.0, None, ALU.mult)
        ts(KAP[:, l, :, 0], KAP[:, l, :, 0], -8.0, None, ALU.mult)
        o, _ = PC['b2']
        ts(NB[:, l, 0:6], PRM[:, l, o:o + 6], -1.0, None, ALU.mult)
        o, _ = PC['mgf']
        ts(NB[:, l, 6:12], PRM[:, l, o:o + 6], -1.0, None, ALU.mult)
        o, _ = PC['lgb']
        ts(NB[:, l, 12:18], PRM[:, l, o:o + 6], 1.0, None, ALU.mult)

    NML = int(dbg[0][2:]) if (dbg and dbg[0].startswith('nm')) else 9

    def norm_mod(dst, t0, n, s_, AA, BBl, BBwh, SQ, RS, TMP):
        for k in range(8):
            act(SQ[:, k, 0:n], X[:, k, t0:t0 + n], AF.Square)
        if NML < 2:
            return
        pb = ps(pj_bank(), n)
        mmg(pb, [(ONESb, SQ[:, k, 0:n]) for k in range(8)])
        if NML < 3:
            return
        act(RS[:, 0:n], pb, AF.Ln, bias=EPSC[:, 0:1], scale=1.0 / D)
        act(RS[:, 0:n], RS[:, 0:n], AF.Exp, scale=-0.5)
        if NML < 4:
            return
        TMl = TMP if isinstance(TMP, list) else [TMP]
        for k in range(8):
            TMP = TMl[k % len(TMl)]
            stt(TMP[:, 0:n], X[:, k, t0:t0 + n], AA[:, k, s_:s_ + 1], RS[:, 0:n], ALU.mult, ALU.mult)
            if NML < 5:
                continue
            act(dst[:, k, 0:n], TMP[:, 0:n], AF.Identity, bias=MOD[:, BBl, BBwh * 8 + k, s_:s_ + 1])

    def proj(HT, W, M, evac, chunks=CHUNKS):
        for (t0, n, s_) in chunks:
            pb = ps(pj_bank(), n, np_=M)
            mmg(pb, [(W[:, k, 0:M], HT[:, k, t0:t0 + n]) for k in range(8)])
            evac(pb, t0, n, s_)

    def finish_group(l, g8, OUT, G, Wo, last, SQ, Y, RS):
        mark = A.top
        act(SQ, OUT, AF.Square)
        RSl = RS if isinstance(RS, list) else [RS]
        for ic, (t0, n, s_) in enumerate(CHUNKS):
            RS = RSl[ic % len(RSl)]
            pb = ps(pj_bank(), n)
            mmg(pb, [(BLK64, SQ[:, t0:t0 + n])])
            act(RS[:, 0:n], pb, AF.Ln, bias=EPSC[:, 0:1], scale=1.0 / 64)
            act(RS[:, 0:n], RS[:, 0:n], AF.Exp, scale=-0.5)
            tt(RS[:, 0:n], RS[:, 0:n], OUT[:, t0:t0 + n], ALU.mult)
            stt(Y[:, t0:t0 + n], RS[:, 0:n], prm(l, 'hng', g8), G[:, t0:t0 + n], ALU.mult, ALU.mult)
        for (t0, n, s_) in CHUNKS:
            if last and s_ == 1:
                continue
            for k in range(8):
                pb = ps(pj_bank(), n)
                mmg(pb, [(Wo[:, k * 128:(k + 1) * 128], Y[:, t0:t0 + n])])
                stt(X[:, k, t0:t0 + n], pb, MOD[:, l, 2 * 8 + k, s_:s_ + 1], X[:, k, t0:t0 + n], ALU.mult, ALU.add)
        A.top = mark

    def group_a(l, a, HT, last):
        mark = A.top
        Wx = alloc([8, 128], BF16); wx_off = A.last
        Wg = alloc([8, 128], BF16)
        Wo = alloc([1, D], BF16, at=wx_off)[:, 0, :]
        GW = alloc([4, 128], BF16)
        wload(Wx, win_d[l][:, a * 128:(a + 1) * 128].rearrange('(k p) f -> p k f', p=128))
        wload(Wg, win_d[l][:, 256 + a * 128:256 + (a + 1) * 128].rearrange('(k p) f -> p k f', p=128))
        wload(GW, gatew_d[l][:, a * 512:(a + 1) * 512].rearrange('p (i c) -> p i c', c=128))
        XA = alloc([1, T], F32)[:, 0, :]
        H2 = XA
        XCv = alloc([1, T], F32)[:, 0, :]
        XCb = alloc([1, T], BF16)[:, 0, :]
        G = alloc([1, T], BF16)[:, 0, :]
        AB = alloc([1, T], F32)[:, 0, :]; ab_off = A.last
        BBf = alloc([1, T], F32)[:, 0, :]; bb_off = A.last
        H = alloc([1, T], F32)[:, 0, :]
        T1 = alloc([1, 512], F32)[:, 0, :]; T2 = alloc([1, 512], F32)[:, 0, :]
        T1b = alloc([1, 512], F32)[:, 0, :]
        bg_alloc()
        bg(2)
        proj(HT, Wx, 128, lambda pb, t0, n, s_: (cp(XA[:, t0:t0 + n], pb), bg(1)))
        wload(Wo, wout_d[l][a * 128:(a + 1) * 128, :])
        ga_rr = [0]

        def gelu_ev(pb, t0, n, s_):
            ga_rr[0] += 1
            T1 = (T1b, T2)[ga_rr[0] % 2]
            bg(1)
            act(T1[:, 0:n], pb, AF.Square)
            ts(T1[:, 0:n], T1[:, 0:n], 0.044715, 1.0, ALU.mult, ALU.add)
            tt(T1[:, 0:n], T1[:, 0:n], pb, ALU.mult)
            act(T1[:, 0:n], T1[:, 0:n], AF.Sigmoid, scale=2.0 * math.sqrt(2.0 / math.pi))
            tt(G[:, t0:t0 + n], T1[:, 0:n], pb, ALU.mult)
        proj(HT, Wg, 128, gelu_ev)
        o, _ = PC['convw']
        cw = lambda j: PRM[:, l, o + a * 4 + j:o + a * 4 + j + 1]
        for (s0, n) in ((0, TC), (TC, TL)):
            act(XCv[:, s0:s0 + n], XA[:, s0:s0 + n], AF.Identity, bias=prm(l, 'convb', a), scale=cw(2))
            for j, sh in ((0, -2), (1, -1), (3, 1)):
                lo = max(0, -sh); hi = n - max(0, sh)
                stt(XCv[:, s0 + lo:s0 + hi], XA[:, s0 + lo + sh:s0 + hi + sh], cw(j), XCv[:, s0 + lo:s0 + hi],
                    ALU.mult, ALU.add)
        act(XCb, XCv, AF.Copy)
        T1A, T2A = T1, T2
        bg(2)
        for d in range(2):
            kap = KAP[:, l, a * 2 + d, 0:1]; kap2 = KAP[:, l, a * 2 + d, 1:2]
            o, _ = PC['lgb']
            br = PRM[:, l, o + a * 4 + d * 2:o + a * 4 + d * 2 + 1]
            bi = PRM[:, l, o + a * 4 + d * 2 + 1:o + a * 4 + d * 2 + 2]
            for ic, (t0, n, s_) in enumerate(CHUNKS):
                T1 = T1A if ic % 2 == 0 else T1b
                bg(2)
                pr = ps(pj_bank(), n)
                mmg(pr, [(GW[:, d * 2 + 0, :], XCb[:, t0:t0 + n])])
                pi = ps(pj_bank(), n)
                mmg(pi, [(GW[:, d * 2 + 1, :], XCb[:, t0:t0 + n])])
                act(T1[:, 0:n], pr, AF.Sigmoid, bias=br)
                act(AB[:, t0:t0 + n], T1[:, 0:n], AF.Exp, scale=kap)
                act(T1[:, 0:n], T1[:, 0:n], AF.Exp, scale=kap2)
                act(T1[:, 0:n], T1[:, 0:n], AF.Sqrt, bias=EPSC[:, 1:2], scale=-1.0)
                act(T2[:, 0:n], pi, AF.Sigmoid, bias=bi)
                tt(T1[:, 0:n], T1[:, 0:n], T2[:, 0:n], ALU.mult)
                tt(BBf[:, t0:t0 + n], T1[:, 0:n], XCv[:, t0:t0 + n], ALU.mult)
            if d == 0:
                scan(H, AB, BBf, 0.0)
            else:
                scan(H2[:, 0:TC][:, ::-1], AB[:, 0:TC][:, ::-1], BBf[:, 0:TC][:, ::-1], 0.0)
                scan(H2[:, TC:T][:, ::-1], AB[:, TC:T][:, ::-1], BBf[:, TC:T][:, ::-1], H2[:, 0:1])
        tt(H, H, H2, ALU.add)
        bg(6)
        finish_group(l, a, H, G, Wo, last, alloc([1, T], BF16, at=ab_off)[:, 0, :],
                     alloc([1, T], BF16, at=bb_off)[:, 0, :], [T1A, T1b])
        if a == 1:
            bg(64)
        bg_flush()
        bg_bufs.clear()
        A.top = mark

    class Stop(Exception):
        pass
    PLIM = int(dbg[0][1:]) if (dbg and dbg[0][0] == 'p') else 99

    def cut(n):
        if PLIM == n:
            raise Stop()

    def group_pair(l, kind, j, HT, LRT, last):
        mark = A.top
        g8 = 2 + kind * 3 + j
        cq = (512 if kind == 0 else 2064) + j * 128
        ck = (896 if kind == 0 else 2448) + j * 128
        cv = (1280 if kind == 0 else 2832) + j * 128
        cg = (1664 if kind == 0 else 3216) + j * 128
        WB2 = [alloc([8, 128], BF16) for _ in range(2)]
        wb_rr = [0]

        def wcol(c0, src=None):
            Wb = WB2[wb_rr[0] % 2]
            wb_rr[0] += 1
            wload(Wb, (win_d[l] if src is None else src)[:, c0:c0 + 128].rearrange('(k p) f -> p k f', p=128), e='scalar')
            return Wb
        Wo = alloc([1, D], BF16)[:, 0, :]
        wload(Wo, wout_d[l][g8 * 128:(g8 + 1) * 128, :])
        if kind == 0:
            W2 = alloc([2, 128], BF16, np_=16)
            for d in range(2):
                wload(W2[:, d, :], w2_d[l][d][:, j * 128:(j + 1) * 128])
        Qb = alloc([1, T], BF16)[:, 0, :]; qb_off = A.last
        Kb = alloc([1, T], BF16)[:, 0, :]; kb_off = A.last
        V1 = alloc([NCH, 130], BF16); VA = alloc([NCH, 128], BF16); VB = alloc([NCH, 128], BF16)
        OUT = alloc([1, T], F32)[:, 0, :]
        PL = alloc([1, T], F32)[:, 0, :]; pl_off = A.last
        QDs = [alloc([1, T], BF16)[:, 0, :] for _ in range(2)]
        KDs = [alloc([1, T], BF16)[:, 0, :] for _ in range(2)]
        DECs = [alloc([1, NCH], F32)[:, 0, :] for _ in range(2)]
        Rsts = [alloc([1, 130], F32)[:, 0, :] for _ in range(2)]
        Sbs = [alloc([2, 128], BF16) for _ in range(2)]
        nSbs = [alloc([2, 128], BF16) for _ in range(2)] if kind == 1 else [None, None]
        T1s = []
        T1bs = []
        for _ in range(2):
            T1s.append(alloc([1, 512], F32)[:, 0, :])
            T1bs.append(alloc([1, 512], BF16, at=A.last)[:, 0, :])
        SbFs = [alloc([2, 130], BF16) for _ in range(2)]
        T2s = [alloc([2, 128], F32) for _ in range(2)] if kind == 1 else [None, None]
        SCbs = [alloc([2, 256], BF16) for _ in range(2)]
        t1_rr = [0]

        def T1n():
            t1_rr[0] += 1
            return T1s[t1_rr[0] % 2]

        def T1bn():
            t1_rr[0] += 1
            return T1bs[t1_rr[0] % 2]
        proj(HT, wcol(cq), 128, lambda pb, t0, n, s_: act(Qb[:, t0:t0 + n], pb, AF.Copy, scale=0.125))
        proj(HT, wcol(ck), 128, lambda pb, t0, n, s_: cp(Kb[:, t0:t0 + n], pb))
        cut(1)
        mset(VA[:, :, 64:128], 0.0); mset(VB[:, :, 0:64], 0.0); mset(V1[:, :, 128:130], 1.0)
        Wv = wcol(cv)
        for cb in range(0, NCH, 4):
            nb4 = min(4, NCH - cb)
            pb = ps(pj_bank(), 512)
            for q in range(nb4):
                c = cb + q
                mmg(pb[:, q * 128:(q + 1) * 128], [(HT[:, k, c * 128:(c + 1) * 128], Wv[:, k, :]) for k in range(8)])
            cp(V1[:, cb:cb + nb4, 0:128], pb[:, 0:nb4 * 128].rearrange('p (q t) -> p q t', t=128))
        cp(VA[:, :, 0:64], V1[:, :, 0:64], e='gpsimd')
        cp(VB[:, :, 64:128], V1[:, :, 64:128], e='gpsimd')
        cut(2)
        sc_ = (1.0 / 16.0) if kind == 0 else 1.0
        for d in range(2):
            QD = QDs[d]; KD = KDs[d]; DEC = DECs[d]
            nb = NB[:, l, (0 if kind == 0 else 6) + j * 2 + d:(0 if kind == 0 else 6) + j * 2 + d + 1]

            def softplus_ev(pb, t0, n, s_):
                t1 = T1n()
                act(t1[:, 0:n], pb, AF.Exp, bias=nb, scale=-1.0)
                act(PL[:, t0:t0 + n], t1[:, 0:n], AF.Ln, bias=EPSC[:, 1:2])
            if kind == 0:
                for (t0, n, s_) in CHUNKS:
                    pb = ps(pj_bank(), n)
                    mmg(pb, [(W2[:, d, :], LRT[:, t0:t0 + n])])
                    softplus_ev(pb, t0, n, s_)
            else:
                proj(HT, wcol(((j * 2 + d) * 2 + 1) * 128, wrep_d[l]), 128, softplus_ev)
            for c in range(NCH):
                plc = PL[:, c * 128:(c + 1) * 128]
                if d == 0:
                    scan(plc, ONESb, plc, 0.0)
                else:
                    scan(plc[:, ::-1], ONESb, plc[:, ::-1], 0.0)
            Pv = PL.rearrange('p (c t) -> p c t', t=128)
            pend = Pv[:, :, 127] if d == 0 else Pv[:, :, 0]
            act(DEC, pend, AF.Exp, scale=-sc_)
            if kind == 0:
                for (t0, n, s_) in CHUNKS:
                    t1 = T1bn()
                    act(t1[:, 0:n], PL[:, t0:t0 + n], AF.Exp, scale=-sc_)
                    tt(QD[:, t0:t0 + n], t1[:, 0:n], Qb[:, t0:t0 + n], ALU.mult)
                    t1 = T1bn()
                    act(t1[:, 0:n], PL[:, t0:t0 + n], AF.Exp, scale=sc_)
                    tt(KD[:, t0:t0 + n], t1[:, 0:n], Kb[:, t0:t0 + n], ALU.mult)
            else:
                for (t0, n, s_) in CHUNKS:
                    t1 = T1bn()
                    act(t1[:, 0:n], PL[:, t0:t0 + n], AF.Exp, scale=-1.0)
                    tt(QD[:, t0:t0 + n], t1[:, 0:n], Qb[:, t0:t0 + n], ALU.mult)
                bi_ = prm(l, 'mgi', j * 2 + d)

                def ig_ev(pb, t0, n, s_, KD=KD):
                    t1 = T1n()
                    tt(t1[:, 0:n], pb, PL[:, t0:t0 + n], ALU.add)
                    act(t1[:, 0:n], t1[:, 0:n], AF.Exp, bias=bi_)
                    tt(KD[:, t0:t0 + n], t1[:, 0:n], Kb[:, t0:t0 + n], ALU.mult)
                proj(HT, wcol(((j * 2 + d) * 2 + 0) * 128, wrep_d[l]), 128, ig_ev)
        cut(3)
        G = alloc([1, T], BF16, at=pl_off)[:, 0, :]
        Wg = wcol(cg)
        if kind == 0:
            proj(HT, Wg, 128, lambda pb, t0, n, s_: act(G[:, t0:t0 + n], pb, AF.Silu))
        else:
            proj(HT, Wg, 128, lambda pb, t0, n, s_: act(G[:, t0:t0 + n], pb, AF.Sigmoid))
        KTs = [alloc([NCH, 128], BF16, at=qb_off), alloc([NCH, 128], BF16, at=kb_off)]
        for d in range(2):
            for cb in range(0, NCH, 4):
                nb4 = min(4, NCH - cb)
                pt = ps(pj_bank(), 512, dt=BF16)
                for q in range(nb4):
                    c = cb + q
                    trp(pt[:, q * 128:(q + 1) * 128], KDs[d][:, c * 128:(c + 1) * 128], IDb)
                src = pt[:, 0:nb4 * 128].rearrange('p (q t) -> p q t', t=128)
                if d == 0:
                    act(KTs[d][:, cb:cb + nb4, :], src, AF.Copy)
                else:
                    cp(KTs[d][:, cb:cb + nb4, :], src)
        cut(4)
        orders = [list(range(NCH)), [1, 0] + list(range(NCH - 1, 1, -1))]
        MASKS = [MASKF, MASKB]
        for d in range(2):
            mset(Sbs[d], 0.0)
            if kind == 1:
                mset(nSbs[d], 0.0)
        pj2 = [0]

        def pjb2():
            pj2[0] += 1
            return (pj2[0] % 2) * 512
        pus = {}

        def stageA(d, i):
            c = orders[d][i]
            tsl = slice(c * 128, (c + 1) * 128)
            par = i % 2
            QD = QDs[d]; KD = KDs[d]
            pscA = ps(2048, 128)
            pscB = ps(2560, 128)
            mmg(pscA, [(KD[0:64, tsl], QD[0:64, tsl])])
            mmg(pscB, [(KD[64:128, tsl], QD[64:128, tsl])])
            psc2 = psum[:, 2048:3072].rearrange('p (h t) -> p h t', t=512)[:, :, 0:128]
            tt(SCbs[d][:, par, :].rearrange('p (h t) -> p h t', t=128), psc2,
               MASKS[d].unsqueeze(1).broadcast_to([128, 2, 128]), ALU.mult)
            pu = ps((6 + d) * 512 + par * 256, 130)
            mmg(pu, [(KTs[d][:, c, :], V1[:, c, :])])
            pus[(d, i)] = pu

        def stageB1(d, i):
            c = orders[d][i]
            tsl = slice(c * 128, (c + 1) * 128)
            par = i % 2
            QD = QDs[d]; DEC = DECs[d]; Rst = Rsts[d]; Sb = Sbs[d]; nSb = nSbs[d]; SCb = SCbs[d]
            pu = pus.pop((d, i))
            if i == 0:
                cp(Rst, pu)
            else:
                cprev = orders[d][i - 1]
                stt(Rst, Rst, DEC[:, cprev:cprev + 1], pu, ALU.mult, ALU.add)
            if i + 1 < NCH:
                nx = (i + 1) % 2
                sbf = SbFs[d][:, nx, :]
                act(sbf, Rst, AF.Copy, scale=DEC[:, c:c + 1])
                tt(Sb[:, nx, :], sbf[:, 0:128], BLK64, ALU.mult, e='gpsimd')
                if kind == 1:
                    ts(nSb[:, nx, :], BLK64, sbf[:, 128:129], 1.0, ALU.mult, ALU.mult, e='gpsimd')
            pob = (d * 2 + par) * 512
            po = ps(pob, 128)
            mmg(po, [(Sb[:, i % 2, :], QD[:, tsl]), (VA[:, c, :], SCb[:, par, 0:128]),
                     (VB[:, c, :], SCb[:, par, 128:256])])
            if kind == 1:
                pd = ps(pob + 128, 128)
                mmg(pd, [(nSb[:, i % 2, :], QD[:, tsl]), (EAb, SCb[:, par, 0:128]), (EBb, SCb[:, par, 128:256])])

        def stageB2a(d, i):
            if kind == 0:
                return
            par = i % 2
            pd = ps((d * 2 + par) * 512 + 128, 128)
            T2 = T2s[d][:, par, :]
            act(T2, pd, AF.Abs)
            act(T2, T2, AF.Ln, bias=EPSC[:, 0:1])
            act(T2, T2, AF.Exp, scale=-1.0)

        def stageB2b(d, i):
            c = orders[d][i]
            tsl = slice(c * 128, (c + 1) * 128)
            par = i % 2
            po = ps((d * 2 + par) * 512, 128)
            if kind == 0:
                tt(OUT[:, tsl], po, OUT[:, tsl], ALU.add)
            else:
                T2 = T2s[d][:, par, :]
                stt(T2, T2, 1.0, po, ALU.min, ALU.mult)
                tt(OUT[:, tsl], OUT[:, tsl], T2, ALU.add, e='gpsimd')
        mset(OUT, 0.0, e='gpsimd')
        stageA(0, 0); stageA(1, 0)
        for i in range(NCH):
            if i + 1 < NCH:
                stageA(0, i + 1); stageA(1, i + 1)
            stageB1(0, i); stageB1(1, i)
            stageB2a(0, i); stageB2a(1, i)
            if i >= 1:
                stageB2b(0, i - 1); stageB2b(1, i - 1)
        stageB2b(0, NCH - 1); stageB2b(1, NCH - 1)
        cut(5)
        cut(6)
        finish_group(l, g8, OUT, G, Wo, last, KDs[0], alloc([1, T], BF16, at=qb_off)[:, 0, :], [T1s[0], T1s[1]])
        A.top = mark

    def mlp(l, last):
        mark = A.top
        NW = 6
        W1 = [alloc([8, 128], BF16) for _ in range(NW)]
        W2b = [alloc([1, D], BF16)[:, 0, :] for _ in range(NW)]
        H2 = alloc([8, 1024], BF16)
        A1T = alloc([32, 1024], BF16); a1t_off = A.last
        SQ = alloc([8, 512], BF16, at=a1t_off); RS = alloc([1, 512], F32)[:, 0, :]; TMP = alloc([1, 512], F32)[:, 0, :]
        TM3 = [alloc([1, 512], F32)[:, 0, :] for _ in range(3)]
        tm_rr = [0]
        passes = [(256, 1024, 0), (1280, 1024, 0)]
        if not last:
            passes = [(0, 256, 1)] + passes
        for (p0, pn, s_) in passes:
            subs = [(p0 + i, min(512, pn - i)) for i in range(0, pn, 512)]
            for (t0, n) in subs:
                norm_mod(H2[:, :, t0 - p0:t0 - p0 + n], t0, n, s_, A2, l, 3, SQ, RS, TM3)
            for f in range(32):
                w1 = W1[f % NW]
                wload_cached(w1, w1_d[l][:, f * 128:(f + 1) * 128].rearrange('(k p) f -> p k f', p=128),
                             w1s_d[l, f], ('w1', l, f))
                for (t0, n) in subs:
                    pb = ps(pj_bank(), n)
                    mmg(pb, [(w1[:, k, :], H2[:, k, t0 - p0:t0 - p0 + n]) for k in range(8)])
                    tm_rr[0] += 1
                    tm = TM3[tm_rr[0] % 3]
                    act(tm[:, 0:n], pb, AF.Relu)
                    tt(A1T[:, f, t0 - p0:t0 - p0 + n], tm[:, 0:n], tm[:, 0:n], ALU.mult)
            for (t0, n) in subs:
                for f in range(32):
                    w2 = W2b[f % NW]
                    wload_cached(w2, wm2_d[l][f * 128:(f + 1) * 128, :], w2s_d[l, f], ('w2', l, f))
                    for k in range(8):
                        pb = ps(k * 512, n)
                        s.op('tensor', (lambda en, pb=pb, w=w2[:, k * 128:(k + 1) * 128],
                                        r=A1T[:, f, t0 - p0:t0 - p0 + n], st_=(f == 0), sp_=(f == 31):
                                        en.matmul(pb, lhsT=w, rhs=r, start=st_, stop=sp_)),
                             reads=[w2[:, k * 128:(k + 1) * 128], A1T[:, f, t0 - p0:t0 - p0 + n]], writes=[pb])
                for k in range(8):
                    pb = ps(k * 512, n)
                    stt(X[:, k, t0:t0 + n], pb, MOD[:, l, 5 * 8 + k, s_:s_ + 1], X[:, k, t0:t0 + n], ALU.mult, ALU.add)
        del STG[2:]
        A.top = mark

    def final():
        mark = A.top
        FG = alloc([1, D], F32)[:, 0, :]
        ld(FG, fg_d)
        OT = [alloc([1, D], F32)[:, 0, :] for _ in range(2)]
        SS = alloc([1, 4], F32)[:, 0, :]
        JK = alloc([1, D], F32)[:, 0, :]
        toks = []
        for it in range(TL // 128):
            ot = OT[it % 2]
            t0 = TC + it * 128
            for half in range(2):
                pb = ps(((it * 2 + half) % 4) * 512, 512)
                for q in range(4):
                    k = half * 4 + q
                    trp(pb[:, q * 128:(q + 1) * 128], X[:, k, t0:t0 + 128], IDf)
                cp(ot[:, half * 512:(half + 1) * 512], pb)
            c0 = it % 2
            s.op('scalar', lambda en, ot=ot, c0=c0: en.activation(out=JK, in_=ot, func=AF.Square, accum_out=SS[:, c0:c0 + 1]),
                 reads=[ot], writes=[JK, SS[:, c0:c0 + 1]])
            act(SS[:, 2 + c0:3 + c0], SS[:, c0:c0 + 1], AF.Ln, bias=EPSC[:, 0:1], scale=1.0 / D)
            act(SS[:, 2 + c0:3 + c0], SS[:, 2 + c0:3 + c0], AF.Exp, scale=-0.5)
            stt(ot, ot, SS[:, 2 + c0:3 + c0], FG, ALU.mult, ALU.mult)
            toks.append(s.dma('sync', out_d[it * 128:(it + 1) * 128, :], ot, sb_reads=[ot]))
        for tk in toks:
            s.wait_tok('sync', tk)
        A.top = mark

    def dump(ap, ncol):
        mark = A.top
        DB = alloc([1, dbg[1]], F32)[:, 0, :]
        mset(DB, 0.0)
        cp(DB[:, 0:ncol], ap)
        tk = s.dma('sync', dbg_d, DB, sb_reads=[DB])
        s.wait_tok('sync', tk)
        A.top = mark

    load_x()
    stage = dbg[0] if dbg else None
    if stage == 'x':
        dump(X[:, int(dbg[2]), :], T)
        n_layers = 0
    for l in range(n_layers):
        last = (l == DEPTH - 1)
        ada(l)
        if stage == 'ada':
            dump(MOD[:, l].rearrange('p a b -> p (a b)'), 96)
            break
        derive(l)
        if stage == 'drv':
            dump(A1.rearrange('p a b -> p (a b)'), 16)
            break
        mark_l = A.top
        HT = alloc([8, T], BF16)
        SQs = [alloc([8, 512], BF16) for _ in range(2)]
        RSs = [alloc([1, 512], F32)[:, 0, :] for _ in range(2)]
        TMPs = [alloc([1, 512], F32)[:, 0, :] for _ in range(3)]
        TMP = TMPs[0]
        for ic, (t0, n, s_) in enumerate(CHUNKS):
            norm_mod(HT[:, :, t0:t0 + n], t0, n, s_, A1, l, 0, SQs[ic % 2], RSs[ic % 2], TMPs)
        A.top = mark_l + 8 * T * 2
        if stage and (stage == 'h%d' % l or stage.startswith('nm')):
            mark = A.top
            HF = alloc([1, T], F32)[:, 0, :]
            cp(HF, HT[:, 0, :])
            dump(HF, T)
            A.top = mark
            break
        LRT = alloc([1, T], BF16, np_=16)[:, 0, :]
        Wl = alloc([8, 16], BF16)
        wload(Wl, win_d[l][:, 2048:2064].rearrange('(k p) f -> p k f', p=128))
        proj(HT, Wl, 16, lambda pb, t0, n, s_: cp(LRT[:, t0:t0 + n], pb))
        bg_setup(l)
        groups = [('a', 0), ('a', 1)] + [('p', 0, j) for j in range(3)] + [('p', 1, j) for j in range(3)]
        if stage and stage.startswith('g'):
            groups = groups[:int(stage[1:]) + 1]
        if stage and stage[0] == 'p':
            groups = [('p', int(dbg[2]), 0)]
        try:
            for g in groups:
                if g[0] == 'a':
                    group_a(l, g[1], HT, last)
                else:
                    group_pair(l, g[1], g[2], HT, LRT, last)
        except Stop:
            pass
        A.top = mark_l
        if stage and (stage.startswith('g') or stage[0] == 'p'):
            dump(X[:, int(dbg[2]), :], T)
            break
        mlp(l, last)
        if stage == 'l%d' % l:
            dump(X[:, int(dbg[2]), :], T)
            break
    if not stage:
        final()
    s.emit()
    print('ninst', s.ninst, 'arena peak', A.peak, 'persistent', P_MARK)
    return nc


_NC = {}


def prep_inputs(inp):
    inp = {k: np.asarray(v) for k, v in inp.items()}
    f = lambda a: np.ascontiguousarray(a, dtype=np.float32)
    shared = {
        'ada_w': f(inp['ada_w']), 'w_in': f(inp['w_in']),
        'wrep': f(np.stack([pack_wrep(inp, l) for l in range(DEPTH)])),
        'gatew': f(np.stack([pack_gatew(inp, l) for l in range(DEPTH)])),
        'gla_w2': f(inp['gla_w2']), 'w_out': f(inp['w_out']), 'mlp_w1': f(inp['mlp_w1']), 'mlp_w2': f(inp['mlp_w2']),
        'prm': f(np.stack([pack_params(inp, l) for l in range(DEPTH)])),
        'fg': f(np.broadcast_to(inp['final_g'][None, :], (128, D))), 'cst': make_consts(),
    }
    maps = []
    for b in range(8):
        cc = np.zeros((128, 16), np.float32)
        cc[:, 0::2] = inp['c'][b].reshape(8, 128).T
        cc[:, 1::2] = inp['c_ctx'].reshape(8, 128).T
        m = dict(shared)
        m['x'] = f(inp['x'][b]); m['ctx'] = f(inp['ctx'][b]); m['cc'] = cc
        maps.append(m)
    return maps


def kernel(**inputs):
    if 'nc' not in _NC:
        _NC['nc'] = build()
    maps = prep_inputs(inputs)
    res = run_bass_kernel_spmd(_NC['nc'], maps, core_ids=list(range(8)))
    return np.stack([np.asarray(res.results[b]['out'], dtype=np.float32) for b in range(8)], axis=0)
```

```python
import math
import numpy as np
import ml_dtypes
import concourse.bass as bass
import concourse.mybir as mybir
from concourse.bass_utils import run_bass_kernel_spmd

F32 = mybir.dt.float32
BF16 = mybir.dt.bfloat16
I32 = mybir.dt.int32
U8 = mybir.dt.uint8
AF = mybir.ActivationFunctionType
ALU = mybir.AluOpType

D = 1024
TL = 2048
TC = 256
T = TL + TC
NCH = T // 128
DEPTH = 2
DIN = 3624
DFF = 4096
EPS = 1e-6
ENG = ['tensor', 'vector', 'scalar', 'gpsimd', 'sync']
DSZ = {F32: 4, BF16: 2, I32: 4, U8: 1}


def _dsz(dt):
    for k, v in DSZ.items():
        if k == dt:
            return v
    return 4


def region(ap):
    es = _dsz(ap.dtype)
    aps = ap.ap
    pstep, pcnt = aps[0]
    off = ap.offset
    if pstep == 0:
        pstep = 1 << 40
    p0 = off // pstep
    fo = off % pstep
    lo = fo
    hi = fo
    for st, c in aps[1:]:
        d = st * (c - 1)
        if d < 0:
            lo += d
        else:
            hi += d
    if ap.tensor.name == 'psum':
        return ('psum', 0, 128, (lo * es) // 2048 * 2048, ((hi + 1) * es + 2047) // 2048 * 2048)
    return (ap.tensor.name, p0, p0 + pcnt, lo * es, (hi + 1) * es)


class Sched:
    def __init__(self, nc, n_dma_sems=32):
        self.nc = nc
        self.sem = {}
        for e in ENG:
            self.sem[e] = nc.semaphore('s_' + e).__enter__()
        self.dma_sems = [nc.semaphore('s_dma%d' % i).__enter__() for i in range(n_dma_sems)]
        self.dma_cnt = [0] * n_dma_sems
        self.dma_next = 0
        self.cnt = {e: 0 for e in ENG}
        self.seen = {e: {} for e in ENG}
        self.q = {e: [] for e in ENG}
        self.acc = {}
        self.ninst = 0

    def _deps(self, reads, writes, e=None):
        deps = {}
        for ap in reads:
            name, p0, p1, f0, f1 = region(ap)
            ps_ = (name == 'psum')
            for r in self.acc.get(name, ()):
                if (r[4] or (ps_ and r[5][0] != e)) and r[0] < p1 and p0 < r[1] and r[2] < f1 and f0 < r[3]:
                    k, v = r[5]
                    if deps.get(k, 0) < v:
                        deps[k] = v
        for ap in writes:
            name, p0, p1, f0, f1 = region(ap)
            for r in self.acc.get(name, ()):
                if r[0] < p1 and p0 < r[1] and r[2] < f1 and f0 < r[3]:
                    k, v = r[5]
                    if deps.get(k, 0) < v:
                        deps[k] = v
        return deps

    def _record(self, reads, writes, tok):
        for ap in writes:
            name, p0, p1, f0, f1 = region(ap)
            lst = self.acc.setdefault(name, [])
            lst[:] = [r for r in lst if not (p0 <= r[0] and r[1] <= p1 and f0 <= r[2] and r[3] <= f1)]
            lst.append((p0, p1, f0, f1, True, tok))
        for ap in reads:
            name, p0, p1, f0, f1 = region(ap)
            lst = self.acc.setdefault(name, [])
            lst[:] = [r for r in lst if not ((not r[4]) and r[5][0] == tok[0] and p0 <= r[0] and r[1] <= p1
                                             and f0 <= r[2] and r[3] <= f1)]
            lst.append((p0, p1, f0, f1, False, tok))

    def _waits(self, e, deps):
        out = []
        seen = self.seen[e]
        for k, v in deps.items():
            if seen.get(k, 0) >= v:
                continue
            seen[k] = v
            if k == e and e == 'tensor':
                continue
            out.append((k, v))
        return out

    def _semof(self, k):
        return self.sem[k] if isinstance(k, str) else self.dma_sems[k[1]]

    def op(self, e, fn, reads=(), writes=()):
        reads = [r for r in reads if r is not None and not isinstance(r, (int, float))]
        deps = self._deps(reads, writes, e)
        waits = self._waits(e, deps)
        self.cnt[e] += 1
        tok = (e, self.cnt[e])
        self._record(reads, writes, tok)
        self.q[e].append((waits, fn, self.sem[e], 1))
        self.ninst += 1
        return tok

    def dma(self, qe, out, in_, sb_reads=(), sb_writes=(), **kw):
        deps = self._deps(sb_reads, sb_writes)
        i = self.dma_next
        self.dma_next = (self.dma_next + 1) % len(self.dma_sems)
        if self.dma_cnt[i] > 0:
            k = ('dma', i)
            if deps.get(k, 0) < self.dma_cnt[i] * 16:
                deps[k] = self.dma_cnt[i] * 16
        waits = self._waits(qe, deps)
        self.dma_cnt[i] += 1
        tok = (('dma', i), self.dma_cnt[i] * 16)
        self._record(sb_reads, sb_writes, tok)
        self.q[qe].append((waits, (lambda eng, out=out, in_=in_, kw=kw: eng.dma_start(out=out, in_=in_, **kw)),
                           self.dma_sems[i], 16))
        self.ninst += 1
        return tok

    def wait_tok(self, e, tok):
        waits = self._waits(e, {tok[0]: tok[1]})
        if waits:
            self.q[e].append((waits, None, None, 0))

    def emit(self):
        nc = self.nc
        with nc.Block() as block:
            def mk(e):
                def body(eng):
                    for waits, fn, sem, inc in self.q[e]:
                        for k, v in waits:
                            eng.wait_ge(self._semof(k), v)
                        if fn is not None:
                            fn(eng).then_inc(sem, inc)
                return body
            block.tensor(mk('tensor'))
            block.vector(mk('vector'))
            block.scalar(mk('scalar'))
            block.gpsimd(mk('gpsimd'))
            block.sync(mk('sync'))


NCST = 128 * 7 + 1 + 32 + 64


def make_consts():
    p = np.arange(128)
    c = np.zeros((128, NCST), np.float32)
    o = 0
    c[:, o:o + 128] = np.eye(128); o += 128
    c[:, o:o + 128] = (p[:, None] <= p[None, :]); o += 128
    c[:, o:o + 128] = (p[:, None] >= p[None, :]); o += 128
    c[:, o:o + 128] = 1.0; o += 128
    c[:, o:o + 128] = (p[:, None] // 64 == p[None, :] // 64); o += 128
    c[:, o:o + 128] = (p[None, :] < 64); o += 128
    c[:, o:o + 128] = (p[None, :] >= 64); o += 128
    c[:, o] = p; o += 1
    c[:, o:o + 32] = np.arange(32)[None, :]; o += 32
    c[:, o:o + 64] = np.arange(64)[None, :]; o += 64
    return c


PC = {}
_o = 0
for _n, _w in [('n1g', 8), ('n2g', 8), ('adab', 48), ('hng', 8), ('convw', 8), ('convb', 2), ('lgb', 8),
               ('lam', 4), ('b2', 6), ('mgi', 6), ('mgf', 6)]:
    PC[_n] = (_o, _w)
    _o += _w
NPRM = _o


def pack_params(inp, l):
    P = np.zeros((128, NPRM), np.float32)

    def put(name, arr):
        o, w = PC[name]
        assert arr.shape == (128, w), (name, arr.shape)
        P[:, o:o + w] = arr
    put('n1g', inp['norm1_g'][l].reshape(8, 128).T)
    put('n2g', inp['norm2_g'][l].reshape(8, 128).T)
    put('adab', inp['ada_b'][l].reshape(48, 128).T)
    put('hng', inp['head_norm_g'][l].reshape(8, 128).T)
    cw = inp['conv_w'][l]
    put('convw', np.concatenate([cw[:, a * 128:(a + 1) * 128].T for a in range(2)], axis=1))
    put('convb', inp['conv_b'][l].reshape(2, 128).T)
    gb = inp['lru_gate_b'][l]
    put('lgb', np.stack([gb[d, g, a * 128:(a + 1) * 128] for a in range(2) for d in range(2) for g in range(2)], axis=1))
    lam = inp['lru_lambda'][l]
    put('lam', np.stack([lam[d, a * 128:(a + 1) * 128] for a in range(2) for d in range(2)], axis=1))
    b2 = inp['gla_b2'][l]
    put('b2', np.stack([b2[d, j * 128:(j + 1) * 128] for j in range(3) for d in range(2)], axis=1))
    mg = inp['mlstm_gate_b'][l]
    put('mgi', np.stack([np.repeat(mg[d, 0, 2 * j:2 * j + 2], 64) for j in range(3) for d in range(2)], axis=1))
    put('mgf', np.stack([np.repeat(mg[d, 1, 2 * j:2 * j + 2], 64) for j in range(3) for d in range(2)], axis=1))
    return P


def pack_gatew(inp, l):
    gw = inp['lru_gate_w'][l]
    out = np.zeros((128, 8, 128), np.float32)
    for a in range(2):
        for d in range(2):
            for g in range(2):
                i = a * 4 + d * 2 + g
                for bb in range(2):
                    out[bb * 64:(bb + 1) * 64, i, bb * 64:(bb + 1) * 64] = gw[d, g, a * 2 + bb]
    return out.reshape(128, 8 * 128)


def pack_wg24(inp, l):
    w = inp['w_in'][l]
    out = np.zeros((D, 120), np.float32)
    out[:, 64:88] = w[:, 3600:3624]
    out[:, 96:120] = w[:, 3600:3624]
    return out


def make_sel():
    sel = np.zeros((128, 12, 128), np.float32)
    for j in range(3):
        for d in range(2):
            for kind in range(2):
                idx = (j * 2 + d) * 2 + kind
                for h in range(2):
                    r = kind * 12 + d * 6 + 2 * j + h
                    sel[64 + r, idx, h * 64:(h + 1) * 64] = 1.0
                    sel[96 + r, idx, h * 64:(h + 1) * 64] = 1.0
    return sel.reshape(128, 12 * 128)


def pack_wrep(inp, l):
    w = inp['w_in'][l]
    blocks = []
    for j in range(3):
        for d in range(2):
            for kind in range(2):
                base = 3600 + kind * 12 + d * 6 + 2 * j
                blocks.append(np.repeat(w[:, base:base + 2], 64, axis=1))
    return np.ascontiguousarray(np.concatenate(blocks, axis=1))


def build(n_layers=DEPTH, dbg=None):
    nc = bass.Bass('TRN2', target_bir_lowering=False)

    def din(name, shape):
        return nc.dram_tensor(name, list(shape), F32, kind='ExternalInput').ap()
    x_d = din('x', [TL, D]); ctx_d = din('ctx', [TC, D]); cc_d = din('cc', [128, 16])
    adaw_d = din('ada_w', [DEPTH, D, 6 * D]); win_d = din('w_in', [DEPTH, D, DIN])
    wg24_d = din('wg24', [DEPTH, D, 120]); sel_d = din('sel', [128, 12 * 128]); gatew_d = din('gatew', [DEPTH, 128, 8 * 128])
    w2_d = din('gla_w2', [DEPTH, 2, 16, 384]); wout_d = din('w_out', [DEPTH, D, D])
    w1_d = din('mlp_w1', [DEPTH, D, DFF]); wm2_d = din('mlp_w2', [DEPTH, DFF, D])
    prm_d = din('prm', [DEPTH, 128, NPRM]); fg_d = din('fg', [128, D]); cst_d = din('cst', [128, NCST])
    out_d = nc.dram_tensor('out', [TL, D], F32, kind='ExternalOutput').ap()
    dbg_d = None
    if dbg:
        dbg_d = nc.dram_tensor('dbg', [128, dbg[1]], F32, kind='ExternalOutput').ap()

    s = Sched(nc)
    ARENA = 206 * 1024
    arena = nc.alloc_sbuf_tensor('arena', [128, ARENA], U8)
    psum = nc.alloc_psum_tensor('psum', [128, 4096], F32)

    class A:
        top = 0
        peak = 0
        last = 0

    def alloc(shape, dt, np_=128, at=None):
        n = 1
        for v in shape:
            n *= v
        nb = (n * _dsz(dt) + 31) // 32 * 32
        if at is None:
            off = A.top
            A.top += nb
            A.peak = max(A.peak, A.top)
        else:
            off = at
        A.last = off
        assert off + nb <= ARENA, ('arena overflow', off + nb)
        ap = arena[:, off:off + n * _dsz(dt)].bitcast(dt)
        if len(shape) == 2:
            ap = ap.rearrange('p (a b) -> p a b', b=shape[1])
        elif len(shape) == 3:
            ap = ap.rearrange('p (a b c) -> p a b c', b=shape[1], c=shape[2])
        return ap[0:np_] if np_ != 128 else ap

    def ps(off, n, dt=F32, np_=128, p0=0):
        if dt == F32:
            return psum[p0:p0 + np_, off:off + n]
        return psum[p0:p0 + np_, off:off + (n + 1) // 2].bitcast(dt)[:, 0:n]

    def R_(*aps):
        return [a for a in aps if a is not None and not isinstance(a, (int, float))]

    def act(out, in_, func, bias=None, scale=None, e='scalar'):
        kw = {}
        if bias is not None:
            kw['bias'] = bias
        if scale is not None:
            kw['scale'] = scale
        s.op(e, lambda en: en.activation(out=out, in_=in_, func=func, **kw), reads=R_(in_, bias, scale), writes=[out])

    def tt(out, in0, in1, op, e='vector'):
        s.op(e, lambda en: en.tensor_tensor(out=out, in0=in0, in1=in1, op=op), reads=[in0, in1], writes=[out])

    def ts(out, in0, s1, s2, op0, op1=None, e='vector'):
        if op1 is None:
            s.op(e, lambda en: en.tensor_scalar(out=out, in0=in0, scalar1=s1, scalar2=None, op0=op0),
                 reads=R_(in0, s1), writes=[out])
        else:
            s.op(e, lambda en: en.tensor_scalar(out=out, in0=in0, scalar1=s1, scalar2=s2, op0=op0, op1=op1),
                 reads=R_(in0, s1, s2), writes=[out])

    def stt(out, in0, sc, in1, op0, op1):
        s.op('vector', lambda en: en.scalar_tensor_tensor(out=out, in0=in0, scalar=sc, in1=in1, op0=op0, op1=op1),
             reads=R_(in0, sc, in1), writes=[out])

    def cp(out, in_, e='vector'):
        s.op(e, lambda en: en.tensor_copy(out=out, in_=in_), reads=[in_], writes=[out])

    def mset(out, v, e='vector'):
        s.op(e, lambda en: en.memset(out, v), writes=[out])

    def mmg(out, pairs):
        n = len(pairs)

        def fn(en):
            inst = None
            for i, (l, r) in enumerate(pairs):
                inst = en.matmul(out, lhsT=l, rhs=r, start=(i == 0), stop=(i == n - 1))
            return inst
        rd = []
        for l, r in pairs:
            rd += [l, r]
        s.op('tensor', fn, reads=rd, writes=[out])

    def trp(out, in_, ident):
        s.op('tensor', lambda en: en.transpose(out, in_, ident), reads=[in_, ident], writes=[out])

    def scan(out, d0, d1, init):
        s.op('vector', lambda en: en.tensor_tensor_scan(out=out, data0=d0, data1=d1, initial=init, op0=ALU.mult,
                                                         op1=ALU.add), reads=R_(d0, d1, init), writes=[out])

    def ld(out, in_, q='sync'):
        s.dma(q, out, in_, sb_writes=[out])

    X = alloc([8, T], F32)
    CST = alloc([NCST], F32)[:, 0, :] if False else alloc([1, NCST], F32)[:, 0, :]
    ld(CST, cst_d)
    IDf = CST[:, 0:128]
    PIDX = CST[:, 896:897]; RIDX = CST[:, 897:929]; CIDX = CST[:, 929:993]
    CB = alloc([7, 128], BF16)
    cp(CB, CST[:, 0:896].rearrange('p (a b) -> p a b', b=128))
    IDb, MASKF, MASKB, ONESb, BLK64, EAb, EBb = [CB[:, i, :] for i in range(7)]
    PRM = alloc([DEPTH, NPRM], F32)
    for l in range(DEPTH):
        ld(PRM[:, l, :], prm_d[l])
    CCT = alloc([1, 16], F32)[:, 0, :]
    ld(CCT, cc_d)
    CCb = alloc([8, 2], BF16)
    act(CCb, CCT.rearrange('p (k s) -> p k s', s=2), AF.Silu)
    MOD = alloc([DEPTH, 48, 2], F32)
    DRV = alloc([DEPTH, 4, 8, 2], F32) if False else None
    A1 = alloc([8, 2], F32); A2 = alloc([8, 2], F32)
    EPSC = alloc([1, 4], F32)[:, 0, :]
    mset(EPSC[:, 0:1], EPS)
    mset(EPSC[:, 1:2], 1.0)
    mset(EPSC[:, 2:3], 0.0)
    KAP = alloc([DEPTH, 4, 2], F32)
    NB = alloc([DEPTH, 18], F32)
    P_MARK = A.top

    def prm(l, name, i=0, w=1):
        o, _ = PC[name]
        return PRM[:, l, o + i:o + i + w]

    STG = [alloc([1, 1024], F32)[:, 0, :] for _ in range(2)]
    stg_rr = [0]

    def wload(out, src, e='gpsimd'):
        st = STG[stg_rr[0] % len(STG)]
        stg_rr[0] += 1
        shp = list(out.shape)
        np_ = shp[0]
        n = 1
        for v in shp[1:]:
            n *= v
        v_ = st[0:np_, 0:n]
        if len(shp) == 3:
            v_ = v_.rearrange('p (a b) -> p a b', b=shp[2])
        s.dma('sync', v_, src, sb_writes=[v_])
        if e == 'scalar':
            act(out, v_, AF.Copy)
        else:
            cp(out, v_, e=e)

    w1s_d = nc.dram_tensor('w1s', [DEPTH, 32, 128, 1024], BF16, kind='Internal').ap()
    w2s_d = nc.dram_tensor('w2s', [DEPTH, 32, 128, 1024], BF16, kind='Internal').ap()
    stok = {}
    cast_rr = [0]

    def wload_cached(out, src, scr, key):
        if key not in stok:
            e = ('vector', 'scalar')[cast_rr[0] % 2]
            cast_rr[0] += 1
            wload(out, src, e=e)
            dst = scr if len(out.shape) == 2 else scr.rearrange('p (k f) -> p k f', f=out.shape[2])
            stok[key] = s.dma('sync', dst, out, sb_reads=[out])
        else:
            s.wait_tok('sync', stok[key])
            s.dma('sync', out, scr if len(out.shape) == 2 else scr.rearrange('p (k f) -> p k f', f=out.shape[2]),
                  sb_writes=[out])

    bg_jobs = []
    bg_bufs = {}
    bg_rr = [0]

    def bg_setup(l):
        del bg_jobs[:]
        for f in range(32):
            bg_jobs.append((w1_d[l][:, f * 128:(f + 1) * 128].rearrange('(k p) f -> p k f', p=128), w1s_d[l, f], ('w1', l, f), 3))
        for f in range(32):
            bg_jobs.append((wm2_d[l][f * 128:(f + 1) * 128, :], w2s_d[l, f], ('w2', l, f), 2))

    bg_pend = []

    def bg_flush():
        while bg_pend:
            scr, bf, key = bg_pend.pop(0)
            stok[key] = s.dma('sync', scr, bf, sb_reads=[bf])

    def bg_alloc():
        bg_bufs['st'] = [alloc([1, 1024], F32)[:, 0, :] for _ in range(2)]
        bg_bufs['bf'] = [alloc([1, 1024], BF16)[:, 0, :] for _ in range(2)]

    def bg(n):
        for _ in range(n):
            if not bg_jobs or 'st' not in bg_bufs:
                return
            src, scr, key, nd = bg_jobs.pop(0)
            i = bg_rr[0] % 2
            bg_rr[0] += 1
            st = bg_bufs['st'][i]; bf = bg_bufs['bf'][i]
            if nd == 3:
                st = st.rearrange('p (k f) -> p k f', f=128); bf = bf.rearrange('p (k f) -> p k f', f=128)
                scr = scr.rearrange('p (k f) -> p k f', f=128)
            s.dma('sync', st, src, sb_writes=[st])
            cp(bf, st, e='gpsimd')
            bg_flush()
            bg_pend.append((scr, bf, key))

    def load_x():
        mark = A.top
        POSR = alloc([4, 32], F32); POSC = alloc([4, 64], F32)
        FRQ = alloc([1, 2], F32)[:, 0, :]
        for h in range(2):
            ts(FRQ[:, h:h + 1], PIDX, float(h * 128), None, ALU.add)
        act(FRQ, FRQ, AF.Exp, scale=-math.log(10000.0) / 256.0)
        TWO_PI = 2.0 * math.pi
        tmpi = alloc([1, 64], I32)[:, 0, :]; tmpf = alloc([1, 64], F32)[:, 0, :]; tmpa = alloc([1, 64], F32)[:, 0, :]
        for kk in range(8):
            h = kk % 2
            grp = kk // 2
            n = 32 if grp < 2 else 64
            idx = RIDX if grp < 2 else CIDX
            dst = POSR[:, kk, :] if grp < 2 else POSC[:, kk - 4, :]
            ph = 0.0 if grp % 2 == 0 else math.pi / 2
            ts(tmpa[:, 0:n], idx, FRQ[:, h:h + 1], ph, ALU.mult, ALU.add)
            ts(tmpf[:, 0:n], tmpa[:, 0:n], 1.0 / TWO_PI, None, ALU.mult)
            cp(tmpi[:, 0:n], tmpf[:, 0:n])
            cp(tmpf[:, 0:n], tmpi[:, 0:n])
            stt(tmpa[:, 0:n], tmpf[:, 0:n], -TWO_PI, tmpa[:, 0:n], ALU.mult, ALU.add)
            ts(tmpa[:, 0:n], tmpa[:, 0:n], 3.14159, -3.14159, ALU.min, ALU.max)
            act(dst, tmpa[:, 0:n], AF.Sin)
        ST = [alloc([1, D], F32)[:, 0, :] for _ in range(3)]
        for it in range(NCH):
            st = ST[it % 3]
            if it < 2:
                ld(st, ctx_d[it * 128:(it + 1) * 128, :])
            else:
                ld(st, x_d[(it - 2) * 128:(it - 1) * 128, :])
            for half in range(2):
                pb = ps(((it * 2 + half) % 4) * 512, 512)
                for q in range(4):
                    k = half * 4 + q
                    trp(pb[:, q * 128:(q + 1) * 128], st[:, k * 128:(k + 1) * 128], IDf)
                dstX = X[:, half * 4:half * 4 + 4, it * 128:(it + 1) * 128]
                if it < 2:
                    cp(dstX, pb.rearrange('p (q t) -> p q t', t=128))
                else:
                    r0 = (it - 2) * 2
                    for q in range(4):
                        k = half * 4 + q
                        o3 = X[:, k, it * 128:(it + 1) * 128].rearrange('p (r c) -> p r c', c=64)
                        i3 = pb[:, q * 128:(q + 1) * 128].rearrange('p (r c) -> p r c', c=64)
                        if k < 4:
                            pos = POSR[:, k, r0:r0 + 2].unsqueeze(2).broadcast_to([128, 2, 64])
                        else:
                            pos = POSC[:, k - 4, :].unsqueeze(1).broadcast_to([128, 2, 64])
                        tt(o3, i3, pos, ALU.add)
        A.top = mark

    CHUNKS = [(0, 256, 1)] + [(256 + 512 * i, 512, 0) for i in range(4)]
    pj_rr = [0]

    def pj_bank():
        b = pj_rr[0] % 4
        pj_rr[0] += 1
        return b * 512

    def ada(l):
        mark = A.top
        WB = [alloc([8, 128], BF16) for _ in range(4)]
        for _ in range(3):
            STG.append(alloc([1, 1024], F32)[:, 0, :])
        pm = ps(pj_bank(), 96)
        for ft in range(48):
            wb = WB[ft % 4]
            wload(wb, adaw_d[l][:, ft * 128:(ft + 1) * 128].rearrange('(k p) f -> p k f', p=128),
                  e=('vector', 'scalar')[ft % 2])
            mmg(pm[:, ft * 2:ft * 2 + 2], [(wb[:, k, :], CCb[:, k, :]) for k in range(8)])
        o, _ = PC['adab']
        tt(MOD[:, l], pm.rearrange('p (a b) -> p a b', b=2),
           PRM[:, l, o:o + 48].unsqueeze(2).broadcast_to([128, 48, 2]), ALU.add)
        del STG[2:]
        A.top = mark

    def derive(l):
        for (AA, nm, wh) in ((A1, 'n1g', 1), (A2, 'n2g', 4)):
            o, _ = PC[nm]
            g = PRM[:, l, o:o + 8].unsqueeze(2).broadcast_to([128, 8, 2])
            ts(AA, MOD[:, l, wh * 8:wh * 8 + 8, :], 1.0, None, ALU.add)
            tt(AA, AA, g, ALU.mult)
        o, _ = PC['lam']
        act(KAP[:, l, :, 0], PRM[:, l, o:o + 4], AF.Exp, scale=-1.0)
        act(KAP[:, l, :, 0], KAP[:, l, :, 0], AF.Ln, bias=EPSC[:, 1:2])
        ts(KAP[:, l, :, 1], KAP[:, l, :, 0], -16.0, None, ALU.mult)
        ts(KAP[:, l, :, 0], KAP[:, l, :, 0], -8.0, None, ALU.mult)
        o, _ = PC['b2']
        ts(NB[:, l, 0:6], PRM[:, l, o:o + 6], -1.0, None, ALU.mult)
        o, _ = PC['mgf']
        ts(NB[:, l, 6:12], PRM[:, l, o:o + 6], -1.0, None, ALU.mult)
        o, _ = PC['lgb']
        ts(NB[:, l, 12:18], PRM[:, l, o:o + 6], 1.0, None, ALU.mult)

    NML = int(dbg[0][2:]) if (dbg and dbg[0].startswith('nm')) else 9

    def norm_mod(dst, t0, n, s_, AA, BBl, BBwh, SQ, RS, TMP):
        for k in range(8):
            act(SQ[:, k, 0:n], X[:, k, t0:t0 + n], AF.Square)
        if NML < 2:
            return
        pb = ps(pj_bank(), n)
        mmg(pb, [(ONESb, SQ[:, k, 0:n]) for k in range(8)])
        if NML < 3:
            return
        act(RS[:, 0:n], pb, AF.Ln, bias=EPSC[:, 0:1], scale=1.0 / D)
        act(RS[:, 0:n], RS[:, 0:n], AF.Exp, scale=-0.5)
        if NML < 4:
            return
        TMl = TMP if isinstance(TMP, list) else [TMP]
        for k in range(8):
            TMP = TMl[k % len(TMl)]
            stt(TMP[:, 0:n], X[:, k, t0:t0 + n], AA[:, k, s_:s_ + 1], RS[:, 0:n], ALU.mult, ALU.mult)
            if NML < 5:
                continue
            act(dst[:, k, 0:n], TMP[:, 0:n], AF.Identity, bias=MOD[:, BBl, BBwh * 8 + k, s_:s_ + 1])

    def proj(HT, W, M, evac, chunks=CHUNKS):
        for (t0, n, s_) in chunks:
            pb = ps(pj_bank(), n, np_=M)
            mmg(pb, [(W[:, k, 0:M], HT[:, k, t0:t0 + n]) for k in range(8)])
            evac(pb, t0, n, s_)

    def finish_group(l, g8, OUT, G, Wo, last, SQ, Y, RS):
        mark = A.top
        act(SQ, OUT, AF.Square)
        RSl = RS if isinstance(RS, list) else [RS]
        for ic, (t0, n, s_) in enumerate(CHUNKS):
            RS = RSl[ic % len(RSl)]
            pb = ps(pj_bank(), n)
            mmg(pb, [(BLK64, SQ[:, t0:t0 + n])])
            act(RS[:, 0:n], pb, AF.Ln, bias=EPSC[:, 0:1], scale=1.0 / 64)
            act(RS[:, 0:n], RS[:, 0:n], AF.Exp, scale=-0.5)
            tt(RS[:, 0:n], RS[:, 0:n], OUT[:, t0:t0 + n], ALU.mult)
            stt(Y[:, t0:t0 + n], RS[:, 0:n], prm(l, 'hng', g8), G[:, t0:t0 + n], ALU.mult, ALU.mult)
        for (t0, n, s_) in CHUNKS:
            if last and s_ == 1:
                continue
            for k in range(8):
                pb = ps(pj_bank(), n)
                mmg(pb, [(Wo[:, k * 128:(k + 1) * 128], Y[:, t0:t0 + n])])
                stt(X[:, k, t0:t0 + n], pb, MOD[:, l, 2 * 8 + k, s_:s_ + 1], X[:, k, t0:t0 + n], ALU.mult, ALU.add)
        A.top = mark

    def group_a(l, a, HT, last):
        mark = A.top
        Wx = alloc([8, 128], BF16); wx_off = A.last
        Wg = alloc([8, 128], BF16)
        Wo = alloc([1, D], BF16, at=wx_off)[:, 0, :]
        GW = alloc([4, 128], BF16)
        wload(Wx, win_d[l][:, a * 128:(a + 1) * 128].rearrange('(k p) f -> p k f', p=128))
        wload(Wg, win_d[l][:, 256 + a * 128:256 + (a + 1) * 128].rearrange('(k p) f -> p k f', p=128))
        wload(GW, gatew_d[l][:, a * 512:(a + 1) * 512].rearrange('p (i c) -> p i c', c=128))
        XA = alloc([1, T], F32)[:, 0, :]
        H2 = XA
        XCv = alloc([1, T], F32)[:, 0, :]
        XCb = alloc([1, T], BF16)[:, 0, :]
        G = alloc([1, T], BF16)[:, 0, :]
        AB = alloc([1, T], F32)[:, 0, :]; ab_off = A.last
        BBf = alloc([1, T], F32)[:, 0, :]; bb_off = A.last
        H = alloc([1, T], F32)[:, 0, :]
        T1 = alloc([1, 512], F32)[:, 0, :]; T2 = alloc([1, 512], F32)[:, 0, :]
        T1b = alloc([1, 512], F32)[:, 0, :]
        bg_alloc()
        bg(2)
        proj(HT, Wx, 128, lambda pb, t0, n, s_: (cp(XA[:, t0:t0 + n], pb), bg(1)))
        wload(Wo, wout_d[l][a * 128:(a + 1) * 128, :])
        ga_rr = [0]

        def gelu_ev(pb, t0, n, s_):
            ga_rr[0] += 1
            T1 = (T1b, T2)[ga_rr[0] % 2]
            bg(1)
            act(T1[:, 0:n], pb, AF.Square)
            ts(T1[:, 0:n], T1[:, 0:n], 0.044715, 1.0, ALU.mult, ALU.add)
            tt(T1[:, 0:n], T1[:, 0:n], pb, ALU.mult)
            act(T1[:, 0:n], T1[:, 0:n], AF.Sigmoid, scale=2.0 * math.sqrt(2.0 / math.pi))
            tt(G[:, t0:t0 + n], T1[:, 0:n], pb, ALU.mult)
        proj(HT, Wg, 128, gelu_ev)
        o, _ = PC['convw']
        cw = lambda j: PRM[:, l, o + a * 4 + j:o + a * 4 + j + 1]
        for (s0, n) in ((0, TC), (TC, TL)):
            act(XCv[:, s0:s0 + n], XA[:, s0:s0 + n], AF.Identity, bias=prm(l, 'convb', a), scale=cw(2))
            for j, sh in ((0, -2), (1, -1), (3, 1)):
                lo = max(0, -sh); hi = n - max(0, sh)
                stt(XCv[:, s0 + lo:s0 + hi], XA[:, s0 + lo + sh:s0 + hi + sh], cw(j), XCv[:, s0 + lo:s0 + hi],
                    ALU.mult, ALU.add)
        act(XCb, XCv, AF.Copy)
        T1A, T2A = T1, T2
        bg(2)
        for d in range(2):
            kap = KAP[:, l, a * 2 + d, 0:1]; kap2 = KAP[:, l, a * 2 + d, 1:2]
            o, _ = PC['lgb']
            br = PRM[:, l, o + a * 4 + d * 2:o + a * 4 + d * 2 + 1]
            bi = PRM[:, l, o + a * 4 + d * 2 + 1:o + a * 4 + d * 2 + 2]
            for ic, (t0, n, s_) in enumerate(CHUNKS):
                T1 = T1A if ic % 2 == 0 else T1b
                bg(2)
                pr = ps(pj_bank(), n)
                mmg(pr, [(GW[:, d * 2 + 0, :], XCb[:, t0:t0 + n])])
                pi = ps(pj_bank(), n)
                mmg(pi, [(GW[:, d * 2 + 1, :], XCb[:, t0:t0 + n])])
                act(T1[:, 0:n], pr, AF.Sigmoid, bias=br)
                act(AB[:, t0:t0 + n], T1[:, 0:n], AF.Exp, scale=kap)
                act(T1[:, 0:n], T1[:, 0:n], AF.Exp, scale=kap2)
                act(T1[:, 0:n], T1[:, 0:n], AF.Sqrt, bias=EPSC[:, 1:2], scale=-1.0)
                act(T2[:, 0:n], pi, AF.Sigmoid, bias=bi)
                tt(T1[:, 0:n], T1[:, 0:n], T2[:, 0:n], ALU.mult)
                tt(BBf[:, t0:t0 + n], T1[:, 0:n], XCv[:, t0:t0 + n], ALU.mult)
            if d == 0:
                scan(H, AB, BBf, 0.0)
            else:
                scan(H2[:, 0:TC][:, ::-1], AB[:, 0:TC][:, ::-1], BBf[:, 0:TC][:, ::-1], 0.0)
                scan(H2[:, TC:T][:, ::-1], AB[:, TC:T][:, ::-1], BBf[:, TC:T][:, ::-1], H2[:, 0:1])
        tt(H, H, H2, ALU.add)
        bg(6)
        finish_group(l, a, H, G, Wo, last, alloc([1, T], BF16, at=ab_off)[:, 0, :],
                     alloc([1, T], BF16, at=bb_off)[:, 0, :], [T1A, T1b])
        if a == 1:
            bg(64)
        bg_flush()
        bg_bufs.clear()
        A.top = mark

    class Stop(Exception):
        pass
    PLIM = int(dbg[0][1:]) if (dbg and dbg[0][0] == 'p') else 99

    def cut(n):
        if PLIM == n:
            raise Stop()

    def group_pair(l, kind, j, HT, LRT, last, G24=None):
        mark = A.top
        g8 = 2 + kind * 3 + j
        cq = (512 if kind == 0 else 2064) + j * 128
        ck = (896 if kind == 0 else 2448) + j * 128
        cv = (1280 if kind == 0 else 2832) + j * 128
        cg = (1664 if kind == 0 else 3216) + j * 128
        WB2 = [alloc([8, 128], BF16) for _ in range(2)]
        wb_rr = [0]

        def wcol(c0, src=None):
            Wb = WB2[wb_rr[0] % 2]
            wb_rr[0] += 1
            wload(Wb, (win_d[l] if src is None else src)[:, c0:c0 + 128].rearrange('(k p) f -> p k f', p=128), e='scalar')
            return Wb
        def gate_bc(idx, evac):
            Wb = WB2[wb_rr[0] % 2]
            wb_rr[0] += 1
            SEL = Wb[:, 0, :]
            wload(SEL, sel_d[:, idx * 128:(idx + 1) * 128], e='scalar')
            for (t0, n, s_) in CHUNKS:
                pb = ps(pj_bank(), n)
                mmg(pb, [(SEL[64:120, :], G24[64:120, t0:t0 + n])])
                evac(pb, t0, n, s_)
        Wo = alloc([1, D], BF16)[:, 0, :]
        wload(Wo, wout_d[l][g8 * 128:(g8 + 1) * 128, :])
        if kind == 0:
            W2 = alloc([2, 128], BF16, np_=16)
            for d in range(2):
                wload(W2[:, d, :], w2_d[l][d][:, j * 128:(j + 1) * 128])
        Qb = alloc([1, T], BF16)[:, 0, :]; qb_off = A.last
        Kb = alloc([1, T], BF16)[:, 0, :]; kb_off = A.last
        V1 = alloc([NCH, 130], BF16); VA = alloc([NCH, 128], BF16); VB = alloc([NCH, 128], BF16)
        OUT = alloc([1, T], F32)[:, 0, :]
        PL = alloc([1, T], F32)[:, 0, :]; pl_off = A.last
        QDs = [alloc([1, T], BF16)[:, 0, :] for _ in range(2)]
        KDs = [alloc([1, T], BF16)[:, 0, :] for _ in range(2)]
        DECs = [alloc([1, NCH], F32)[:, 0, :] for _ in range(2)]
        Rsts = [alloc([1, 130], F32)[:, 0, :] for _ in range(2)]
        Sbs = [alloc([2, 128], BF16) for _ in range(2)]
        nSbs = [alloc([2, 128], BF16) for _ in range(2)] if kind == 1 else [None, None]
        T1s = []
        T1bs = []
        for _ in range(2):
            T1s.append(alloc([1, 512], F32)[:, 0, :])
            T1bs.append(alloc([1, 512], BF16, at=A.last)[:, 0, :])
        SbFs = [alloc([2, 130], BF16) for _ in range(2)]
        T2s = [alloc([2, 128], F32) for _ in range(2)] if kind == 1 else [None, None]
        SCbs = [alloc([2, 256], BF16) for _ in range(2)]
        t1_rr = [0]

        def T1n():
            t1_rr[0] += 1
            return T1s[t1_rr[0] % 2]

        def T1bn():
            t1_rr[0] += 1
            return T1bs[t1_rr[0] % 2]
        proj(HT, wcol(cq), 128, lambda pb, t0, n, s_: act(Qb[:, t0:t0 + n], pb, AF.Copy, scale=0.125))
        proj(HT, wcol(ck), 128, lambda pb, t0, n, s_: cp(Kb[:, t0:t0 + n], pb))
        cut(1)
        mset(VA[:, :, 64:128], 0.0); mset(VB[:, :, 0:64], 0.0); mset(V1[:, :, 128:130], 1.0)
        Wv = wcol(cv)
        for cb in range(0, NCH, 4):
            nb4 = min(4, NCH - cb)
            pb = ps(pj_bank(), 512)
            for q in range(nb4):
                c = cb + q
                mmg(pb[:, q * 128:(q + 1) * 128], [(HT[:, k, c * 128:(c + 1) * 128], Wv[:, k, :]) for k in range(8)])
            cp(V1[:, cb:cb + nb4, 0:128], pb[:, 0:nb4 * 128].rearrange('p (q t) -> p q t', t=128))
        cp(VA[:, :, 0:64], V1[:, :, 0:64], e='gpsimd')
        cp(VB[:, :, 64:128], V1[:, :, 64:128], e='gpsimd')
        cut(2)
        sc_ = (1.0 / 16.0) if kind == 0 else 1.0
        for d in range(2):
            QD = QDs[d]; KD = KDs[d]; DEC = DECs[d]
            nb = NB[:, l, (0 if kind == 0 else 6) + j * 2 + d:(0 if kind == 0 else 6) + j * 2 + d + 1]

            def softplus_ev(pb, t0, n, s_):
                t1 = T1n()
                act(t1[:, 0:n], pb, AF.Exp, bias=nb, scale=-1.0)
                act(PL[:, t0:t0 + n], t1[:, 0:n], AF.Ln, bias=EPSC[:, 1:2])
            if kind == 0:
                for (t0, n, s_) in CHUNKS:
                    pb = ps(pj_bank(), n)
                    mmg(pb, [(W2[:, d, :], LRT[:, t0:t0 + n])])
                    softplus_ev(pb, t0, n, s_)
            else:
                gate_bc((j * 2 + d) * 2 + 1, softplus_ev)
            for c in range(NCH):
                plc = PL[:, c * 128:(c + 1) * 128]
                if d == 0:
                    scan(plc, ONESb, plc, 0.0)
                else:
                    scan(plc[:, ::-1], ONESb, plc[:, ::-1], 0.0)
            Pv = PL.rearrange('p (c t) -> p c t', t=128)
            pend = Pv[:, :, 127] if d == 0 else Pv[:, :, 0]
            act(DEC, pend, AF.Exp, scale=-sc_)
            if kind == 0:
                for (t0, n, s_) in CHUNKS:
                    t1 = T1bn()
                    act(t1[:, 0:n], PL[:, t0:t0 + n], AF.Exp, scale=-sc_)
                    tt(QD[:, t0:t0 + n], t1[:, 0:n], Qb[:, t0:t0 + n], ALU.mult)
                    t1 = T1bn()
                    act(t1[:, 0:n], PL[:, t0:t0 + n], AF.Exp, scale=sc_)
                    tt(KD[:, t0:t0 + n], t1[:, 0:n], Kb[:, t0:t0 + n], ALU.mult)
            else:
                for (t0, n, s_) in CHUNKS:
                    t1 = T1bn()
                    act(t1[:, 0:n], PL[:, t0:t0 + n], AF.Exp, scale=-1.0)
                    tt(QD[:, t0:t0 + n], t1[:, 0:n], Qb[:, t0:t0 + n], ALU.mult)
                bi_ = prm(l, 'mgi', j * 2 + d)

                def ig_ev(pb, t0, n, s_, KD=KD):
                    t1 = T1n()
                    tt(t1[:, 0:n], pb, PL[:, t0:t0 + n], ALU.add)
                    act(t1[:, 0:n], t1[:, 0:n], AF.Exp, bias=bi_)
                    tt(KD[:, t0:t0 + n], t1[:, 0:n], Kb[:, t0:t0 + n], ALU.mult)
                gate_bc((j * 2 + d) * 2 + 0, ig_ev)
        cut(3)
        G = alloc([1, T], BF16, at=pl_off)[:, 0, :]
        Wg = wcol(cg)
        if kind == 0:
            proj(HT, Wg, 128, lambda pb, t0, n, s_: act(G[:, t0:t0 + n], pb, AF.Silu))
        else:
            proj(HT, Wg, 128, lambda pb, t0, n, s_: act(G[:, t0:t0 + n], pb, AF.Sigmoid))
        KTs = [alloc([NCH, 128], BF16, at=qb_off), alloc([NCH, 128], BF16, at=kb_off)]
        for d in range(2):
            for cb in range(0, NCH, 4):
                nb4 = min(4, NCH - cb)
                pt = ps(pj_bank(), 512, dt=BF16)
                for q in range(nb4):
                    c = cb + q
                    trp(pt[:, q * 128:(q + 1) * 128], KDs[d][:, c * 128:(c + 1) * 128], IDb)
                src = pt[:, 0:nb4 * 128].rearrange('p (q t) -> p q t', t=128)
                if d == 0:
                    act(KTs[d][:, cb:cb + nb4, :], src, AF.Copy)
                else:
                    cp(KTs[d][:, cb:cb + nb4, :], src)
        cut(4)
        orders = [list(range(NCH)), [1, 0] + list(range(NCH - 1, 1, -1))]
        MASKS = [MASKF, MASKB]
        for d in range(2):
            mset(Sbs[d], 0.0)
            if kind == 1:
                mset(nSbs[d], 0.0)
        pj2 = [0]

        def pjb2():
            pj2[0] += 1
            return (pj2[0] % 2) * 512
        pus = {}

        def stageA(d, i):
            c = orders[d][i]
            tsl = slice(c * 128, (c + 1) * 128)
            par = i % 2
            QD = QDs[d]; KD = KDs[d]
            pscA = ps(2048, 128)
            pscB = ps(2560, 128)
            mmg(pscA, [(KD[0:64, tsl], QD[0:64, tsl])])
            mmg(pscB, [(KD[64:128, tsl], QD[64:128, tsl])])
            psc2 = psum[:, 2048:3072].rearrange('p (h t) -> p h t', t=512)[:, :, 0:128]
            tt(SCbs[d][:, par, :].rearrange('p (h t) -> p h t', t=128), psc2,
               MASKS[d].unsqueeze(1).broadcast_to([128, 2, 128]), ALU.mult)
            npu = 130 if kind == 1 else 128
            pu = ps((6 + d) * 512 + par * 256, npu)
            mmg(pu, [(KTs[d][:, c, :], V1[:, c, 0:npu])])
            pus[(d, i)] = pu

        def stageB1(d, i):
            c = orders[d][i]
            tsl = slice(c * 128, (c + 1) * 128)
            par = i % 2
            QD = QDs[d]; DEC = DECs[d]; Rst = Rsts[d]; Sb = Sbs[d]; nSb = nSbs[d]; SCb = SCbs[d]
            pu = pus.pop((d, i))
            npu = 130 if kind == 1 else 128
            Rst = Rst[:, 0:npu]
            if i == 0:
                cp(Rst, pu)
            else:
                cprev = orders[d][i - 1]
                stt(Rst, Rst, DEC[:, cprev:cprev + 1], pu, ALU.mult, ALU.add)
            if i + 1 < NCH:
                nx = (i + 1) % 2
                if kind == 0:
                    sbf = SbFs[d][:, nx, 0:128]
                    act(sbf, Rst, AF.Copy, scale=DEC[:, c:c + 1])
                    tt(Sb[:, nx, :], sbf[:, 0:128], BLK64, ALU.mult, e='gpsimd')
                else:
                    stt(Sb[:, nx, :], Rst[:, 0:128], DEC[:, c:c + 1], BLK64, ALU.mult, ALU.mult)
                    ts(nSb[:, nx, :], BLK64, Rst[:, 128:129], DEC[:, c:c + 1], ALU.mult, ALU.mult, e='gpsimd')
            pob = (d * 2 + par) * 512
            po = ps(pob, 128)
            mmg(po, [(Sb[:, i % 2, :], QD[:, tsl]), (VA[:, c, :], SCb[:, par, 0:128]),
                     (VB[:, c, :], SCb[:, par, 128:256])])
            if kind == 1:
                pd = ps(pob + 128, 128)
                mmg(pd, [(nSb[:, i % 2, :], QD[:, tsl]), (EAb, SCb[:, par, 0:128]), (EBb, SCb[:, par, 128:256])])

        def stageB2a(d, i):
            if kind == 0:
                return
            par = i % 2
            pd = ps((d * 2 + par) * 512 + 128, 128)
            T2 = T2s[d][:, par, :]
            act(T2, pd, AF.Abs)
            act(T2, T2, AF.Ln, bias=EPSC[:, 0:1])
            act(T2, T2, AF.Exp, scale=-1.0)

        def stageB2b(d, i):
            c = orders[d][i]
            tsl = slice(c * 128, (c + 1) * 128)
            par = i % 2
            po = ps((d * 2 + par) * 512, 128)
            if kind == 0:
                tt(OUT[:, tsl], po, OUT[:, tsl], ALU.add)
            else:
                T2 = T2s[d][:, par, :]
                stt(T2, T2, 1.0, po, ALU.min, ALU.mult)
                tt(OUT[:, tsl], OUT[:, tsl], T2, ALU.add, e='gpsimd')
        mset(OUT, 0.0, e='gpsimd')
        stageA(0, 0); stageA(1, 0)
        for i in range(NCH):
            if i + 1 < NCH:
                stageA(0, i + 1); stageA(1, i + 1)
            stageB1(0, i); stageB1(1, i)
            stageB2a(0, i); stageB2a(1, i)
            if i >= 1:
                stageB2b(0, i - 1); stageB2b(1, i - 1)
        stageB2b(0, NCH - 1); stageB2b(1, NCH - 1)
        cut(5)
        cut(6)
        finish_group(l, g8, OUT, G, Wo, last, KDs[0], alloc([1, T], BF16, at=qb_off)[:, 0, :], [T1s[0], T1s[1]])
        A.top = mark

    def mlp(l, last):
        mark = A.top
        NW = 6
        W1 = [alloc([8, 128], BF16) for _ in range(NW)]
        W2b = [alloc([1, D], BF16)[:, 0, :] for _ in range(NW)]
        H2 = alloc([8, 1024], BF16)
        A1T = alloc([32, 1024], BF16); a1t_off = A.last
        SQ = alloc([8, 512], BF16, at=a1t_off); RS = alloc([1, 512], F32)[:, 0, :]; TMP = alloc([1, 512], F32)[:, 0, :]
        TM3 = [alloc([1, 512], F32)[:, 0, :] for _ in range(3)]
        tm_rr = [0]
        passes = [(256, 1024, 0), (1280, 1024, 0)]
        if not last:
            passes = [(0, 256, 1)] + passes
        for (p0, pn, s_) in passes:
            subs = [(p0 + i, min(512, pn - i)) for i in range(0, pn, 512)]
            for (t0, n) in subs:
                norm_mod(H2[:, :, t0 - p0:t0 - p0 + n], t0, n, s_, A2, l, 3, SQ, RS, TM3)
            for f in range(32):
                w1 = W1[f % NW]
                wload_cached(w1, w1_d[l][:, f * 128:(f + 1) * 128].rearrange('(k p) f -> p k f', p=128),
                             w1s_d[l, f], ('w1', l, f))
                for (t0, n) in subs:
                    pb = ps(pj_bank(), n)
                    mmg(pb, [(w1[:, k, :], H2[:, k, t0 - p0:t0 - p0 + n]) for k in range(8)])
                    tm_rr[0] += 1
                    tm = TM3[tm_rr[0] % 3]
                    act(tm[:, 0:n], pb, AF.Relu)
                    tt(A1T[:, f, t0 - p0:t0 - p0 + n], tm[:, 0:n], tm[:, 0:n], ALU.mult)
            for (t0, n) in subs:
                for f in range(32):
                    w2 = W2b[f % NW]
                    wload_cached(w2, wm2_d[l][f * 128:(f + 1) * 128, :], w2s_d[l, f], ('w2', l, f))
                    for k in range(8):
                        pb = ps(k * 512, n)
                        s.op('tensor', (lambda en, pb=pb, w=w2[:, k * 128:(k + 1) * 128],
                                        r=A1T[:, f, t0 - p0:t0 - p0 + n], st_=(f == 0), sp_=(f == 31):
                                        en.matmul(pb, lhsT=w, rhs=r, start=st_, stop=sp_)),
                             reads=[w2[:, k * 128:(k + 1) * 128], A1T[:, f, t0 - p0:t0 - p0 + n]], writes=[pb])
                for k in range(8):
                    pb = ps(k * 512, n)
                    stt(X[:, k, t0:t0 + n], pb, MOD[:, l, 5 * 8 + k, s_:s_ + 1], X[:, k, t0:t0 + n], ALU.mult, ALU.add)
        del STG[2:]
        A.top = mark

    def final():
        mark = A.top
        FG = alloc([1, D], F32)[:, 0, :]
        ld(FG, fg_d)
        OT = [alloc([1, D], F32)[:, 0, :] for _ in range(2)]
        SS = alloc([1, 4], F32)[:, 0, :]
        JK = alloc([1, D], F32)[:, 0, :]
        toks = []
        for it in range(TL // 128):
            ot = OT[it % 2]
            t0 = TC + it * 128
            for half in range(2):
                pb = ps(((it * 2 + half) % 4) * 512, 512)
                for q in range(4):
                    k = half * 4 + q
                    trp(pb[:, q * 128:(q + 1) * 128], X[:, k, t0:t0 + 128], IDf)
                cp(ot[:, half * 512:(half + 1) * 512], pb)
            c0 = it % 2
            s.op('scalar', lambda en, ot=ot, c0=c0: en.activation(out=JK, in_=ot, func=AF.Square, accum_out=SS[:, c0:c0 + 1]),
                 reads=[ot], writes=[JK, SS[:, c0:c0 + 1]])
            act(SS[:, 2 + c0:3 + c0], SS[:, c0:c0 + 1], AF.Ln, bias=EPSC[:, 0:1], scale=1.0 / D)
            act(SS[:, 2 + c0:3 + c0], SS[:, 2 + c0:3 + c0], AF.Exp, scale=-0.5)
            stt(ot, ot, SS[:, 2 + c0:3 + c0], FG, ALU.mult, ALU.mult)
            toks.append(s.dma('sync', out_d[it * 128:(it + 1) * 128, :], ot, sb_reads=[ot]))
        for tk in toks:
            s.wait_tok('sync', tk)
        A.top = mark

    def dump(ap, ncol):
        mark = A.top
        DB = alloc([1, dbg[1]], F32)[:, 0, :]
        mset(DB, 0.0)
        cp(DB[:, 0:ncol], ap)
        tk = s.dma('sync', dbg_d, DB, sb_reads=[DB])
        s.wait_tok('sync', tk)
        A.top = mark

    load_x()
    stage = dbg[0] if dbg else None
    if stage == 'x':
        dump(X[:, int(dbg[2]), :], T)
        n_layers = 0
    for l in range(n_layers):
        last = (l == DEPTH - 1)
        ada(l)
        if stage == 'ada':
            dump(MOD[:, l].rearrange('p a b -> p (a b)'), 96)
            break
        derive(l)
        if stage == 'drv':
            dump(A1.rearrange('p a b -> p (a b)'), 16)
            break
        mark_l = A.top
        HT = alloc([8, T], BF16)
        SQs = [alloc([8, 512], BF16) for _ in range(2)]
        RSs = [alloc([1, 512], F32)[:, 0, :] for _ in range(2)]
        TMPs = [alloc([1, 512], F32)[:, 0, :] for _ in range(3)]
        TMP = TMPs[0]
        for ic, (t0, n, s_) in enumerate(CHUNKS):
            norm_mod(HT[:, :, t0:t0 + n], t0, n, s_, A1, l, 0, SQs[ic % 2], RSs[ic % 2], TMPs)
        A.top = mark_l + 8 * T * 2
        if stage and (stage == 'h%d' % l or stage.startswith('nm')):
            mark = A.top
            HF = alloc([1, T], F32)[:, 0, :]
            cp(HF, HT[:, 0, :])
            dump(HF, T)
            A.top = mark
            break
        LRB = alloc([1, T], BF16)[:, 0, :]
        LRT = LRB[0:16]
        markw = A.top
        Wl = alloc([8, 16], BF16)
        wload(Wl, win_d[l][:, 2048:2064].rearrange('(k p) f -> p k f', p=128))
        proj(HT, Wl, 16, lambda pb, t0, n, s_: cp(LRT[:, t0:t0 + n], pb))
        W24 = alloc([8, 120], BF16)
        mset(LRB[64:128, :], 0.0)
        wload(W24, wg24_d[l].rearrange('(k p) f -> p k f', p=128), e='scalar')

        def g24_ev(pb, t0, n, s_):
            act(LRB[64:88, t0:t0 + n], pb[64:88, :], AF.Copy)
            act(LRB[96:120, t0:t0 + n], pb[96:120, :], AF.Copy)
            tt(LRB[96:120, t0:t0 + n], pb[96:120, :], LRB[96:120, t0:t0 + n], ALU.subtract)
        proj(HT, W24, 120, g24_ev)
        G24 = LRB
        A.top = markw
        bg_setup(l)
        groups = [('a', 0), ('a', 1)] + [('p', 0, j) for j in range(3)] + [('p', 1, j) for j in range(3)]
        if stage and stage.startswith('g'):
            groups = groups[:int(stage[1:]) + 1]
        if stage and stage[0] == 'p':
            groups = [('p', int(dbg[2]), 0)]
        try:
            for g in groups:
                if g[0] == 'a':
                    group_a(l, g[1], HT, last)
                else:
                    group_pair(l, g[1], g[2], HT, LRT, last, G24)
        except Stop:
            pass
        A.top = mark_l
        if stage and (stage.startswith('g') or stage[0] == 'p'):
            dump(X[:, int(dbg[2]), :], T)
            break
        mlp(l, last)
        if stage == 'l%d' % l:
            dump(X[:, int(dbg[2]), :], T)
            break
    if not stage:
        final()
    s.emit()
    print('ninst', s.ninst, 'arena peak', A.peak, 'persistent', P_MARK)
    return nc


_NC = {}


def prep_inputs(inp):
    inp = {k: np.asarray(v) for k, v in inp.items()}
    f = lambda a: np.ascontiguousarray(a, dtype=np.float32)
    shared = {
        'ada_w': f(inp['ada_w']), 'w_in': f(inp['w_in']),
        'wg24': f(np.stack([pack_wg24(inp, l) for l in range(DEPTH)])), 'sel': make_sel(),
        'gatew': f(np.stack([pack_gatew(inp, l) for l in range(DEPTH)])),
        'gla_w2': f(inp['gla_w2']), 'w_out': f(inp['w_out']), 'mlp_w1': f(inp['mlp_w1']), 'mlp_w2': f(inp['mlp_w2']),
        'prm': f(np.stack([pack_params(inp, l) for l in range(DEPTH)])),
        'fg': f(np.broadcast_to(inp['final_g'][None, :], (128, D))), 'cst': make_consts(),
    }
    maps = []
    for b in range(8):
        cc = np.zeros((128, 16), np.float32)
        cc[:, 0::2] = inp['c'][b].reshape(8, 128).T
        cc[:, 1::2] = inp['c_ctx'].reshape(8, 128).T
        m = dict(shared)
        m['x'] = f(inp['x'][b]); m['ctx'] = f(inp['ctx'][b]); m['cc'] = cc
        maps.append(m)
    return maps


def kernel(**inputs):
    if 'nc' not in _NC:
        _NC['nc'] = build()
    maps = prep_inputs(inputs)
    res = run_bass_kernel_spmd(_NC['nc'], maps, core_ids=list(range(8)))
    return np.stack([np.asarray(res.results[b]['out'], dtype=np.float32) for b in range(8)], axis=0)
```

```python
import math
import numpy as np
import ml_dtypes
import concourse.bass as bass
import concourse.mybir as mybir
from concourse.bass_utils import run_bass_kernel_spmd

F32 = mybir.dt.float32
BF16 = mybir.dt.bfloat16
I32 = mybir.dt.int32
U8 = mybir.dt.uint8
AF = mybir.ActivationFunctionType
ALU = mybir.AluOpType

D = 1024
TL = 2048
TC = 256
T = TL + TC
NCH = T // 128
DEPTH = 2
DIN = 3624
DFF = 4096
EPS = 1e-6
ENG = ['tensor', 'vector', 'scalar', 'gpsimd', 'sync']
DSZ = {F32: 4, BF16: 2, I32: 4, U8: 1}


def _dsz(dt):
    for k, v in DSZ.items():
        if k == dt:
            return v
    return 4


def region(ap):
    es = _dsz(ap.dtype)
    aps = ap.ap
    pstep, pcnt = aps[0]
    off = ap.offset
    if pstep == 0:
        pstep = 1 << 40
    p0 = off // pstep
    fo = off % pstep
    lo = fo
    hi = fo
    for st, c in aps[1:]:
        d = st * (c - 1)
        if d < 0:
            lo += d
        else:
            hi += d
    if ap.tensor.name == 'psum':
        return ('psum', 0, 128, (lo * es) // 2048 * 2048, ((hi + 1) * es + 2047) // 2048 * 2048)
    return (ap.tensor.name, p0, p0 + pcnt, lo * es, (hi + 1) * es)


class Sched:
    def __init__(self, nc, n_dma_sems=32):
        self.nc = nc
        self.sem = {}
        for e in ENG:
            self.sem[e] = nc.semaphore('s_' + e).__enter__()
        self.dma_sems = [nc.semaphore('s_dma%d' % i).__enter__() for i in range(n_dma_sems)]
        self.dma_cnt = [0] * n_dma_sems
        self.dma_next = 0
        self.n_pool = n_dma_sems
        self.cnt = {e: 0 for e in ENG}
        self.seen = {e: {} for e in ENG}
        self.q = {e: [] for e in ENG}
        self.acc = {}
        self.ninst = 0

    def _deps(self, reads, writes, e=None):
        deps = {}
        for ap in reads:
            name, p0, p1, f0, f1 = region(ap)
            ps_ = (name == 'psum')
            for r in self.acc.get(name, ()):
                if (r[4] or (ps_ and r[5][0] != e)) and r[0] < p1 and p0 < r[1] and r[2] < f1 and f0 < r[3]:
                    k, v = r[5]
                    if deps.get(k, 0) < v:
                        deps[k] = v
        for ap in writes:
            name, p0, p1, f0, f1 = region(ap)
            for r in self.acc.get(name, ()):
                if r[0] < p1 and p0 < r[1] and r[2] < f1 and f0 < r[3]:
                    k, v = r[5]
                    if deps.get(k, 0) < v:
                        deps[k] = v
        return deps

    def _record(self, reads, writes, tok):
        for ap in writes:
            name, p0, p1, f0, f1 = region(ap)
            lst = self.acc.setdefault(name, [])
            lst[:] = [r for r in lst if not (p0 <= r[0] and r[1] <= p1 and f0 <= r[2] and r[3] <= f1)]
            lst.append((p0, p1, f0, f1, True, tok))
        for ap in reads:
            name, p0, p1, f0, f1 = region(ap)
            lst = self.acc.setdefault(name, [])
            lst[:] = [r for r in lst if not ((not r[4]) and r[5][0] == tok[0] and p0 <= r[0] and r[1] <= p1
                                             and f0 <= r[2] and r[3] <= f1)]
            lst.append((p0, p1, f0, f1, False, tok))

    def _waits(self, e, deps):
        out = []
        seen = self.seen[e]
        for k, v in deps.items():
            if seen.get(k, 0) >= v:
                continue
            seen[k] = v
            if k == e and e == 'tensor':
                continue
            out.append((k, v))
        return out

    def _semof(self, k):
        return self.sem[k] if isinstance(k, str) else self.dma_sems[k[1]]

    def op(self, e, fn, reads=(), writes=()):
        reads = [r for r in reads if r is not None and not isinstance(r, (int, float))]
        deps = self._deps(reads, writes, e)
        waits = self._waits(e, deps)
        self.cnt[e] += 1
        tok = (e, self.cnt[e])
        self._record(reads, writes, tok)
        self.q[e].append((waits, fn, self.sem[e], 1))
        self.ninst += 1
        return tok

    def dma(self, qe, out, in_, sb_reads=(), sb_writes=(), **kw):
        deps = self._deps(sb_reads, sb_writes)
        i = self.dma_next
        self.dma_next = (self.dma_next + 1) % self.n_pool
        if self.dma_cnt[i] > 0:
            k = ('dma', i)
            if deps.get(k, 0) < self.dma_cnt[i] * 16:
                deps[k] = self.dma_cnt[i] * 16
        waits = self._waits(qe, deps)
        self.dma_cnt[i] += 1
        tok = (('dma', i), self.dma_cnt[i] * 16)
        self._record(sb_reads, sb_writes, tok)
        self.q[qe].append((waits, (lambda eng, out=out, in_=in_, kw=kw: eng.dma_start(out=out, in_=in_, **kw)),
                           self.dma_sems[i], 16))
        self.ninst += 1
        return tok

    def dma_fresh(self, qe, out, in_, **kw):
        sem = self.nc.semaphore('s_f%d' % len(self.dma_sems)).__enter__()
        self.dma_sems.append(sem)
        self.dma_cnt.append(1)
        i = len(self.dma_sems) - 1
        tok = (('dma', i), 16)
        self.q[qe].append(([], (lambda eng, out=out, in_=in_, kw=kw: eng.dma_start(out=out, in_=in_, **kw)), sem, 16))
        self.ninst += 1
        return tok

    def wait_tok(self, e, tok):
        waits = self._waits(e, {tok[0]: tok[1]})
        if waits:
            self.q[e].append((waits, None, None, 0))

    def emit(self):
        nc = self.nc
        with nc.Block() as block:
            def mk(e):
                def body(eng):
                    for waits, fn, sem, inc in self.q[e]:
                        for k, v in waits:
                            eng.wait_ge(self._semof(k), v)
                        if fn is not None:
                            fn(eng).then_inc(sem, inc)
                return body
            block.tensor(mk('tensor'))
            block.vector(mk('vector'))
            block.scalar(mk('scalar'))
            block.gpsimd(mk('gpsimd'))
            block.sync(mk('sync'))


NCST = 128 * 7 + 1 + 32 + 64


def make_consts():
    p = np.arange(128)
    c = np.zeros((128, NCST), np.float32)
    o = 0
    c[:, o:o + 128] = np.eye(128); o += 128
    c[:, o:o + 128] = (p[:, None] <= p[None, :]); o += 128
    c[:, o:o + 128] = (p[:, None] >= p[None, :]); o += 128
    c[:, o:o + 128] = 1.0; o += 128
    c[:, o:o + 128] = (p[:, None] // 64 == p[None, :] // 64); o += 128
    c[:, o:o + 128] = (p[None, :] < 64); o += 128
    c[:, o:o + 128] = (p[None, :] >= 64); o += 128
    c[:, o] = p; o += 1
    c[:, o:o + 32] = np.arange(32)[None, :]; o += 32
    c[:, o:o + 64] = np.arange(64)[None, :]; o += 64
    return c


PC = {}
_o = 0
for _n, _w in [('n1g', 8), ('n2g', 8), ('adab', 48), ('hng', 8), ('convw', 8), ('convb', 2), ('lgb', 8),
               ('lam', 4), ('b2', 6), ('mgi', 6), ('mgf', 6)]:
    PC[_n] = (_o, _w)
    _o += _w
NPRM = _o


def pack_params(inp, l):
    P = np.zeros((128, NPRM), np.float32)

    def put(name, arr):
        o, w = PC[name]
        assert arr.shape == (128, w), (name, arr.shape)
        P[:, o:o + w] = arr
    put('n1g', inp['norm1_g'][l].reshape(8, 128).T)
    put('n2g', inp['norm2_g'][l].reshape(8, 128).T)
    put('adab', inp['ada_b'][l].reshape(48, 128).T)
    put('hng', inp['head_norm_g'][l].reshape(8, 128).T)
    cw = inp['conv_w'][l]
    put('convw', np.concatenate([cw[:, a * 128:(a + 1) * 128].T for a in range(2)], axis=1))
    put('convb', inp['conv_b'][l].reshape(2, 128).T)
    gb = inp['lru_gate_b'][l]
    put('lgb', np.stack([gb[d, g, a * 128:(a + 1) * 128] for a in range(2) for d in range(2) for g in range(2)], axis=1))
    lam = inp['lru_lambda'][l]
    put('lam', np.stack([lam[d, a * 128:(a + 1) * 128] for a in range(2) for d in range(2)], axis=1))
    b2 = inp['gla_b2'][l]
    put('b2', np.stack([b2[d, j * 128:(j + 1) * 128] for j in range(3) for d in range(2)], axis=1))
    mg = inp['mlstm_gate_b'][l]
    put('mgi', np.stack([np.repeat(mg[d, 0, 2 * j:2 * j + 2], 64) for j in range(3) for d in range(2)], axis=1))
    put('mgf', np.stack([np.repeat(mg[d, 1, 2 * j:2 * j + 2], 64) for j in range(3) for d in range(2)], axis=1))
    return P


def pack_gatew(inp, l):
    gw = inp['lru_gate_w'][l]
    out = np.zeros((128, 8, 128), np.float32)
    for a in range(2):
        for d in range(2):
            for g in range(2):
                i = a * 4 + d * 2 + g
                for bb in range(2):
                    out[bb * 64:(bb + 1) * 64, i, bb * 64:(bb + 1) * 64] = gw[d, g, a * 2 + bb]
    return out.reshape(128, 8 * 128)


def pack_wg24(inp, l):
    w = inp['w_in'][l]
    out = np.zeros((D, 120), np.float32)
    out[:, 64:88] = w[:, 3600:3624]
    out[:, 96:120] = w[:, 3600:3624]
    return out


def make_sel():
    sel = np.zeros((128, 12, 128), np.float32)
    for j in range(3):
        for d in range(2):
            for kind in range(2):
                idx = (j * 2 + d) * 2 + kind
                for h in range(2):
                    r = kind * 12 + d * 6 + 2 * j + h
                    sel[64 + r, idx, h * 64:(h + 1) * 64] = 1.0
                    sel[96 + r, idx, h * 64:(h + 1) * 64] = 1.0
    return sel.reshape(128, 12 * 128)


def pack_wrep(inp, l):
    w = inp['w_in'][l]
    blocks = []
    for j in range(3):
        for d in range(2):
            for kind in range(2):
                base = 3600 + kind * 12 + d * 6 + 2 * j
                blocks.append(np.repeat(w[:, base:base + 2], 64, axis=1))
    return np.ascontiguousarray(np.concatenate(blocks, axis=1))


def build(n_layers=DEPTH, dbg=None):
    nc = bass.Bass('TRN2', target_bir_lowering=False)

    def din(name, shape):
        return nc.dram_tensor(name, list(shape), F32, kind='ExternalInput').ap()
    x_d = din('x', [TL, D]); ctx_d = din('ctx', [TC, D]); cc_d = din('cc', [128, 16])
    adaw_d = din('ada_w', [DEPTH, D, 6 * D]); win_d = din('w_in', [DEPTH, D, DIN])
    wg24_d = din('wg24', [DEPTH, D, 120]); sel_d = din('sel', [128, 12 * 128]); gatew_d = din('gatew', [DEPTH, 128, 8 * 128])
    w2_d = din('gla_w2', [DEPTH, 2, 16, 384]); wout_d = din('w_out', [DEPTH, D, D])
    w1_d = din('mlp_w1', [DEPTH, D, DFF]); wm2_d = din('mlp_w2', [DEPTH, DFF, D])
    prm_d = din('prm', [DEPTH, 128, NPRM]); fg_d = din('fg', [128, D]); cst_d = din('cst', [128, NCST])
    out_d = nc.dram_tensor('out', [TL, D], F32, kind='ExternalOutput').ap()
    dbg_d = None
    if dbg:
        dbg_d = nc.dram_tensor('dbg', [128, dbg[1]], F32, kind='ExternalOutput').ap()

    s = Sched(nc)
    ARENA = 206 * 1024
    arena = nc.alloc_sbuf_tensor('arena', [128, ARENA], U8)
    psum = nc.alloc_psum_tensor('psum', [128, 4096], F32)

    class A:
        top = 0
        peak = 0
        last = 0

    def alloc(shape, dt, np_=128, at=None):
        n = 1
        for v in shape:
            n *= v
        nb = (n * _dsz(dt) + 31) // 32 * 32
        if at is None:
            off = A.top
            A.top += nb
            A.peak = max(A.peak, A.top)
        else:
            off = at
        A.last = off
        assert off + nb <= ARENA, ('arena overflow', off + nb)
        ap = arena[:, off:off + n * _dsz(dt)].bitcast(dt)
        if len(shape) == 2:
            ap = ap.rearrange('p (a b) -> p a b', b=shape[1])
        elif len(shape) == 3:
            ap = ap.rearrange('p (a b c) -> p a b c', b=shape[1], c=shape[2])
        return ap[0:np_] if np_ != 128 else ap

    def ps(off, n, dt=F32, np_=128, p0=0):
        if dt == F32:
            return psum[p0:p0 + np_, off:off + n]
        return psum[p0:p0 + np_, off:off + (n + 1) // 2].bitcast(dt)[:, 0:n]

    def R_(*aps):
        return [a for a in aps if a is not None and not isinstance(a, (int, float))]

    def act(out, in_, func, bias=None, scale=None, e='scalar'):
        kw = {}
        if bias is not None:
            kw['bias'] = bias
        if scale is not None:
            kw['scale'] = scale
        s.op(e, lambda en: en.activation(out=out, in_=in_, func=func, **kw), reads=R_(in_, bias, scale), writes=[out])

    def tt(out, in0, in1, op, e='vector'):
        s.op(e, lambda en: en.tensor_tensor(out=out, in0=in0, in1=in1, op=op), reads=[in0, in1], writes=[out])

    def ts(out, in0, s1, s2, op0, op1=None, e='vector'):
        if op1 is None:
            s.op(e, lambda en: en.tensor_scalar(out=out, in0=in0, scalar1=s1, scalar2=None, op0=op0),
                 reads=R_(in0, s1), writes=[out])
        else:
            s.op(e, lambda en: en.tensor_scalar(out=out, in0=in0, scalar1=s1, scalar2=s2, op0=op0, op1=op1),
                 reads=R_(in0, s1, s2), writes=[out])

    def stt(out, in0, sc, in1, op0, op1):
        s.op('vector', lambda en: en.scalar_tensor_tensor(out=out, in0=in0, scalar=sc, in1=in1, op0=op0, op1=op1),
             reads=R_(in0, sc, in1), writes=[out])

    def cp(out, in_, e='vector'):
        s.op(e, lambda en: en.tensor_copy(out=out, in_=in_), reads=[in_], writes=[out])

    def mset(out, v, e='vector'):
        s.op(e, lambda en: en.memset(out, v), writes=[out])

    def mmg(out, pairs):
        n = len(pairs)

        def fn(en):
            inst = None
            for i, (l, r) in enumerate(pairs):
                inst = en.matmul(out, lhsT=l, rhs=r, start=(i == 0), stop=(i == n - 1))
            return inst
        rd = []
        for l, r in pairs:
            rd += [l, r]
        s.op('tensor', fn, reads=rd, writes=[out])

    def trp(out, in_, ident):
        s.op('tensor', lambda en: en.transpose(out, in_, ident), reads=[in_, ident], writes=[out])

    def scan(out, d0, d1, init):
        s.op('vector', lambda en: en.tensor_tensor_scan(out=out, data0=d0, data1=d1, initial=init, op0=ALU.mult,
                                                         op1=ALU.add), reads=R_(d0, d1, init), writes=[out])

    def ld(out, in_, q='sync'):
        s.dma(q, out, in_, sb_writes=[out])

    X = alloc([8, T], F32)
    CST = alloc([NCST], F32)[:, 0, :] if False else alloc([1, NCST], F32)[:, 0, :]
    ld(CST, cst_d)
    IDf = CST[:, 0:128]
    PIDX = CST[:, 896:897]; RIDX = CST[:, 897:929]; CIDX = CST[:, 929:993]
    CB = alloc([7, 128], BF16)
    cp(CB, CST[:, 0:896].rearrange('p (a b) -> p a b', b=128))
    IDb, MASKF, MASKB, ONESb, BLK64, EAb, EBb = [CB[:, i, :] for i in range(7)]
    PRM = alloc([DEPTH, NPRM], F32)
    for l in range(DEPTH):
        ld(PRM[:, l, :], prm_d[l])
    CCT = alloc([1, 16], F32)[:, 0, :]
    ld(CCT, cc_d)
    CCb = alloc([8, 2], BF16)
    act(CCb, CCT.rearrange('p (k s) -> p k s', s=2), AF.Silu)
    MOD = alloc([DEPTH, 48, 2], F32)
    DRV = alloc([DEPTH, 4, 8, 2], F32) if False else None
    A1 = alloc([8, 2], F32); A2 = alloc([8, 2], F32)
    EPSC = alloc([1, 4], F32)[:, 0, :]
    mset(EPSC[:, 0:1], EPS)
    mset(EPSC[:, 1:2], 1.0)
    mset(EPSC[:, 2:3], 0.0)
    KAP = alloc([DEPTH, 4, 2], F32)
    NB = alloc([DEPTH, 18], F32)
    P_MARK = A.top

    def prm(l, name, i=0, w=1):
        o, _ = PC[name]
        return PRM[:, l, o + i:o + i + w]

    STG = [alloc([1, 1024], F32)[:, 0, :] for _ in range(2)]
    stg_rr = [0]

    def wload(out, src, e='gpsimd'):
        st = STG[stg_rr[0] % len(STG)]
        stg_rr[0] += 1
        shp = list(out.shape)
        np_ = shp[0]
        n = 1
        for v in shp[1:]:
            n *= v
        v_ = st[0:np_, 0:n]
        if len(shp) == 3:
            v_ = v_.rearrange('p (a b) -> p a b', b=shp[2])
        s.dma('sync', v_, src, sb_writes=[v_])
        if e == 'scalar':
            act(out, v_, AF.Copy)
        else:
            cp(out, v_, e=e)

    w1s_d = nc.dram_tensor('w1s', [DEPTH, 32, 128, 1024], BF16, kind='Internal').ap()
    w2s_d = nc.dram_tensor('w2s', [DEPTH, 32, 128, 1024], BF16, kind='Internal').ap()
    stok = {}
    cast_rr = [0]

    def wload_cached(out, src, scr, key):
        if key not in stok:
            e = ('vector', 'scalar')[cast_rr[0] % 2]
            cast_rr[0] += 1
            wload(out, src, e=e)
            dst = scr if len(out.shape) == 2 else scr.rearrange('p (k f) -> p k f', f=out.shape[2])
            stok[key] = s.dma('sync', dst, out, sb_reads=[out])
        else:
            tk_ = stok[key]
            for t_ in (tk_ if isinstance(tk_, list) else [tk_]):
                s.wait_tok('sync', t_)
            s.dma('sync', out, scr if len(out.shape) == 2 else scr.rearrange('p (k f) -> p k f', f=out.shape[2]),
                  sb_writes=[out])

    bg_jobs = []
    bg_bufs = {}
    bg_rr = [0]

    def bg_setup(l):
        del bg_jobs[:]
        for f in range(32):
            bg_jobs.append((w1_d[l][:, f * 128:(f + 1) * 128].rearrange('(k p) f -> p k f', p=128), w1s_d[l, f], ('w1', l, f), 3))
        for f in range(32):
            bg_jobs.append((wm2_d[l][f * 128:(f + 1) * 128, :], w2s_d[l, f], ('w2', l, f), 2))

    bg_pend = []

    def bg_flush():
        while bg_pend:
            scr, bf, key = bg_pend.pop(0)
            stok[key] = s.dma('sync', scr, bf, sb_reads=[bf])

    def bg_alloc():
        bg_bufs['st'] = [alloc([1, 1024], F32)[:, 0, :] for _ in range(2)]
        bg_bufs['bf'] = [alloc([1, 1024], BF16)[:, 0, :] for _ in range(2)]

    def bg(n):
        for _ in range(n):
            if not bg_jobs or 'st' not in bg_bufs:
                return
            src, scr, key, nd = bg_jobs.pop(0)
            i = bg_rr[0] % 2
            bg_rr[0] += 1
            st = bg_bufs['st'][i]; bf = bg_bufs['bf'][i]
            if nd == 3:
                st = st.rearrange('p (k f) -> p k f', f=128); bf = bf.rearrange('p (k f) -> p k f', f=128)
                scr = scr.rearrange('p (k f) -> p k f', f=128)
            s.dma('sync', st, src, sb_writes=[st])
            cp(bf, st, e='gpsimd')
            bg_flush()
            bg_pend.append((scr, bf, key))

    def convert_all(layers):
        for l in layers:
            w1toks = []
            for k in range(8):
                w1toks.append(s.dma_fresh(
                    'gpsimd', w1s_d[l][:, :, k * 128:(k + 1) * 128].rearrange('f p c -> p f c'),
                    w1_d[l][k * 128:(k + 1) * 128, :].rearrange('p (f c) -> p f c', c=128)))
            for f in range(32):
                stok[('w1', l, f)] = w1toks
            for g in range(2):
                tk = s.dma_fresh('gpsimd', w2s_d[l, g * 16:(g + 1) * 16],
                                 wm2_d[l][g * 2048:(g + 1) * 2048, :].rearrange('(f p) d -> f p d', p=128))
                for f in range(g * 16, (g + 1) * 16):
                    stok[('w2', l, f)] = tk

    def load_x():
        mark = A.top
        POSR = alloc([4, 32], F32); POSC = alloc([4, 64], F32)
        FRQ = alloc([1, 2], F32)[:, 0, :]
        for h in range(2):
            ts(FRQ[:, h:h + 1], PIDX, float(h * 128), None, ALU.add)
        act(FRQ, FRQ, AF.Exp, scale=-math.log(10000.0) / 256.0)
        TWO_PI = 2.0 * math.pi
        tmpi = alloc([1, 64], I32)[:, 0, :]; tmpf = alloc([1, 64], F32)[:, 0, :]; tmpa = alloc([1, 64], F32)[:, 0, :]
        for kk in range(8):
            h = kk % 2
            grp = kk // 2
            n = 32 if grp < 2 else 64
            idx = RIDX if grp < 2 else CIDX
            dst = POSR[:, kk, :] if grp < 2 else POSC[:, kk - 4, :]
            ph = 0.0 if grp % 2 == 0 else math.pi / 2
            ts(tmpa[:, 0:n], idx, FRQ[:, h:h + 1], ph, ALU.mult, ALU.add)
            ts(tmpf[:, 0:n], tmpa[:, 0:n], 1.0 / TWO_PI, None, ALU.mult)
            cp(tmpi[:, 0:n], tmpf[:, 0:n])
            cp(tmpf[:, 0:n], tmpi[:, 0:n])
            stt(tmpa[:, 0:n], tmpf[:, 0:n], -TWO_PI, tmpa[:, 0:n], ALU.mult, ALU.add)
            ts(tmpa[:, 0:n], tmpa[:, 0:n], 3.14159, -3.14159, ALU.min, ALU.max)
            act(dst, tmpa[:, 0:n], AF.Sin)
        ST = [alloc([1, D], F32)[:, 0, :] for _ in range(3)]
        for it in range(NCH):
            st = ST[it % 3]
            if it < 2:
                ld(st, ctx_d[it * 128:(it + 1) * 128, :])
            else:
                ld(st, x_d[(it - 2) * 128:(it - 1) * 128, :])
            for half in range(2):
                pb = ps(((it * 2 + half) % 4) * 512, 512)
                for q in range(4):
                    k = half * 4 + q
                    trp(pb[:, q * 128:(q + 1) * 128], st[:, k * 128:(k + 1) * 128], IDf)
                dstX = X[:, half * 4:half * 4 + 4, it * 128:(it + 1) * 128]
                if it < 2:
                    cp(dstX, pb.rearrange('p (q t) -> p q t', t=128))
                else:
                    r0 = (it - 2) * 2
                    for q in range(4):
                        k = half * 4 + q
                        o3 = X[:, k, it * 128:(it + 1) * 128].rearrange('p (r c) -> p r c', c=64)
                        i3 = pb[:, q * 128:(q + 1) * 128].rearrange('p (r c) -> p r c', c=64)
                        if k < 4:
                            pos = POSR[:, k, r0:r0 + 2].unsqueeze(2).broadcast_to([128, 2, 64])
                        else:
                            pos = POSC[:, k - 4, :].unsqueeze(1).broadcast_to([128, 2, 64])
                        tt(o3, i3, pos, ALU.add)
        A.top = mark

    CHUNKS = [(0, 256, 1)] + [(256 + 512 * i, 512, 0) for i in range(4)]
    pj_rr = [0]

    def pj_bank():
        b = pj_rr[0] % 4
        pj_rr[0] += 1
        return b * 512

    def ada(l):
        mark = A.top
        WB = [alloc([8, 128], BF16) for _ in range(4)]
        for _ in range(3):
            STG.append(alloc([1, 1024], F32)[:, 0, :])
        pm = ps(pj_bank(), 96)
        for ft in range(48):
            wb = WB[ft % 4]
            wload(wb, adaw_d[l][:, ft * 128:(ft + 1) * 128].rearrange('(k p) f -> p k f', p=128),
                  e=('vector', 'scalar')[ft % 2])
            mmg(pm[:, ft * 2:ft * 2 + 2], [(wb[:, k, :], CCb[:, k, :]) for k in range(8)])
        o, _ = PC['adab']
        tt(MOD[:, l], pm.rearrange('p (a b) -> p a b', b=2),
           PRM[:, l, o:o + 48].unsqueeze(2).broadcast_to([128, 48, 2]), ALU.add)
        del STG[2:]
        A.top = mark

    def derive(l):
        for (AA, nm, wh) in ((A1, 'n1g', 1), (A2, 'n2g', 4)):
            o, _ = PC[nm]
            g = PRM[:, l, o:o + 8].unsqueeze(2).broadcast_to([128, 8, 2])
            ts(AA, MOD[:, l, wh * 8:wh * 8 + 8, :], 1.0, None, ALU.add)
            tt(AA, AA, g, ALU.mult)
        o, _ = PC['lam']
        act(KAP[:, l, :, 0], PRM[:, l, o:o + 4], AF.Exp, scale=-1.0)
        act(KAP[:, l, :, 0], KAP[:, l, :, 0], AF.Ln, bias=EPSC[:, 1:2])
        ts(KAP[:, l, :, 1], KAP[:, l, :, 0], -16.0, None, ALU.mult)
        ts(KAP[:, l, :, 0], KAP[:, l, :, 0], -8.0, None, ALU.mult)
        o, _ = PC['b2']
        ts(NB[:, l, 0:6], PRM[:, l, o:o + 6], -1.0, None, ALU.mult)
        o, _ = PC['mgf']
        ts(NB[:, l, 6:12], PRM[:, l, o:o + 6], -1.0, None, ALU.mult)
        o, _ = PC['lgb']
        ts(NB[:, l, 12:18], PRM[:, l, o:o + 6], 1.0, None, ALU.mult)

    NML = int(dbg[0][2:]) if (dbg and dbg[0].startswith('nm')) else 9

    def norm_mod(dst, t0, n, s_, AA, BBl, BBwh, SQ, RS, TMP):
        for k in range(8):
            act(SQ[:, k, 0:n], X[:, k, t0:t0 + n], AF.Square)
        if NML < 2:
            return
        pb = ps(pj_bank(), n)
        mmg(pb, [(ONESb, SQ[:, k, 0:n]) for k in range(8)])
        if NML < 3:
            return
        act(RS[:, 0:n], pb, AF.Ln, bias=EPSC[:, 0:1], scale=1.0 / D)
        act(RS[:, 0:n], RS[:, 0:n], AF.Exp, scale=-0.5)
        if NML < 4:
            return
        TMl = TMP if isinstance(TMP, list) else [TMP]
        for k in range(8):
            TMP = TMl[k % len(TMl)]
            stt(TMP[:, 0:n], X[:, k, t0:t0 + n], AA[:, k, s_:s_ + 1], RS[:, 0:n], ALU.mult, ALU.mult)
            if NML < 5:
                continue
            act(dst[:, k, 0:n], TMP[:, 0:n], AF.Identity, bias=MOD[:, BBl, BBwh * 8 + k, s_:s_ + 1])

    def proj(HT, W, M, evac, chunks=CHUNKS):
        for (t0, n, s_) in chunks:
            pb = ps(pj_bank(), n, np_=M)
            mmg(pb, [(W[:, k, 0:M], HT[:, k, t0:t0 + n]) for k in range(8)])
            evac(pb, t0, n, s_)

    def finish_group(l, g8, OUT, G, Wo, last, SQ, Y, RS):
        mark = A.top
        act(SQ, OUT, AF.Square)
        RSl = RS if isinstance(RS, list) else [RS]
        for ic, (t0, n, s_) in enumerate(CHUNKS):
            RS = RSl[ic % len(RSl)]
            pb = ps(pj_bank(), n)
            mmg(pb, [(BLK64, SQ[:, t0:t0 + n])])
            act(RS[:, 0:n], pb, AF.Ln, bias=EPSC[:, 0:1], scale=1.0 / 64)
            act(RS[:, 0:n], RS[:, 0:n], AF.Exp, scale=-0.5)
            tt(RS[:, 0:n], RS[:, 0:n], OUT[:, t0:t0 + n], ALU.mult)
            stt(Y[:, t0:t0 + n], RS[:, 0:n], prm(l, 'hng', g8), G[:, t0:t0 + n], ALU.mult, ALU.mult)
        for (t0, n, s_) in CHUNKS:
            if last and s_ == 1:
                continue
            for k in range(8):
                pb = ps(pj_bank(), n)
                mmg(pb, [(Wo[:, k * 128:(k + 1) * 128], Y[:, t0:t0 + n])])
                stt(X[:, k, t0:t0 + n], pb, MOD[:, l, 2 * 8 + k, s_:s_ + 1], X[:, k, t0:t0 + n], ALU.mult, ALU.add)
        A.top = mark

    def group_a(l, a, HT, last):
        mark = A.top
        Wx = alloc([8, 128], BF16); wx_off = A.last
        Wg = alloc([8, 128], BF16)
        Wo = alloc([1, D], BF16, at=wx_off)[:, 0, :]
        GW = alloc([4, 128], BF16)
        wload(Wx, win_d[l][:, a * 128:(a + 1) * 128].rearrange('(k p) f -> p k f', p=128), e='scalar')
        wload(Wg, win_d[l][:, 256 + a * 128:256 + (a + 1) * 128].rearrange('(k p) f -> p k f', p=128), e='scalar')
        wload(GW, gatew_d[l][:, a * 512:(a + 1) * 512].rearrange('p (i c) -> p i c', c=128), e='scalar')
        XA = alloc([1, T], F32)[:, 0, :]
        H2 = XA
        XCv = alloc([1, T], F32)[:, 0, :]
        XCb = alloc([1, T], BF16)[:, 0, :]
        G = alloc([1, T], BF16)[:, 0, :]
        AB = alloc([1, T], F32)[:, 0, :]; ab_off = A.last
        BBf = alloc([1, T], F32)[:, 0, :]; bb_off = A.last
        H = alloc([1, T], F32)[:, 0, :]
        T1 = alloc([1, 512], F32)[:, 0, :]; T2 = alloc([1, 512], F32)[:, 0, :]
        T1b = alloc([1, 512], F32)[:, 0, :]
        proj(HT, Wx, 128, lambda pb, t0, n, s_: (cp(XA[:, t0:t0 + n], pb), bg(1)))
        wload(Wo, wout_d[l][a * 128:(a + 1) * 128, :], e='scalar')
        cut(1)
        ga_rr = [0]

        def gelu_ev(pb, t0, n, s_):
            ga_rr[0] += 1
            T1 = (T1b, T2)[ga_rr[0] % 2]
            bg(1)
            act(T1[:, 0:n], pb, AF.Square)
            ts(T1[:, 0:n], T1[:, 0:n], 0.044715, 1.0, ALU.mult, ALU.add)
            tt(T1[:, 0:n], T1[:, 0:n], pb, ALU.mult)
            act(T1[:, 0:n], T1[:, 0:n], AF.Sigmoid, scale=2.0 * math.sqrt(2.0 / math.pi))
            tt(G[:, t0:t0 + n], T1[:, 0:n], pb, ALU.mult)
        proj(HT, Wg, 128, gelu_ev)
        cut(2)
        o, _ = PC['convw']
        cw = lambda j: PRM[:, l, o + a * 4 + j:o + a * 4 + j + 1]
        for (s0, n) in ((0, TC), (TC, TL)):
            act(XCv[:, s0:s0 + n], XA[:, s0:s0 + n], AF.Identity, bias=prm(l, 'convb', a), scale=cw(2))
            for j, sh in ((0, -2), (1, -1), (3, 1)):
                lo = max(0, -sh); hi = n - max(0, sh)
                stt(XCv[:, s0 + lo:s0 + hi], XA[:, s0 + lo + sh:s0 + hi + sh], cw(j), XCv[:, s0 + lo:s0 + hi],
                    ALU.mult, ALU.add)
        act(XCb, XCv, AF.Copy)
        cut(3)
        T1A, T2A = T1, T2
        bg(2)
        for d in range(2):
            kap = KAP[:, l, a * 2 + d, 0:1]; kap2 = KAP[:, l, a * 2 + d, 1:2]
            o, _ = PC['lgb']
            br = PRM[:, l, o + a * 4 + d * 2:o + a * 4 + d * 2 + 1]
            bi = PRM[:, l, o + a * 4 + d * 2 + 1:o + a * 4 + d * 2 + 2]
            for (t0, n, s_) in CHUNKS:
                bg(2)
                pr = ps(pj_bank(), n)
                mmg(pr, [(GW[:, d * 2 + 0, :], XCb[:, t0:t0 + n])])
                pi = ps(pj_bank(), n)
                mmg(pi, [(GW[:, d * 2 + 1, :], XCb[:, t0:t0 + n])])
                act(AB[:, t0:t0 + n], pr, AF.Sigmoid, bias=br)
                act(BBf[:, t0:t0 + n], pi, AF.Sigmoid, bias=bi)
            for ic, (t0, n, s_) in enumerate(CHUNKS):
                T1 = T1A if ic % 2 == 0 else T1b
                act(T1[:, 0:n], AB[:, t0:t0 + n], AF.Exp, scale=kap2)
                act(AB[:, t0:t0 + n], AB[:, t0:t0 + n], AF.Exp, scale=kap)
                act(T1[:, 0:n], T1[:, 0:n], AF.Ln, bias=EPSC[:, 1:2], scale=-1.0)
                act(T1[:, 0:n], T1[:, 0:n], AF.Exp, scale=0.5)
                tt(T1[:, 0:n], T1[:, 0:n], BBf[:, t0:t0 + n], ALU.mult)
                tt(BBf[:, t0:t0 + n], T1[:, 0:n], XCv[:, t0:t0 + n], ALU.mult)
            if d == 0:
                scan(H, AB, BBf, 0.0)
            else:
                scan(H2[:, 0:TC][:, ::-1], AB[:, 0:TC][:, ::-1], BBf[:, 0:TC][:, ::-1], 0.0)
                scan(H2[:, TC:T][:, ::-1], AB[:, TC:T][:, ::-1], BBf[:, TC:T][:, ::-1], H2[:, 0:1])
        cut(4)
        tt(H, H, H2, ALU.add)
        bg(6)
        finish_group(l, a, H, G, Wo, last, alloc([1, T], BF16, at=ab_off)[:, 0, :],
                     alloc([1, T], BF16, at=bb_off)[:, 0, :], [T1A, T1b])
        if a == 1:
            bg(64)
        bg_flush()
        bg_bufs.clear()
        A.top = mark

    class Stop(Exception):
        pass
    PLIM = int(dbg[0][1:]) if (dbg and dbg[0][0] in 'pq') else 99

    def cut(n):
        if PLIM == n:
            raise Stop()

    def group_pair(l, kind, j, HT, LRT, last, G24=None):
        mark = A.top
        g8 = 2 + kind * 3 + j
        cq = (512 if kind == 0 else 2064) + j * 128
        ck = (896 if kind == 0 else 2448) + j * 128
        cv = (1280 if kind == 0 else 2832) + j * 128
        cg = (1664 if kind == 0 else 3216) + j * 128
        WB2 = [alloc([8, 128], BF16) for _ in range(2)]
        wb_rr = [0]

        def wcol(c0, src=None):
            Wb = WB2[wb_rr[0] % 2]
            wb_rr[0] += 1
            wload(Wb, (win_d[l] if src is None else src)[:, c0:c0 + 128].rearrange('(k p) f -> p k f', p=128), e='scalar')
            return Wb
        def gate_bc(idx, evac):
            Wb = WB2[wb_rr[0] % 2]
            wb_rr[0] += 1
            SEL = Wb[:, 0, :]
            wload(SEL, sel_d[:, idx * 128:(idx + 1) * 128], e='scalar')
            for (t0, n, s_) in CHUNKS:
                pb = ps(pj_bank(), n)
                mmg(pb, [(SEL[64:120, :], G24[64:120, t0:t0 + n])])
                evac(pb, t0, n, s_)
        Wo = alloc([1, D], BF16)[:, 0, :]
        wload(Wo, wout_d[l][g8 * 128:(g8 + 1) * 128, :])
        if kind == 0:
            W2 = alloc([2, 128], BF16, np_=16)
            for d in range(2):
                wload(W2[:, d, :], w2_d[l][d][:, j * 128:(j + 1) * 128])
        Qb = alloc([1, T], BF16)[:, 0, :]; qb_off = A.last
        Kb = alloc([1, T], BF16)[:, 0, :]; kb_off = A.last
        V1 = alloc([NCH, 130], BF16); VA = alloc([NCH, 128], BF16); VB = alloc([NCH, 128], BF16)
        OUT = alloc([1, T], F32)[:, 0, :]
        PL = alloc([1, T], F32)[:, 0, :]; pl_off = A.last
        QDs = [alloc([1, T], BF16)[:, 0, :] for _ in range(2)]
        KDs = [alloc([1, T], BF16)[:, 0, :] for _ in range(2)]
        DECs = [alloc([1, NCH], F32)[:, 0, :] for _ in range(2)]
        Rsts = [alloc([1, 130], F32)[:, 0, :] for _ in range(2)]
        Sbs = [alloc([2, 128], BF16) for _ in range(2)]
        nSbs = [alloc([2, 128], BF16) for _ in range(2)] if kind == 1 else [None, None]
        T1s = []
        T1bs = []
        for _ in range(2):
            T1s.append(alloc([1, 512], F32)[:, 0, :])
            T1bs.append(alloc([1, 512], BF16, at=A.last)[:, 0, :])
        SbFs = [alloc([2, 130], BF16) for _ in range(2)]
        T2s = [alloc([2, 128], F32) for _ in range(2)] if kind == 1 else [None, None]
        SCbs = [alloc([2, 256], BF16) for _ in range(2)]
        t1_rr = [0]

        def T1n():
            t1_rr[0] += 1
            return T1s[t1_rr[0] % 2]

        def T1bn():
            t1_rr[0] += 1
            return T1bs[t1_rr[0] % 2]
        proj(HT, wcol(cq), 128, lambda pb, t0, n, s_: act(Qb[:, t0:t0 + n], pb, AF.Copy, scale=0.125))
        proj(HT, wcol(ck), 128, lambda pb, t0, n, s_: cp(Kb[:, t0:t0 + n], pb))
        cut(1)
        mset(VA[:, :, 64:128], 0.0); mset(VB[:, :, 0:64], 0.0); mset(V1[:, :, 128:130], 1.0)
        Wv = wcol(cv)
        for cb in range(0, NCH, 4):
            nb4 = min(4, NCH - cb)
            pb = ps(pj_bank(), 512)
            for q in range(nb4):
                c = cb + q
                mmg(pb[:, q * 128:(q + 1) * 128], [(HT[:, k, c * 128:(c + 1) * 128], Wv[:, k, :]) for k in range(8)])
            cp(V1[:, cb:cb + nb4, 0:128], pb[:, 0:nb4 * 128].rearrange('p (q t) -> p q t', t=128))
        cp(VA[:, :, 0:64], V1[:, :, 0:64], e='gpsimd')
        cp(VB[:, :, 64:128], V1[:, :, 64:128], e='gpsimd')
        cut(2)
        sc_ = (1.0 / 16.0) if kind == 0 else 1.0
        for d in range(2):
            QD = QDs[d]; KD = KDs[d]; DEC = DECs[d]
            nb = NB[:, l, (0 if kind == 0 else 6) + j * 2 + d:(0 if kind == 0 else 6) + j * 2 + d + 1]

            def softplus_ev(pb, t0, n, s_):
                t1 = T1n()
                act(t1[:, 0:n], pb, AF.Exp, bias=nb, scale=-1.0)
                act(PL[:, t0:t0 + n], t1[:, 0:n], AF.Ln, bias=EPSC[:, 1:2])
            if kind == 0:
                for (t0, n, s_) in CHUNKS:
                    pb = ps(pj_bank(), n)
                    mmg(pb, [(W2[:, d, :], LRT[:, t0:t0 + n])])
                    softplus_ev(pb, t0, n, s_)
            else:
                gate_bc((j * 2 + d) * 2 + 1, softplus_ev)
            for c in range(NCH):
                plc = PL[:, c * 128:(c + 1) * 128]
                if d == 0:
                    scan(plc, ONESb, plc, 0.0)
                else:
                    scan(plc[:, ::-1], ONESb, plc[:, ::-1], 0.0)
            Pv = PL.rearrange('p (c t) -> p c t', t=128)
            pend = Pv[:, :, 127] if d == 0 else Pv[:, :, 0]
            act(DEC, pend, AF.Exp, scale=-sc_)
            if kind == 0:
                for (t0, n, s_) in CHUNKS:
                    t1 = T1bn()
                    act(t1[:, 0:n], PL[:, t0:t0 + n], AF.Exp, scale=-sc_)
                    tt(QD[:, t0:t0 + n], t1[:, 0:n], Qb[:, t0:t0 + n], ALU.mult)
                    t1 = T1bn()
                    act(t1[:, 0:n], PL[:, t0:t0 + n], AF.Exp, scale=sc_)
                    tt(KD[:, t0:t0 + n], t1[:, 0:n], Kb[:, t0:t0 + n], ALU.mult)
            else:
                for (t0, n, s_) in CHUNKS:
                    t1 = T1bn()
                    act(t1[:, 0:n], PL[:, t0:t0 + n], AF.Exp, scale=-1.0)
                    tt(QD[:, t0:t0 + n], t1[:, 0:n], Qb[:, t0:t0 + n], ALU.mult)
                bi_ = prm(l, 'mgi', j * 2 + d)

                def ig_ev(pb, t0, n, s_, KD=KD):
                    t1 = T1n()
                    tt(t1[:, 0:n], pb, PL[:, t0:t0 + n], ALU.add)
                    act(t1[:, 0:n], t1[:, 0:n], AF.Exp, bias=bi_)
                    tt(KD[:, t0:t0 + n], t1[:, 0:n], Kb[:, t0:t0 + n], ALU.mult)
                gate_bc((j * 2 + d) * 2 + 0, ig_ev)
        cut(3)
        G = alloc([1, T], BF16, at=pl_off)[:, 0, :]
        Wg = wcol(cg)
        if kind == 0:
            proj(HT, Wg, 128, lambda pb, t0, n, s_: act(G[:, t0:t0 + n], pb, AF.Silu))
        else:
            proj(HT, Wg, 128, lambda pb, t0, n, s_: act(G[:, t0:t0 + n], pb, AF.Sigmoid))
        KTs = [alloc([NCH, 128], BF16, at=qb_off), alloc([NCH, 128], BF16, at=kb_off)]
        for d in range(2):
            for cb in range(0, NCH, 4):
                nb4 = min(4, NCH - cb)
                pt = ps(pj_bank(), 512, dt=BF16)
                for q in range(nb4):
                    c = cb + q
                    trp(pt[:, q * 128:(q + 1) * 128], KDs[d][:, c * 128:(c + 1) * 128], IDb)
                src = pt[:, 0:nb4 * 128].rearrange('p (q t) -> p q t', t=128)
                if d == 0:
                    act(KTs[d][:, cb:cb + nb4, :], src, AF.Copy)
                else:
                    cp(KTs[d][:, cb:cb + nb4, :], src)
        cut(4)
        orders = [list(range(NCH)), [1, 0] + list(range(NCH - 1, 1, -1))]
        MASKS = [MASKF, MASKB]
        for d in range(2):
            mset(Sbs[d], 0.0)
            if kind == 1:
                mset(nSbs[d], 0.0)
        pj2 = [0]

        def pjb2():
            pj2[0] += 1
            return (pj2[0] % 2) * 512
        pus = {}

        def stageA(d, i):
            c = orders[d][i]
            tsl = slice(c * 128, (c + 1) * 128)
            par = i % 2
            QD = QDs[d]; KD = KDs[d]
            pscA = ps(2048, 128)
            pscB = ps(2560, 128)
            mmg(pscA, [(KD[0:64, tsl], QD[0:64, tsl])])
            mmg(pscB, [(KD[64:128, tsl], QD[64:128, tsl])])
            psc2 = psum[:, 2048:3072].rearrange('p (h t) -> p h t', t=512)[:, :, 0:128]
            tt(SCbs[d][:, par, :].rearrange('p (h t) -> p h t', t=128), psc2,
               MASKS[d].unsqueeze(1).broadcast_to([128, 2, 128]), ALU.mult)
            npu = 130 if kind == 1 else 128
            pu = ps((6 + d) * 512 + par * 256, npu)
            mmg(pu, [(KTs[d][:, c, :], V1[:, c, 0:npu])])
            pus[(d, i)] = pu

        def stageB1(d, i):
            c = orders[d][i]
            tsl = slice(c * 128, (c + 1) * 128)
            par = i % 2
            QD = QDs[d]; DEC = DECs[d]; Rst = Rsts[d]; Sb = Sbs[d]; nSb = nSbs[d]; SCb = SCbs[d]
            pu = pus.pop((d, i))
            npu = 130 if kind == 1 else 128
            Rst = Rst[:, 0:npu]
            if i == 0:
                cp(Rst, pu)
            else:
                cprev = orders[d][i - 1]
                stt(Rst, Rst, DEC[:, cprev:cprev + 1], pu, ALU.mult, ALU.add)
            if i + 1 < NCH:
                nx = (i + 1) % 2
                if kind == 0:
                    sbf = SbFs[d][:, nx, 0:128]
                    act(sbf, Rst, AF.Copy, scale=DEC[:, c:c + 1])
                    tt(Sb[:, nx, :], sbf[:, 0:128], BLK64, ALU.mult, e='gpsimd')
                else:
                    stt(Sb[:, nx, :], Rst[:, 0:128], DEC[:, c:c + 1], BLK64, ALU.mult, ALU.mult)
                    ts(nSb[:, nx, :], BLK64, Rst[:, 128:129], DEC[:, c:c + 1], ALU.mult, ALU.mult, e='gpsimd')
            pob = (d * 2 + par) * 512
            po = ps(pob, 128)
            mmg(po, [(Sb[:, i % 2, :], QD[:, tsl]), (VA[:, c, :], SCb[:, par, 0:128]),
                     (VB[:, c, :], SCb[:, par, 128:256])])
            if kind == 1:
                pd = ps(pob + 128, 128)
                mmg(pd, [(nSb[:, i % 2, :], QD[:, tsl]), (EAb, SCb[:, par, 0:128]), (EBb, SCb[:, par, 128:256])])

        def stageB2a(d, i):
            if kind == 0:
                return
            par = i % 2
            pd = ps((d * 2 + par) * 512 + 128, 128)
            T2 = T2s[d][:, par, :]
            act(T2, pd, AF.Abs)
            act(T2, T2, AF.Ln, bias=EPSC[:, 0:1])
            act(T2, T2, AF.Exp, scale=-1.0)

        def stageB2b(d, i):
            c = orders[d][i]
            tsl = slice(c * 128, (c + 1) * 128)
            par = i % 2
            po = ps((d * 2 + par) * 512, 128)
            if kind == 0:
                tt(OUT[:, tsl], po, OUT[:, tsl], ALU.add)
            else:
                T2 = T2s[d][:, par, :]
                stt(T2, T2, 1.0, po, ALU.min, ALU.mult)
                tt(OUT[:, tsl], OUT[:, tsl], T2, ALU.add, e='gpsimd')
        mset(OUT, 0.0, e='gpsimd')
        stageA(0, 0); stageA(1, 0)
        for i in range(NCH):
            if i + 1 < NCH:
                stageA(0, i + 1); stageA(1, i + 1)
            stageB1(0, i); stageB1(1, i)
            stageB2a(0, i); stageB2a(1, i)
            if i >= 1:
                stageB2b(0, i - 1); stageB2b(1, i - 1)
        stageB2b(0, NCH - 1); stageB2b(1, NCH - 1)
        cut(5)
        cut(6)
        finish_group(l, g8, OUT, G, Wo, last, KDs[0], alloc([1, T], BF16, at=qb_off)[:, 0, :], [T1s[0], T1s[1]])
        A.top = mark

    def mlp(l, last):
        mark = A.top
        NW = 6
        W1 = [alloc([8, 128], BF16) for _ in range(NW)]
        W2b = [alloc([1, D], BF16)[:, 0, :] for _ in range(NW)]
        H2 = alloc([8, 1024], BF16)
        A1T = alloc([32, 1024], BF16); a1t_off = A.last
        SQ = alloc([8, 512], BF16, at=a1t_off); RS = alloc([1, 512], F32)[:, 0, :]; TMP = alloc([1, 512], F32)[:, 0, :]
        TM3 = [alloc([1, 512], F32)[:, 0, :] for _ in range(3)]
        tm_rr = [0]
        passes = [(256, 1024, 0), (1280, 1024, 0)]
        if not last:
            passes = [(0, 256, 1)] + passes
        for (p0, pn, s_) in passes:
            subs = [(p0 + i, min(512, pn - i)) for i in range(0, pn, 512)]
            for (t0, n) in subs:
                norm_mod(H2[:, :, t0 - p0:t0 - p0 + n], t0, n, s_, A2, l, 3, SQ, RS, TM3)
            for f in range(32):
                w1 = W1[f % NW]
                wload_cached(w1, w1_d[l][:, f * 128:(f + 1) * 128].rearrange('(k p) f -> p k f', p=128),
                             w1s_d[l, f], ('w1', l, f))
                for (t0, n) in subs:
                    pb = ps(pj_bank(), n)
                    mmg(pb, [(w1[:, k, :], H2[:, k, t0 - p0:t0 - p0 + n]) for k in range(8)])
                    tm_rr[0] += 1
                    tm = TM3[tm_rr[0] % 3]
                    act(tm[:, 0:n], pb, AF.Relu)
                    tt(A1T[:, f, t0 - p0:t0 - p0 + n], tm[:, 0:n], tm[:, 0:n], ALU.mult)
            for (t0, n) in subs:
                for f in range(32):
                    w2 = W2b[f % NW]
                    wload_cached(w2, wm2_d[l][f * 128:(f + 1) * 128, :], w2s_d[l, f], ('w2', l, f))
                    for k in range(8):
                        pb = ps(k * 512, n)
                        s.op('tensor', (lambda en, pb=pb, w=w2[:, k * 128:(k + 1) * 128],
                                        r=A1T[:, f, t0 - p0:t0 - p0 + n], st_=(f == 0), sp_=(f == 31):
                                        en.matmul(pb, lhsT=w, rhs=r, start=st_, stop=sp_)),
                             reads=[w2[:, k * 128:(k + 1) * 128], A1T[:, f, t0 - p0:t0 - p0 + n]], writes=[pb])
                for k in range(8):
                    pb = ps(k * 512, n)
                    stt(X[:, k, t0:t0 + n], pb, MOD[:, l, 5 * 8 + k, s_:s_ + 1], X[:, k, t0:t0 + n], ALU.mult, ALU.add)
        del STG[2:]
        A.top = mark

    def final():
        mark = A.top
        FG = alloc([1, D], F32)[:, 0, :]
        ld(FG, fg_d)
        OT = [alloc([1, D], F32)[:, 0, :] for _ in range(2)]
        SS = alloc([1, 4], F32)[:, 0, :]
        JK = alloc([1, D], F32)[:, 0, :]
        toks = []
        for it in range(TL // 128):
            ot = OT[it % 2]
            t0 = TC + it * 128
            for half in range(2):
                pb = ps(((it * 2 + half) % 4) * 512, 512)
                for q in range(4):
                    k = half * 4 + q
                    trp(pb[:, q * 128:(q + 1) * 128], X[:, k, t0:t0 + 128], IDf)
                cp(ot[:, half * 512:(half + 1) * 512], pb)
            c0 = it % 2
            s.op('scalar', lambda en, ot=ot, c0=c0: en.activation(out=JK, in_=ot, func=AF.Square, accum_out=SS[:, c0:c0 + 1]),
                 reads=[ot], writes=[JK, SS[:, c0:c0 + 1]])
            act(SS[:, 2 + c0:3 + c0], SS[:, c0:c0 + 1], AF.Ln, bias=EPSC[:, 0:1], scale=1.0 / D)
            act(SS[:, 2 + c0:3 + c0], SS[:, 2 + c0:3 + c0], AF.Exp, scale=-0.5)
            stt(ot, ot, SS[:, 2 + c0:3 + c0], FG, ALU.mult, ALU.mult)
            toks.append(s.dma('sync', out_d[it * 128:(it + 1) * 128, :], ot, sb_reads=[ot]))
        for tk in toks:
            s.wait_tok('sync', tk)
        A.top = mark

    def dump(ap, ncol):
        mark = A.top
        DB = alloc([1, dbg[1]], F32)[:, 0, :]
        mset(DB, 0.0)
        cp(DB[:, 0:ncol], ap)
        tk = s.dma('sync', dbg_d, DB, sb_reads=[DB])
        s.wait_tok('sync', tk)
        A.top = mark

    load_x()
    convert_all([0])
    stage = dbg[0] if dbg else None
    if stage == 'x':
        dump(X[:, int(dbg[2]), :], T)
        n_layers = 0
    for l in range(n_layers):
        last = (l == DEPTH - 1)
        ada(l)
        if stage == 'ada':
            dump(MOD[:, l].rearrange('p a b -> p (a b)'), 96)
            break
        derive(l)
        if stage == 'drv':
            dump(A1.rearrange('p a b -> p (a b)'), 16)
            break
        mark_l = A.top
        HT = alloc([8, T], BF16)
        SQs = [alloc([8, 512], BF16) for _ in range(2)]
        RSs = [alloc([1, 512], F32)[:, 0, :] for _ in range(2)]
        TMPs = [alloc([1, 512], F32)[:, 0, :] for _ in range(3)]
        TMP = TMPs[0]
        for ic, (t0, n, s_) in enumerate(CHUNKS):
            norm_mod(HT[:, :, t0:t0 + n], t0, n, s_, A1, l, 0, SQs[ic % 2], RSs[ic % 2], TMPs)
        A.top = mark_l + 8 * T * 2
        if stage and (stage == 'h%d' % l or stage.startswith('nm')):
            mark = A.top
            HF = alloc([1, T], F32)[:, 0, :]
            cp(HF, HT[:, 0, :])
            dump(HF, T)
            A.top = mark
            break
        LRB = alloc([1, T], BF16)[:, 0, :]
        LRT = LRB[0:16]
        markw = A.top
        Wl = alloc([8, 16], BF16)
        wload(Wl, win_d[l][:, 2048:2064].rearrange('(k p) f -> p k f', p=128), e='scalar')
        proj(HT, Wl, 16, lambda pb, t0, n, s_: cp(LRT[:, t0:t0 + n], pb))
        W24 = alloc([8, 120], BF16)
        mset(LRB[64:128, :], 0.0)
        wload(W24, wg24_d[l].rearrange('(k p) f -> p k f', p=128), e='scalar')

        def g24_ev(pb, t0, n, s_):
            act(LRB[64:88, t0:t0 + n], pb[64:88, :], AF.Copy)
            act(LRB[96:120, t0:t0 + n], pb[96:120, :], AF.Copy)
            tt(LRB[96:120, t0:t0 + n], pb[96:120, :], LRB[96:120, t0:t0 + n], ALU.subtract)
        proj(HT, W24, 120, g24_ev)
        G24 = LRB
        A.top = markw
        bg_setup(l)
        groups = [('a', 0), ('a', 1)] + [('p', 0, j) for j in range(3)] + [('p', 1, j) for j in range(3)]
        if stage and stage.startswith('g'):
            groups = groups[:int(stage[1:]) + 1]
        if stage and stage[0] == 'p':
            groups = [('p', int(dbg[2]), 0)]
        if stage and stage[0] == 'q':
            groups = [('a', 0)]
        try:
            for g in groups:
                if g[0] == 'a':
                    group_a(l, g[1], HT, last)
                else:
                    group_pair(l, g[1], g[2], HT, LRT, last, G24)
        except Stop:
            pass
        A.top = mark_l
        if stage and (stage.startswith('g') or stage[0] in 'pq'):
            dump(X[:, int(dbg[2]), :], T)
            break
        if l + 1 < DEPTH:
            convert_all([l + 1])
        mlp(l, last)
        if stage == 'l%d' % l:
            dump(X[:, int(dbg[2]), :], T)
            break
    if not stage:
        final()
    s.emit()
    print('ninst', s.ninst, 'arena peak', A.peak, 'persistent', P_MARK)
    return nc


_NC = {}


def prep_inputs(inp):
    inp = {k: np.asarray(v) for k, v in inp.items()}
    f = lambda a: np.ascontiguousarray(a, dtype=np.float32)
    shared = {
        'ada_w': f(inp['ada_w']), 'w_in': f(inp['w_in']),
        'wg24': f(np.stack([pack_wg24(inp, l) for l in range(DEPTH)])), 'sel': make_sel(),
        'gatew': f(np.stack([pack_gatew(inp, l) for l in range(DEPTH)])),
        'gla_w2': f(inp['gla_w2']), 'w_out': f(inp['w_out']), 'mlp_w1': f(inp['mlp_w1']), 'mlp_w2': f(inp['mlp_w2']),
        'prm': f(np.stack([pack_params(inp, l) for l in range(DEPTH)])),
        'fg': f(np.broadcast_to(inp['final_g'][None, :], (128, D))), 'cst': make_consts(),
    }
    maps = []
    for b in range(8):
        cc = np.zeros((128, 16), np.float32)
        cc[:, 0::2] = inp['c'][b].reshape(8, 128).T
        cc[:, 1::2] = inp['c_ctx'].reshape(8, 128).T
        m = dict(shared)
        m['x'] = f(inp['x'][b]); m['ctx'] = f(inp['ctx'][b]); m['cc'] = cc
        maps.append(m)
    return maps


def kernel(**inputs):
    if 'nc' not in _NC:
        _NC['nc'] = build()
    maps = prep_inputs(inputs)
    res = run_bass_kernel_spmd(_NC['nc'], maps, core_ids=list(range(8)))
    return np.stack([np.asarray(res.results[b]['out'], dtype=np.float32) for b in range(8)], axis=0)
```
